# Optimizing a Trainium2 kernel written in Bass

```python
import jax, jax.numpy as jnp
from jax import lax
import numpy as np

D_MODEL = 1024
BATCH = 2
SEQ = 8192
DEPTH = 2

N_A_LAYERS = DEPTH // 2
N_B_LAYERS = DEPTH - N_A_LAYERS

EPS = 1e-6
PLE_DIM = 256
D_FF = ((8 * D_MODEL // 3 + 255) // 256) * 256

A_KEY_DIM = 128
A_HEADS = D_MODEL // A_KEY_DIM
A_VAL_DIM = D_MODEL // A_HEADS
A_WIDTH = A_HEADS * A_KEY_DIM
A_CHUNK = 64

B_HEADS = 16
B_NOPE = 128
B_ROPE = 64
B_VDIM = 128
B_Q_LORA = 512
B_KV_LORA = 256
Q_BLOCK = 128
ROPE_THETA = 10000.0

kernel_name = 'yoco_hgrn2_mla_macaron_ple'


def _rmsnorm(x, g):
    x32 = x.astype(jnp.float32)
    y = x32 * lax.rsqrt(jnp.mean(x32 * x32, axis=-1, keepdims=True) + EPS)
    return (y * g.astype(jnp.float32)).astype(x.dtype)


def _swiglu(h, w_in, w_out):
    gate, up = jnp.split(h @ w_in, 2, axis=-1)
    return (jax.nn.silu(gate) * up) @ w_out


def _rope(x, positions):
    d = x.shape[-1]
    inv_freq = 1.0 / (ROPE_THETA ** (jnp.arange(0, d, 2, dtype=jnp.float32) / d))
    ang = positions.astype(jnp.float32)[..., None] * inv_freq
    ang = ang.reshape(ang.shape[:2] + (1,) * (x.ndim - 3) + (d // 2,))
    cos, sin = jnp.cos(ang), jnp.sin(ang)
    x32 = x.astype(jnp.float32)
    x1, x2 = x32[..., : d // 2], x32[..., d // 2:]
    return jnp.concatenate([x1 * cos - x2 * sin, x2 * cos + x1 * sin], axis=-1).astype(x.dtype)


def _hgrn2_chunk(state, inp):
    q, k, v, g = inp
    b = jnp.cumsum(g, axis=2)
    o_inter = jnp.einsum('bhtk,bhkv->bhtv', q * jnp.exp(b), state)
    causal = jnp.tril(jnp.ones((A_CHUNK, A_CHUNK), dtype=bool))[:, :, None]
    diff = b[:, :, :, None, :] - b[:, :, None, :, :]
    decay = jnp.where(causal, jnp.exp(jnp.where(causal, diff, 0.0)), 0.0)
    scores = jnp.einsum('bhtk,bhsk,bhtsk->bhts', q, k, decay)
    o_intra = jnp.einsum('bhts,bhsv->bhtv', scores, v)
    b_last = b[:, :, -1:, :]
    new_state = jnp.exp(b_last[:, :, 0, :])[..., None] * state + jnp.einsum(
        'bhsk,bhsv->bhkv', k * jnp.exp(b_last - b), v)
    return new_state, o_inter + o_intra


def _hgrn2_mixer(h, w_in, lb, out_gain, w_out):
    bsz, seq, _ = h.shape
    q, f_logit, i, g = jnp.split(h @ w_in, 4, axis=-1)
    f_logit = f_logit.astype(jnp.float32)
    lb = lb.astype(jnp.float32)
    log_f = jnp.log(lb + (1.0 - lb) * jax.nn.sigmoid(f_logit))
    k = (1.0 - lb) * jax.nn.sigmoid(-f_logit)
    n_chunks = seq // A_CHUNK

    def to_chunks(t, d):
        t = t.astype(jnp.float32).reshape(bsz, n_chunks, A_CHUNK, A_HEADS, d)
        return t.transpose(1, 0, 3, 2, 4)

    qc = to_chunks(q, A_KEY_DIM) * (A_KEY_DIM ** -0.5)
    kc = to_chunks(k, A_KEY_DIM)
    vc = to_chunks(i, A_VAL_DIM)
    gc = to_chunks(log_f, A_KEY_DIM)
    s0 = jnp.zeros((bsz, A_HEADS, A_KEY_DIM, A_VAL_DIM), jnp.float32)
    _, o = lax.scan(_hgrn2_chunk, s0, (qc, kc, vc, gc))
    o = o.transpose(1, 0, 3, 2, 4).reshape(bsz, seq, A_HEADS, A_VAL_DIM)
    o = _rmsnorm(o, out_gain).reshape(bsz, seq, A_HEADS * A_VAL_DIM)
    o = o * jax.nn.sigmoid(g.astype(jnp.float32))
    return o.astype(h.dtype) @ w_out


def _mla_shared_kv(x, positions, norm_in, w_down, latent_norm, w_up):
    bsz, seq, _ = x.shape
    h = _rmsnorm(x, norm_in)
    c_kv, k_r = jnp.split(h @ w_down, [B_KV_LORA], axis=-1)
    c_kv = _rmsnorm(c_kv, latent_norm)
    kv = (c_kv @ w_up).reshape(bsz, seq, B_HEADS, B_NOPE + B_VDIM)
    k_nope, v = jnp.split(kv, [B_NOPE], axis=-1)
    k_rope = _rope(k_r, positions)
    return k_nope, k_rope, v


def _mla_mixer(h, positions, k_nope, k_rope, v, w_dq, q_norm, w_uq, w_out):
    bsz, seq, _ = h.shape
    c_q = _rmsnorm(h @ w_dq, q_norm)
    q = (c_q @ w_uq).reshape(bsz, seq, B_HEADS, B_NOPE + B_ROPE)
    q_nope, q_rope = jnp.split(q, [B_NOPE], axis=-1)
    q_rope = _rope(q_rope, positions)
    n_blocks = seq // Q_BLOCK
    scale = (B_NOPE + B_ROPE) ** -0.5

    def blocks(t):
        return t.reshape((bsz, n_blocks, Q_BLOCK) + t.shape[2:]).swapaxes(0, 1)

    key_idx = jnp.arange(seq)

    def attend(args):
        qn, qr, blk = args
        s = (jnp.einsum('bqhd,bkhd->bhqk', qn, k_nope)
             + jnp.einsum('bqhd,bkd->bhqk', qr, k_rope)).astype(jnp.float32) * scale
        q_idx = blk * Q_BLOCK + jnp.arange(Q_BLOCK)
        mask = q_idx[:, None] >= key_idx[None, :]
        s = jnp.where(mask[None, None], s, -jnp.inf)
        pr = jax.nn.softmax(s, axis=-1).astype(v.dtype)
        return jnp.einsum('bhqk,bkhd->bqhd', pr, v)

    o = lax.map(attend, (blocks(q_nope), blocks(q_rope), jnp.arange(n_blocks, dtype=jnp.int32)))
    o = o.swapaxes(0, 1).reshape(bsz, seq, B_HEADS * B_VDIM)
    return o @ w_out


def setup_inputs(seed: int = 0) -> dict:
    key = jax.random.key(seed)
    ks = jax.random.split(key, 24)
    f32 = jnp.float32

    def w(k, shape, fan_in):
        return jax.random.normal(k, shape, f32) * (fan_in ** -0.5)

    def gain(k, shape):
        return 1.0 + 0.02 * jax.random.normal(k, shape, f32)

    offset = jax.random.randint(ks[2], (BATCH, 1), 0, 1024, dtype=jnp.int32)
    positions = offset + jnp.arange(SEQ, dtype=jnp.int32)[None, :]
    return {
        'x': jax.random.normal(ks[0], (BATCH, SEQ, D_MODEL), f32),
        'p': jax.random.normal(ks[1], (DEPTH, BATCH, SEQ, PLE_DIM), f32),
        'positions': positions,
        'norm_gains': gain(ks[3], (DEPTH, 4, D_MODEL)),
        'ffn_w_in': w(ks[4], (DEPTH, 2, D_MODEL, 2 * D_FF), D_MODEL),
        'ffn_w_out': w(ks[5], (DEPTH, 2, D_FF, D_MODEL), D_FF),
        'ple_w_gate': w(ks[6], (DEPTH, D_MODEL, D_MODEL), D_MODEL),
        'ple_w_in': w(ks[7], (DEPTH, PLE_DIM, D_MODEL), PLE_DIM),
        'a_w_in': w(ks[8], (N_A_LAYERS, D_MODEL, 4 * A_WIDTH), D_MODEL),
        'a_lb_logits': 0.1 * jax.random.normal(ks[9], (N_A_LAYERS + 1, A_WIDTH), f32),
        'a_out_gain': gain(ks[10], (N_A_LAYERS, A_VAL_DIM)),
        'a_w_out': w(ks[11], (N_A_LAYERS, A_WIDTH, D_MODEL), A_WIDTH),
        'kv_norm_in': gain(ks[12], (D_MODEL,)),
        'kv_w_down': w(ks[13], (D_MODEL, B_KV_LORA + B_ROPE), D_MODEL),
        'kv_latent_norm': gain(ks[14], (B_KV_LORA,)),
        'kv_w_up': w(ks[15], (B_KV_LORA, B_HEADS * (B_NOPE + B_VDIM)), B_KV_LORA),
        'b_w_dq': w(ks[16], (N_B_LAYERS, D_MODEL, B_Q_LORA), D_MODEL),
        'b_q_norm': gain(ks[17], (N_B_LAYERS, B_Q_LORA)),
        'b_w_uq': w(ks[18], (N_B_LAYERS, B_Q_LORA, B_HEADS * (B_NOPE + B_ROPE)), B_Q_LORA),
        'b_w_out': w(ks[19], (N_B_LAYERS, B_HEADS * B_VDIM, D_MODEL), B_HEADS * B_VDIM),
        'final_norm': gain(ks[20], (D_MODEL,)),
    }


def reference(x, p, positions, norm_gains, ffn_w_in, ffn_w_out, ple_w_gate, ple_w_in,
              a_w_in, a_lb_logits, a_out_gain, a_w_out,
              kv_norm_in, kv_w_down, kv_latent_norm, kv_w_up,
              b_w_dq, b_q_norm, b_w_uq, b_w_out, final_norm):
    lower_bounds = jnp.cumsum(jax.nn.softmax(a_lb_logits.astype(jnp.float32), axis=0), axis=0)
    shared = None
    if N_A_LAYERS == 0:
        shared = _mla_shared_kv(x, positions, kv_norm_in, kv_w_down, kv_latent_norm, kv_w_up)
    for li in range(DEPTH):
        g = norm_gains[li]
        x = x + 0.5 * _swiglu(_rmsnorm(x, g[0]), ffn_w_in[li, 0], ffn_w_out[li, 0])
        h = _rmsnorm(x, g[1])
        if li < N_A_LAYERS:
            x = x + _hgrn2_mixer(h, a_w_in[li], lower_bounds[li], a_out_gain[li], a_w_out[li])
        else:
            bi = li - N_A_LAYERS
            k_nope, k_rope, v = shared
            x = x + _mla_mixer(h, positions, k_nope, k_rope, v,
                               b_w_dq[bi], b_q_norm[bi], b_w_uq[bi], b_w_out[bi])
        x = x + 0.5 * _swiglu(_rmsnorm(x, g[2]), ffn_w_in[li, 1], ffn_w_out[li, 1])
        gate = jax.nn.sigmoid((_rmsnorm(x, g[3]) @ ple_w_gate[li]).astype(jnp.float32))
        x = x + (gate * (p[li] @ ple_w_in[li]).astype(jnp.float32)).astype(x.dtype)
        if li == N_A_LAYERS - 1:
            shared = _mla_shared_kv(x, positions, kv_norm_in, kv_w_down, kv_latent_norm, kv_w_up)
    return _rmsnorm(x, final_norm)
```

```python
import contextlib
import numpy as np
import ml_dtypes
import concourse.bass as bass
import concourse.mybir as mybir
from concourse.bass_utils import run_bass_kernel_spmd

F32 = mybir.dt.float32
BF16 = mybir.dt.bfloat16
I32 = mybir.dt.int32
AF = mybir.ActivationFunctionType
ALU = mybir.AluOpType

NDSEM = 8
NCORES = 8
D = 1024
KC = 8
DFF = 2816
NJ = 22
SEQ = 8192
TOK = 2048
G = 1024
NT = 512
EPS = 1e-6
NH_A = 8
CH = 64
PLE = 256
NH_B = 16
QL = 512
KVL = 256
ROPE = 64


class Buf:
    __slots__ = ("w", "r", "name", "relaxed", "wl", "pr")

    def __init__(self, name=""):
        self.w = None
        self.r = []
        self.name = name
        self.relaxed = False
        self.wl = {}
        self.pr = []


class T:
    def __init__(self, t, name):
        self.t = t
        self.b = Buf(name)

    def __getitem__(self, idx):
        return self.t[idx]


class TV(T):
    def __init__(self, fn, name):
        self.fn = fn
        self.b = Buf(name)

    def __getitem__(self, idx):
        return self.fn(idx)


def _b(x):
    return x.b if isinstance(x, T) else x


class Prog:
    def __init__(self):
        self.nc = bass.Bass("TRN2", target_bir_lowering=False)
        self.es = contextlib.ExitStack()
        nc = self.nc
        self.eng = {"pe": nc.tensor, "act": nc.scalar, "dve": nc.vector, "pool": nc.gpsimd, "sp": nc.sync}
        self.q = {k: [] for k in self.eng}
        self.cnt = {k: 0 for k in self.eng}
        self.seen = {k: {} for k in self.eng}
        self.esem = {}
        for k in ("pe", "act", "dve", "pool"):
            self.esem[k] = self.es.enter_context(nc.semaphore("es_" + k))
        self.dsem = {}
        self.dcnt = {}
        self.dtok = {}
        for k in ("sp", "act", "pool"):
            self.dsem[k] = [self.es.enter_context(nc.semaphore("ds_%s%d" % (k, i))) for i in range(NDSEM)]
            self.dcnt[k] = 0
            self.dtok[k] = [None] * NDSEM
        self.same_engine_sync = True
        self.nins = 0
        self.defer = None
        self.ccsem = self.es.enter_context(nc.semaphore("ccsem"))
        self.ccn = 0
        self.ARENA = 103936
        self.arena = self.es.enter_context(nc.sbuf_tensor("arena", [128, self.ARENA], BF16))
        self.atop = 0
        self.ps = [self.psum("ps%d" % i, [128, NT], F32) for i in range(8)]

    def dram(self, name, shape, dtype, kind):
        return T(self.nc.dram_tensor(name, list(shape), dtype, kind=kind), name)

    def sbuf(self, name, shape, dtype):
        shape = list(shape)
        n = int(np.prod(shape[1:]))
        nb = n * (4 if dtype in (F32, I32) else 2)
        nb = (nb + 63) // 64 * 64
        assert self.atop + nb <= self.ARENA * 2, "arena overflow: %s needs %d, top %d" % (name, nb, self.atop)
        ap = self.arena[:, self.atop // 2:(self.atop + nb) // 2]
        self.atop += nb
        self.hiwater = max(getattr(self, "hiwater", 0), self.atop)
        if dtype != BF16:
            ap = ap.bitcast(dtype)
        ap = ap[:, 0:n]
        if len(shape) == 3:
            ap = ap.rearrange("p (a b) -> p a b", b=shape[2])
        return T(ap, name)

    def mark(self):
        return self.atop

    def reset(self, m):
        self.barrier()
        print("arena high-water %d of %d bytes" % (getattr(self, "hiwater", 0), self.ARENA * 2))
        self.hiwater = 0
        self.atop = m

    def coll(self, kind, src, dst, reads, writes):
        waits = self._filter("pool", self._deps(reads, writes))
        self.ccn += 1
        tok = (self.ccsem, self.ccn)

        def run(engh, waits=waits, src=src, dst=dst, kind=kind):
            for (s_, v) in waits:
                engh.wait_ge(s_, v)
            engh.collective_compute(kind, ALU.bypass, replica_groups=[[0, 1, 2, 3], [4, 5, 6, 7]], ins=[src], outs=[dst]).then_inc(self.ccsem, 1)

        self.q["pool"].append(run)
        self._commit(tok, reads, writes)
        self.nins += 1
        return tok

    def gather(self, out, in_, idx, reads=(), writes=()):
        qn = "pool"
        n = self.dcnt[qn]
        slot = n % NDSEM
        sem = self.dsem[qn][slot]
        val = 16 * (n // NDSEM + 1)
        deps = self._deps(reads, writes)
        if self.dtok[qn][slot] is not None:
            deps.append(self.dtok[qn][slot])
        waits = self._filter(qn, deps)
        tok = (sem, val)
        self.dtok[qn][slot] = tok
        self.dcnt[qn] += 1

        def run(engh, waits=waits, out=out, in_=in_, idx=idx, sem=sem):
            for (s_, v) in waits:
                engh.wait_ge(s_, v)
            engh.indirect_dma_start(out=out, out_offset=None, in_=in_, in_offset=bass.IndirectOffsetOnAxis(ap=idx, axis=0)).then_inc(sem, 16)

        self.q[qn].append(run)
        self._commit(tok, reads, writes)
        self.nins += 1
        return tok

    def psum(self, name, shape, dtype):
        return T(self.es.enter_context(self.nc.psum_tensor(name, list(shape), dtype)), name)

    def _deps(self, reads, writes, e=None):
        deps = []
        own = self.esem.get(e)

        def add(b, tok):
            if b.relaxed and own is not None and tok[0] is own:
                return
            deps.append(tok)

        for b in reads:
            b = _b(b)
            if b.relaxed:
                for tk in b.wl.values():
                    add(b, tk)
            elif b.w is not None:
                add(b, b.w)
        for b in writes:
            b = _b(b)
            if b.relaxed:
                for tk in (b.r if b.r else b.pr):
                    add(b, tk)
                continue
            if b.w is not None:
                add(b, b.w)
            for tk in b.r:
                add(b, tk)
        return deps

    def _filter(self, e, deps):
        seen = self.seen[e]
        best = {}
        for (s, v) in deps:
            if e == "pe" and s is self.esem["pe"]:
                continue
            if (not self.same_engine_sync) and e in self.esem and s is self.esem[e]:
                continue
            k = id(s)
            if seen.get(k, 0) >= v:
                continue
            if k not in best or best[k][1] < v:
                best[k] = (s, v)
        for k, (s, v) in best.items():
            seen[k] = v
        return list(best.values())

    def _commit(self, tok, reads, writes):
        for b in reads:
            _b(b).r.append(tok)
        for b in writes:
            b = _b(b)
            b.w = tok
            if b.relaxed:
                if b.r:
                    b.wl = {}
                    b.pr = b.r
                k = id(tok[0])
                if k not in b.wl or b.wl[k][1] < tok[1]:
                    b.wl[k] = tok
            b.r = []

    def op(self, e, fn, reads=(), writes=()):
        if self.defer is not None:
            self.defer.append(lambda: self._op(e, fn, reads, writes))
            return None
        return self._op(e, fn, reads, writes)

    def _op(self, e, fn, reads=(), writes=()):
        waits = self._filter(e, self._deps(reads, writes, e))
        sem = self.esem[e]
        self.cnt[e] += 1
        tok = (sem, self.cnt[e])

        def run(engh, waits=waits, fn=fn, sem=sem):
            for (s, v) in waits:
                engh.wait_ge(s, v)
            fn(engh).then_inc(sem, 1)

        self.q[e].append(run)
        self._commit(tok, reads, writes)
        self.nins += 1
        return tok

    def dma(self, qn, out, in_, reads=(), writes=(), **kw):
        if self.defer is not None:
            self.defer.append(lambda: self._dma(qn, out, in_, reads, writes, **kw))
            return None
        return self._dma(qn, out, in_, reads, writes, **kw)

    def run_streams(self, streams):
        idx = [0] * len(streams)
        live = True
        while live:
            live = False
            for i, st in enumerate(streams):
                if idx[i] < len(st):
                    st[idx[i]]()
                    idx[i] += 1
                    live = True

    def _dma(self, qn, out, in_, reads=(), writes=(), **kw):
        n = self.dcnt[qn]
        slot = n % NDSEM
        sem = self.dsem[qn][slot]
        val = 16 * (n // NDSEM + 1)
        deps = self._deps(reads, writes)
        if self.dtok[qn][slot] is not None:
            deps.append(self.dtok[qn][slot])
        waits = self._filter(qn, deps)
        tok = (sem, val)
        self.dtok[qn][slot] = tok
        self.dcnt[qn] += 1

        def run(engh, waits=waits, out=out, in_=in_, sem=sem, kw=kw):
            for (s, v) in waits:
                engh.wait_ge(s, v)
            engh.dma_start(out=out, in_=in_, **kw).then_inc(sem, 16)

        self.q[qn].append(run)
        self._commit(tok, reads, writes)
        self.nins += 1
        return tok

    def _alltoks(self):
        deps = []
        for qn in self.dtok:
            deps.extend([t for t in self.dtok[qn] if t is not None])
        for e in self.esem:
            if self.cnt[e]:
                deps.append((self.esem[e], self.cnt[e]))
        return deps

    def barrier(self):
        deps = self._alltoks()
        for e in ("pe", "act", "dve", "pool", "sp"):
            waits = self._filter(e, deps)
            if waits:
                def run(engh, waits=waits):
                    for (s, v) in waits:
                        engh.wait_ge(s, v)
                self.q[e].append(run)

    def finish(self):
        waits = self._filter("sp", self._alltoks())

        def run(engh, waits=waits):
            for (s, v) in waits:
                engh.wait_ge(s, v)

        self.q["sp"].append(run)
        nc = self.nc
        with nc.Block() as block:
            for k, name in (("sp", "sync"), ("pe", "tensor"), ("act", "scalar"), ("dve", "vector"), ("pool", "gpsimd")):
                if not self.q[k]:
                    continue

                def body(engh, fns=self.q[k]):
                    for f in fns:
                        f(engh)

                getattr(block, name)(body)
        self.es.close()
        return nc


class Ctx:
    def __init__(self, P, ncons=56):
        self.P = P
        self.ps = P.ps
        self.cons = P.sbuf("cons", [128, ncons], F32)
        self.gs = P.sbuf("gs", [128, ncons], F32)
        self.ones = P.sbuf("ones", [128, 128], BF16)
        self.x = P.sbuf("x", [128, KC, G], F32)
        self.h = P.sbuf("h", [128, KC, G], BF16)
        self.sq = P.sbuf("sq", [128, KC, NT], BF16)
        self.rstd = P.sbuf("rstd", [128, NT], F32)
        self.rstd2 = P.sbuf("rstd2", [128, NT], F32)
        self.su = [P.sbuf("su%d" % i, [128, NT], F32) for i in range(2)]
        self.act = P.sbuf("act", [128, NJ, G], BF16)
        self.wsl = [P.sbuf("wsl%d" % i, [128, 3072], BF16) for i in range(4)]
        self.wi = 0
        self.sg = [P.sbuf("sg%d" % i, [128, NT], F32) for i in range(2)]
        P.op("pool", lambda e: e.memset(self.ones[:], 1.0), writes=[self.ones])
        for tt in (self.x, self.h, self.act, self.sq):
            tt.b.relaxed = True

    def wslot(self):
        s = self.wsl[self.wi % 4]
        self.wi += 1
        return s


def rmsnorm_fm(C, src, srcT, kcn, n0, n, gcol, dst, dstT, pb, dn, src_all=None):
    P = C.P
    sq = C.sq
    if src_all is not None:
        P.op("act", lambda e: e.activation(out=sq[:, 0:kcn, 0:n], in_=src_all, func=AF.Square), reads=[srcT], writes=[sq])
    else:
        for kc in range(kcn):
            P.op("act", lambda e, kc=kc: e.activation(out=sq[:, kc, 0:n], in_=src(kc), func=AF.Square),
                 reads=[srcT], writes=[sq])
    for kc in range(kcn):
        P.op("pe", lambda e, kc=kc: e.matmul(pb[:, 0:n], lhsT=C.ones[:], rhs=sq[:, kc, 0:n],
                                             start=(kc == 0), stop=(kc == kcn - 1)),
             reads=[C.ones, sq], writes=[pb])
    P.op("act", lambda e: e.activation(out=C.rstd[:, 0:n], in_=pb[:, 0:n], func=AF.Ln, bias=float(dn * EPS)),
         reads=[pb], writes=[C.rstd])
    P.op("act", lambda e: e.activation(out=C.rstd[:, 0:n], in_=C.rstd[:, 0:n], func=AF.Exp, scale=-0.5),
         reads=[C.rstd], writes=[C.rstd])
    for kc in range(kcn):
        if kcn == KC and kc % 2 == 1:
            tmp = C.sg[(kc // 2) % 2]
            P.op("act", lambda e, kc=kc, tmp=tmp: e.activation(out=tmp[:, 0:n], in_=src(kc), func=AF.Identity, scale=gcol(kc)),
                 reads=[srcT, C.gs], writes=[tmp])
            P.op("pool", lambda e, kc=kc, tmp=tmp: e.tensor_tensor(out=dst(kc), in0=tmp[:, 0:n], in1=C.rstd[:, 0:n], op=ALU.mult),
                 reads=[tmp, C.rstd], writes=[dstT])
        else:
            P.op("dve", lambda e, kc=kc: e.scalar_tensor_tensor(out=dst(kc), in0=src(kc), scalar=gcol(kc),
                                                                in1=C.rstd[:, 0:n], op0=ALU.mult, op1=ALU.mult),
                 reads=[srcT, C.gs, C.rstd], writes=[dstT])


def ffn_group(C, win, wout, gcol0, nt=G // NT):
    P = C.P
    rstds = [C.rstd, C.rstd2]
    for t in range(nt):
        sl = slice(t * NT, (t + 1) * NT)
        for kc in range(KC):
            if kc % 2 == 0:
                P.op("dve", lambda e, kc=kc, sl=sl: e.tensor_scalar(out=C.h[:, kc, sl], in0=C.x[:, kc, sl], scalar1=C.gs[:, gcol0 + kc:gcol0 + kc + 1],
                                                                   scalar2=None, op0=ALU.mult), reads=[C.x, C.gs], writes=[C.h])
            else:
                P.op("act", lambda e, kc=kc, sl=sl: e.activation(out=C.h[:, kc, sl], in_=C.x[:, kc, sl], func=AF.Identity,
                                                                 scale=C.gs[:, gcol0 + kc:gcol0 + kc + 1]), reads=[C.x, C.gs], writes=[C.h])
    for t in range(nt):
        sl = slice(t * NT, (t + 1) * NT)
        rs = rstds[t % 2]
        pb = C.ps[6 + t % 2]
        P.op("act", lambda e, sl=sl: e.activation(out=C.sq[:, :, 0:NT], in_=C.x[:, :, sl], func=AF.Square), reads=[C.x], writes=[C.sq])
        for kc in range(KC):
            P.op("pe", lambda e, kc=kc, pb=pb: e.matmul(pb[:], lhsT=C.ones[:], rhs=C.sq[:, kc, 0:NT], start=(kc == 0), stop=(kc == KC - 1)),
                 reads=[C.ones, C.sq], writes=[pb])
        P.op("act", lambda e, pb=pb, rs=rs: e.activation(out=rs[:], in_=pb[:], func=AF.Ln, bias=float(D * EPS)), reads=[pb], writes=[rs])
        P.op("act", lambda e, rs=rs: e.activation(out=rs[:], in_=rs[:], func=AF.Exp, scale=-0.5), reads=[rs], writes=[rs])
    pi = 0
    for j in range(NJ):
        ws = C.wslot()
        P.dma("pool", ws[:, 0:2048], win[j], reads=[win], writes=[ws])
        for t in range(nt):
            pg = C.ps[(pi % 2) * 2]
            pu = C.ps[(pi % 2) * 2 + 1]
            sg = C.sg[pi % 2]
            su = C.su[pi % 2]
            rs = rstds[t % 2]
            pi += 1
            for half, pb in ((0, pg), (1, pu)):
                for kc in range(KC):
                    P.op("pe", lambda e, kc=kc, half=half, pb=pb, ws=ws, t=t: e.matmul(
                        pb[:], lhsT=ws[:, kc * 256 + half * 128:kc * 256 + half * 128 + 128],
                        rhs=C.h[:, kc, t * NT:(t + 1) * NT], start=(kc == 0), stop=(kc == KC - 1)),
                        reads=[ws, C.h], writes=[pb])
            P.op("dve", lambda e, pg=pg, sg=sg, rs=rs: e.tensor_tensor(out=sg[:], in0=pg[:], in1=rs[:], op=ALU.mult), reads=[pg, rs], writes=[sg])
            P.op("act", lambda e, sg=sg: e.activation(out=sg[:], in_=sg[:], func=AF.Silu), reads=[sg], writes=[sg])
            P.op("dve", lambda e, pu=pu, su=su, rs=rs: e.tensor_tensor(out=su[:], in0=pu[:], in1=rs[:], op=ALU.mult), reads=[pu, rs], writes=[su])
            P.op("pool", lambda e, su=su, sg=sg, j=j, t=t: e.tensor_tensor(
                out=C.act[:, j, t * NT:(t + 1) * NT], in0=sg[:], in1=su[:], op=ALU.mult),
                reads=[sg, su], writes=[C.act])
    for m in range(KC):
        ws = C.wslot()
        P.dma("pool", ws[:, 0:NJ * 128], wout[m], reads=[wout], writes=[ws])
        for t in range(nt):
            pb = C.ps[4 + (pi % 2)]
            pi += 1
            for j in range(NJ):
                P.op("pe", lambda e, j=j, pb=pb, ws=ws, t=t: e.matmul(
                    pb[:], lhsT=ws[:, j * 128:(j + 1) * 128], rhs=C.act[:, j, t * NT:(t + 1) * NT],
                    start=(j == 0), stop=(j == NJ - 1)), reads=[ws, C.act], writes=[pb])
            P.op("dve", lambda e, pb=pb, m=m, t=t: e.scalar_tensor_tensor(
                out=C.x[:, m, t * NT:(t + 1) * NT], in0=pb[:], scalar=0.5, in1=C.x[:, m, t * NT:(t + 1) * NT],
                op0=ALU.mult, op1=ALU.add), reads=[pb, C.x], writes=[C.x])


def tile_ffn_in(w):
    wg = w[:, :DFF].reshape(KC, 128, NJ, 128)
    wu = w[:, DFF:].reshape(KC, 128, NJ, 128)
    s = np.stack([wg, wu], axis=3)
    return np.ascontiguousarray(s.transpose(2, 1, 0, 3, 4).reshape(NJ, 128, KC * 256))


def tile_ffn_out(w):
    return np.ascontiguousarray(w.reshape(NJ, 128, KC, 128).transpose(2, 1, 0, 3).reshape(KC, 128, NJ * 128))


def tile_lin(w):
    k, n = w.shape
    return np.ascontiguousarray(w.reshape(k // 128, 128, n // 128, 128).transpose(2, 1, 0, 3).reshape(n // 128, 128, k))


def tile_rows(w):
    k, n = w.shape
    return np.ascontiguousarray(w.reshape(k // 128, 128, n).transpose(1, 0, 2))


def col(v):
    return np.ascontiguousarray(v.reshape(-1, 128).T)


def fm(xtok):
    return np.ascontiguousarray(xtok.T)


def lin_fm(C, wdram, chunk, kcn, t0, n, pb, wslot=None, src=None):
    P = C.P
    src = C.h if src is None else src
    if wslot is None:
        wslot = C.wslot()
        P.dma("pool", wslot[:, 0:kcn * 128], wdram[chunk], reads=[wdram], writes=[wslot])
    for kc in range(kcn):
        P.op("pe", lambda e, kc=kc: e.matmul(pb[:, 0:n], lhsT=wslot[:, kc * 128:(kc + 1) * 128],
                                             rhs=src[:, kc, t0:t0 + n], start=(kc == 0), stop=(kc == kcn - 1)),
             reads=[wslot, src], writes=[pb])
    return wslot


def phase1(P, C, t, stage=99):
    NC1 = 32
    xin = t["xin"]
    win = t["win"]
    wout = t["wout"]
    wq = t["wq"]
    wf = t["wf"]
    wg = t["wg"]
    wv = t["wv"]
    cons = t["cons"]
    identd = t["ident"]
    hmaskd = t["hmask"]
    x1o = t["x1o"]
    oTo = t["oTo"]
    qco = t["qco"]
    sgo = t["sgo"]
    STo = t["STo"]
    DTo = t["DTo"]
    ident = P.sbuf("identb", [128, 128], BF16)
    hmask = P.sbuf("hmaskb", [128, 512], F32)
    wvs = P.sbuf("wvs", [128, KC, D], BF16)
    vtok = P.sbuf("vtok", [128, G // 128, D], BF16)
    lb = P.sbuf("lb", [128, 8], F32)
    oml = P.sbuf("oml", [128, 8], F32)
    noml = P.sbuf("noml", [128, 8], F32)
    rmask = P.sbuf("rmask", [128, NT], F32)
    zer8 = P.sbuf("zer8", [128, 8], F32)
    actf = C.act[:].rearrange("p j t -> p (j t)").bitcast(F32)
    scr = [[T(actf[:, (s_ * 7 + i) * NT:(s_ * 7 + i + 1) * NT], "hs%d_%d" % (s_, i)) for i in range(7)] for s_ in range(2)]
    qt = [P.sbuf("qt%d" % i, [128, NT], BF16) for i in range(2)]
    kt = [P.sbuf("kt%d" % i, [128, NT], BF16) for i in range(2)]
    kh = [P.sbuf("kh%d" % i, [128, NT], BF16) for i in range(2)]
    khT = [[P.sbuf("khT%d_%d" % (i, par), [128, NT], BF16) for par in range(2)] for i in range(2)]
    scm = [P.sbuf("scm%d" % i, [128, NT], BF16) for i in range(2)]
    qc = [P.sbuf("qc%d" % i, [128, NT], BF16) for i in range(2)]
    oev = [P.sbuf("oev%d" % i, [128, NT], F32) for i in range(2)]
    sgs = C.sg
    dch = [P.sbuf("dch%d" % i, [128, 8], F32) for i in range(2)]
    S = [P.sbuf("S%d" % i, [128, 128], F32) for i in range(NH_A)]
    Sb = [[P.sbuf("Sb%d_%d" % (i, c), [128, 128], BF16) for c in range(9)] for i in range(2)]
    Dexc = [P.sbuf("Dexc%d" % i, [128, 9], F32) for i in range(NH_A)]
    Slast = [P.sbuf("Slast%d" % i, [128, 128], BF16) for i in range(NH_A)]

    P.dma("sp", C.cons[:, 0:NC1], cons[:], reads=[cons], writes=[C.cons])
    P.dma("pool", ident[:], identd[:], writes=[ident])
    P.dma("sp", hmask[:], hmaskd[:], writes=[hmask])
    P.op("dve", lambda e: e.tensor_scalar(out=C.gs[:, 0:NC1], in0=C.cons[:, 0:NC1], scalar1=float(np.sqrt(D)), scalar2=None, op0=ALU.mult),
         reads=[C.cons], writes=[C.gs])
    P.op("dve", lambda e: e.tensor_tensor(out=lb[:], in0=C.cons[:, 16:24], in1=C.cons[:, 24:32], op=ALU.subtract), reads=[C.cons], writes=[lb])
    P.op("act", lambda e: e.activation(out=oml[:], in_=lb[:], func=AF.Sigmoid, scale=-1.0), reads=[lb], writes=[oml])
    P.op("act", lambda e: e.activation(out=lb[:], in_=lb[:], func=AF.Sigmoid), reads=[lb], writes=[lb])
    P.op("dve", lambda e: e.tensor_scalar(out=noml[:], in0=oml[:], scalar1=-1.0, scalar2=None, op0=ALU.mult), reads=[oml], writes=[noml])
    P.op("pool", lambda e: e.memset(rmask[:], 1.0), writes=[rmask])
    P.op("pool", lambda e: e.memset(rmask[:].rearrange("p (c t) -> p c t", t=CH)[:, :, 0:1], 0.0), writes=[rmask])
    P.op("pool", lambda e: e.memset(zer8[:], 0.0), writes=[zer8])
    for i in range(2):
        for par in range(2):
            P.op("pool", lambda e, i=i, par=par: e.memset(khT[i][par][:], 0.0), writes=[khT[i][par]])
    for hd in range(NH_A):
        P.op("pool", lambda e, hd=hd: e.memset(S[hd][:], 0.0), writes=[S[hd]])
        P.op("pool", lambda e, hd=hd: e.memset(Slast[hd][:], 0.0), writes=[Slast[hd]])
        P.op("pool", lambda e, hd=hd: e.memset(Dexc[hd][:], 1.0), writes=[Dexc[hd]])

    it = 0
    for g in range(TOK // G):
        P.dma("sp", C.x[:], xin[:, :, g * G:(g + 1) * G], reads=[xin], writes=[C.x])
        ffn_group(C, win, wout, 0)
        P.dma("sp", x1o[:, :, g * G:(g + 1) * G], C.x[:], reads=[C.x], writes=[x1o])
        P.barrier()
        for t in range(G // NT):
            rmsnorm_fm(C, lambda kc, t=t: C.x[:, kc, t * NT:(t + 1) * NT], C.x, KC, t * NT, NT,
                       lambda kc: C.gs[:, 8 + kc:9 + kc],
                       lambda kc, t=t: C.h[:, kc, t * NT:(t + 1) * NT], C.h, C.ps[7], D, src_all=C.x[:, :, t * NT:(t + 1) * NT])
        if g == 0:
            for kc in range(KC):
                P.dma("pool", wvs[:, kc, :], wv[:, kc * D:(kc + 1) * D], reads=[wv], writes=[wvs])
        for tb in range(G // 128):
            for half in range(2):
                pb = C.ps[(tb * 2 + half) % 4]
                for kc in range(KC):
                    P.op("pe", lambda e, kc=kc, tb=tb, half=half, pb=pb: e.matmul(
                        pb[:], lhsT=C.h[:, kc, tb * 128:(tb + 1) * 128], rhs=wvs[:, kc, half * 512:(half + 1) * 512],
                        start=(kc == 0), stop=(kc == KC - 1)), reads=[C.h, wvs], writes=[pb])
                P.op("act", lambda e, tb=tb, half=half, pb=pb: e.copy(out=vtok[:, tb, half * 512:(half + 1) * 512], in_=pb[:]),
                     reads=[pb], writes=[vtok])
        if stage < 2:
            continue
        for c in range(8):
            ws = None
            for t in range(G // NT):
                pb = C.ps[(c * 2 + t) % 4]
                ws = lin_fm(C, wg, c, KC, t * NT, NT, pb, wslot=ws)
                sg = sgs[(c * 2 + t) % 2]
                P.op("act", lambda e, pb=pb, sg=sg: e.activation(out=sg[:], in_=pb[:], func=AF.Sigmoid), reads=[pb], writes=[sg])
                P.dma("sp", sgo[:, c, g * G + t * NT:g * G + (t + 1) * NT], sg[:], reads=[sg], writes=[sgo])
        if stage < 3:
            continue
        def front(hd, t, i2):
            sig, fv, kk, bb, eb, enb, ek = scr[i2]
            tok0 = g * G + t * NT
            pq = C.ps[4 * i2]
            pf = C.ps[4 * i2 + 1]
            wsq = C.wsl[2 * i2]
            wsf = C.wsl[2 * i2 + 1]
            if t == 0:
                P.dma("pool", wsq[:, 0:KC * 128], wq[hd], reads=[wq], writes=[wsq])
                P.dma("pool", wsf[:, 0:KC * 128], wf[hd], reads=[wf], writes=[wsf])
            lin_fm(C, wq, hd, KC, t * NT, NT, pq, wslot=wsq)
            lin_fm(C, wf, hd, KC, t * NT, NT, pf, wslot=wsf)
            c3 = lambda a: a[:].rearrange("p (c t) -> p c t", t=CH)
            P.op("act", lambda e, pf=pf: e.activation(out=sig[:], in_=pf[:], func=AF.Sigmoid), reads=[pf], writes=[sig])
            P.op("dve", lambda e, hd=hd: e.tensor_scalar(out=fv[:], in0=sig[:], scalar1=oml[:, hd:hd + 1], scalar2=lb[:, hd:hd + 1],
                                                       op0=ALU.mult, op1=ALU.add), reads=[sig, oml, lb], writes=[fv])
            P.op("act", lambda e: e.activation(out=fv[:], in_=fv[:], func=AF.Ln), reads=[fv], writes=[fv])
            P.op("dve", lambda e, hd=hd: e.tensor_scalar(out=kk[:], in0=sig[:], scalar1=noml[:, hd:hd + 1], scalar2=oml[:, hd:hd + 1],
                                                       op0=ALU.mult, op1=ALU.add), reads=[sig, noml, oml], writes=[kk])
            P.op("dve", lambda e: e.tensor_tensor_scan(out=bb[:], data0=rmask[:], data1=fv[:], initial=0.0, op0=ALU.mult, op1=ALU.add),
                 reads=[rmask, fv], writes=[bb])
            P.op("act", lambda e: e.activation(out=eb[:], in_=bb[:], func=AF.Exp), reads=[bb], writes=[eb])
            P.op("act", lambda e: e.activation(out=enb[:], in_=bb[:], func=AF.Exp, scale=-1.0), reads=[bb], writes=[enb])
            P.op("dve", lambda e, pq=pq, i2=i2: e.scalar_tensor_tensor(out=qt[i2][:], in0=pq[:], scalar=float(128 ** -0.5), in1=eb[:],
                                                                      op0=ALU.mult, op1=ALU.mult), reads=[pq, eb], writes=[qt[i2]])
            P.op("pool", lambda e, i2=i2: e.tensor_tensor(out=kt[i2][:], in0=kk[:], in1=enb[:], op=ALU.mult), reads=[kk, enb], writes=[kt[i2]])
            P.op("dve", lambda e: e.tensor_tensor(out=c3(ek), in0=c3(bb)[:, :, CH - 1:CH].to_broadcast([128, 8, CH]), in1=c3(bb),
                                                  op=ALU.subtract), reads=[bb], writes=[ek])
            P.op("act", lambda e: e.activation(out=ek[:], in_=ek[:], func=AF.Exp), reads=[ek], writes=[ek])
            P.op("pool", lambda e, i2=i2: e.tensor_tensor(out=kh[i2][:], in0=kk[:], in1=ek[:], op=ALU.mult), reads=[kk, ek], writes=[kh[i2]])
            P.op("act", lambda e, i2=i2: e.activation(out=dch[i2][:], in_=c3(bb)[:, :, CH - 1], func=AF.Exp), reads=[bb], writes=[dch[i2]])
            P.op("dve", lambda e, hd=hd, i2=i2: e.tensor_tensor_scan(out=Dexc[hd][:, 1:9], data0=dch[i2][:], data1=zer8[:],
                                                                     initial=Dexc[hd][:, 0:1], op0=ALU.mult, op1=ALU.add),
                 reads=[dch[i2], zer8, Dexc[hd]], writes=[Dexc[hd]])
            P.op("dve", lambda e, hd=hd, i2=i2: e.tensor_tensor(out=c3(qc[i2]), in0=c3(qt[i2]),
                                                                in1=Dexc[hd][:, 0:8].unsqueeze(2).to_broadcast([128, 8, CH]), op=ALU.mult),
                 reads=[qt[i2], Dexc[hd]], writes=[qc[i2]])
            P.dma("sp", qco[hd, :, tok0:tok0 + NT], qc[i2][:], reads=[qc[i2]], writes=[qco])
            P.op("dve", lambda e, hd=hd: e.tensor_copy(out=Dexc[hd][:, 0:1], in_=Dexc[hd][:, 8:9]), reads=[Dexc[hd]], writes=[Dexc[hd]])


        def back(hd, t, i2):
            tok0 = g * G + t * NT
            ptr = C.ps[4 * i2 + 2]
            ptrb = ptr[:].bitcast(BF16)
            for pr in range(4):
                P.op("pe", lambda e, pr=pr, i2=i2: e.transpose(ptrb[:, pr * 128:(pr + 1) * 128], kh[i2][:, pr * 128:(pr + 1) * 128], ident[:]),
                     reads=[kh[i2], ident], writes=[ptr])
            for par in range(2):
                P.op("act", lambda e, i2=i2, par=par: e.copy(out=khT[i2][par][par * 64:(par + 1) * 64, :], in_=ptrb[par * 64:(par + 1) * 64, 0:NT]),
                     reads=[ptr], writes=[khT[i2][par]])
            psc = C.ps[4 * i2 + 2]
            for pr in range(4):
                P.op("pe", lambda e, pr=pr, i2=i2: e.matmul(psc[:, pr * 128:(pr + 1) * 128], lhsT=kt[i2][:, pr * 128:(pr + 1) * 128],
                                                            rhs=qt[i2][:, pr * 128:(pr + 1) * 128], start=True, stop=True),
                     reads=[kt[i2], qt[i2]], writes=[psc])
            P.op("dve", lambda e, i2=i2: e.tensor_tensor(out=scm[i2][:], in0=psc[:], in1=hmask[:], op=ALU.mult), reads=[psc, hmask], writes=[scm[i2]])
            pU = C.ps[4 * i2 + 3]
            P.op("pool", lambda e, i2=i2, hd=hd: e.tensor_copy(out=Sb[i2][0][:], in_=Slast[hd][:]), reads=[Slast[hd]], writes=[Sb[i2][0]])
            for half in range(2):
                for c4 in range(4):
                    c = half * 4 + c4
                    pr, par = c // 2, c % 2
                    tb = t * 4 + pr
                    P.op("pe", lambda e, c4=c4, pr=pr, par=par, tb=tb, i2=i2, hd=hd: e.matmul(
                        pU[:, c4 * 128:(c4 + 1) * 128], lhsT=khT[i2][par][:, pr * 128:(pr + 1) * 128],
                        rhs=vtok[:, tb, hd * 128:(hd + 1) * 128], start=True, stop=True),
                        reads=[khT[i2][par], vtok], writes=[pU])
                for c4 in range(4):
                    c = half * 4 + c4
                    P.op("dve", lambda e, c=c, c4=c4, i2=i2, hd=hd: e.scalar_tensor_tensor(
                        out=S[hd][:], in0=S[hd][:], scalar=dch[i2][:, c:c + 1], in1=pU[:, c4 * 128:(c4 + 1) * 128],
                        op0=ALU.mult, op1=ALU.add), reads=[S[hd], dch[i2], pU], writes=[S[hd]])
                    dst = Sb[i2][c + 1] if c < 7 else Slast[hd]
                    P.op("pool", lambda e, dst=dst, hd=hd: e.tensor_copy(out=dst[:], in_=S[hd][:]), reads=[S[hd]], writes=[dst])
            po = C.ps[4 * i2 + 2]
            for pr in range(4):
                tb = t * 4 + pr
                P.op("pe", lambda e, pr=pr, tb=tb, i2=i2, hd=hd: e.matmul(po[:, pr * 128:(pr + 1) * 128], lhsT=vtok[:, tb, hd * 128:(hd + 1) * 128],
                                                                          rhs=scm[i2][:, pr * 128:(pr + 1) * 128], start=True, stop=False),
                     reads=[vtok, scm[i2]], writes=[po])
                for par in range(2):
                    c = pr * 2 + par
                    P.op("pe", lambda e, c=c, i2=i2: e.matmul(po[:, c * 64:(c + 1) * 64], lhsT=Sb[i2][c][:], rhs=qt[i2][:, c * 64:(c + 1) * 64],
                                                              start=False, stop=(c % 2 == 1)), reads=[Sb[i2][c], qt[i2]], writes=[po])
            P.op("act", lambda e, i2=i2: e.copy(out=oev[i2][:], in_=po[:]), reads=[po], writes=[oev[i2]])
            P.dma("sp", oTo[hd, :, tok0:tok0 + NT], oev[i2][:], reads=[oev[i2]], writes=[oTo])


        streams = []
        for s_ in range(2):
            P.defer = []
            for hd in range(s_, NH_A, 2):
                for t in range(G // NT):
                    front(hd, t, s_)
                    back(hd, t, s_)
            streams.append(P.defer)
            P.defer = None
        P.run_streams(streams)
        P.barrier()
    for hd in range(NH_A):
        P.dma("sp", STo[:, hd, :], S[hd][:], reads=[S[hd]], writes=[STo])
        P.op("dve", lambda e, hd=hd: e.tensor_copy(out=zer8[:, hd:hd + 1], in_=Dexc[hd][:, 0:1]), reads=[Dexc[hd]], writes=[zer8])
    P.dma("sp", DTo[:], zer8[:], reads=[zer8], writes=[DTo])


NC2 = 56


def ple_group(C, wgate, wple, pin, g, gcol0, pbf):
    P = C.P
    nt = G // NT
    for t in range(nt):
        rmsnorm_fm(C, lambda kc, t=t: C.x[:, kc, t * NT:(t + 1) * NT], C.x, KC, t * NT, NT,
                   lambda kc: C.gs[:, gcol0 + kc:gcol0 + kc + 1],
                   lambda kc, t=t: C.h[:, kc, t * NT:(t + 1) * NT], C.h, C.ps[7], D, src_all=C.x[:, :, t * NT:(t + 1) * NT])
    for kc in range(2):
        P.dma("pool", pbf[:, kc, :], pin[:, kc, g * G:(g + 1) * G], reads=[pin], writes=[pbf])
    i = 0
    for m in range(KC):
        wsg = None
        wsp = None
        for t in range(nt):
            pg = C.ps[(i % 2) * 2]
            pp = C.ps[(i % 2) * 2 + 1]
            sg = C.sg[i % 2]
            i += 1
            wsg = lin_fm(C, wgate, m, KC, t * NT, NT, pg, wslot=wsg)
            wsp = lin_fm(C, wple, m, 2, t * NT, NT, pp, wslot=wsp, src=pbf)
            P.op("act", lambda e, pg=pg, sg=sg: e.activation(out=sg[:], in_=pg[:], func=AF.Sigmoid), reads=[pg], writes=[sg])
            P.op("dve", lambda e, pp=pp, sg=sg: e.tensor_tensor(out=sg[:], in0=sg[:], in1=pp[:], op=ALU.mult), reads=[sg, pp], writes=[sg])
            P.op("pool", lambda e, sg=sg, m=m, t=t: e.tensor_tensor(out=C.x[:, m, t * NT:(t + 1) * NT], in0=C.x[:, m, t * NT:(t + 1) * NT],
                                                                  in1=sg[:], op=ALU.add), reads=[sg, C.x], writes=[C.x])


def rope_table(C, posd, tok0, n, tab, tmpf, tmpi, invcol, phcol):
    P = C.P
    P.dma("sp", tmpi[:, 0:n], posd[0:1, tok0:tok0 + n].to_broadcast([128, n]), reads=[posd], writes=[tmpi])
    P.op("dve", lambda e: e.tensor_copy(out=tab[:, 0:n], in_=tmpi[:, 0:n]), reads=[tmpi], writes=[tab])
    P.op("dve", lambda e: e.tensor_scalar(out=tab[:, 0:n], in0=tab[:, 0:n], scalar1=C.cons[:, invcol:invcol + 1],
                                          scalar2=C.cons[:, phcol:phcol + 1], op0=ALU.mult, op1=ALU.add), reads=[tab, C.cons], writes=[tab])
    P.op("dve", lambda e: e.tensor_scalar(out=tmpf[:, 0:n], in0=tab[:, 0:n], scalar1=float(1 / (2 * np.pi)), scalar2=None, op0=ALU.mult),
         reads=[tab], writes=[tmpf])
    P.op("dve", lambda e: e.tensor_copy(out=tmpi[:, 0:n], in_=tmpf[:, 0:n]), reads=[tmpf], writes=[tmpi])
    P.op("dve", lambda e: e.tensor_copy(out=tmpf[:, 0:n], in_=tmpi[:, 0:n]), reads=[tmpi], writes=[tmpf])
    P.op("dve", lambda e: e.scalar_tensor_tensor(out=tab[:, 0:n], in0=tmpf[:, 0:n], scalar=float(-2 * np.pi), in1=tab[:, 0:n],
                                                 op0=ALU.mult, op1=ALU.add), reads=[tmpf, tab], writes=[tab])
    P.op("dve", lambda e: e.tensor_scalar(out=tmpf[:, 0:n], in0=tab[:, 0:n], scalar1=float(np.pi), scalar2=float(-2 * np.pi),
                                          op0=ALU.is_gt, op1=ALU.mult), reads=[tab], writes=[tmpf])
    P.op("dve", lambda e: e.tensor_tensor(out=tab[:, 0:n], in0=tab[:, 0:n], in1=tmpf[:, 0:n], op=ALU.add), reads=[tab, tmpf], writes=[tab])
    P.op("dve", lambda e: e.tensor_scalar(out=tmpf[:, 0:n], in0=tab[:, 0:n], scalar1=float(-np.pi), scalar2=float(2 * np.pi),
                                          op0=ALU.is_lt, op1=ALU.mult), reads=[tab], writes=[tmpf])
    P.op("dve", lambda e: e.tensor_tensor(out=tab[:, 0:n], in0=tab[:, 0:n], in1=tmpf[:, 0:n], op=ALU.add), reads=[tab, tmpf], writes=[tab])
    P.op("act", lambda e: e.activation(out=tab[:, 0:n], in_=tab[:, 0:n], func=AF.Sin), reads=[tab], writes=[tab])


def phase2(P, C, t, dbg=False):
    x1i = t["x1i"]
    oTi = t["oTi"]
    qci = t["qci"]
    sgi = t["sgi"]
    STi = t["STi"]
    DTi = t["DTi"]
    pin = t["pin"]
    posd = t["pos"]
    cons = t["cons"]
    stk = t["stk"]
    wao = t["wao"]
    win0 = t["win0"]
    wout0 = t["wout0"]
    wpg = t["wpg"]
    wpi = t["wpi"]
    wkd = t["wkd"]
    win1 = t["win1"]
    wout1 = t["wout1"]
    wdq = t["wdq"]
    x5o = t["x5o"]
    lat_dst = t["lat_dst"]
    lat_done = t["lat_done"]
    Sst = P.sbuf("Sst", [128, NH_A, 128], F32)
    Sstb = P.sbuf("Sstb", [128, NH_A, 128], BF16)
    Ul = P.sbuf("Ul", [128, NH_A, 128], F32)
    Dl = P.sbuf("Dl", [128, NH_A], F32)
    ot = [P.sbuf("ot%d" % i, [128, NT], F32) for i in range(2)]
    qcl = [P.sbuf("qcl%d" % i, [128, NT], BF16) for i in range(2)]
    sgl = [P.sbuf("sgl%d" % i, [128, NT], F32) for i in range(2)]
    pbf = P.sbuf("pbf", [128, 2, G], BF16)
    latf = P.sbuf("latf", [128, 4, NT], F32)
    latb = P.sbuf("latb", [128, 4, NT], BF16)
    tab = P.sbuf("tab", [128, G], F32)
    tmpf = P.sbuf("tmpf", [128, G], F32)
    tmpi = P.sbuf("tmpi", [128, G], I32)
    stkb = P.sbuf("stkb", [128, 64], BF16)
    rT = P.sbuf("rT", [128, NT], BF16)
    krb = P.sbuf("krb", [64, NT], BF16)

    P.dma("sp", C.cons[:, 0:NC2], cons[:], reads=[cons], writes=[C.cons])
    P.dma("pool", stkb[:], stk[:], writes=[stkb])
    for (a, b, sc) in ((0, 40, np.sqrt(D)), (40, 42, np.sqrt(KVL)), (42, 46, np.sqrt(QL)), (46, 47, np.sqrt(128.0))):
        P.op("dve", lambda e, a=a, b=b, sc=sc: e.tensor_scalar(out=C.gs[:, a:b], in0=C.cons[:, a:b], scalar1=float(sc), scalar2=None, op0=ALU.mult),
             reads=[C.cons], writes=[C.gs])
    P.op("pool", lambda e: e.memset(Sst[:], 0.0), writes=[Sst])
    for r in range(4):
        P.dma("sp", Ul[:], STi[r], reads=[STi], writes=[Ul])
        P.dma("sp", Dl[:], DTi[r], reads=[DTi], writes=[Dl])
        w = C.cons[:, 49 + r:50 + r]
        P.op("dve", lambda e, w=w: e.tensor_scalar(out=Dl[:], in0=Dl[:], scalar1=-1.0, scalar2=w, op0=ALU.add, op1=ALU.mult), reads=[Dl, C.cons], writes=[Dl])
        P.op("dve", lambda e: e.tensor_scalar(out=Dl[:], in0=Dl[:], scalar1=1.0, scalar2=None, op0=ALU.add), reads=[Dl], writes=[Dl])
        P.op("dve", lambda e, w=w: e.tensor_scalar(out=Ul[:], in0=Ul[:], scalar1=w, scalar2=None, op0=ALU.mult), reads=[Ul, C.cons], writes=[Ul])
        for hd in range(NH_A):
            P.op("dve", lambda e, hd=hd: e.scalar_tensor_tensor(out=Sst[:, hd, :], in0=Sst[:, hd, :], scalar=Dl[:, hd:hd + 1], in1=Ul[:, hd, :],
                                                                op0=ALU.mult, op1=ALU.add), reads=[Sst, Dl, Ul], writes=[Sst])
    P.op("act", lambda e: e.copy(out=Sstb[:], in_=Sst[:]), reads=[Sst], writes=[Sstb])

    nt = G // NT
    it = 0
    for g in range(TOK // G):
        P.dma("sp", C.x[:], x1i[:, :, g * G:(g + 1) * G], reads=[x1i], writes=[C.x])
        for hd in range(NH_A):
            for t in range(nt):
                i2 = it % 2
                it += 1
                tok0 = g * G + t * NT
                P.dma("sp", ot[i2][:], oTi[hd, :, tok0:tok0 + NT], reads=[oTi], writes=[ot[i2]])
                P.dma("sp", qcl[i2][:], qci[hd, :, tok0:tok0 + NT], reads=[qci], writes=[qcl[i2]])
                P.dma("sp", sgl[i2][:], sgi[:, hd, tok0:tok0 + NT], reads=[sgi], writes=[sgl[i2]])
                po = C.ps[i2]
                P.op("pe", lambda e, hd=hd, i2=i2, po=po: e.matmul(po[:], lhsT=Sstb[:, hd, :], rhs=qcl[i2][:], start=True, stop=True),
                     reads=[Sstb, qcl[i2]], writes=[po])
                P.op("dve", lambda e, i2=i2, po=po: e.tensor_tensor(out=ot[i2][:], in0=ot[i2][:], in1=po[:], op=ALU.add), reads=[ot[i2], po], writes=[ot[i2]])
                rmsnorm_fm(C, lambda kc, i2=i2: ot[i2][:], ot[i2], 1, 0, NT, lambda kc: C.gs[:, 46:47],
                           lambda kc: latf[:, 0, :], latf, C.ps[7], 128)
                P.op("dve", lambda e, hd=hd, t=t, i2=i2: e.tensor_tensor(out=C.h[:, hd, t * NT:(t + 1) * NT], in0=latf[:, 0, :], in1=sgl[i2][:], op=ALU.mult),
                     reads=[latf, sgl[i2]], writes=[C.h])
        i = 0
        for m in range(KC):
            ws = None
            for t in range(nt):
                pb = C.ps[2 + i % 2]
                i += 1
                ws = lin_fm(C, wao, m, KC, t * NT, NT, pb, wslot=ws)
                P.op("dve", lambda e, pb=pb, m=m, t=t: e.tensor_tensor(out=C.x[:, m, t * NT:(t + 1) * NT], in0=C.x[:, m, t * NT:(t + 1) * NT], in1=pb[:], op=ALU.add),
                     reads=[pb, C.x], writes=[C.x])
        ffn_group(C, win0, wout0, 0)
        ple_group(C, wpg, wpi, pin, g, 8, pbf)
        rope_table(C, posd, g * G, G, tab, tmpf, tmpi, 47, 48)
        for t in range(nt):
            rmsnorm_fm(C, lambda kc, t=t: C.x[:, kc, t * NT:(t + 1) * NT], C.x, KC, t * NT, NT,
                       lambda kc: C.gs[:, 16 + kc:17 + kc],
                       lambda kc, t=t: C.h[:, kc, t * NT:(t + 1) * NT], C.h, C.ps[7], D, src_all=C.x[:, :, t * NT:(t + 1) * NT])
        wsl3 = [None, None, None]
        for t in range(nt):
            tok0 = g * G + t * NT
            for c in range(3):
                pb = C.ps[c]
                wsl3[c] = lin_fm(C, wkd, c, KC, t * NT, NT, pb, wslot=wsl3[c])
                if c < 2:
                    P.op("act", lambda e, c=c, pb=pb: e.copy(out=latf[:, c, :], in_=pb[:]), reads=[pb], writes=[latf])
                else:
                    P.op("dve", lambda e, pb=pb, t=t: e.tensor_tensor(out=rT[:], in0=pb[:], in1=tab[:, t * NT:(t + 1) * NT], op=ALU.mult),
                         reads=[pb, tab], writes=[rT])
            pk = C.ps[3]
            P.op("pe", lambda e, pk=pk: e.matmul(pk[0:64, :], lhsT=stkb[:], rhs=rT[:], start=True, stop=True), reads=[stkb, rT], writes=[pk])
            P.op("act", lambda e, pk=pk: e.copy(out=krb[0:64, :], in_=pk[0:64, :]), reads=[pk], writes=[krb])
            cq_ap, ckv_ap, kr_ap, latT = lat_dst(tok0 // NT)
            P.dma("sp", kr_ap, krb[0:64, :], reads=[krb], writes=[latT])
            rmsnorm_fm(C, lambda kc: latf[:, kc, :], latf, 2, 0, NT, lambda kc: C.gs[:, 40 + kc:41 + kc],
                       lambda kc: latb[:, kc, :], latb, C.ps[7], KVL)
            P.dma("sp", ckv_ap, latb[:, 0:2, :], reads=[latb], writes=[latT])
        ffn_group(C, win1, wout1, 24)
        P.dma("sp", x5o[:, :, g * G:(g + 1) * G], C.x[:], reads=[C.x], writes=[x5o])
        for t in range(nt):
            rmsnorm_fm(C, lambda kc, t=t: C.x[:, kc, t * NT:(t + 1) * NT], C.x, KC, t * NT, NT,
                       lambda kc: C.gs[:, 32 + kc:33 + kc],
                       lambda kc, t=t: C.h[:, kc, t * NT:(t + 1) * NT], C.h, C.ps[7], D, src_all=C.x[:, :, t * NT:(t + 1) * NT])
        wsl4 = [None] * 4
        for t in range(nt):
            tok0 = g * G + t * NT
            for c in range(4):
                pb = C.ps[c]
                wsl4[c] = lin_fm(C, wdq, c, KC, t * NT, NT, pb, wslot=wsl4[c])
                P.op("act", lambda e, c=c, pb=pb: e.copy(out=latf[:, c, :], in_=pb[:]), reads=[pb], writes=[latf])
            rmsnorm_fm(C, lambda kc: latf[:, kc, :], latf, 4, 0, NT, lambda kc: C.gs[:, 42 + kc:43 + kc],
                       lambda kc: latb[:, kc, :], latb, C.ps[7], QL)
            cq_ap, ckv_ap, kr_ap, latT = lat_dst(tok0 // NT)
            P.dma("sp", cq_ap, latb[:], reads=[latb], writes=[latT])
            lat_done(tok0 // NT)


HPC = 4
NQT = SEQ // NT
SCALE = float((128 + ROPE) ** -0.5)


def phase3(P, C0, t, nqt=NQT, nheads=HPC, stop=9):
    lat = t["lat"]
    cq_src = t["cq_src"]
    ckv_src = t["ckv_src"]
    kr_src = t["kr_src"]
    o_dst = t["o_dst"]
    head_done = t["head_done"]
    posd = t["pos"]
    cons = t["cons"]
    stk = t["stk"]
    trid = t["tri"]
    wuq = t["wuq"]
    wuk = t["wuk"]
    wuv = t["wuv"]

    class CC:
        pass
    C = CC()
    C.P = P
    C.ps = P.ps
    C.cons = P.sbuf("cons", [128, 8], F32)
    ones = P.sbuf("ones", [128, 128], BF16)
    onesf = P.sbuf("onesf", [128, 128], F32)
    ckv = P.sbuf("ckv", [128, 2, SEQ], BF16)
    Bk = P.sbuf("Bk", [128, SEQ], BF16)
    Ak = [P.sbuf("Ak%d" % i, [128, SEQ], BF16) for i in range(2)]
    Vh = [P.sbuf("Vh%d" % i, [128, SEQ // 128, 128], BF16) for i in range(2)]
    tab = P.sbuf("tab", [128, SEQ], F32)
    stkb = P.sbuf("stkb", [128, 64], BF16)
    tri = P.sbuf("trib", [128, 128], F32)
    wq = [P.sbuf("wq%d" % i, [128, 2, QL], BF16) for i in range(2)]
    wk = [P.sbuf("wk%d" % i, [128, KVL], BF16) for i in range(2)]
    wv = [P.sbuf("wv%d" % i, [128, KVL], BF16) for i in range(2)]
    mx = P.sbuf("mx", [128, 8], F32)
    mtmp = P.sbuf("setup_tmp", [128, 4096], F32)
    m_alias = P.mark()

    P.dma("sp", C.cons[:], cons[:], reads=[cons], writes=[C.cons])
    P.dma("pool", stkb[:], stk[:], writes=[stkb])
    P.dma("sp", tri[:], trid[:], writes=[tri])
    P.op("pool", lambda e: e.memset(ones[:], 1.0), writes=[ones])
    P.op("pool", lambda e: e.memset(onesf[:], 1.0), writes=[onesf])
    P.op("pool", lambda e: e.memset(Bk[:], 0.0), writes=[Bk])
    P.op("pool", lambda e: e.memset(Bk[64:65, :], 1.0), writes=[Bk])
    P.op("pool", lambda e: e.memset(mx[:], 0.0), writes=[mx])
    for q in range(nqt):
        for kc in range(2):
            P.dma("sp", ckv[:, kc, q * NT:(q + 1) * NT], ckv_src(kc, q), reads=[lat], writes=[ckv])
        P.dma("sp", Bk[0:64, q * NT:(q + 1) * NT], kr_src(q), reads=[lat], writes=[Bk])
    tmpf = T(mtmp[:, 0:2048], "tmpf")
    tmpi = T(mtmp[:, 2048:4096].bitcast(I32), "tmpi")
    for q4 in range(4):
        tv = T(tab[:, q4 * 2048:(q4 + 1) * 2048], "tabv%d" % q4)
        rope_table(C, posd, q4 * 2048, 2048, tv, tmpf, tmpi, 0, 1)
    P.barrier()
    P.atop = m_alias - 4096 * 4
    aq = [P.sbuf("aq%d" % i, [128, NT], BF16) for i in range(3)]
    bq = [P.sbuf("bq%d" % i, [128, NT], BF16) for i in range(3)]
    cql = [P.sbuf("cql%d" % i, [128, 4, NT], BF16) for i in range(3)]
    sqA = [P.sbuf("sqA%d" % i, [128, NT], BF16) for i in range(2)]
    sqB = [P.sbuf("sqB%d" % i, [128, NT], BF16) for i in range(2)]
    rT = [P.sbuf("rT%d" % i, [128, NT], BF16) for i in range(2)]
    tq = P.sbuf("tq", [128, NT], F32)
    NPT = 6
    pt = [P.sbuf("pt%d" % i, [128, NT], BF16) for i in range(NPT)]
    acc = [[P.sbuf("acc%d_%d" % (j, i), [128, NT], F32) for i in range(2)] for j in range(2)]
    rl = P.sbuf("rl", [128, NT], F32)
    on = [P.sbuf("on%d" % i, [128, NT], BF16) for i in range(2)]
    for i in range(3):
        P.op("pool", lambda e, i=i: e.memset(bq[i][:], 0.0), writes=[bq[i]])
    for i in range(2):
        P.op("pool", lambda e, i=i: e.memset(sqB[i][:], 0.0), writes=[sqB[i]])

    state = {"bank": 0, "wide": True, "sq": 0}

    def nxt_bank():
        state["bank"] += 1
        return C.ps[state["bank"] % 8] if state["wide"] else C.ps[6 + state["bank"] % 2]

    def colmax(pb, col):
        P.op("dve", lambda e: e.tensor_reduce(out=mx[:, 3:4], in_=pb[:], axis=mybir.AxisListType.X, op=ALU.max), reads=[pb], writes=[mx])
        P.op("dve", lambda e: e.tensor_tensor(out=mx[:, col:col + 1], in0=mx[:, col:col + 1], in1=mx[:, 3:4], op=ALU.max), reads=[mx], writes=[mx])

    for kt in range(nqt):
        pb = nxt_bank()
        sB = sqB[kt % 2]
        P.op("act", lambda e, kt=kt, sB=sB: e.activation(out=sB[0:64, :], in_=Bk[0:64, kt * NT:(kt + 1) * NT], func=AF.Square), reads=[Bk], writes=[sB])
        P.op("pe", lambda e, sB=sB, pb=pb: e.matmul(pb[:], lhsT=ones[:], rhs=sB[:], start=True, stop=True), reads=[ones, sB], writes=[pb])
        colmax(pb, 0)

    def load_w(hn):
        P.dma("pool", wq[hn % 2][:, 0, :], wuq[2 * hn], reads=[wuq], writes=[wq[hn % 2]])
        P.dma("pool", wq[hn % 2][:, 1, :], wuq[2 * hn + 1], reads=[wuq], writes=[wq[hn % 2]])
        P.dma("pool", wk[hn % 2][:], wuk[hn], reads=[wuk], writes=[wk[hn % 2]])
        P.dma("pool", wv[hn % 2][:], wuv[hn], reads=[wuv], writes=[wv[hn % 2]])
        P.op("pool", lambda e: e.memset(mx[:, 1 + hn % 2:2 + hn % 2], 0.0), writes=[mx])

    def kv_tile(hn, kt):
        A, V, wkk, wvv = Ak[hn % 2], Vh[hn % 2], wk[hn % 2], wv[hn % 2]
        pb = nxt_bank()
        for kc in range(2):
            P.op("pe", lambda e, kc=kc: e.matmul(pb[:], lhsT=wkk[:, kc * 128:(kc + 1) * 128], rhs=ckv[:, kc, kt * NT:(kt + 1) * NT],
                                                 start=(kc == 0), stop=(kc == 1)), reads=[wkk, ckv], writes=[pb])
        yield
        P.op("act", lambda e: e.copy(out=A[:, kt * NT:(kt + 1) * NT], in_=pb[:]), reads=[pb], writes=[A])
        state["sq"] += 1
        sA = sqA[state["sq"] % 2]
        P.op("act", lambda e: e.activation(out=sA[:], in_=pb[:], func=AF.Square), reads=[pb], writes=[sA])
        yield
        pn = nxt_bank()
        P.op("pe", lambda e: e.matmul(pn[:], lhsT=ones[:], rhs=sA[:], start=True, stop=True), reads=[ones, sA], writes=[pn])
        yield
        colmax(pn, 1 + hn % 2)
        pv = nxt_bank()
        for k4 in range(4):
            kb = kt * 4 + k4
            for kc in range(2):
                P.op("pe", lambda e, kc=kc, kb=kb, k4=k4: e.matmul(pv[:, k4 * 128:(k4 + 1) * 128], lhsT=ckv[:, kc, kb * 128:(kb + 1) * 128],
                                                                   rhs=wvv[:, kc * 128:(kc + 1) * 128], start=(kc == 0), stop=(kc == 1)),
                     reads=[ckv, wvv], writes=[pv])
        yield
        P.op("dve", lambda e: e.tensor_copy(out=V[:, kt * 4:(kt + 1) * 4, :].rearrange("p a b -> p (a b)"), in_=pv[:]), reads=[pv], writes=[V])

    def kv_finish(hn):
        c = 4 + hn % 2
        P.op("dve", lambda e: e.tensor_tensor(out=mx[:, c:c + 1], in0=mx[:, 1 + hn % 2:2 + hn % 2], in1=mx[:, 0:1], op=ALU.add), reads=[mx], writes=[mx])
        yield
        P.op("act", lambda e: e.activation(out=mx[:, c:c + 1], in_=mx[:, c:c + 1], func=AF.Ln), reads=[mx], writes=[mx])
        P.op("act", lambda e: e.activation(out=mx[:, c:c + 1], in_=mx[:, c:c + 1], func=AF.Exp, scale=0.5), reads=[mx], writes=[mx])
        yield
        P.op("dve", lambda e: e.tensor_scalar(out=mx[:, c:c + 1], in0=mx[:, c:c + 1], scalar1=-1.008 * SCALE, scalar2=None, op0=ALU.mult), reads=[mx], writes=[mx])
        yield

    cq_loaded = set()

    def load_cq(hn, qt):
        g = hn * nqt + qt
        if g in cq_loaded or hn >= nheads:
            return
        cq_loaded.add(g)
        P.dma("sp", cql[g % 3][:], cq_src(qt * NT, NT), reads=[lat], writes=[cql[g % 3]])

    def q_tile(hn, qt):
        g = hn * nqt + qt
        a, b, cq, wqq = aq[g % 3], bq[g % 3], cql[g % 3], wq[hn % 2]
        if g not in cq_loaded:
            load_cq(hn, qt)
        pa = nxt_bank()
        pr = nxt_bank()
        for half, pb in ((0, pa), (1, pr)):
            for kc in range(4):
                P.op("pe", lambda e, kc=kc, half=half, pb=pb: e.matmul(pb[:], lhsT=wqq[:, half, kc * 128:(kc + 1) * 128], rhs=cq[:, kc, :],
                                                                       start=(kc == 0), stop=(kc == 3)), reads=[wqq, cq], writes=[pb])
        yield
        P.op("act", lambda e: e.activation(out=a[:], in_=pa[:], func=AF.Identity, scale=SCALE), reads=[pa], writes=[a])
        state["sq"] += 1
        sA, sB = sqA[state["sq"] % 2], sqB[state["sq"] % 2]
        P.op("act", lambda e: e.activation(out=sA[:], in_=pa[:], func=AF.Square), reads=[pa], writes=[sA])
        r = rT[g % 2]
        P.op("dve", lambda e: e.tensor_tensor(out=r[:], in0=pr[:], in1=tab[:, qt * NT:(qt + 1) * NT], op=ALU.mult), reads=[pr, tab], writes=[r])
        yield
        pk = nxt_bank()
        P.op("pe", lambda e: e.matmul(pk[0:64, :], lhsT=stkb[:], rhs=r[:], start=True, stop=True), reads=[stkb, r], writes=[pk])
        yield
        P.op("act", lambda e: e.activation(out=b[0:64, :], in_=pk[0:64, :], func=AF.Identity, scale=SCALE), reads=[pk], writes=[b])
        P.op("act", lambda e: e.activation(out=sB[0:64, :], in_=pk[0:64, :], func=AF.Square), reads=[pk], writes=[sB])
        yield
        pn = nxt_bank()
        P.op("pe", lambda e: e.matmul(pn[:], lhsT=ones[:], rhs=sA[:], start=True, stop=False), reads=[ones, sA], writes=[pn])
        P.op("pe", lambda e: e.matmul(pn[:], lhsT=ones[:], rhs=sB[:], start=False, stop=True), reads=[ones, sB], writes=[pn])
        yield
        P.op("act", lambda e: e.activation(out=tq[64:65, :], in_=pn[64:65, :], func=AF.Ln), reads=[pn], writes=[tq])
        P.op("act", lambda e: e.activation(out=tq[64:65, :], in_=tq[64:65, :], func=AF.Exp, scale=0.5), reads=[tq], writes=[tq])
        yield
        P.op("dve", lambda e: e.tensor_scalar(out=b[64:65, :], in0=tq[64:65, :], scalar1=mx[64:65, 4 + hn % 2:5 + hn % 2], scalar2=None, op0=ALU.mult),
             reads=[tq, mx], writes=[b])

    if stop <= 1:
        return
    def run(gen):
        for _ in gen:
            pass

    def chain(*gens):
        for g_ in gens:
            yield from g_

    load_w(0)
    for kt in range(nqt):
        run(kv_tile(0, kt))
    run(kv_finish(0))
    if stop <= 2:
        return
    run(q_tile(0, 0))
    load_cq(0, 1)
    state["wide"] = False
    if stop <= 3 or stop >= 30:
        return

    side = []
    qgen = {}
    for h in range(nheads):
        if h + 1 < nheads:
            load_w(h + 1)
        blocks = [(qt, kb) for qt in range(nqt) for kb in range(4 * qt + 4)]

        def geom(qt, kb):
            c0 = 0 if kb < 4 * qt else (kb - 4 * qt) * 128
            return c0, NT - c0

        def emit_qk(i):
            qt, kb = blocks[i]
            c0, ncol = geom(qt, kb)
            g = h * nqt + qt
            a, b = aq[g % 3], bq[g % 3]
            A_ = Ak[h % 2]
            pss = C.ps[i % 3]
            P.op("pe", lambda e: e.matmul(pss[:, 0:ncol], lhsT=A_[:, kb * 128:(kb + 1) * 128], rhs=a[:, c0:NT], start=True, stop=False),
                 reads=[A_, a], writes=[pss])
            P.op("pe", lambda e: e.matmul(pss[:, 0:ncol], lhsT=Bk[:, kb * 128:(kb + 1) * 128], rhs=b[:, c0:NT], start=False, stop=True),
                 reads=[Bk, b], writes=[pss])

        def emit_rest(i):
            qt, kb = blocks[i]
            nkb = 4 * qt + 4
            c0, ncol = geom(qt, kb)
            pss = C.ps[i % 3]
            p = pt[i % NPT]
            po = C.ps[3 + qt % 2]
            ac = acc[qt % 2]
            V_ = Vh[h % 2]
            if kb == 0:
                P.op("pool", lambda e: e.memset(ac[0][:], 0.0), writes=[ac[0]])
                P.op("pool", lambda e: e.memset(ac[1][:], 0.0), writes=[ac[1]])
            P.op("act", lambda e: e.activation(out=p[:, 0:ncol], in_=pss[:, 0:ncol], func=AF.Exp), reads=[pss], writes=[p])
            if kb >= 4 * qt:
                P.op("pool", lambda e: e.tensor_tensor(out=p[:, 0:128], in0=p[:, 0:128], in1=tri[:], op=ALU.mult), reads=[p, tri], writes=[p])
            ai = i % 2
            P.op("dve" if ai == 0 else "pool", lambda e: e.tensor_tensor(out=ac[ai][:, c0:NT], in0=ac[ai][:, c0:NT], in1=p[:, 0:ncol], op=ALU.add),
                 reads=[p, ac[ai]], writes=[ac[ai]])
            P.op("pe", lambda e: e.matmul(po[:, c0:NT], lhsT=V_[:, kb, :], rhs=p[:, 0:ncol], start=(kb == 0), stop=(kb == nkb - 1)),
                 reads=[V_, p], writes=[po])
            if kb == nkb - 1:
                pending.append((i + 2, lambda qt=qt, po=po, ac=ac: finalize(qt, po, ac)))
            if kb == 0:
                if qt + 1 < nqt:
                    qgen[(h, qt + 1)] = q_tile(h, qt + 1)
                    side.append(qgen[(h, qt + 1)])
                    load_cq(h, qt + 2) if qt + 2 < nqt else load_cq(h + 1, 0)
                if h + 1 < nheads:
                    if qt == nqt - 1:
                        qgen[(h + 1, 0)] = chain(kv_tile(h + 1, qt), kv_finish(h + 1), q_tile(h + 1, 0))
                        side.append(qgen[(h + 1, 0)])
                        load_cq(h + 1, 1)
                    else:
                        side.append(kv_tile(h + 1, qt))

        def finalize(qt, po, ac):
            pl = C.ps[5]
            P.op("pe", lambda e: e.matmul(pl[:], lhsT=onesf[:], rhs=ac[0][:], start=True, stop=False), reads=[onesf, ac[0]], writes=[pl])
            P.op("pe", lambda e: e.matmul(pl[:], lhsT=onesf[:], rhs=ac[1][:], start=False, stop=True), reads=[onesf, ac[1]], writes=[pl])
            P.op("dve", lambda e: e.reciprocal(out=rl[:], in_=pl[:]), reads=[pl], writes=[rl])
            o = on[qt % 2]
            P.op("dve", lambda e: e.tensor_tensor(out=o[:], in0=po[:], in1=rl[:], op=ALU.mult), reads=[po, rl], writes=[o])
            oap, oT_ = o_dst(h, qt * NT, NT)
            P.dma("sp", oap, o[:], reads=[o], writes=[oT_])
            if qt % 8 == 7:
                head_done(h, qt // 8)

        LOOK = 2
        pending = []

        def qk(j):
            qt_, kb_ = blocks[j]
            if kb_ == 0 and (h, qt_) in qgen:
                gq = qgen.pop((h, qt_))
                while side:
                    g0 = side.pop(0)
                    run(g0)
                    if g0 is gq:
                        break
            emit_qk(j)

        for i in range(min(LOOK, len(blocks))):
            qk(i)
        for i in range(len(blocks)):
            if i + LOOK < len(blocks):
                qk(i + LOOK)
            emit_rest(i)
            while pending and (pending[0][0] <= i or i == len(blocks) - 1):
                pending.pop(0)[1]()
            if side:
                try:
                    next(side[0])
                except StopIteration:
                    side.pop(0)
        for g_ in side:
            run(g_)
        side.clear()


def phase4(P, C, t):
    x5i = t["x5i"]
    load_o = t["load_o"]
    pin = t["pin"]
    cons = t["cons"]
    wbo = t["wbo"]
    win = t["win"]
    wout = t["wout"]
    wpg = t["wpg"]
    wpi = t["wpi"]
    yo = t["yo"]
    ob = P.sbuf("ob", [128, NH_B, G], BF16)
    pbf = P.sbuf("pbf", [128, 2, G], BF16)
    yf = P.sbuf("yf", [128, KC, NT], F32)
    P.dma("sp", C.cons[:, 0:24], cons[:], reads=[cons], writes=[C.cons])
    P.op("dve", lambda e: e.tensor_scalar(out=C.gs[:, 0:24], in0=C.cons[:, 0:24], scalar1=float(np.sqrt(D)), scalar2=None, op0=ALU.mult),
         reads=[C.cons], writes=[C.gs])
    nt = G // NT
    for g in range(TOK // G):
        P.dma("sp", C.x[:], x5i[:, :, g * G:(g + 1) * G], reads=[x5i], writes=[C.x])
        load_o(g, ob)
        i = 0
        for m in range(KC):
            ws = None
            for t in range(nt):
                pb = C.ps[i % 2]
                i += 1
                ws = lin_fm(C, wbo, m, NH_B, t * NT, NT, pb, wslot=ws, src=ob)
                P.op("dve", lambda e, pb=pb, m=m, t=t: e.tensor_tensor(out=C.x[:, m, t * NT:(t + 1) * NT], in0=C.x[:, m, t * NT:(t + 1) * NT], in1=pb[:], op=ALU.add),
                     reads=[pb, C.x], writes=[C.x])
        ffn_group(C, win, wout, 0)
        ple_group(C, wpg, wpi, pin, g, 8, pbf)
        for t in range(nt):
            rmsnorm_fm(C, lambda kc, t=t: C.x[:, kc, t * NT:(t + 1) * NT], C.x, KC, t * NT, NT,
                       lambda kc: C.gs[:, 16 + kc:17 + kc], lambda kc: yf[:, kc, :], yf, C.ps[7], D, src_all=C.x[:, :, t * NT:(t + 1) * NT])
            P.dma("sp", yo[:, :, g * G + t * NT:g * G + (t + 1) * NT], yf[:], reads=[yf], writes=[yo])


def xfm(x):
    return np.ascontiguousarray(np.ascontiguousarray(x.T).reshape(-1, 128, x.shape[0]).transpose(1, 0, 2))


def unfm(a):
    a = np.asarray(a)
    return np.ascontiguousarray(a.transpose(1, 0, 2).reshape(-1, a.shape[2]).T)


def _rope_consts():
    invf = (1.0 / (np.float32(10000.0) ** (np.arange(0, ROPE, 2, dtype=np.float32) / np.float32(ROPE)))).astype(np.float32)
    invf4 = np.tile(invf, 4)
    ph = np.concatenate([np.full(64, np.pi / 2), np.full(32, np.pi), np.zeros(32)]).astype(np.float32)
    return invf4, ph


def _hmask():
    s = np.arange(128)[:, None]
    t = np.arange(128)[None, :]
    m = ((s // 64 == t // 64) & (s <= t)).astype(np.float32)
    return np.ascontiguousarray(np.tile(m, (1, 4)))


_NC_CACHE = {}
_PREP_ONLY = False

W_SHAPES = {
    "ffn00_in": [NJ, 128, KC * 256], "ffn00_out": [KC, 128, NJ * 128], "ffn01_in": [NJ, 128, KC * 256], "ffn01_out": [KC, 128, NJ * 128],
    "ffn10_in": [NJ, 128, KC * 256], "ffn10_out": [KC, 128, NJ * 128], "ffn11_in": [NJ, 128, KC * 256], "ffn11_out": [KC, 128, NJ * 128],
    "wq": [8, 128, D], "wf": [8, 128, D], "wg": [8, 128, D], "wv": [128, KC * D], "wao": [8, 128, D],
    "wpg0": [8, 128, D], "wpi0": [8, 128, 256], "wkd": [3, 128, D], "wdq": [4, 128, D],
    "wuq": [HPC * 2, 128, QL], "wuk": [HPC, 128, KVL], "wuv": [HPC, 128, KVL],
    "wbo": [8, 128, NH_B * 128], "wpg1": [8, 128, D], "wpi1": [8, 128, 256],
    "cons1": [128, 32], "cons2": [128, NC2], "cons3": [128, 8], "cons4": [128, 24],
    "ident": [128, 128], "hmask": [128, 512], "stk": [128, 64], "tri": [128, 128],
    "xin": [128, KC, TOK], "p0": [128, 2, TOK], "p1": [128, 2, TOK],
}


def build_fused(upto=4):
    P = Prog()
    nc = P.nc
    E = {k: P.dram(k, v, F32, "ExternalInput") for k, v in W_SHAPES.items()}
    E["posl"] = P.dram("posl", [1, TOK], I32, "ExternalInput")
    E["posb"] = P.dram("posb", [1, SEQ], I32, "ExternalInput")
    E["idxo"] = P.dram("idxo", [128, 8], I32, "ExternalInput")
    yo = P.dram("yo", [128, KC, TOK], F32, "ExternalOutput")

    def internal(name, shape, dtype):
        return T(nc.dram_tensor(name, list(shape), dtype), name)

    x1 = internal("x1", [128, KC, TOK], F32)
    oT = internal("oT", [NH_A, 128, TOK], F32)
    qc = internal("qc", [NH_A, 128, TOK], BF16)
    sg = internal("sg", [128, KC, TOK], F32)
    SDin = internal("SDin", [128, 1032], F32)
    SDall = internal("SDall", [512, 1032], F32)
    x5 = internal("x5", [128, KC, TOK], F32)
    NTL = TOK // NT
    latin = [internal("latin%d" % i, [128, 7 * NT], BF16) for i in range(NTL)]
    latall = [internal("latall%d" % i, [512, 7 * NT], BF16) for i in range(NTL)]
    latdep = Buf("latall")
    for q in latall:
        q.b = latdep
    ohin = [[internal("ohin%d_%d" % (h, hf), [128, SEQ // 2], BF16) for hf in range(2)] for h in range(HPC)]
    ohall = [internal("ohall%d" % h, [1024, SEQ // 2], BF16) for h in range(HPC)]

    m0 = P.mark()
    C = Ctx(P)
    mC = P.mark()
    STo = T(SDin.t[:, 0:1024].rearrange("p (h d) -> p h d", h=NH_A), "STo")
    DTo = T(SDin.t[:, 1024:1032], "DTo")
    phase1(P, C, {"xin": E["xin"], "win": E["ffn00_in"], "wout": E["ffn00_out"], "wq": E["wq"], "wf": E["wf"], "wg": E["wg"], "wv": E["wv"],
                  "cons": E["cons1"], "ident": E["ident"], "hmask": E["hmask"], "x1o": x1, "oTo": oT, "qco": qc, "sgo": sg, "STo": STo, "DTo": DTo})
    P.reset(mC)
    P.coll("AllGather", SDin.t.ap().opt(), SDall.t.ap().opt(), reads=[STo, DTo], writes=[SDall])
    if upto == 1:
        P.barrier()
        for g in range(TOK // G):
            tmpx = P.sbuf("tmpx1", [128, KC, G], F32)
            P.dma("sp", tmpx[:], x1[:, :, g * G:(g + 1) * G], reads=[x1], writes=[tmpx])
            P.dma("sp", yo[:, :, g * G:(g + 1) * G], tmpx[:], reads=[tmpx], writes=[yo])
        return P.finish()
    STi = TV(lambda r: SDall.t[r * 128:(r + 1) * 128, 0:1024].rearrange("p (h d) -> p h d", h=NH_A), "STi")
    STi.b = SDall.b
    DTi = TV(lambda r: SDall.t[r * 128:(r + 1) * 128, 1024:1032], "DTi")
    DTi.b = SDall.b
    def lat_dst(i):
        v = latin[i].t[:, :].rearrange("p (c t) -> p c t", c=7)
        return v[:, 0:4, :], v[:, 4:6, :], v[0:64, 6, :], latin[i]

    def lat_done(i):
        P.coll("AllGather", latin[i].t.ap().opt(), latall[i].t.ap().opt(), reads=[latin[i]], writes=[latall[i]])

    phase2(P, C, {"x1i": x1, "oTi": oT, "qci": qc, "sgi": sg, "STi": STi, "DTi": DTi, "pin": E["p0"], "pos": E["posl"], "cons": E["cons2"],
                  "stk": E["stk"], "wao": E["wao"], "win0": E["ffn01_in"], "wout0": E["ffn01_out"], "wpg": E["wpg0"], "wpi": E["wpi0"],
                  "wkd": E["wkd"], "win1": E["ffn10_in"], "wout1": E["ffn10_out"], "wdq": E["wdq"],
                  "x5o": x5, "lat_dst": lat_dst, "lat_done": lat_done})
    P.reset(m0)
    if upto == 2:
        P.barrier()
        for g in range(TOK // G):
            tmpx = P.sbuf("tmpx2", [128, KC, G], F32)
            P.dma("sp", tmpx[:], x5[:, :, g * G:(g + 1) * G], reads=[x5], writes=[tmpx])
            P.dma("sp", yo[:, :, g * G:(g + 1) * G], tmpx[:], reads=[tmpx], writes=[yo])
        return P.finish()
    def latv(q):
        rr, i = q // NTL, q % NTL
        return latall[i].t[rr * 128:(rr + 1) * 128, :].rearrange("p (c t) -> p c t", c=7)

    def head_done(h, hf):
        P.coll("AllGather", ohin[h][hf].t.ap().opt(), ohall[h].t[hf * 512:(hf + 1) * 512, :], reads=[ohin[h][hf]], writes=[ohall[h]])

    def o_dst(h, t0, n):
        hf, tl = t0 // (SEQ // 2), t0 % (SEQ // 2)
        return ohin[h][hf].t[:, tl:tl + n], ohin[h][hf]

    phase3(P, None, {"lat": latall[0], "cq_src": lambda t0, n: latv(t0 // NT)[:, 0:4, :],
                     "ckv_src": lambda kc, q: latv(q)[:, 4 + kc, :], "kr_src": lambda q: latv(q)[0:64, 6, :],
                     "o_dst": o_dst, "head_done": head_done,
                     "pos": E["posb"], "cons": E["cons3"], "stk": E["stk"], "tri": E["tri"], "wuq": E["wuq"], "wuk": E["wuk"], "wuv": E["wuv"]})
    P.reset(m0)
    if upto == 3:
        P.barrier()
        for g in range(TOK // G):
            tmpx = P.sbuf("tmpx3", [128, KC, G], F32)
            P.dma("sp", tmpx[:], x5[:, :, g * G:(g + 1) * G], reads=[x5], writes=[tmpx])
            P.dma("sp", yo[:, :, g * G:(g + 1) * G], tmpx[:], reads=[tmpx], writes=[yo])
        return P.finish()
    C = Ctx(P)
    idxs = P.sbuf("idxs", [128, 8], I32)
    P.dma("sp", idxs[:], E["idxo"][:], reads=[E["idxo"]], writes=[idxs])

    def load_o(g, ob):
        for h in range(HPC):
            rows = ohall[h].t[:, :].rearrange("r (b t) -> (r b) t", t=G)
            for rr in range(4):
                P.gather(ob[:, rr * HPC + h, :], rows, idxs[:, rr * 2 + g:rr * 2 + g + 1], reads=[ohall[h], idxs], writes=[ob])

    phase4(P, C, {"x5i": x5, "load_o": load_o, "pin": E["p1"], "cons": E["cons4"], "wbo": E["wbo"], "win": E["ffn11_in"], "wout": E["ffn11_out"],
                  "wpg": E["wpg1"], "wpi": E["wpi1"], "yo": yo})
    print("fused instructions", P.nins)
    return P.finish()


def _get(name, fn):
    if name not in _NC_CACHE:
        _NC_CACHE[name] = fn()
    return _NC_CACHE[name]


def kernel(x, p, positions, norm_gains, ffn_w_in, ffn_w_out, ple_w_gate, ple_w_in,
           a_w_in, a_lb_logits, a_out_gain, a_w_out,
           kv_norm_in, kv_w_down, kv_latent_norm, kv_w_up,
           b_w_dq, b_q_norm, b_w_uq, b_w_out, final_norm):
    f32 = lambda a: np.ascontiguousarray(np.asarray(a, dtype=np.float32))
    x, p, norm_gains, ffn_w_in, ffn_w_out = f32(x), f32(p), f32(norm_gains), f32(ffn_w_in), f32(ffn_w_out)
    ple_w_gate, ple_w_in, a_w_in, a_lb_logits, a_out_gain, a_w_out = map(f32, (ple_w_gate, ple_w_in, a_w_in, a_lb_logits, a_out_gain, a_w_out))
    kv_norm_in, kv_w_down, kv_latent_norm, kv_w_up = map(f32, (kv_norm_in, kv_w_down, kv_latent_norm, kv_w_up))
    b_w_dq, b_q_norm, b_w_uq, b_w_out, final_norm = map(f32, (b_w_dq, b_q_norm, b_w_uq, b_w_out, final_norm))
    positions = np.ascontiguousarray(np.asarray(positions, dtype=np.int32))
    cores = list(range(NCORES))
    invf4, ph = _rope_consts()
    g = norm_gains
    aw = a_w_in[0]
    wd = kv_w_down
    wkd = np.concatenate([wd[:, 0:256], wd[:, 256:320], wd[:, 288:320], wd[:, 256:288]], axis=1)
    cons3 = np.zeros((128, 8), np.float32)
    cons3[:, 0] = invf4
    cons3[:, 1] = ph
    kk = np.arange(128)[:, None]
    qq = np.arange(128)[None, :]
    shared = {
        "ffn00_in": tile_ffn_in(ffn_w_in[0, 0]), "ffn00_out": tile_ffn_out(ffn_w_out[0, 0]),
        "ffn01_in": tile_ffn_in(ffn_w_in[0, 1]), "ffn01_out": tile_ffn_out(ffn_w_out[0, 1]),
        "ffn10_in": tile_ffn_in(ffn_w_in[1, 0]), "ffn10_out": tile_ffn_out(ffn_w_out[1, 0]),
        "ffn11_in": tile_ffn_in(ffn_w_in[1, 1]), "ffn11_out": tile_ffn_out(ffn_w_out[1, 1]),
        "wq": tile_lin(aw[:, 0:1024]), "wf": tile_lin(aw[:, 1024:2048]), "wg": tile_lin(aw[:, 3072:4096]),
        "wv": tile_rows(aw[:, 2048:3072]).reshape(128, -1), "wao": tile_lin(a_w_out[0]),
        "wpg0": tile_lin(ple_w_gate[0]), "wpi0": tile_lin(ple_w_in[0]), "wkd": tile_lin(wkd), "wdq": tile_lin(b_w_dq[0]),
        "wbo": tile_lin(b_w_out[0]), "wpg1": tile_lin(ple_w_gate[1]), "wpi1": tile_lin(ple_w_in[1]),
        "cons1": np.ascontiguousarray(np.concatenate([col(g[0, 0]), col(g[0, 1]), col(a_lb_logits[0]), col(a_lb_logits[1])], axis=1)),
        "cons3": cons3,
        "cons4": np.ascontiguousarray(np.concatenate([col(g[1, 2]), col(g[1, 3]), col(final_norm)], axis=1)),
        "ident": np.eye(128, dtype=np.float32), "hmask": _hmask(),
        "stk": np.concatenate([np.eye(64), np.eye(64)], axis=0).astype(np.float32), "tri": (kk <= qq).astype(np.float32),
    }
    uq = b_w_uq[0]
    per_r = []
    for r in range(4):
        wuq, wuk, wuv = [], [], []
        for h in range(4 * r, 4 * r + 4):
            q = uq[:, h * 192:(h + 1) * 192]
            wuq.append(tile_lin(q[:, 0:128])[0])
            wuq.append(tile_lin(np.concatenate([q[:, 128:192], q[:, 160:192], q[:, 128:160]], axis=1))[0])
            kv = kv_w_up[:, h * 256:(h + 1) * 256]
            wuk.append(tile_lin(kv[:, 0:128])[0])
            wuv.append(tile_rows(kv[:, 128:256]).reshape(128, 256))
        wsel = np.zeros((128, 4), np.float32)
        wsel[:, :r] = 1.0
        cons2 = np.concatenate([col(g[0, 2]), col(g[0, 3]), col(kv_norm_in), col(g[1, 0]), col(g[1, 1]), col(kv_latent_norm),
                                col(b_q_norm[0]), col(a_out_gain[0]), invf4[:, None], ph[:, None], wsel,
                                np.zeros((128, NC2 - 53), np.float32)], axis=1).astype(np.float32)
        idxo = np.zeros((128, 8), np.int32)
        pp = np.arange(128)
        for rr in range(4):
            for gg in range(2):
                tb = r * 2 + gg
                idxo[:, rr * 2 + gg] = ((tb // 4) * 512 + rr * 128 + pp) * 4 + tb % 4
        per_r.append({"wuq": np.stack(wuq), "wuk": np.stack(wuk), "wuv": np.stack(wuv), "cons2": np.ascontiguousarray(cons2), "idxo": idxo})
    in_maps = []
    for c in cores:
        b, r = c // 4, c % 4
        sl = slice(r * TOK, (r + 1) * TOK)
        in_maps.append({"xin": xfm(x[b, sl]), "p0": xfm(p[0, b, sl]), "p1": xfm(p[1, b, sl]),
                        "posl": np.ascontiguousarray(positions[b:b + 1, sl]), "posb": np.ascontiguousarray(positions[b:b + 1]),
                        **shared, **per_r[r]})
    if _PREP_ONLY:
        return in_maps
    res = run_bass_kernel_spmd(_get("fused", build_fused), in_maps, core_ids=cores).results
    out = np.empty((2, SEQ, D), np.float32)
    for c in cores:
        b, r = c // 4, c % 4
        out[b, r * TOK:(r + 1) * TOK] = unfm(res[c]["yo"])
    return out
```

```python
import contextlib
import numpy as np
import ml_dtypes
import concourse.bass as bass
import concourse.mybir as mybir
from concourse.bass_utils import run_bass_kernel_spmd

F32 = mybir.dt.float32
BF16 = mybir.dt.bfloat16
I32 = mybir.dt.int32
AF = mybir.ActivationFunctionType
ALU = mybir.AluOpType

NDSEM = 8
NCORES = 8
D = 1024
KC = 8
DFF = 2816
NJ = 22
SEQ = 8192
TOK = 2048
G = 1024
NT = 512
EPS = 1e-6
NH_A = 8
CH = 64
PLE = 256
NH_B = 16
QL = 512
KVL = 256
ROPE = 64


class Buf:
    __slots__ = ("w", "r", "name", "relaxed", "wl", "pr")

    def __init__(self, name=""):
        self.w = None
        self.r = []
        self.name = name
        self.relaxed = False
        self.wl = {}
        self.pr = []


class T:
    def __init__(self, t, name):
        self.t = t
        self.b = Buf(name)

    def __getitem__(self, idx):
        return self.t[idx]


class TV(T):
    def __init__(self, fn, name):
        self.fn = fn
        self.b = Buf(name)

    def __getitem__(self, idx):
        return self.fn(idx)


def _b(x):
    return x.b if isinstance(x, T) else x


class Prog:
    def __init__(self):
        self.nc = bass.Bass("TRN2", target_bir_lowering=False)
        self.es = contextlib.ExitStack()
        nc = self.nc
        self.eng = {"pe": nc.tensor, "act": nc.scalar, "dve": nc.vector, "pool": nc.gpsimd, "sp": nc.sync}
        self.q = {k: [] for k in self.eng}
        self.cnt = {k: 0 for k in self.eng}
        self.seen = {k: {} for k in self.eng}
        self.esem = {}
        for k in ("pe", "act", "dve", "pool"):
            self.esem[k] = self.es.enter_context(nc.semaphore("es_" + k))
        self.dsem = {}
        self.dcnt = {}
        self.dtok = {}
        for k in ("sp", "act", "pool"):
            self.dsem[k] = [self.es.enter_context(nc.semaphore("ds_%s%d" % (k, i))) for i in range(NDSEM)]
            self.dcnt[k] = 0
            self.dtok[k] = [None] * NDSEM
        self.same_engine_sync = True
        self.nins = 0
        self.defer = None
        self.ccsem = self.es.enter_context(nc.semaphore("ccsem"))
        self.ccn = 0
        self.ARENA = 103936
        self.arena = self.es.enter_context(nc.sbuf_tensor("arena", [128, self.ARENA], BF16))
        self.atop = 0
        self.ps = [self.psum("ps%d" % i, [128, NT], F32) for i in range(8)]

    def dram(self, name, shape, dtype, kind):
        return T(self.nc.dram_tensor(name, list(shape), dtype, kind=kind), name)

    def sbuf(self, name, shape, dtype):
        shape = list(shape)
        n = int(np.prod(shape[1:]))
        nb = n * (4 if dtype in (F32, I32) else 2)
        nb = (nb + 63) // 64 * 64
        assert self.atop + nb <= self.ARENA * 2, "arena overflow: %s needs %d, top %d" % (name, nb, self.atop)
        ap = self.arena[:, self.atop // 2:(self.atop + nb) // 2]
        self.atop += nb
        self.hiwater = max(getattr(self, "hiwater", 0), self.atop)
        if dtype != BF16:
            ap = ap.bitcast(dtype)
        ap = ap[:, 0:n]
        if len(shape) == 3:
            ap = ap.rearrange("p (a b) -> p a b", b=shape[2])
        return T(ap, name)

    def mark(self):
        return self.atop

    def reset(self, m):
        self.barrier()
        print("arena high-water %d of %d bytes" % (getattr(self, "hiwater", 0), self.ARENA * 2))
        self.hiwater = 0
        self.atop = m

    def coll(self, kind, src, dst, reads, writes):
        waits = self._filter("pool", self._deps(reads, writes))
        self.ccn += 1
        tok = (self.ccsem, self.ccn)

        def run(engh, waits=waits, src=src, dst=dst, kind=kind):
            for (s_, v) in waits:
                engh.wait_ge(s_, v)
            engh.collective_compute(kind, ALU.bypass, replica_groups=[[0, 1, 2, 3], [4, 5, 6, 7]], ins=[src], outs=[dst]).then_inc(self.ccsem, 1)

        self.q["pool"].append(run)
        self._commit(tok, reads, writes)
        self.nins += 1
        return tok

    def gather(self, out, in_, idx, reads=(), writes=()):
        qn = "pool"
        n = self.dcnt[qn]
        slot = n % NDSEM
        sem = self.dsem[qn][slot]
        val = 16 * (n // NDSEM + 1)
        deps = self._deps(reads, writes)
        if self.dtok[qn][slot] is not None:
            deps.append(self.dtok[qn][slot])
        waits = self._filter(qn, deps)
        tok = (sem, val)
        self.dtok[qn][slot] = tok
        self.dcnt[qn] += 1

        def run(engh, waits=waits, out=out, in_=in_, idx=idx, sem=sem):
            for (s_, v) in waits:
                engh.wait_ge(s_, v)
            engh.indirect_dma_start(out=out, out_offset=None, in_=in_, in_offset=bass.IndirectOffsetOnAxis(ap=idx, axis=0)).then_inc(sem, 16)

        self.q[qn].append(run)
        self._commit(tok, reads, writes)
        self.nins += 1
        return tok

    def psum(self, name, shape, dtype):
        return T(self.es.enter_context(self.nc.psum_tensor(name, list(shape), dtype)), name)

    def _deps(self, reads, writes, e=None):
        deps = []
        own = self.esem.get(e)

        def add(b, tok):
            if b.relaxed and own is not None and tok[0] is own:
                return
            deps.append(tok)

        for b in reads:
            b = _b(b)
            if b.relaxed:
                for tk in b.wl.values():
                    add(b, tk)
            elif b.w is not None:
                add(b, b.w)
        for b in writes:
            b = _b(b)
            if b.relaxed:
                for tk in (b.r if b.r else b.pr):
                    add(b, tk)
                continue
            if b.w is not None:
                add(b, b.w)
            for tk in b.r:
                add(b, tk)
        return deps

    def _filter(self, e, deps):
        seen = self.seen[e]
        best = {}
        for (s, v) in deps:
            if e == "pe" and s is self.esem["pe"]:
                continue
            if (not self.same_engine_sync) and e in self.esem and s is self.esem[e]:
                continue
            k = id(s)
            if seen.get(k, 0) >= v:
                continue
            if k not in best or best[k][1] < v:
                best[k] = (s, v)
        for k, (s, v) in best.items():
            seen[k] = v
        return list(best.values())

    def _commit(self, tok, reads, writes):
        for b in reads:
            _b(b).r.append(tok)
        for b in writes:
            b = _b(b)
            b.w = tok
            if b.relaxed:
                if b.r:
                    b.wl = {}
                    b.pr = b.r
                k = id(tok[0])
                if k not in b.wl or b.wl[k][1] < tok[1]:
                    b.wl[k] = tok
            b.r = []

    def op(self, e, fn, reads=(), writes=()):
        if self.defer is not None:
            self.defer.append(lambda: self._op(e, fn, reads, writes))
            return None
        return self._op(e, fn, reads, writes)

    def _op(self, e, fn, reads=(), writes=()):
        waits = self._filter(e, self._deps(reads, writes, e))
        sem = self.esem[e]
        self.cnt[e] += 1
        tok = (sem, self.cnt[e])

        def run(engh, waits=waits, fn=fn, sem=sem):
            for (s, v) in waits:
                engh.wait_ge(s, v)
            fn(engh).then_inc(sem, 1)

        self.q[e].append(run)
        self._commit(tok, reads, writes)
        self.nins += 1
        return tok

    def dma(self, qn, out, in_, reads=(), writes=(), **kw):
        if self.defer is not None:
            self.defer.append(lambda: self._dma(qn, out, in_, reads, writes, **kw))
            return None
        return self._dma(qn, out, in_, reads, writes, **kw)

    def run_streams(self, streams):
        idx = [0] * len(streams)
        live = True
        while live:
            live = False
            for i, st in enumerate(streams):
                if idx[i] < len(st):
                    st[idx[i]]()
                    idx[i] += 1
                    live = True

    def _dma(self, qn, out, in_, reads=(), writes=(), **kw):
        n = self.dcnt[qn]
        slot = n % NDSEM
        sem = self.dsem[qn][slot]
        val = 16 * (n // NDSEM + 1)
        deps = self._deps(reads, writes)
        if self.dtok[qn][slot] is not None:
            deps.append(self.dtok[qn][slot])
        waits = self._filter(qn, deps)
        tok = (sem, val)
        self.dtok[qn][slot] = tok
        self.dcnt[qn] += 1

        def run(engh, waits=waits, out=out, in_=in_, sem=sem, kw=kw):
            for (s, v) in waits:
                engh.wait_ge(s, v)
            engh.dma_start(out=out, in_=in_, **kw).then_inc(sem, 16)

        self.q[qn].append(run)
        self._commit(tok, reads, writes)
        self.nins += 1
        return tok

    def _alltoks(self):
        deps = []
        for qn in self.dtok:
            deps.extend([t for t in self.dtok[qn] if t is not None])
        for e in self.esem:
            if self.cnt[e]:
                deps.append((self.esem[e], self.cnt[e]))
        return deps

    def barrier(self):
        deps = self._alltoks()
        for e in ("pe", "act", "dve", "pool", "sp"):
            waits = self._filter(e, deps)
            if waits:
                def run(engh, waits=waits):
                    for (s, v) in waits:
                        engh.wait_ge(s, v)
                self.q[e].append(run)

    def finish(self):
        waits = self._filter("sp", self._alltoks())

        def run(engh, waits=waits):
            for (s, v) in waits:
                engh.wait_ge(s, v)

        self.q["sp"].append(run)
        nc = self.nc
        with nc.Block() as block:
            for k, name in (("sp", "sync"), ("pe", "tensor"), ("act", "scalar"), ("dve", "vector"), ("pool", "gpsimd")):
                if not self.q[k]:
                    continue

                def body(engh, fns=self.q[k]):
                    for f in fns:
                        f(engh)

                getattr(block, name)(body)
        self.es.close()
        return nc


class Ctx:
    def __init__(self, P, ncons=56):
        self.P = P
        self.ps = P.ps
        self.cons = P.sbuf("cons", [128, ncons], F32)
        self.gs = P.sbuf("gs", [128, ncons], F32)
        self.ones = P.sbuf("ones", [128, 128], BF16)
        self.x = P.sbuf("x", [128, KC, G], F32)
        self.h = P.sbuf("h", [128, KC, G], BF16)
        self.sq = P.sbuf("sq", [128, KC, NT], BF16)
        self.rstd = P.sbuf("rstd", [128, NT], F32)
        self.rstd2 = P.sbuf("rstd2", [128, NT], F32)
        self.su = [P.sbuf("su%d" % i, [128, NT], F32) for i in range(2)]
        self.act = P.sbuf("act", [128, NJ, G], BF16)
        self.wsl = [P.sbuf("wsl%d" % i, [128, 3072], BF16) for i in range(4)]
        self.wi = 0
        self.sg = [P.sbuf("sg%d" % i, [128, NT], F32) for i in range(2)]
        P.op("pool", lambda e: e.memset(self.ones[:], 1.0), writes=[self.ones])
        for tt in (self.x, self.h, self.act, self.sq):
            tt.b.relaxed = True

    def wslot(self):
        s = self.wsl[self.wi % 4]
        self.wi += 1
        return s


def rmsnorm_fm(C, src, srcT, kcn, n0, n, gcol, dst, dstT, pb, dn, src_all=None):
    P = C.P
    sq = C.sq
    if src_all is not None:
        P.op("act", lambda e: e.activation(out=sq[:, 0:kcn, 0:n], in_=src_all, func=AF.Square), reads=[srcT], writes=[sq])
    else:
        for kc in range(kcn):
            P.op("act", lambda e, kc=kc: e.activation(out=sq[:, kc, 0:n], in_=src(kc), func=AF.Square),
                 reads=[srcT], writes=[sq])
    for kc in range(kcn):
        P.op("pe", lambda e, kc=kc: e.matmul(pb[:, 0:n], lhsT=C.ones[:], rhs=sq[:, kc, 0:n],
                                             start=(kc == 0), stop=(kc == kcn - 1)),
             reads=[C.ones, sq], writes=[pb])
    P.op("act", lambda e: e.activation(out=C.rstd[:, 0:n], in_=pb[:, 0:n], func=AF.Ln, bias=float(dn * EPS)),
         reads=[pb], writes=[C.rstd])
    P.op("act", lambda e: e.activation(out=C.rstd[:, 0:n], in_=C.rstd[:, 0:n], func=AF.Exp, scale=-0.5),
         reads=[C.rstd], writes=[C.rstd])
    for kc in range(kcn):
        if kcn == KC and kc % 2 == 1:
            tmp = C.sg[(kc // 2) % 2]
            P.op("act", lambda e, kc=kc, tmp=tmp: e.activation(out=tmp[:, 0:n], in_=src(kc), func=AF.Identity, scale=gcol(kc)),
                 reads=[srcT, C.gs], writes=[tmp])
            P.op("pool", lambda e, kc=kc, tmp=tmp: e.tensor_tensor(out=dst(kc), in0=tmp[:, 0:n], in1=C.rstd[:, 0:n], op=ALU.mult),
                 reads=[tmp, C.rstd], writes=[dstT])
        else:
            P.op("dve", lambda e, kc=kc: e.scalar_tensor_tensor(out=dst(kc), in0=src(kc), scalar=gcol(kc),
                                                                in1=C.rstd[:, 0:n], op0=ALU.mult, op1=ALU.mult),
                 reads=[srcT, C.gs, C.rstd], writes=[dstT])


def ffn_group(C, win, wout, gcol0, nt=G // NT):
    P = C.P
    rstds = [C.rstd, C.rstd2]
    for t in range(nt):
        sl = slice(t * NT, (t + 1) * NT)
        for kc in range(KC):
            if kc % 2 == 0:
                P.op("dve", lambda e, kc=kc, sl=sl: e.tensor_scalar(out=C.h[:, kc, sl], in0=C.x[:, kc, sl], scalar1=C.gs[:, gcol0 + kc:gcol0 + kc + 1],
                                                                   scalar2=None, op0=ALU.mult), reads=[C.x, C.gs], writes=[C.h])
            else:
                P.op("act", lambda e, kc=kc, sl=sl: e.activation(out=C.h[:, kc, sl], in_=C.x[:, kc, sl], func=AF.Identity,
                                                                 scale=C.gs[:, gcol0 + kc:gcol0 + kc + 1]), reads=[C.x, C.gs], writes=[C.h])
    for t in range(nt):
        sl = slice(t * NT, (t + 1) * NT)
        rs = rstds[t % 2]
        pb = C.ps[6 + t % 2]
        P.op("act", lambda e, sl=sl: e.activation(out=C.sq[:, :, 0:NT], in_=C.x[:, :, sl], func=AF.Square), reads=[C.x], writes=[C.sq])
        for kc in range(KC):
            P.op("pe", lambda e, kc=kc, pb=pb: e.matmul(pb[:], lhsT=C.ones[:], rhs=C.sq[:, kc, 0:NT], start=(kc == 0), stop=(kc == KC - 1)),
                 reads=[C.ones, C.sq], writes=[pb])
        P.op("act", lambda e, pb=pb, rs=rs: e.activation(out=rs[:], in_=pb[:], func=AF.Ln, bias=float(D * EPS)), reads=[pb], writes=[rs])
        P.op("act", lambda e, rs=rs: e.activation(out=rs[:], in_=rs[:], func=AF.Exp, scale=-0.5), reads=[rs], writes=[rs])
    pi = 0
    for j in range(NJ):
        ws = C.wslot()
        P.dma("pool", ws[:, 0:2048], win[j], reads=[win], writes=[ws])
        for t in range(nt):
            pg = C.ps[(pi % 2) * 2]
            pu = C.ps[(pi % 2) * 2 + 1]
            sg = C.sg[pi % 2]
            su = C.su[pi % 2]
            rs = rstds[t % 2]
            pi += 1
            for half, pb in ((0, pg), (1, pu)):
                for kc in range(KC):
                    P.op("pe", lambda e, kc=kc, half=half, pb=pb, ws=ws, t=t: e.matmul(
                        pb[:], lhsT=ws[:, kc * 256 + half * 128:kc * 256 + half * 128 + 128],
                        rhs=C.h[:, kc, t * NT:(t + 1) * NT], start=(kc == 0), stop=(kc == KC - 1)),
                        reads=[ws, C.h], writes=[pb])
            P.op("dve", lambda e, pg=pg, sg=sg, rs=rs: e.tensor_tensor(out=sg[:], in0=pg[:], in1=rs[:], op=ALU.mult), reads=[pg, rs], writes=[sg])
            P.op("act", lambda e, sg=sg: e.activation(out=sg[:], in_=sg[:], func=AF.Silu), reads=[sg], writes=[sg])
            P.op("dve", lambda e, pu=pu, su=su, rs=rs: e.tensor_tensor(out=su[:], in0=pu[:], in1=rs[:], op=ALU.mult), reads=[pu, rs], writes=[su])
            P.op("dve", lambda e, su=su, sg=sg, j=j, t=t: e.tensor_tensor(
                out=C.act[:, j, t * NT:(t + 1) * NT], in0=sg[:], in1=su[:], op=ALU.mult),
                reads=[sg, su], writes=[C.act])
    for m in range(KC):
        ws = C.wslot()
        P.dma("pool", ws[:, 0:NJ * 128], wout[m], reads=[wout], writes=[ws])
        for t in range(nt):
            pb = C.ps[4 + (pi % 2)]
            pi += 1
            for j in range(NJ):
                P.op("pe", lambda e, j=j, pb=pb, ws=ws, t=t: e.matmul(
                    pb[:], lhsT=ws[:, j * 128:(j + 1) * 128], rhs=C.act[:, j, t * NT:(t + 1) * NT],
                    start=(j == 0), stop=(j == NJ - 1)), reads=[ws, C.act], writes=[pb])
            P.op("dve", lambda e, pb=pb, m=m, t=t: e.scalar_tensor_tensor(
                out=C.x[:, m, t * NT:(t + 1) * NT], in0=pb[:], scalar=0.5, in1=C.x[:, m, t * NT:(t + 1) * NT],
                op0=ALU.mult, op1=ALU.add), reads=[pb, C.x], writes=[C.x])


def tile_ffn_in(w):
    wg = w[:, :DFF].reshape(KC, 128, NJ, 128)
    wu = w[:, DFF:].reshape(KC, 128, NJ, 128)
    s = np.stack([wg, wu], axis=3)
    return np.ascontiguousarray(s.transpose(2, 1, 0, 3, 4).reshape(NJ, 128, KC * 256))


def tile_ffn_out(w):
    return np.ascontiguousarray(w.reshape(NJ, 128, KC, 128).transpose(2, 1, 0, 3).reshape(KC, 128, NJ * 128))


def tile_lin(w):
    k, n = w.shape
    return np.ascontiguousarray(w.reshape(k // 128, 128, n // 128, 128).transpose(2, 1, 0, 3).reshape(n // 128, 128, k))


def tile_rows(w):
    k, n = w.shape
    return np.ascontiguousarray(w.reshape(k // 128, 128, n).transpose(1, 0, 2))


def col(v):
    return np.ascontiguousarray(v.reshape(-1, 128).T)


def fm(xtok):
    return np.ascontiguousarray(xtok.T)


def lin_fm(C, wdram, chunk, kcn, t0, n, pb, wslot=None, src=None):
    P = C.P
    src = C.h if src is None else src
    if wslot is None:
        wslot = C.wslot()
        P.dma("pool", wslot[:, 0:kcn * 128], wdram[chunk], reads=[wdram], writes=[wslot])
    for kc in range(kcn):
        P.op("pe", lambda e, kc=kc: e.matmul(pb[:, 0:n], lhsT=wslot[:, kc * 128:(kc + 1) * 128],
                                             rhs=src[:, kc, t0:t0 + n], start=(kc == 0), stop=(kc == kcn - 1)),
             reads=[wslot, src], writes=[pb])
    return wslot


def phase1(P, C, t, stage=99):
    NC1 = 32
    xin = t["xin"]
    win = t["win"]
    wout = t["wout"]
    wq = t["wq"]
    wf = t["wf"]
    wg = t["wg"]
    wv = t["wv"]
    cons = t["cons"]
    identd = t["ident"]
    hmaskd = t["hmask"]
    x1o = t["x1o"]
    oTo = t["oTo"]
    qco = t["qco"]
    sgo = t["sgo"]
    STo = t["STo"]
    DTo = t["DTo"]
    ident = P.sbuf("identb", [128, 128], BF16)
    hmask = P.sbuf("hmaskb", [128, 512], F32)
    wvs = P.sbuf("wvs", [128, KC, D], BF16)
    vtok = P.sbuf("vtok", [128, G // 128, D], BF16)
    lb = P.sbuf("lb", [128, 8], F32)
    oml = P.sbuf("oml", [128, 8], F32)
    noml = P.sbuf("noml", [128, 8], F32)
    rmask = P.sbuf("rmask", [128, NT], F32)
    zer8 = P.sbuf("zer8", [128, 8], F32)
    actf = C.act[:].rearrange("p j t -> p (j t)").bitcast(F32)
    scr = [[T(actf[:, (s_ * 7 + i) * NT:(s_ * 7 + i + 1) * NT], "hs%d_%d" % (s_, i)) for i in range(7)] for s_ in range(2)]
    qt = [P.sbuf("qt%d" % i, [128, NT], BF16) for i in range(2)]
    kt = [P.sbuf("kt%d" % i, [128, NT], BF16) for i in range(2)]
    kh = [P.sbuf("kh%d" % i, [128, NT], BF16) for i in range(2)]
    khT = [[P.sbuf("khT%d_%d" % (i, par), [128, NT], BF16) for par in range(2)] for i in range(2)]
    scm = [P.sbuf("scm%d" % i, [128, NT], BF16) for i in range(2)]
    qc = [P.sbuf("qc%d" % i, [128, NT], BF16) for i in range(2)]
    oev = [P.sbuf("oev%d" % i, [128, NT], F32) for i in range(2)]
    sgs = C.sg
    dch = [P.sbuf("dch%d" % i, [128, 8], F32) for i in range(2)]
    S = [P.sbuf("S%d" % i, [128, 128], F32) for i in range(NH_A)]
    Sb = [[P.sbuf("Sb%d_%d" % (i, c), [128, 128], BF16) for c in range(9)] for i in range(2)]
    Dexc = [P.sbuf("Dexc%d" % i, [128, 9], F32) for i in range(NH_A)]
    Slast = [P.sbuf("Slast%d" % i, [128, 128], BF16) for i in range(NH_A)]

    P.dma("sp", C.cons[:, 0:NC1], cons[:], reads=[cons], writes=[C.cons])
    P.dma("pool", ident[:], identd[:], writes=[ident])
    P.dma("sp", hmask[:], hmaskd[:], writes=[hmask])
    P.op("dve", lambda e: e.tensor_scalar(out=C.gs[:, 0:NC1], in0=C.cons[:, 0:NC1], scalar1=float(np.sqrt(D)), scalar2=None, op0=ALU.mult),
         reads=[C.cons], writes=[C.gs])
    P.op("dve", lambda e: e.tensor_tensor(out=lb[:], in0=C.cons[:, 16:24], in1=C.cons[:, 24:32], op=ALU.subtract), reads=[C.cons], writes=[lb])
    P.op("act", lambda e: e.activation(out=oml[:], in_=lb[:], func=AF.Sigmoid, scale=-1.0), reads=[lb], writes=[oml])
    P.op("act", lambda e: e.activation(out=lb[:], in_=lb[:], func=AF.Sigmoid), reads=[lb], writes=[lb])
    P.op("dve", lambda e: e.tensor_scalar(out=noml[:], in0=oml[:], scalar1=-1.0, scalar2=None, op0=ALU.mult), reads=[oml], writes=[noml])
    P.op("pool", lambda e: e.memset(rmask[:], 1.0), writes=[rmask])
    P.op("pool", lambda e: e.memset(rmask[:].rearrange("p (c t) -> p c t", t=CH)[:, :, 0:1], 0.0), writes=[rmask])
    P.op("pool", lambda e: e.memset(zer8[:], 0.0), writes=[zer8])
    for i in range(2):
        for par in range(2):
            P.op("pool", lambda e, i=i, par=par: e.memset(khT[i][par][:], 0.0), writes=[khT[i][par]])
    for hd in range(NH_A):
        P.op("pool", lambda e, hd=hd: e.memset(S[hd][:], 0.0), writes=[S[hd]])
        P.op("pool", lambda e, hd=hd: e.memset(Slast[hd][:], 0.0), writes=[Slast[hd]])
        P.op("pool", lambda e, hd=hd: e.memset(Dexc[hd][:], 1.0), writes=[Dexc[hd]])

    it = 0
    for g in range(TOK // G):
        P.dma("sp", C.x[:], xin[:, :, g * G:(g + 1) * G], reads=[xin], writes=[C.x])
        ffn_group(C, win, wout, 0)
        P.dma("sp", x1o[:, :, g * G:(g + 1) * G], C.x[:], reads=[C.x], writes=[x1o])
        P.barrier()
        for t in range(G // NT):
            rmsnorm_fm(C, lambda kc, t=t: C.x[:, kc, t * NT:(t + 1) * NT], C.x, KC, t * NT, NT,
                       lambda kc: C.gs[:, 8 + kc:9 + kc],
                       lambda kc, t=t: C.h[:, kc, t * NT:(t + 1) * NT], C.h, C.ps[7], D, src_all=C.x[:, :, t * NT:(t + 1) * NT])
        if g == 0:
            for kc in range(KC):
                P.dma("pool", wvs[:, kc, :], wv[:, kc * D:(kc + 1) * D], reads=[wv], writes=[wvs])
        for tb in range(G // 128):
            for half in range(2):
                pb = C.ps[(tb * 2 + half) % 4]
                for kc in range(KC):
                    P.op("pe", lambda e, kc=kc, tb=tb, half=half, pb=pb: e.matmul(
                        pb[:], lhsT=C.h[:, kc, tb * 128:(tb + 1) * 128], rhs=wvs[:, kc, half * 512:(half + 1) * 512],
                        start=(kc == 0), stop=(kc == KC - 1)), reads=[C.h, wvs], writes=[pb])
                P.op("act", lambda e, tb=tb, half=half, pb=pb: e.copy(out=vtok[:, tb, half * 512:(half + 1) * 512], in_=pb[:]),
                     reads=[pb], writes=[vtok])
        if stage < 2:
            continue
        for c in range(8):
            ws = None
            for t in range(G // NT):
                pb = C.ps[(c * 2 + t) % 4]
                ws = lin_fm(C, wg, c, KC, t * NT, NT, pb, wslot=ws)
                sg = sgs[(c * 2 + t) % 2]
                P.op("act", lambda e, pb=pb, sg=sg: e.activation(out=sg[:], in_=pb[:], func=AF.Sigmoid), reads=[pb], writes=[sg])
                P.dma("sp", sgo[:, c, g * G + t * NT:g * G + (t + 1) * NT], sg[:], reads=[sg], writes=[sgo])
        if stage < 3:
            continue
        def front(hd, t, i2):
            sig, fv, kk, bb, eb, enb, ek = scr[i2]
            tok0 = g * G + t * NT
            pq = C.ps[4 * i2]
            pf = C.ps[4 * i2 + 1]
            wsq = C.wsl[2 * i2]
            wsf = C.wsl[2 * i2 + 1]
            if t == 0:
                P.dma("pool", wsq[:, 0:KC * 128], wq[hd], reads=[wq], writes=[wsq])
                P.dma("pool", wsf[:, 0:KC * 128], wf[hd], reads=[wf], writes=[wsf])
            lin_fm(C, wq, hd, KC, t * NT, NT, pq, wslot=wsq)
            lin_fm(C, wf, hd, KC, t * NT, NT, pf, wslot=wsf)
            c3 = lambda a: a[:].rearrange("p (c t) -> p c t", t=CH)
            P.op("act", lambda e, pf=pf: e.activation(out=sig[:], in_=pf[:], func=AF.Sigmoid), reads=[pf], writes=[sig])
            P.op("dve", lambda e, hd=hd: e.tensor_scalar(out=fv[:], in0=sig[:], scalar1=oml[:, hd:hd + 1], scalar2=lb[:, hd:hd + 1],
                                                       op0=ALU.mult, op1=ALU.add), reads=[sig, oml, lb], writes=[fv])
            P.op("act", lambda e: e.activation(out=fv[:], in_=fv[:], func=AF.Ln), reads=[fv], writes=[fv])
            P.op("dve", lambda e, hd=hd: e.tensor_scalar(out=kk[:], in0=sig[:], scalar1=noml[:, hd:hd + 1], scalar2=oml[:, hd:hd + 1],
                                                       op0=ALU.mult, op1=ALU.add), reads=[sig, noml, oml], writes=[kk])
            P.op("dve", lambda e: e.tensor_tensor_scan(out=bb[:], data0=rmask[:], data1=fv[:], initial=0.0, op0=ALU.mult, op1=ALU.add),
                 reads=[rmask, fv], writes=[bb])
            P.op("act", lambda e: e.activation(out=eb[:], in_=bb[:], func=AF.Exp), reads=[bb], writes=[eb])
            P.op("act", lambda e: e.activation(out=enb[:], in_=bb[:], func=AF.Exp, scale=-1.0), reads=[bb], writes=[enb])
            P.op("dve", lambda e, pq=pq, i2=i2: e.scalar_tensor_tensor(out=qt[i2][:], in0=pq[:], scalar=float(128 ** -0.5), in1=eb[:],
                                                                      op0=ALU.mult, op1=ALU.mult), reads=[pq, eb], writes=[qt[i2]])
            P.op("pool", lambda e, i2=i2: e.tensor_tensor(out=kt[i2][:], in0=kk[:], in1=enb[:], op=ALU.mult), reads=[kk, enb], writes=[kt[i2]])
            P.op("dve", lambda e: e.tensor_tensor(out=c3(ek), in0=c3(bb)[:, :, CH - 1:CH].to_broadcast([128, 8, CH]), in1=c3(bb),
                                                  op=ALU.subtract), reads=[bb], writes=[ek])
            P.op("act", lambda e: e.activation(out=ek[:], in_=ek[:], func=AF.Exp), reads=[ek], writes=[ek])
            P.op("pool", lambda e, i2=i2: e.tensor_tensor(out=kh[i2][:], in0=kk[:], in1=ek[:], op=ALU.mult), reads=[kk, ek], writes=[kh[i2]])
            P.op("act", lambda e, i2=i2: e.activation(out=dch[i2][:], in_=c3(bb)[:, :, CH - 1], func=AF.Exp), reads=[bb], writes=[dch[i2]])
            P.op("dve", lambda e, hd=hd, i2=i2: e.tensor_tensor_scan(out=Dexc[hd][:, 1:9], data0=dch[i2][:], data1=zer8[:],
                                                                     initial=Dexc[hd][:, 0:1], op0=ALU.mult, op1=ALU.add),
                 reads=[dch[i2], zer8, Dexc[hd]], writes=[Dexc[hd]])
            P.op("dve", lambda e, hd=hd, i2=i2: e.tensor_tensor(out=c3(qc[i2]), in0=c3(qt[i2]),
                                                                in1=Dexc[hd][:, 0:8].unsqueeze(2).to_broadcast([128, 8, CH]), op=ALU.mult),
                 reads=[qt[i2], Dexc[hd]], writes=[qc[i2]])
            P.dma("sp", qco[hd, :, tok0:tok0 + NT], qc[i2][:], reads=[qc[i2]], writes=[qco])
            P.op("dve", lambda e, hd=hd: e.tensor_copy(out=Dexc[hd][:, 0:1], in_=Dexc[hd][:, 8:9]), reads=[Dexc[hd]], writes=[Dexc[hd]])


        def back(hd, t, i2):
            tok0 = g * G + t * NT
            ptr = C.ps[4 * i2 + 2]
            ptrb = ptr[:].bitcast(BF16)
            for pr in range(4):
                P.op("pe", lambda e, pr=pr, i2=i2: e.transpose(ptrb[:, pr * 128:(pr + 1) * 128], kh[i2][:, pr * 128:(pr + 1) * 128], ident[:]),
                     reads=[kh[i2], ident], writes=[ptr])
            for par in range(2):
                P.op("act", lambda e, i2=i2, par=par: e.copy(out=khT[i2][par][par * 64:(par + 1) * 64, :], in_=ptrb[par * 64:(par + 1) * 64, 0:NT]),
                     reads=[ptr], writes=[khT[i2][par]])
            psc = C.ps[4 * i2 + 2]
            for pr in range(4):
                P.op("pe", lambda e, pr=pr, i2=i2: e.matmul(psc[:, pr * 128:(pr + 1) * 128], lhsT=kt[i2][:, pr * 128:(pr + 1) * 128],
                                                            rhs=qt[i2][:, pr * 128:(pr + 1) * 128], start=True, stop=True),
                     reads=[kt[i2], qt[i2]], writes=[psc])
            P.op("dve", lambda e, i2=i2: e.tensor_tensor(out=scm[i2][:], in0=psc[:], in1=hmask[:], op=ALU.mult), reads=[psc, hmask], writes=[scm[i2]])
            pU = C.ps[4 * i2 + 3]
            P.op("pool", lambda e, i2=i2, hd=hd: e.tensor_copy(out=Sb[i2][0][:], in_=Slast[hd][:]), reads=[Slast[hd]], writes=[Sb[i2][0]])
            for half in range(2):
                for c4 in range(4):
                    c = half * 4 + c4
                    pr, par = c // 2, c % 2
                    tb = t * 4 + pr
                    P.op("pe", lambda e, c4=c4, pr=pr, par=par, tb=tb, i2=i2, hd=hd: e.matmul(
                        pU[:, c4 * 128:(c4 + 1) * 128], lhsT=khT[i2][par][:, pr * 128:(pr + 1) * 128],
                        rhs=vtok[:, tb, hd * 128:(hd + 1) * 128], start=True, stop=True),
                        reads=[khT[i2][par], vtok], writes=[pU])
                for c4 in range(4):
                    c = half * 4 + c4
                    P.op("dve", lambda e, c=c, c4=c4, i2=i2, hd=hd: e.scalar_tensor_tensor(
                        out=S[hd][:], in0=S[hd][:], scalar=dch[i2][:, c:c + 1], in1=pU[:, c4 * 128:(c4 + 1) * 128],
                        op0=ALU.mult, op1=ALU.add), reads=[S[hd], dch[i2], pU], writes=[S[hd]])
                    dst = Sb[i2][c + 1] if c < 7 else Slast[hd]
                    P.op("pool", lambda e, dst=dst, hd=hd: e.tensor_copy(out=dst[:], in_=S[hd][:]), reads=[S[hd]], writes=[dst])
            po = C.ps[4 * i2 + 2]
            for pr in range(4):
                tb = t * 4 + pr
                P.op("pe", lambda e, pr=pr, tb=tb, i2=i2, hd=hd: e.matmul(po[:, pr * 128:(pr + 1) * 128], lhsT=vtok[:, tb, hd * 128:(hd + 1) * 128],
                                                                          rhs=scm[i2][:, pr * 128:(pr + 1) * 128], start=True, stop=False),
                     reads=[vtok, scm[i2]], writes=[po])
                for par in range(2):
                    c = pr * 2 + par
                    P.op("pe", lambda e, c=c, i2=i2: e.matmul(po[:, c * 64:(c + 1) * 64], lhsT=Sb[i2][c][:], rhs=qt[i2][:, c * 64:(c + 1) * 64],
                                                              start=False, stop=(c % 2 == 1)), reads=[Sb[i2][c], qt[i2]], writes=[po])
            P.op("act", lambda e, i2=i2: e.copy(out=oev[i2][:], in_=po[:]), reads=[po], writes=[oev[i2]])
            P.dma("sp", oTo[hd, :, tok0:tok0 + NT], oev[i2][:], reads=[oev[i2]], writes=[oTo])


        streams = []
        for s_ in range(2):
            P.defer = []
            for hd in range(s_, NH_A, 2):
                for t in range(G // NT):
                    front(hd, t, s_)
                    back(hd, t, s_)
            streams.append(P.defer)
            P.defer = None
        P.run_streams(streams)
        P.barrier()
    for hd in range(NH_A):
        P.dma("sp", STo[:, hd, :], S[hd][:], reads=[S[hd]], writes=[STo])
        P.op("dve", lambda e, hd=hd: e.tensor_copy(out=zer8[:, hd:hd + 1], in_=Dexc[hd][:, 0:1]), reads=[Dexc[hd]], writes=[zer8])
    P.dma("sp", DTo[:], zer8[:], reads=[zer8], writes=[DTo])


NC2 = 56


def ple_group(C, wgate, wple, pin, g, gcol0, pbf):
    P = C.P
    nt = G // NT
    for t in range(nt):
        rmsnorm_fm(C, lambda kc, t=t: C.x[:, kc, t * NT:(t + 1) * NT], C.x, KC, t * NT, NT,
                   lambda kc: C.gs[:, gcol0 + kc:gcol0 + kc + 1],
                   lambda kc, t=t: C.h[:, kc, t * NT:(t + 1) * NT], C.h, C.ps[7], D, src_all=C.x[:, :, t * NT:(t + 1) * NT])
    for kc in range(2):
        P.dma("pool", pbf[:, kc, :], pin[:, kc, g * G:(g + 1) * G], reads=[pin], writes=[pbf])
    i = 0
    for m in range(KC):
        wsg = None
        wsp = None
        for t in range(nt):
            pg = C.ps[(i % 2) * 2]
            pp = C.ps[(i % 2) * 2 + 1]
            sg = C.sg[i % 2]
            i += 1
            wsg = lin_fm(C, wgate, m, KC, t * NT, NT, pg, wslot=wsg)
            wsp = lin_fm(C, wple, m, 2, t * NT, NT, pp, wslot=wsp, src=pbf)
            P.op("act", lambda e, pg=pg, sg=sg: e.activation(out=sg[:], in_=pg[:], func=AF.Sigmoid), reads=[pg], writes=[sg])
            P.op("dve", lambda e, pp=pp, sg=sg: e.tensor_tensor(out=sg[:], in0=sg[:], in1=pp[:], op=ALU.mult), reads=[sg, pp], writes=[sg])
            P.op("pool", lambda e, sg=sg, m=m, t=t: e.tensor_tensor(out=C.x[:, m, t * NT:(t + 1) * NT], in0=C.x[:, m, t * NT:(t + 1) * NT],
                                                                  in1=sg[:], op=ALU.add), reads=[sg, C.x], writes=[C.x])


def rope_table(C, posd, tok0, n, tab, tmpf, tmpi, invcol, phcol):
    P = C.P
    P.dma("sp", tmpi[:, 0:n], posd[0:1, tok0:tok0 + n].to_broadcast([128, n]), reads=[posd], writes=[tmpi])
    P.op("dve", lambda e: e.tensor_copy(out=tab[:, 0:n], in_=tmpi[:, 0:n]), reads=[tmpi], writes=[tab])
    P.op("dve", lambda e: e.tensor_scalar(out=tab[:, 0:n], in0=tab[:, 0:n], scalar1=C.cons[:, invcol:invcol + 1],
                                          scalar2=C.cons[:, phcol:phcol + 1], op0=ALU.mult, op1=ALU.add), reads=[tab, C.cons], writes=[tab])
    P.op("dve", lambda e: e.tensor_scalar(out=tmpf[:, 0:n], in0=tab[:, 0:n], scalar1=float(1 / (2 * np.pi)), scalar2=None, op0=ALU.mult),
         reads=[tab], writes=[tmpf])
    P.op("dve", lambda e: e.tensor_copy(out=tmpi[:, 0:n], in_=tmpf[:, 0:n]), reads=[tmpf], writes=[tmpi])
    P.op("dve", lambda e: e.tensor_copy(out=tmpf[:, 0:n], in_=tmpi[:, 0:n]), reads=[tmpi], writes=[tmpf])
    P.op("dve", lambda e: e.scalar_tensor_tensor(out=tab[:, 0:n], in0=tmpf[:, 0:n], scalar=float(-2 * np.pi), in1=tab[:, 0:n],
                                                 op0=ALU.mult, op1=ALU.add), reads=[tmpf, tab], writes=[tab])
    P.op("dve", lambda e: e.tensor_scalar(out=tmpf[:, 0:n], in0=tab[:, 0:n], scalar1=float(np.pi), scalar2=float(-2 * np.pi),
                                          op0=ALU.is_gt, op1=ALU.mult), reads=[tab], writes=[tmpf])
    P.op("dve", lambda e: e.tensor_tensor(out=tab[:, 0:n], in0=tab[:, 0:n], in1=tmpf[:, 0:n], op=ALU.add), reads=[tab, tmpf], writes=[tab])
    P.op("dve", lambda e: e.tensor_scalar(out=tmpf[:, 0:n], in0=tab[:, 0:n], scalar1=float(-np.pi), scalar2=float(2 * np.pi),
                                          op0=ALU.is_lt, op1=ALU.mult), reads=[tab], writes=[tmpf])
    P.op("dve", lambda e: e.tensor_tensor(out=tab[:, 0:n], in0=tab[:, 0:n], in1=tmpf[:, 0:n], op=ALU.add), reads=[tab, tmpf], writes=[tab])
    P.op("act", lambda e: e.activation(out=tab[:, 0:n], in_=tab[:, 0:n], func=AF.Sin), reads=[tab], writes=[tab])


def phase2(P, C, t, dbg=False):
    x1i = t["x1i"]
    oTi = t["oTi"]
    qci = t["qci"]
    sgi = t["sgi"]
    STi = t["STi"]
    DTi = t["DTi"]
    pin = t["pin"]
    posd = t["pos"]
    cons = t["cons"]
    stk = t["stk"]
    wao = t["wao"]
    win0 = t["win0"]
    wout0 = t["wout0"]
    wpg = t["wpg"]
    wpi = t["wpi"]
    wkd = t["wkd"]
    win1 = t["win1"]
    wout1 = t["wout1"]
    wdq = t["wdq"]
    x5o = t["x5o"]
    lat_dst = t["lat_dst"]
    lat_done = t["lat_done"]
    Sst = P.sbuf("Sst", [128, NH_A, 128], F32)
    Sstb = P.sbuf("Sstb", [128, NH_A, 128], BF16)
    Ul = P.sbuf("Ul", [128, NH_A, 128], F32)
    Dl = P.sbuf("Dl", [128, NH_A], F32)
    ot = [P.sbuf("ot%d" % i, [128, NT], F32) for i in range(2)]
    qcl = [P.sbuf("qcl%d" % i, [128, NT], BF16) for i in range(2)]
    sgl = [P.sbuf("sgl%d" % i, [128, NT], F32) for i in range(2)]
    pbf = P.sbuf("pbf", [128, 2, G], BF16)
    latf = P.sbuf("latf", [128, 4, NT], F32)
    latb = P.sbuf("latb", [128, 4, NT], BF16)
    tab = P.sbuf("tab", [128, G], F32)
    tmpf = P.sbuf("tmpf", [128, G], F32)
    tmpi = P.sbuf("tmpi", [128, G], I32)
    stkb = P.sbuf("stkb", [128, 64], BF16)
    rT = P.sbuf("rT", [128, NT], BF16)
    krb = P.sbuf("krb", [64, NT], BF16)

    P.dma("sp", C.cons[:, 0:NC2], cons[:], reads=[cons], writes=[C.cons])
    P.dma("pool", stkb[:], stk[:], writes=[stkb])
    for (a, b, sc) in ((0, 40, np.sqrt(D)), (40, 42, np.sqrt(KVL)), (42, 46, np.sqrt(QL)), (46, 47, np.sqrt(128.0))):
        P.op("dve", lambda e, a=a, b=b, sc=sc: e.tensor_scalar(out=C.gs[:, a:b], in0=C.cons[:, a:b], scalar1=float(sc), scalar2=None, op0=ALU.mult),
             reads=[C.cons], writes=[C.gs])
    P.op("pool", lambda e: e.memset(Sst[:], 0.0), writes=[Sst])
    for r in range(4):
        P.dma("sp", Ul[:], STi[r], reads=[STi], writes=[Ul])
        P.dma("sp", Dl[:], DTi[r], reads=[DTi], writes=[Dl])
        w = C.cons[:, 49 + r:50 + r]
        P.op("dve", lambda e, w=w: e.tensor_scalar(out=Dl[:], in0=Dl[:], scalar1=-1.0, scalar2=w, op0=ALU.add, op1=ALU.mult), reads=[Dl, C.cons], writes=[Dl])
        P.op("dve", lambda e: e.tensor_scalar(out=Dl[:], in0=Dl[:], scalar1=1.0, scalar2=None, op0=ALU.add), reads=[Dl], writes=[Dl])
        P.op("dve", lambda e, w=w: e.tensor_scalar(out=Ul[:], in0=Ul[:], scalar1=w, scalar2=None, op0=ALU.mult), reads=[Ul, C.cons], writes=[Ul])
        for hd in range(NH_A):
            P.op("dve", lambda e, hd=hd: e.scalar_tensor_tensor(out=Sst[:, hd, :], in0=Sst[:, hd, :], scalar=Dl[:, hd:hd + 1], in1=Ul[:, hd, :],
                                                                op0=ALU.mult, op1=ALU.add), reads=[Sst, Dl, Ul], writes=[Sst])
    P.op("act", lambda e: e.copy(out=Sstb[:], in_=Sst[:]), reads=[Sst], writes=[Sstb])

    nt = G // NT
    it = 0
    for g in range(TOK // G):
        P.dma("sp", C.x[:], x1i[:, :, g * G:(g + 1) * G], reads=[x1i], writes=[C.x])
        for hd in range(NH_A):
            for t in range(nt):
                i2 = it % 2
                it += 1
                tok0 = g * G + t * NT
                P.dma("sp", ot[i2][:], oTi[hd, :, tok0:tok0 + NT], reads=[oTi], writes=[ot[i2]])
                P.dma("sp", qcl[i2][:], qci[hd, :, tok0:tok0 + NT], reads=[qci], writes=[qcl[i2]])
                P.dma("sp", sgl[i2][:], sgi[:, hd, tok0:tok0 + NT], reads=[sgi], writes=[sgl[i2]])
                po = C.ps[i2]
                P.op("pe", lambda e, hd=hd, i2=i2, po=po: e.matmul(po[:], lhsT=Sstb[:, hd, :], rhs=qcl[i2][:], start=True, stop=True),
                     reads=[Sstb, qcl[i2]], writes=[po])
                P.op("dve", lambda e, i2=i2, po=po: e.tensor_tensor(out=ot[i2][:], in0=ot[i2][:], in1=po[:], op=ALU.add), reads=[ot[i2], po], writes=[ot[i2]])
                rmsnorm_fm(C, lambda kc, i2=i2: ot[i2][:], ot[i2], 1, 0, NT, lambda kc: C.gs[:, 46:47],
                           lambda kc: latf[:, 0, :], latf, C.ps[7], 128)
                P.op("dve", lambda e, hd=hd, t=t, i2=i2: e.tensor_tensor(out=C.h[:, hd, t * NT:(t + 1) * NT], in0=latf[:, 0, :], in1=sgl[i2][:], op=ALU.mult),
                     reads=[latf, sgl[i2]], writes=[C.h])
        i = 0
        for m in range(KC):
            ws = None
            for t in range(nt):
                pb = C.ps[2 + i % 2]
                i += 1
                ws = lin_fm(C, wao, m, KC, t * NT, NT, pb, wslot=ws)
                P.op("dve", lambda e, pb=pb, m=m, t=t: e.tensor_tensor(out=C.x[:, m, t * NT:(t + 1) * NT], in0=C.x[:, m, t * NT:(t + 1) * NT], in1=pb[:], op=ALU.add),
                     reads=[pb, C.x], writes=[C.x])
        ffn_group(C, win0, wout0, 0)
        ple_group(C, wpg, wpi, pin, g, 8, pbf)
        rope_table(C, posd, g * G, G, tab, tmpf, tmpi, 47, 48)
        for t in range(nt):
            rmsnorm_fm(C, lambda kc, t=t: C.x[:, kc, t * NT:(t + 1) * NT], C.x, KC, t * NT, NT,
                       lambda kc: C.gs[:, 16 + kc:17 + kc],
                       lambda kc, t=t: C.h[:, kc, t * NT:(t + 1) * NT], C.h, C.ps[7], D, src_all=C.x[:, :, t * NT:(t + 1) * NT])
        wsl3 = [None, None, None]
        for t in range(nt):
            tok0 = g * G + t * NT
            for c in range(3):
                pb = C.ps[c]
                wsl3[c] = lin_fm(C, wkd, c, KC, t * NT, NT, pb, wslot=wsl3[c])
                if c < 2:
                    P.op("act", lambda e, c=c, pb=pb: e.copy(out=latf[:, c, :], in_=pb[:]), reads=[pb], writes=[latf])
                else:
                    P.op("dve", lambda e, pb=pb, t=t: e.tensor_tensor(out=rT[:], in0=pb[:], in1=tab[:, t * NT:(t + 1) * NT], op=ALU.mult),
                         reads=[pb, tab], writes=[rT])
            pk = C.ps[3]
            P.op("pe", lambda e, pk=pk: e.matmul(pk[0:64, :], lhsT=stkb[:], rhs=rT[:], start=True, stop=True), reads=[stkb, rT], writes=[pk])
            P.op("act", lambda e, pk=pk: e.copy(out=krb[0:64, :], in_=pk[0:64, :]), reads=[pk], writes=[krb])
            cq_ap, ckv_ap, kr_ap, latT = lat_dst(tok0 // NT)
            P.dma("sp", kr_ap, krb[0:64, :], reads=[krb], writes=[latT])
            rmsnorm_fm(C, lambda kc: latf[:, kc, :], latf, 2, 0, NT, lambda kc: C.gs[:, 40 + kc:41 + kc],
                       lambda kc: latb[:, kc, :], latb, C.ps[7], KVL)
            P.dma("sp", ckv_ap, latb[:, 0:2, :], reads=[latb], writes=[latT])
        ffn_group(C, win1, wout1, 24)
        P.dma("sp", x5o[:, :, g * G:(g + 1) * G], C.x[:], reads=[C.x], writes=[x5o])
        for t in range(nt):
            rmsnorm_fm(C, lambda kc, t=t: C.x[:, kc, t * NT:(t + 1) * NT], C.x, KC, t * NT, NT,
                       lambda kc: C.gs[:, 32 + kc:33 + kc],
                       lambda kc, t=t: C.h[:, kc, t * NT:(t + 1) * NT], C.h, C.ps[7], D, src_all=C.x[:, :, t * NT:(t + 1) * NT])
        wsl4 = [None] * 4
        for t in range(nt):
            tok0 = g * G + t * NT
            for c in range(4):
                pb = C.ps[c]
                wsl4[c] = lin_fm(C, wdq, c, KC, t * NT, NT, pb, wslot=wsl4[c])
                P.op("act", lambda e, c=c, pb=pb: e.copy(out=latf[:, c, :], in_=pb[:]), reads=[pb], writes=[latf])
            rmsnorm_fm(C, lambda kc: latf[:, kc, :], latf, 4, 0, NT, lambda kc: C.gs[:, 42 + kc:43 + kc],
                       lambda kc: latb[:, kc, :], latb, C.ps[7], QL)
            cq_ap, ckv_ap, kr_ap, latT = lat_dst(tok0 // NT)
            P.dma("sp", cq_ap, latb[:], reads=[latb], writes=[latT])
            lat_done(tok0 // NT)


HPC = 4
NQT = SEQ // NT
SCALE = float((128 + ROPE) ** -0.5)


def phase3(P, C0, t, nqt=NQT, nheads=HPC, stop=9):
    lat = t["lat"]
    cq_src = t["cq_src"]
    ckv_src = t["ckv_src"]
    kr_src = t["kr_src"]
    o_dst = t["o_dst"]
    head_done = t["head_done"]
    posd = t["pos"]
    cons = t["cons"]
    stk = t["stk"]
    trid = t["tri"]
    wuq = t["wuq"]
    wuk = t["wuk"]
    wuv = t["wuv"]

    class CC:
        pass
    C = CC()
    C.P = P
    C.ps = P.ps
    C.cons = P.sbuf("cons", [128, 8], F32)
    ones = P.sbuf("ones", [128, 128], BF16)
    onesf = P.sbuf("onesf", [128, 128], F32)
    ckv = P.sbuf("ckv", [128, 2, SEQ], BF16)
    Bk = P.sbuf("Bk", [128, SEQ], BF16)
    Ak = [P.sbuf("Ak%d" % i, [128, SEQ], BF16) for i in range(2)]
    Vh = [P.sbuf("Vh%d" % i, [128, SEQ // 128, 128], BF16) for i in range(2)]
    tab = P.sbuf("tab", [128, SEQ], F32)
    stkb = P.sbuf("stkb", [128, 64], BF16)
    tri = P.sbuf("trib", [128, 128], F32)
    wq = [P.sbuf("wq%d" % i, [128, 2, QL], BF16) for i in range(2)]
    wk = [P.sbuf("wk%d" % i, [128, KVL], BF16) for i in range(2)]
    wv = [P.sbuf("wv%d" % i, [128, KVL], BF16) for i in range(2)]
    mx = P.sbuf("mx", [128, 8], F32)
    mtmp = P.sbuf("setup_tmp", [128, 4096], F32)
    m_alias = P.mark()

    P.dma("sp", C.cons[:], cons[:], reads=[cons], writes=[C.cons])
    P.dma("pool", stkb[:], stk[:], writes=[stkb])
    P.dma("sp", tri[:], trid[:], writes=[tri])
    P.op("pool", lambda e: e.memset(ones[:], 1.0), writes=[ones])
    P.op("pool", lambda e: e.memset(onesf[:], 1.0), writes=[onesf])
    P.op("pool", lambda e: e.memset(Bk[:], 0.0), writes=[Bk])
    P.op("pool", lambda e: e.memset(Bk[64:65, :], 1.0), writes=[Bk])
    P.op("pool", lambda e: e.memset(mx[:], 0.0), writes=[mx])
    for q in range(nqt):
        for kc in range(2):
            P.dma("sp", ckv[:, kc, q * NT:(q + 1) * NT], ckv_src(kc, q), reads=[lat], writes=[ckv])
        P.dma("sp", Bk[0:64, q * NT:(q + 1) * NT], kr_src(q), reads=[lat], writes=[Bk])
    tmpf = T(mtmp[:, 0:2048], "tmpf")
    tmpi = T(mtmp[:, 2048:4096].bitcast(I32), "tmpi")
    for q4 in range(4):
        tv = T(tab[:, q4 * 2048:(q4 + 1) * 2048], "tabv%d" % q4)
        rope_table(C, posd, q4 * 2048, 2048, tv, tmpf, tmpi, 0, 1)
    P.barrier()
    P.atop = m_alias - 4096 * 4
    aq = [P.sbuf("aq%d" % i, [128, NT], BF16) for i in range(3)]
    bq = [P.sbuf("bq%d" % i, [128, NT], BF16) for i in range(3)]
    cql = [P.sbuf("cql%d" % i, [128, 4, NT], BF16) for i in range(3)]
    sqA = [P.sbuf("sqA%d" % i, [128, NT], BF16) for i in range(2)]
    sqB = [P.sbuf("sqB%d" % i, [128, NT], BF16) for i in range(2)]
    rT = [P.sbuf("rT%d" % i, [128, NT], BF16) for i in range(2)]
    tq = P.sbuf("tq", [128, NT], F32)
    NPT = 6
    pt = [P.sbuf("pt%d" % i, [128, NT], BF16) for i in range(NPT)]
    acc = [[P.sbuf("acc%d_%d" % (j, i), [128, NT], F32) for i in range(2)] for j in range(2)]
    rl = P.sbuf("rl", [128, NT], F32)
    on = [P.sbuf("on%d" % i, [128, NT], BF16) for i in range(2)]
    for i in range(3):
        P.op("pool", lambda e, i=i: e.memset(bq[i][:], 0.0), writes=[bq[i]])
    for i in range(2):
        P.op("pool", lambda e, i=i: e.memset(sqB[i][:], 0.0), writes=[sqB[i]])

    state = {"bank": 0, "wide": True, "sq": 0}

    def nxt_bank():
        state["bank"] += 1
        return C.ps[state["bank"] % 8] if state["wide"] else C.ps[6 + state["bank"] % 2]

    def colmax(pb, col):
        P.op("dve", lambda e: e.tensor_reduce(out=mx[:, 3:4], in_=pb[:], axis=mybir.AxisListType.X, op=ALU.max), reads=[pb], writes=[mx])
        P.op("dve", lambda e: e.tensor_tensor(out=mx[:, col:col + 1], in0=mx[:, col:col + 1], in1=mx[:, 3:4], op=ALU.max), reads=[mx], writes=[mx])

    for kt in range(nqt):
        pb = nxt_bank()
        sB = sqB[kt % 2]
        P.op("act", lambda e, kt=kt, sB=sB: e.activation(out=sB[0:64, :], in_=Bk[0:64, kt * NT:(kt + 1) * NT], func=AF.Square), reads=[Bk], writes=[sB])
        P.op("pe", lambda e, sB=sB, pb=pb: e.matmul(pb[:], lhsT=ones[:], rhs=sB[:], start=True, stop=True), reads=[ones, sB], writes=[pb])
        colmax(pb, 0)

    def load_w(hn):
        P.dma("pool", wq[hn % 2][:, 0, :], wuq[2 * hn], reads=[wuq], writes=[wq[hn % 2]])
        P.dma("pool", wq[hn % 2][:, 1, :], wuq[2 * hn + 1], reads=[wuq], writes=[wq[hn % 2]])
        P.dma("pool", wk[hn % 2][:], wuk[hn], reads=[wuk], writes=[wk[hn % 2]])
        P.dma("pool", wv[hn % 2][:], wuv[hn], reads=[wuv], writes=[wv[hn % 2]])
        P.op("pool", lambda e: e.memset(mx[:, 1 + hn % 2:2 + hn % 2], 0.0), writes=[mx])

    def kv_tile(hn, kt):
        A, V, wkk, wvv = Ak[hn % 2], Vh[hn % 2], wk[hn % 2], wv[hn % 2]
        pb = nxt_bank()
        for kc in range(2):
            P.op("pe", lambda e, kc=kc: e.matmul(pb[:], lhsT=wkk[:, kc * 128:(kc + 1) * 128], rhs=ckv[:, kc, kt * NT:(kt + 1) * NT],
                                                 start=(kc == 0), stop=(kc == 1)), reads=[wkk, ckv], writes=[pb])
        yield
        P.op("act", lambda e: e.copy(out=A[:, kt * NT:(kt + 1) * NT], in_=pb[:]), reads=[pb], writes=[A])
        state["sq"] += 1
        sA = sqA[state["sq"] % 2]
        P.op("act", lambda e: e.activation(out=sA[:], in_=pb[:], func=AF.Square), reads=[pb], writes=[sA])
        yield
        pn = nxt_bank()
        P.op("pe", lambda e: e.matmul(pn[:], lhsT=ones[:], rhs=sA[:], start=True, stop=True), reads=[ones, sA], writes=[pn])
        yield
        colmax(pn, 1 + hn % 2)
        pv = nxt_bank()
        for k4 in range(4):
            kb = kt * 4 + k4
            for kc in range(2):
                P.op("pe", lambda e, kc=kc, kb=kb, k4=k4: e.matmul(pv[:, k4 * 128:(k4 + 1) * 128], lhsT=ckv[:, kc, kb * 128:(kb + 1) * 128],
                                                                   rhs=wvv[:, kc * 128:(kc + 1) * 128], start=(kc == 0), stop=(kc == 1)),
                     reads=[ckv, wvv], writes=[pv])
        yield
        P.op("dve", lambda e: e.tensor_copy(out=V[:, kt * 4:(kt + 1) * 4, :].rearrange("p a b -> p (a b)"), in_=pv[:]), reads=[pv], writes=[V])

    def kv_finish(hn):
        c = 4 + hn % 2
        P.op("dve", lambda e: e.tensor_tensor(out=mx[:, c:c + 1], in0=mx[:, 1 + hn % 2:2 + hn % 2], in1=mx[:, 0:1], op=ALU.add), reads=[mx], writes=[mx])
        yield
        P.op("act", lambda e: e.activation(out=mx[:, c:c + 1], in_=mx[:, c:c + 1], func=AF.Ln), reads=[mx], writes=[mx])
        P.op("act", lambda e: e.activation(out=mx[:, c:c + 1], in_=mx[:, c:c + 1], func=AF.Exp, scale=0.5), reads=[mx], writes=[mx])
        yield
        P.op("dve", lambda e: e.tensor_scalar(out=mx[:, c:c + 1], in0=mx[:, c:c + 1], scalar1=-1.008 * SCALE, scalar2=None, op0=ALU.mult), reads=[mx], writes=[mx])
        yield

    cq_loaded = set()

    def load_cq(hn, qt):
        g = hn * nqt + qt
        if g in cq_loaded or hn >= nheads:
            return
        cq_loaded.add(g)
        P.dma("sp", cql[g % 3][:], cq_src(qt * NT, NT), reads=[lat], writes=[cql[g % 3]])

    def q_tile(hn, qt):
        g = hn * nqt + qt
        a, b, cq, wqq = aq[g % 3], bq[g % 3], cql[g % 3], wq[hn % 2]
        if g not in cq_loaded:
            load_cq(hn, qt)
        pa = nxt_bank()
        pr = nxt_bank()
        for half, pb in ((0, pa), (1, pr)):
            for kc in range(4):
                P.op("pe", lambda e, kc=kc, half=half, pb=pb: e.matmul(pb[:], lhsT=wqq[:, half, kc * 128:(kc + 1) * 128], rhs=cq[:, kc, :],
                                                                       start=(kc == 0), stop=(kc == 3)), reads=[wqq, cq], writes=[pb])
        yield
        P.op("act", lambda e: e.activation(out=a[:], in_=pa[:], func=AF.Identity, scale=SCALE), reads=[pa], writes=[a])
        state["sq"] += 1
        sA, sB = sqA[state["sq"] % 2], sqB[state["sq"] % 2]
        P.op("act", lambda e: e.activation(out=sA[:], in_=pa[:], func=AF.Square), reads=[pa], writes=[sA])
        r = rT[g % 2]
        P.op("dve", lambda e: e.tensor_tensor(out=r[:], in0=pr[:], in1=tab[:, qt * NT:(qt + 1) * NT], op=ALU.mult), reads=[pr, tab], writes=[r])
        yield
        pk = nxt_bank()
        P.op("pe", lambda e: e.matmul(pk[0:64, :], lhsT=stkb[:], rhs=r[:], start=True, stop=True), reads=[stkb, r], writes=[pk])
        yield
        P.op("act", lambda e: e.activation(out=b[0:64, :], in_=pk[0:64, :], func=AF.Identity, scale=SCALE), reads=[pk], writes=[b])
        P.op("act", lambda e: e.activation(out=sB[0:64, :], in_=pk[0:64, :], func=AF.Square), reads=[pk], writes=[sB])
        yield
        pn = nxt_bank()
        P.op("pe", lambda e: e.matmul(pn[:], lhsT=ones[:], rhs=sA[:], start=True, stop=False), reads=[ones, sA], writes=[pn])
        P.op("pe", lambda e: e.matmul(pn[:], lhsT=ones[:], rhs=sB[:], start=False, stop=True), reads=[ones, sB], writes=[pn])
        yield
        P.op("act", lambda e: e.activation(out=tq[64:65, :], in_=pn[64:65, :], func=AF.Ln), reads=[pn], writes=[tq])
        P.op("act", lambda e: e.activation(out=tq[64:65, :], in_=tq[64:65, :], func=AF.Exp, scale=0.5), reads=[tq], writes=[tq])
        yield
        P.op("dve", lambda e: e.tensor_scalar(out=b[64:65, :], in0=tq[64:65, :], scalar1=mx[64:65, 4 + hn % 2:5 + hn % 2], scalar2=None, op0=ALU.mult),
             reads=[tq, mx], writes=[b])

    if stop <= 1:
        return
    def run(gen):
        for _ in gen:
            pass

    def chain(*gens):
        for g_ in gens:
            yield from g_

    load_w(0)
    for kt in range(nqt):
        run(kv_tile(0, kt))
    run(kv_finish(0))
    if stop <= 2:
        return
    run(q_tile(0, 0))
    load_cq(0, 1)
    state["wide"] = False
    if stop <= 3 or stop >= 30:
        return

    side = []
    qgen = {}
    for h in range(nheads):
        if h + 1 < nheads:
            load_w(h + 1)
        blocks = [(qt, kb) for qt in range(nqt) for kb in range(4 * qt + 4)]

        def geom(qt, kb):
            c0 = 0 if kb < 4 * qt else (kb - 4 * qt) * 128
            return c0, NT - c0

        def emit_qk(i):
            qt, kb = blocks[i]
            c0, ncol = geom(qt, kb)
            g = h * nqt + qt
            a, b = aq[g % 3], bq[g % 3]
            A_ = Ak[h % 2]
            pss = C.ps[i % 3]
            P.op("pe", lambda e: e.matmul(pss[:, 0:ncol], lhsT=A_[:, kb * 128:(kb + 1) * 128], rhs=a[:, c0:NT], start=True, stop=False),
                 reads=[A_, a], writes=[pss])
            P.op("pe", lambda e: e.matmul(pss[:, 0:ncol], lhsT=Bk[:, kb * 128:(kb + 1) * 128], rhs=b[:, c0:NT], start=False, stop=True),
                 reads=[Bk, b], writes=[pss])

        def emit_rest(i):
            qt, kb = blocks[i]
            nkb = 4 * qt + 4
            c0, ncol = geom(qt, kb)
            pss = C.ps[i % 3]
            p = pt[i % NPT]
            po = C.ps[3 + qt % 2]
            ac = acc[qt % 2]
            V_ = Vh[h % 2]
            if kb == 0:
                P.op("pool", lambda e: e.memset(ac[0][:], 0.0), writes=[ac[0]])
                P.op("pool", lambda e: e.memset(ac[1][:], 0.0), writes=[ac[1]])
            P.op("act", lambda e: e.activation(out=p[:, 0:ncol], in_=pss[:, 0:ncol], func=AF.Exp), reads=[pss], writes=[p])
            if kb >= 4 * qt:
                P.op("pool", lambda e: e.tensor_tensor(out=p[:, 0:128], in0=p[:, 0:128], in1=tri[:], op=ALU.mult), reads=[p, tri], writes=[p])
            ai = i % 2
            P.op("dve" if ai == 0 else "pool", lambda e: e.tensor_tensor(out=ac[ai][:, c0:NT], in0=ac[ai][:, c0:NT], in1=p[:, 0:ncol], op=ALU.add),
                 reads=[p, ac[ai]], writes=[ac[ai]])
            P.op("pe", lambda e: e.matmul(po[:, c0:NT], lhsT=V_[:, kb, :], rhs=p[:, 0:ncol], start=(kb == 0), stop=(kb == nkb - 1)),
                 reads=[V_, p], writes=[po])
            if kb == nkb - 1:
                pending.append((i + 2, lambda qt=qt, po=po, ac=ac: finalize(qt, po, ac)))
            if kb == 0:
                if qt + 1 < nqt:
                    qgen[(h, qt + 1)] = q_tile(h, qt + 1)
                    side.append(qgen[(h, qt + 1)])
                    load_cq(h, qt + 2) if qt + 2 < nqt else load_cq(h + 1, 0)
                if h + 1 < nheads:
                    if qt == nqt - 1:
                        qgen[(h + 1, 0)] = chain(kv_tile(h + 1, qt), kv_finish(h + 1), q_tile(h + 1, 0))
                        side.append(qgen[(h + 1, 0)])
                        load_cq(h + 1, 1)
                    else:
                        side.append(kv_tile(h + 1, qt))

        def finalize(qt, po, ac):
            pl = C.ps[5]
            P.op("pe", lambda e: e.matmul(pl[:], lhsT=onesf[:], rhs=ac[0][:], start=True, stop=False), reads=[onesf, ac[0]], writes=[pl])
            P.op("pe", lambda e: e.matmul(pl[:], lhsT=onesf[:], rhs=ac[1][:], start=False, stop=True), reads=[onesf, ac[1]], writes=[pl])
            P.op("dve", lambda e: e.reciprocal(out=rl[:], in_=pl[:]), reads=[pl], writes=[rl])
            o = on[qt % 2]
            P.op("dve", lambda e: e.tensor_tensor(out=o[:], in0=po[:], in1=rl[:], op=ALU.mult), reads=[po, rl], writes=[o])
            oap, oT_ = o_dst(h, qt * NT, NT)
            P.dma("sp", oap, o[:], reads=[o], writes=[oT_])
            if qt % 8 == 7:
                head_done(h, qt // 8)

        LOOK = 2
        pending = []

        def qk(j):
            qt_, kb_ = blocks[j]
            if kb_ == 0 and (h, qt_) in qgen:
                gq = qgen.pop((h, qt_))
                while side:
                    g0 = side.pop(0)
                    run(g0)
                    if g0 is gq:
                        break
            emit_qk(j)

        for i in range(min(LOOK, len(blocks))):
            qk(i)
        for i in range(len(blocks)):
            if i + LOOK < len(blocks):
                qk(i + LOOK)
            emit_rest(i)
            while pending and (pending[0][0] <= i or i == len(blocks) - 1):
                pending.pop(0)[1]()
            if side:
                try:
                    next(side[0])
                except StopIteration:
                    side.pop(0)
        for g_ in side:
            run(g_)
        side.clear()


def phase4(P, C, t):
    x5i = t["x5i"]
    load_o = t["load_o"]
    pin = t["pin"]
    cons = t["cons"]
    wbo = t["wbo"]
    win = t["win"]
    wout = t["wout"]
    wpg = t["wpg"]
    wpi = t["wpi"]
    yo = t["yo"]
    ob = P.sbuf("ob", [128, NH_B, G], BF16)
    pbf = P.sbuf("pbf", [128, 2, G], BF16)
    yf = P.sbuf("yf", [128, KC, NT], F32)
    P.dma("sp", C.cons[:, 0:24], cons[:], reads=[cons], writes=[C.cons])
    P.op("dve", lambda e: e.tensor_scalar(out=C.gs[:, 0:24], in0=C.cons[:, 0:24], scalar1=float(np.sqrt(D)), scalar2=None, op0=ALU.mult),
         reads=[C.cons], writes=[C.gs])
    nt = G // NT
    for g in range(TOK // G):
        P.dma("sp", C.x[:], x5i[:, :, g * G:(g + 1) * G], reads=[x5i], writes=[C.x])
        load_o(g, ob)
        i = 0
        for m in range(KC):
            ws = None
            for t in range(nt):
                pb = C.ps[i % 2]
                i += 1
                ws = lin_fm(C, wbo, m, NH_B, t * NT, NT, pb, wslot=ws, src=ob)
                P.op("dve", lambda e, pb=pb, m=m, t=t: e.tensor_tensor(out=C.x[:, m, t * NT:(t + 1) * NT], in0=C.x[:, m, t * NT:(t + 1) * NT], in1=pb[:], op=ALU.add),
                     reads=[pb, C.x], writes=[C.x])
        ffn_group(C, win, wout, 0)
        ple_group(C, wpg, wpi, pin, g, 8, pbf)
        for t in range(nt):
            rmsnorm_fm(C, lambda kc, t=t: C.x[:, kc, t * NT:(t + 1) * NT], C.x, KC, t * NT, NT,
                       lambda kc: C.gs[:, 16 + kc:17 + kc], lambda kc: yf[:, kc, :], yf, C.ps[7], D, src_all=C.x[:, :, t * NT:(t + 1) * NT])
            P.dma("sp", yo[:, :, g * G + t * NT:g * G + (t + 1) * NT], yf[:], reads=[yf], writes=[yo])


def xfm(x):
    return np.ascontiguousarray(np.ascontiguousarray(x.T).reshape(-1, 128, x.shape[0]).transpose(1, 0, 2))


def unfm(a):
    a = np.asarray(a)
    return np.ascontiguousarray(a.transpose(1, 0, 2).reshape(-1, a.shape[2]).T)


def _rope_consts():
    invf = (1.0 / (np.float32(10000.0) ** (np.arange(0, ROPE, 2, dtype=np.float32) / np.float32(ROPE)))).astype(np.float32)
    invf4 = np.tile(invf, 4)
    ph = np.concatenate([np.full(64, np.pi / 2), np.full(32, np.pi), np.zeros(32)]).astype(np.float32)
    return invf4, ph


def _hmask():
    s = np.arange(128)[:, None]
    t = np.arange(128)[None, :]
    m = ((s // 64 == t // 64) & (s <= t)).astype(np.float32)
    return np.ascontiguousarray(np.tile(m, (1, 4)))


_NC_CACHE = {}
_PREP_ONLY = False

W_SHAPES = {
    "ffn00_in": [NJ, 128, KC * 256], "ffn00_out": [KC, 128, NJ * 128], "ffn01_in": [NJ, 128, KC * 256], "ffn01_out": [KC, 128, NJ * 128],
    "ffn10_in": [NJ, 128, KC * 256], "ffn10_out": [KC, 128, NJ * 128], "ffn11_in": [NJ, 128, KC * 256], "ffn11_out": [KC, 128, NJ * 128],
    "wq": [8, 128, D], "wf": [8, 128, D], "wg": [8, 128, D], "wv": [128, KC * D], "wao": [8, 128, D],
    "wpg0": [8, 128, D], "wpi0": [8, 128, 256], "wkd": [3, 128, D], "wdq": [4, 128, D],
    "wuq": [HPC * 2, 128, QL], "wuk": [HPC, 128, KVL], "wuv": [HPC, 128, KVL],
    "wbo": [8, 128, NH_B * 128], "wpg1": [8, 128, D], "wpi1": [8, 128, 256],
    "cons1": [128, 32], "cons2": [128, NC2], "cons3": [128, 8], "cons4": [128, 24],
    "ident": [128, 128], "hmask": [128, 512], "stk": [128, 64], "tri": [128, 128],
    "xin": [128, KC, TOK], "p0": [128, 2, TOK], "p1": [128, 2, TOK],
}


def build_fused(upto=4):
    P = Prog()
    nc = P.nc
    E = {k: P.dram(k, v, F32, "ExternalInput") for k, v in W_SHAPES.items()}
    E["posl"] = P.dram("posl", [1, TOK], I32, "ExternalInput")
    E["posb"] = P.dram("posb", [1, SEQ], I32, "ExternalInput")
    E["idxo"] = P.dram("idxo", [128, 8], I32, "ExternalInput")
    yo = P.dram("yo", [128, KC, TOK], F32, "ExternalOutput")

    def internal(name, shape, dtype):
        return T(nc.dram_tensor(name, list(shape), dtype), name)

    x1 = internal("x1", [128, KC, TOK], F32)
    oT = internal("oT", [NH_A, 128, TOK], F32)
    qc = internal("qc", [NH_A, 128, TOK], BF16)
    sg = internal("sg", [128, KC, TOK], F32)
    SDin = internal("SDin", [128, 1032], F32)
    SDall = internal("SDall", [512, 1032], F32)
    x5 = internal("x5", [128, KC, TOK], F32)
    NTL = TOK // NT
    latin = [internal("latin%d" % i, [128, 7 * NT], BF16) for i in range(NTL)]
    latall = [internal("latall%d" % i, [512, 7 * NT], BF16) for i in range(NTL)]
    latdep = Buf("latall")
    for q in latall:
        q.b = latdep
    ohin = [[internal("ohin%d_%d" % (h, hf), [128, SEQ // 2], BF16) for hf in range(2)] for h in range(HPC)]
    ohall = [internal("ohall%d" % h, [1024, SEQ // 2], BF16) for h in range(HPC)]

    m0 = P.mark()
    C = Ctx(P)
    mC = P.mark()
    STo = T(SDin.t[:, 0:1024].rearrange("p (h d) -> p h d", h=NH_A), "STo")
    DTo = T(SDin.t[:, 1024:1032], "DTo")
    phase1(P, C, {"xin": E["xin"], "win": E["ffn00_in"], "wout": E["ffn00_out"], "wq": E["wq"], "wf": E["wf"], "wg": E["wg"], "wv": E["wv"],
                  "cons": E["cons1"], "ident": E["ident"], "hmask": E["hmask"], "x1o": x1, "oTo": oT, "qco": qc, "sgo": sg, "STo": STo, "DTo": DTo})
    P.reset(mC)
    P.coll("AllGather", SDin.t.ap().opt(), SDall.t.ap().opt(), reads=[STo, DTo], writes=[SDall])
    if upto == 1:
        P.barrier()
        for g in range(TOK // G):
            tmpx = P.sbuf("tmpx1", [128, KC, G], F32)
            P.dma("sp", tmpx[:], x1[:, :, g * G:(g + 1) * G], reads=[x1], writes=[tmpx])
            P.dma("sp", yo[:, :, g * G:(g + 1) * G], tmpx[:], reads=[tmpx], writes=[yo])
        return P.finish()
    STi = TV(lambda r: SDall.t[r * 128:(r + 1) * 128, 0:1024].rearrange("p (h d) -> p h d", h=NH_A), "STi")
    STi.b = SDall.b
    DTi = TV(lambda r: SDall.t[r * 128:(r + 1) * 128, 1024:1032], "DTi")
    DTi.b = SDall.b
    def lat_dst(i):
        v = latin[i].t[:, :].rearrange("p (c t) -> p c t", c=7)
        return v[:, 0:4, :], v[:, 4:6, :], v[0:64, 6, :], latin[i]

    def lat_done(i):
        P.coll("AllGather", latin[i].t.ap().opt(), latall[i].t.ap().opt(), reads=[latin[i]], writes=[latall[i]])

    phase2(P, C, {"x1i": x1, "oTi": oT, "qci": qc, "sgi": sg, "STi": STi, "DTi": DTi, "pin": E["p0"], "pos": E["posl"], "cons": E["cons2"],
                  "stk": E["stk"], "wao": E["wao"], "win0": E["ffn01_in"], "wout0": E["ffn01_out"], "wpg": E["wpg0"], "wpi": E["wpi0"],
                  "wkd": E["wkd"], "win1": E["ffn10_in"], "wout1": E["ffn10_out"], "wdq": E["wdq"],
                  "x5o": x5, "lat_dst": lat_dst, "lat_done": lat_done})
    P.reset(m0)
    if upto == 2:
        P.barrier()
        for g in range(TOK // G):
            tmpx = P.sbuf("tmpx2", [128, KC, G], F32)
            P.dma("sp", tmpx[:], x5[:, :, g * G:(g + 1) * G], reads=[x5], writes=[tmpx])
            P.dma("sp", yo[:, :, g * G:(g + 1) * G], tmpx[:], reads=[tmpx], writes=[yo])
        return P.finish()
    def latv(q):
        rr, i = q // NTL, q % NTL
        return latall[i].t[rr * 128:(rr + 1) * 128, :].rearrange("p (c t) -> p c t", c=7)

    def head_done(h, hf):
        P.coll("AllGather", ohin[h][hf].t.ap().opt(), ohall[h].t[hf * 512:(hf + 1) * 512, :], reads=[ohin[h][hf]], writes=[ohall[h]])

    def o_dst(h, t0, n):
        hf, tl = t0 // (SEQ // 2), t0 % (SEQ // 2)
        return ohin[h][hf].t[:, tl:tl + n], ohin[h][hf]

    phase3(P, None, {"lat": latall[0], "cq_src": lambda t0, n: latv(t0 // NT)[:, 0:4, :],
                     "ckv_src": lambda kc, q: latv(q)[:, 4 + kc, :], "kr_src": lambda q: latv(q)[0:64, 6, :],
                     "o_dst": o_dst, "head_done": head_done,
                     "pos": E["posb"], "cons": E["cons3"], "stk": E["stk"], "tri": E["tri"], "wuq": E["wuq"], "wuk": E["wuk"], "wuv": E["wuv"]})
    P.reset(m0)
    if upto == 3:
        P.barrier()
        for g in range(TOK // G):
            tmpx = P.sbuf("tmpx3", [128, KC, G], F32)
            P.dma("sp", tmpx[:], x5[:, :, g * G:(g + 1) * G], reads=[x5], writes=[tmpx])
            P.dma("sp", yo[:, :, g * G:(g + 1) * G], tmpx[:], reads=[tmpx], writes=[yo])
        return P.finish()
    C = Ctx(P)
    idxs = P.sbuf("idxs", [128, 8], I32)
    P.dma("sp", idxs[:], E["idxo"][:], reads=[E["idxo"]], writes=[idxs])

    def load_o(g, ob):
        for h in range(HPC):
            rows = ohall[h].t[:, :].rearrange("r (b t) -> (r b) t", t=G)
            for rr in range(4):
                P.gather(ob[:, rr * HPC + h, :], rows, idxs[:, rr * 2 + g:rr * 2 + g + 1], reads=[ohall[h], idxs], writes=[ob])

    phase4(P, C, {"x5i": x5, "load_o": load_o, "pin": E["p1"], "cons": E["cons4"], "wbo": E["wbo"], "win": E["ffn11_in"], "wout": E["ffn11_out"],
                  "wpg": E["wpg1"], "wpi": E["wpi1"], "yo": yo})
    print("fused instructions", P.nins)
    return P.finish()


def _get(name, fn):
    if name not in _NC_CACHE:
        _NC_CACHE[name] = fn()
    return _NC_CACHE[name]


def kernel(x, p, positions, norm_gains, ffn_w_in, ffn_w_out, ple_w_gate, ple_w_in,
           a_w_in, a_lb_logits, a_out_gain, a_w_out,
           kv_norm_in, kv_w_down, kv_latent_norm, kv_w_up,
           b_w_dq, b_q_norm, b_w_uq, b_w_out, final_norm):
    f32 = lambda a: np.ascontiguousarray(np.asarray(a, dtype=np.float32))
    x, p, norm_gains, ffn_w_in, ffn_w_out = f32(x), f32(p), f32(norm_gains), f32(ffn_w_in), f32(ffn_w_out)
    ple_w_gate, ple_w_in, a_w_in, a_lb_logits, a_out_gain, a_w_out = map(f32, (ple_w_gate, ple_w_in, a_w_in, a_lb_logits, a_out_gain, a_w_out))
    kv_norm_in, kv_w_down, kv_latent_norm, kv_w_up = map(f32, (kv_norm_in, kv_w_down, kv_latent_norm, kv_w_up))
    b_w_dq, b_q_norm, b_w_uq, b_w_out, final_norm = map(f32, (b_w_dq, b_q_norm, b_w_uq, b_w_out, final_norm))
    positions = np.ascontiguousarray(np.asarray(positions, dtype=np.int32))
    cores = list(range(NCORES))
    invf4, ph = _rope_consts()
    g = norm_gains
    aw = a_w_in[0]
    wd = kv_w_down
    wkd = np.concatenate([wd[:, 0:256], wd[:, 256:320], wd[:, 288:320], wd[:, 256:288]], axis=1)
    cons3 = np.zeros((128, 8), np.float32)
    cons3[:, 0] = invf4
    cons3[:, 1] = ph
    kk = np.arange(128)[:, None]
    qq = np.arange(128)[None, :]
    shared = {
        "ffn00_in": tile_ffn_in(ffn_w_in[0, 0]), "ffn00_out": tile_ffn_out(ffn_w_out[0, 0]),
        "ffn01_in": tile_ffn_in(ffn_w_in[0, 1]), "ffn01_out": tile_ffn_out(ffn_w_out[0, 1]),
        "ffn10_in": tile_ffn_in(ffn_w_in[1, 0]), "ffn10_out": tile_ffn_out(ffn_w_out[1, 0]),
        "ffn11_in": tile_ffn_in(ffn_w_in[1, 1]), "ffn11_out": tile_ffn_out(ffn_w_out[1, 1]),
        "wq": tile_lin(aw[:, 0:1024]), "wf": tile_lin(aw[:, 1024:2048]), "wg": tile_lin(aw[:, 3072:4096]),
        "wv": tile_rows(aw[:, 2048:3072]).reshape(128, -1), "wao": tile_lin(a_w_out[0]),
        "wpg0": tile_lin(ple_w_gate[0]), "wpi0": tile_lin(ple_w_in[0]), "wkd": tile_lin(wkd), "wdq": tile_lin(b_w_dq[0]),
        "wbo": tile_lin(b_w_out[0]), "wpg1": tile_lin(ple_w_gate[1]), "wpi1": tile_lin(ple_w_in[1]),
        "cons1": np.ascontiguousarray(np.concatenate([col(g[0, 0]), col(g[0, 1]), col(a_lb_logits[0]), col(a_lb_logits[1])], axis=1)),
        "cons3": cons3,
        "cons4": np.ascontiguousarray(np.concatenate([col(g[1, 2]), col(g[1, 3]), col(final_norm)], axis=1)),
        "ident": np.eye(128, dtype=np.float32), "hmask": _hmask(),
        "stk": np.concatenate([np.eye(64), np.eye(64)], axis=0).astype(np.float32), "tri": (kk <= qq).astype(np.float32),
    }
    uq = b_w_uq[0]
    per_r = []
    for r in range(4):
        wuq, wuk, wuv = [], [], []
        for h in range(4 * r, 4 * r + 4):
            q = uq[:, h * 192:(h + 1) * 192]
            wuq.append(tile_lin(q[:, 0:128])[0])
            wuq.append(tile_lin(np.concatenate([q[:, 128:192], q[:, 160:192], q[:, 128:160]], axis=1))[0])
            kv = kv_w_up[:, h * 256:(h + 1) * 256]
            wuk.append(tile_lin(kv[:, 0:128])[0])
            wuv.append(tile_rows(kv[:, 128:256]).reshape(128, 256))
        wsel = np.zeros((128, 4), np.float32)
        wsel[:, :r] = 1.0
        cons2 = np.concatenate([col(g[0, 2]), col(g[0, 3]), col(kv_norm_in), col(g[1, 0]), col(g[1, 1]), col(kv_latent_norm),
                                col(b_q_norm[0]), col(a_out_gain[0]), invf4[:, None], ph[:, None], wsel,
                                np.zeros((128, NC2 - 53), np.float32)], axis=1).astype(np.float32)
        idxo = np.zeros((128, 8), np.int32)
        pp = np.arange(128)
        for rr in range(4):
            for gg in range(2):
                tb = r * 2 + gg
                idxo[:, rr * 2 + gg] = ((tb // 4) * 512 + rr * 128 + pp) * 4 + tb % 4
        per_r.append({"wuq": np.stack(wuq), "wuk": np.stack(wuk), "wuv": np.stack(wuv), "cons2": np.ascontiguousarray(cons2), "idxo": idxo})
    in_maps = []
    for c in cores:
        b, r = c // 4, c % 4
        sl = slice(r * TOK, (r + 1) * TOK)
        in_maps.append({"xin": xfm(x[b, sl]), "p0": xfm(p[0, b, sl]), "p1": xfm(p[1, b, sl]),
                        "posl": np.ascontiguousarray(positions[b:b + 1, sl]), "posb": np.ascontiguousarray(positions[b:b + 1]),
                        **shared, **per_r[r]})
    if _PREP_ONLY:
        return in_maps
    res = run_bass_kernel_spmd(_get("fused", build_fused), in_maps, core_ids=cores).results
    out = np.empty((2, SEQ, D), np.float32)
    for c in cores:
        b, r = c // 4, c % 4
        out[b, r * TOK:(r + 1) * TOK] = unfm(res[c]["yo"])
    return out
```

```python
import contextlib
import numpy as np
import ml_dtypes
import concourse.bass as bass
import concourse.mybir as mybir
from concourse.bass_utils import run_bass_kernel_spmd

F32 = mybir.dt.float32
BF16 = mybir.dt.bfloat16
I32 = mybir.dt.int32
AF = mybir.ActivationFunctionType
ALU = mybir.AluOpType

NDSEM = 8
NCORES = 8
D = 1024
KC = 8
DFF = 2816
NJ = 22
SEQ = 8192
TOK = 2048
G = 1024
NT = 512
EPS = 1e-6
NH_A = 8
CH = 64
PLE = 256
NH_B = 16
QL = 512
KVL = 256
ROPE = 64


class Buf:
    __slots__ = ("w", "r", "name", "relaxed", "wl", "pr")

    def __init__(self, name=""):
        self.w = None
        self.r = []
        self.name = name
        self.relaxed = False
        self.wl = {}
        self.pr = []


class T:
    def __init__(self, t, name):
        self.t = t
        self.b = Buf(name)

    def __getitem__(self, idx):
        return self.t[idx]


class TV(T):
    def __init__(self, fn, name):
        self.fn = fn
        self.b = Buf(name)

    def __getitem__(self, idx):
        return self.fn(idx)


def _b(x):
    return x.b if isinstance(x, T) else x


class Prog:
    def __init__(self):
        self.nc = bass.Bass("TRN2", target_bir_lowering=False)
        self.es = contextlib.ExitStack()
        nc = self.nc
        self.eng = {"pe": nc.tensor, "act": nc.scalar, "dve": nc.vector, "pool": nc.gpsimd, "sp": nc.sync}
        self.q = {k: [] for k in self.eng}
        self.cnt = {k: 0 for k in self.eng}
        self.seen = {k: {} for k in self.eng}
        self.esem = {}
        for k in ("pe", "act", "dve", "pool"):
            self.esem[k] = self.es.enter_context(nc.semaphore("es_" + k))
        self.dsem = {}
        self.dcnt = {}
        self.dtok = {}
        for k in ("sp", "act", "pool"):
            self.dsem[k] = [self.es.enter_context(nc.semaphore("ds_%s%d" % (k, i))) for i in range(NDSEM)]
            self.dcnt[k] = 0
            self.dtok[k] = [None] * NDSEM
        self.same_engine_sync = True
        self.nins = 0
        self.defer = None
        self.ccsem = self.es.enter_context(nc.semaphore("ccsem"))
        self.ccn = 0
        self.ARENA = 103936
        self.arena = self.es.enter_context(nc.sbuf_tensor("arena", [128, self.ARENA], BF16))
        self.atop = 0
        self.ps = [self.psum("ps%d" % i, [128, NT], F32) for i in range(8)]

    def dram(self, name, shape, dtype, kind):
        return T(self.nc.dram_tensor(name, list(shape), dtype, kind=kind), name)

    def sbuf(self, name, shape, dtype):
        shape = list(shape)
        n = int(np.prod(shape[1:]))
        nb = n * (4 if dtype in (F32, I32) else 2)
        nb = (nb + 63) // 64 * 64
        assert self.atop + nb <= self.ARENA * 2, "arena overflow: %s needs %d, top %d" % (name, nb, self.atop)
        ap = self.arena[:, self.atop // 2:(self.atop + nb) // 2]
        self.atop += nb
        self.hiwater = max(getattr(self, "hiwater", 0), self.atop)
        if dtype != BF16:
            ap = ap.bitcast(dtype)
        ap = ap[:, 0:n]
        if len(shape) == 3:
            ap = ap.rearrange("p (a b) -> p a b", b=shape[2])
        return T(ap, name)

    def mark(self):
        return self.atop

    def reset(self, m):
        self.barrier()
        print("arena high-water %d of %d bytes" % (getattr(self, "hiwater", 0), self.ARENA * 2))
        self.hiwater = 0
        self.atop = m

    def coll(self, kind, src, dst, reads, writes):
        waits = self._filter("pool", self._deps(reads, writes))
        self.ccn += 1
        tok = (self.ccsem, self.ccn)

        def run(engh, waits=waits, src=src, dst=dst, kind=kind):
            for (s_, v) in waits:
                engh.wait_ge(s_, v)
            engh.collective_compute(kind, ALU.bypass, replica_groups=[[0, 1, 2, 3], [4, 5, 6, 7]], ins=[src], outs=[dst]).then_inc(self.ccsem, 1)

        self.q["pool"].append(run)
        self._commit(tok, reads, writes)
        self.nins += 1
        return tok

    def gather(self, out, in_, idx, reads=(), writes=()):
        qn = "pool"
        n = self.dcnt[qn]
        slot = n % NDSEM
        sem = self.dsem[qn][slot]
        val = 16 * (n // NDSEM + 1)
        deps = self._deps(reads, writes)
        if self.dtok[qn][slot] is not None:
            deps.append(self.dtok[qn][slot])
        waits = self._filter(qn, deps)
        tok = (sem, val)
        self.dtok[qn][slot] = tok
        self.dcnt[qn] += 1

        def run(engh, waits=waits, out=out, in_=in_, idx=idx, sem=sem):
            for (s_, v) in waits:
                engh.wait_ge(s_, v)
            engh.indirect_dma_start(out=out, out_offset=None, in_=in_, in_offset=bass.IndirectOffsetOnAxis(ap=idx, axis=0)).then_inc(sem, 16)

        self.q[qn].append(run)
        self._commit(tok, reads, writes)
        self.nins += 1
        return tok

    def psum(self, name, shape, dtype):
        return T(self.es.enter_context(self.nc.psum_tensor(name, list(shape), dtype)), name)

    def _deps(self, reads, writes, e=None):
        deps = []
        own = self.esem.get(e)

        def add(b, tok):
            if b.relaxed and own is not None and tok[0] is own:
                return
            deps.append(tok)

        for b in reads:
            b = _b(b)
            if b.relaxed:
                for tk in b.wl.values():
                    add(b, tk)
            elif b.w is not None:
                add(b, b.w)
        for b in writes:
            b = _b(b)
            if b.relaxed:
                for tk in (b.r if b.r else b.pr):
                    add(b, tk)
                continue
            if b.w is not None:
                add(b, b.w)
            for tk in b.r:
                add(b, tk)
        return deps

    def _filter(self, e, deps):
        seen = self.seen[e]
        best = {}
        for (s, v) in deps:
            if e == "pe" and s is self.esem["pe"]:
                continue
            if (not self.same_engine_sync) and e in self.esem and s is self.esem[e]:
                continue
            k = id(s)
            if seen.get(k, 0) >= v:
                continue
            if k not in best or best[k][1] < v:
                best[k] = (s, v)
        for k, (s, v) in best.items():
            seen[k] = v
        return list(best.values())

    def _commit(self, tok, reads, writes):
        for b in reads:
            _b(b).r.append(tok)
        for b in writes:
            b = _b(b)
            b.w = tok
            if b.relaxed:
                if b.r:
                    b.wl = {}
                    b.pr = b.r
                k = id(tok[0])
                if k not in b.wl or b.wl[k][1] < tok[1]:
                    b.wl[k] = tok
            b.r = []

    def op(self, e, fn, reads=(), writes=()):
        if self.defer is not None:
            self.defer.append(lambda: self._op(e, fn, reads, writes))
            return None
        return self._op(e, fn, reads, writes)

    def _op(self, e, fn, reads=(), writes=()):
        waits = self._filter(e, self._deps(reads, writes, e))
        sem = self.esem[e]
        self.cnt[e] += 1
        tok = (sem, self.cnt[e])

        def run(engh, waits=waits, fn=fn, sem=sem):
            for (s, v) in waits:
                engh.wait_ge(s, v)
            fn(engh).then_inc(sem, 1)

        self.q[e].append(run)
        self._commit(tok, reads, writes)
        self.nins += 1
        return tok

    def dma(self, qn, out, in_, reads=(), writes=(), **kw):
        if self.defer is not None:
            self.defer.append(lambda: self._dma(qn, out, in_, reads, writes, **kw))
            return None
        return self._dma(qn, out, in_, reads, writes, **kw)

    def run_streams(self, streams):
        idx = [0] * len(streams)
        live = True
        while live:
            live = False
            for i, st in enumerate(streams):
                if idx[i] < len(st):
                    st[idx[i]]()
                    idx[i] += 1
                    live = True

    def _dma(self, qn, out, in_, reads=(), writes=(), **kw):
        n = self.dcnt[qn]
        slot = n % NDSEM
        sem = self.dsem[qn][slot]
        val = 16 * (n // NDSEM + 1)
        deps = self._deps(reads, writes)
        if self.dtok[qn][slot] is not None:
            deps.append(self.dtok[qn][slot])
        waits = self._filter(qn, deps)
        tok = (sem, val)
        self.dtok[qn][slot] = tok
        self.dcnt[qn] += 1

        def run(engh, waits=waits, out=out, in_=in_, sem=sem, kw=kw):
            for (s, v) in waits:
                engh.wait_ge(s, v)
            engh.dma_start(out=out, in_=in_, **kw).then_inc(sem, 16)

        self.q[qn].append(run)
        self._commit(tok, reads, writes)
        self.nins += 1
        return tok

    def _alltoks(self):
        deps = []
        for qn in self.dtok:
            deps.extend([t for t in self.dtok[qn] if t is not None])
        for e in self.esem:
            if self.cnt[e]:
                deps.append((self.esem[e], self.cnt[e]))
        return deps

    def barrier(self):
        deps = self._alltoks()
        for e in ("pe", "act", "dve", "pool", "sp"):
            waits = self._filter(e, deps)
            if waits:
                def run(engh, waits=waits):
                    for (s, v) in waits:
                        engh.wait_ge(s, v)
                self.q[e].append(run)

    def finish(self):
        waits = self._filter("sp", self._alltoks())

        def run(engh, waits=waits):
            for (s, v) in waits:
                engh.wait_ge(s, v)

        self.q["sp"].append(run)
        nc = self.nc
        with nc.Block() as block:
            for k, name in (("sp", "sync"), ("pe", "tensor"), ("act", "scalar"), ("dve", "vector"), ("pool", "gpsimd")):
                if not self.q[k]:
                    continue

                def body(engh, fns=self.q[k]):
                    for f in fns:
                        f(engh)

                getattr(block, name)(body)
        self.es.close()
        return nc


class Ctx:
    def __init__(self, P, ncons=56):
        self.P = P
        self.ps = P.ps
        self.cons = P.sbuf("cons", [128, ncons], F32)
        self.gs = P.sbuf("gs", [128, ncons], F32)
        self.ones = P.sbuf("ones", [128, 128], BF16)
        self.x = P.sbuf("x", [128, KC, G], F32)
        self.h = P.sbuf("h", [128, KC, G], BF16)
        self.sq = P.sbuf("sq", [128, KC, NT], BF16)
        self.rstd = P.sbuf("rstd", [128, NT], F32)
        self.rstd2 = P.sbuf("rstd2", [128, NT], F32)
        self.su = [P.sbuf("su%d" % i, [128, NT], F32) for i in range(2)]
        self.act = P.sbuf("act", [128, NJ, G], BF16)
        self.wsl = [P.sbuf("wsl%d" % i, [128, 3072], BF16) for i in range(4)]
        self.wi = 0
        self.sg = [P.sbuf("sg%d" % i, [128, NT], F32) for i in range(2)]
        P.op("pool", lambda e: e.memset(self.ones[:], 1.0), writes=[self.ones])
        for tt in (self.x, self.h, self.act, self.sq):
            tt.b.relaxed = True

    def wslot(self):
        s = self.wsl[self.wi % 4]
        self.wi += 1
        return s


def rmsnorm_fm(C, src, srcT, kcn, n0, n, gcol, dst, dstT, pb, dn, src_all=None):
    P = C.P
    sq = C.sq
    if src_all is not None:
        P.op("act", lambda e: e.activation(out=sq[:, 0:kcn, 0:n], in_=src_all, func=AF.Square), reads=[srcT], writes=[sq])
    else:
        for kc in range(kcn):
            P.op("act", lambda e, kc=kc: e.activation(out=sq[:, kc, 0:n], in_=src(kc), func=AF.Square),
                 reads=[srcT], writes=[sq])
    for kc in range(kcn):
        P.op("pe", lambda e, kc=kc: e.matmul(pb[:, 0:n], lhsT=C.ones[:], rhs=sq[:, kc, 0:n],
                                             start=(kc == 0), stop=(kc == kcn - 1)),
             reads=[C.ones, sq], writes=[pb])
    P.op("act", lambda e: e.activation(out=C.rstd[:, 0:n], in_=pb[:, 0:n], func=AF.Ln, bias=float(dn * EPS)),
         reads=[pb], writes=[C.rstd])
    P.op("act", lambda e: e.activation(out=C.rstd[:, 0:n], in_=C.rstd[:, 0:n], func=AF.Exp, scale=-0.5),
         reads=[C.rstd], writes=[C.rstd])
    for kc in range(kcn):
        if False and kcn == KC and kc % 2 == 1:
            tmp = C.sg[(kc // 2) % 2]
            P.op("act", lambda e, kc=kc, tmp=tmp: e.activation(out=tmp[:, 0:n], in_=src(kc), func=AF.Identity, scale=gcol(kc)),
                 reads=[srcT, C.gs], writes=[tmp])
            P.op("pool", lambda e, kc=kc, tmp=tmp: e.tensor_tensor(out=dst(kc), in0=tmp[:, 0:n], in1=C.rstd[:, 0:n], op=ALU.mult),
                 reads=[tmp, C.rstd], writes=[dstT])
        else:
            P.op("dve", lambda e, kc=kc: e.scalar_tensor_tensor(out=dst(kc), in0=src(kc), scalar=gcol(kc),
                                                                in1=C.rstd[:, 0:n], op0=ALU.mult, op1=ALU.mult),
                 reads=[srcT, C.gs, C.rstd], writes=[dstT])


def ffn_group(C, win, wout, gcol0, nt=G // NT):
    P = C.P
    rstds = [C.rstd, C.rstd2]
    for t in range(nt):
        sl = slice(t * NT, (t + 1) * NT)
        for kc in range(KC):
            if kc % 2 == 0:
                P.op("dve", lambda e, kc=kc, sl=sl: e.tensor_scalar(out=C.h[:, kc, sl], in0=C.x[:, kc, sl], scalar1=C.gs[:, gcol0 + kc:gcol0 + kc + 1],
                                                                   scalar2=None, op0=ALU.mult), reads=[C.x, C.gs], writes=[C.h])
            else:
                P.op("act", lambda e, kc=kc, sl=sl: e.activation(out=C.h[:, kc, sl], in_=C.x[:, kc, sl], func=AF.Identity,
                                                                 scale=C.gs[:, gcol0 + kc:gcol0 + kc + 1]), reads=[C.x, C.gs], writes=[C.h])
    for t in range(nt):
        sl = slice(t * NT, (t + 1) * NT)
        rs = rstds[t % 2]
        pb = C.ps[6 + t % 2]
        P.op("act", lambda e, sl=sl: e.activation(out=C.sq[:, :, 0:NT], in_=C.x[:, :, sl], func=AF.Square), reads=[C.x], writes=[C.sq])
        for kc in range(KC):
            P.op("pe", lambda e, kc=kc, pb=pb: e.matmul(pb[:], lhsT=C.ones[:], rhs=C.sq[:, kc, 0:NT], start=(kc == 0), stop=(kc == KC - 1)),
                 reads=[C.ones, C.sq], writes=[pb])
        P.op("act", lambda e, pb=pb, rs=rs: e.activation(out=rs[:], in_=pb[:], func=AF.Ln, bias=float(D * EPS)), reads=[pb], writes=[rs])
        P.op("act", lambda e, rs=rs: e.activation(out=rs[:], in_=rs[:], func=AF.Exp, scale=-0.5), reads=[rs], writes=[rs])
    pi = 0
    for j in range(NJ):
        ws = C.wslot()
        P.dma("pool", ws[:, 0:2048], win[j], reads=[win], writes=[ws])
        for t in range(nt):
            pg = C.ps[(pi % 2) * 2]
            pu = C.ps[(pi % 2) * 2 + 1]
            sg = C.sg[pi % 2]
            su = C.su[pi % 2]
            rs = rstds[t % 2]
            pi += 1
            for half, pb in ((0, pg), (1, pu)):
                for kc in range(KC):
                    P.op("pe", lambda e, kc=kc, half=half, pb=pb, ws=ws, t=t: e.matmul(
                        pb[:], lhsT=ws[:, kc * 256 + half * 128:kc * 256 + half * 128 + 128],
                        rhs=C.h[:, kc, t * NT:(t + 1) * NT], start=(kc == 0), stop=(kc == KC - 1)),
                        reads=[ws, C.h], writes=[pb])
            P.op("dve", lambda e, pg=pg, sg=sg, rs=rs: e.tensor_tensor(out=sg[:], in0=pg[:], in1=rs[:], op=ALU.mult), reads=[pg, rs], writes=[sg])
            P.op("act", lambda e, sg=sg: e.activation(out=sg[:], in_=sg[:], func=AF.Silu), reads=[sg], writes=[sg])
            P.op("dve", lambda e, pu=pu, su=su, rs=rs: e.tensor_tensor(out=su[:], in0=pu[:], in1=rs[:], op=ALU.mult), reads=[pu, rs], writes=[su])
            P.op("dve", lambda e, su=su, sg=sg, j=j, t=t: e.tensor_tensor(
                out=C.act[:, j, t * NT:(t + 1) * NT], in0=sg[:], in1=su[:], op=ALU.mult),
                reads=[sg, su], writes=[C.act])
    for m in range(KC):
        ws = C.wslot()
        P.dma("pool", ws[:, 0:NJ * 128], wout[m], reads=[wout], writes=[ws])
        for t in range(nt):
            pb = C.ps[4 + (pi % 2)]
            pi += 1
            for j in range(NJ):
                P.op("pe", lambda e, j=j, pb=pb, ws=ws, t=t: e.matmul(
                    pb[:], lhsT=ws[:, j * 128:(j + 1) * 128], rhs=C.act[:, j, t * NT:(t + 1) * NT],
                    start=(j == 0), stop=(j == NJ - 1)), reads=[ws, C.act], writes=[pb])
            P.op("dve", lambda e, pb=pb, m=m, t=t: e.scalar_tensor_tensor(
                out=C.x[:, m, t * NT:(t + 1) * NT], in0=pb[:], scalar=0.5, in1=C.x[:, m, t * NT:(t + 1) * NT],
                op0=ALU.mult, op1=ALU.add), reads=[pb, C.x], writes=[C.x])


def tile_ffn_in(w):
    wg = w[:, :DFF].reshape(KC, 128, NJ, 128)
    wu = w[:, DFF:].reshape(KC, 128, NJ, 128)
    s = np.stack([wg, wu], axis=3)
    return np.ascontiguousarray(s.transpose(2, 1, 0, 3, 4).reshape(NJ, 128, KC * 256))


def tile_ffn_out(w):
    return np.ascontiguousarray(w.reshape(NJ, 128, KC, 128).transpose(2, 1, 0, 3).reshape(KC, 128, NJ * 128))


def tile_lin(w):
    k, n = w.shape
    return np.ascontiguousarray(w.reshape(k // 128, 128, n // 128, 128).transpose(2, 1, 0, 3).reshape(n // 128, 128, k))


def tile_rows(w):
    k, n = w.shape
    return np.ascontiguousarray(w.reshape(k // 128, 128, n).transpose(1, 0, 2))


def col(v):
    return np.ascontiguousarray(v.reshape(-1, 128).T)


def fm(xtok):
    return np.ascontiguousarray(xtok.T)


def lin_fm(C, wdram, chunk, kcn, t0, n, pb, wslot=None, src=None):
    P = C.P
    src = C.h if src is None else src
    if wslot is None:
        wslot = C.wslot()
        P.dma("pool", wslot[:, 0:kcn * 128], wdram[chunk], reads=[wdram], writes=[wslot])
    for kc in range(kcn):
        P.op("pe", lambda e, kc=kc: e.matmul(pb[:, 0:n], lhsT=wslot[:, kc * 128:(kc + 1) * 128],
                                             rhs=src[:, kc, t0:t0 + n], start=(kc == 0), stop=(kc == kcn - 1)),
             reads=[wslot, src], writes=[pb])
    return wslot


def phase1(P, C, t, stage=99):
    NC1 = 32
    xin = t["xin"]
    win = t["win"]
    wout = t["wout"]
    wq = t["wq"]
    wf = t["wf"]
    wg = t["wg"]
    wv = t["wv"]
    cons = t["cons"]
    identd = t["ident"]
    hmaskd = t["hmask"]
    x1o = t["x1o"]
    oTo = t["oTo"]
    qco = t["qco"]
    sgo = t["sgo"]
    STo = t["STo"]
    DTo = t["DTo"]
    ident = P.sbuf("identb", [128, 128], BF16)
    hmask = P.sbuf("hmaskb", [128, 512], F32)
    wvs = P.sbuf("wvs", [128, KC, D], BF16)
    vtok = P.sbuf("vtok", [128, G // 128, D], BF16)
    lb = P.sbuf("lb", [128, 8], F32)
    oml = P.sbuf("oml", [128, 8], F32)
    noml = P.sbuf("noml", [128, 8], F32)
    rmask = P.sbuf("rmask", [128, NT], F32)
    zer8 = P.sbuf("zer8", [128, 8], F32)
    actf = C.act[:].rearrange("p j t -> p (j t)").bitcast(F32)
    scr = [[T(actf[:, (s_ * 7 + i) * NT:(s_ * 7 + i + 1) * NT], "hs%d_%d" % (s_, i)) for i in range(7)] for s_ in range(2)]
    qt = [P.sbuf("qt%d" % i, [128, NT], BF16) for i in range(2)]
    kt = [P.sbuf("kt%d" % i, [128, NT], BF16) for i in range(2)]
    kh = [P.sbuf("kh%d" % i, [128, NT], BF16) for i in range(2)]
    khT = [[P.sbuf("khT%d_%d" % (i, par), [128, NT], BF16) for par in range(2)] for i in range(2)]
    scm = [P.sbuf("scm%d" % i, [128, NT], BF16) for i in range(2)]
    qc = [P.sbuf("qc%d" % i, [128, NT], BF16) for i in range(2)]
    oev = [P.sbuf("oev%d" % i, [128, NT], F32) for i in range(2)]
    sgs = C.sg
    dch = [P.sbuf("dch%d" % i, [128, 8], F32) for i in range(2)]
    S = [P.sbuf("S%d" % i, [128, 128], F32) for i in range(NH_A)]
    Sb = [[P.sbuf("Sb%d_%d" % (i, c), [128, 128], BF16) for c in range(9)] for i in range(2)]
    Dexc = [P.sbuf("Dexc%d" % i, [128, 9], F32) for i in range(NH_A)]
    Slast = [P.sbuf("Slast%d" % i, [128, 128], BF16) for i in range(NH_A)]

    P.dma("sp", C.cons[:, 0:NC1], cons[:], reads=[cons], writes=[C.cons])
    P.dma("pool", ident[:], identd[:], writes=[ident])
    P.dma("sp", hmask[:], hmaskd[:], writes=[hmask])
    P.op("dve", lambda e: e.tensor_scalar(out=C.gs[:, 0:NC1], in0=C.cons[:, 0:NC1], scalar1=float(np.sqrt(D)), scalar2=None, op0=ALU.mult),
         reads=[C.cons], writes=[C.gs])
    P.op("dve", lambda e: e.tensor_tensor(out=lb[:], in0=C.cons[:, 16:24], in1=C.cons[:, 24:32], op=ALU.subtract), reads=[C.cons], writes=[lb])
    P.op("act", lambda e: e.activation(out=oml[:], in_=lb[:], func=AF.Sigmoid, scale=-1.0), reads=[lb], writes=[oml])
    P.op("act", lambda e: e.activation(out=lb[:], in_=lb[:], func=AF.Sigmoid), reads=[lb], writes=[lb])
    P.op("dve", lambda e: e.tensor_scalar(out=noml[:], in0=oml[:], scalar1=-1.0, scalar2=None, op0=ALU.mult), reads=[oml], writes=[noml])
    P.op("pool", lambda e: e.memset(rmask[:], 1.0), writes=[rmask])
    P.op("pool", lambda e: e.memset(rmask[:].rearrange("p (c t) -> p c t", t=CH)[:, :, 0:1], 0.0), writes=[rmask])
    P.op("pool", lambda e: e.memset(zer8[:], 0.0), writes=[zer8])
    for i in range(2):
        for par in range(2):
            P.op("pool", lambda e, i=i, par=par: e.memset(khT[i][par][:], 0.0), writes=[khT[i][par]])
    for hd in range(NH_A):
        P.op("pool", lambda e, hd=hd: e.memset(S[hd][:], 0.0), writes=[S[hd]])
        P.op("pool", lambda e, hd=hd: e.memset(Slast[hd][:], 0.0), writes=[Slast[hd]])
        P.op("pool", lambda e, hd=hd: e.memset(Dexc[hd][:], 1.0), writes=[Dexc[hd]])

    it = 0
    for g in range(TOK // G):
        P.dma("sp", C.x[:], xin[:, :, g * G:(g + 1) * G], reads=[xin], writes=[C.x])
        ffn_group(C, win, wout, 0)
        P.dma("sp", x1o[:, :, g * G:(g + 1) * G], C.x[:], reads=[C.x], writes=[x1o])
        P.barrier()
        for t in range(G // NT):
            rmsnorm_fm(C, lambda kc, t=t: C.x[:, kc, t * NT:(t + 1) * NT], C.x, KC, t * NT, NT,
                       lambda kc: C.gs[:, 8 + kc:9 + kc],
                       lambda kc, t=t: C.h[:, kc, t * NT:(t + 1) * NT], C.h, C.ps[7], D, src_all=C.x[:, :, t * NT:(t + 1) * NT])
        if g == 0:
            for kc in range(KC):
                P.dma("pool", wvs[:, kc, :], wv[:, kc * D:(kc + 1) * D], reads=[wv], writes=[wvs])
        for tb in range(G // 128):
            for half in range(2):
                pb = C.ps[(tb * 2 + half) % 4]
                for kc in range(KC):
                    P.op("pe", lambda e, kc=kc, tb=tb, half=half, pb=pb: e.matmul(
                        pb[:], lhsT=C.h[:, kc, tb * 128:(tb + 1) * 128], rhs=wvs[:, kc, half * 512:(half + 1) * 512],
                        start=(kc == 0), stop=(kc == KC - 1)), reads=[C.h, wvs], writes=[pb])
                P.op("act", lambda e, tb=tb, half=half, pb=pb: e.copy(out=vtok[:, tb, half * 512:(half + 1) * 512], in_=pb[:]),
                     reads=[pb], writes=[vtok])
        if stage < 2:
            continue
        for c in range(8):
            ws = None
            for t in range(G // NT):
                pb = C.ps[(c * 2 + t) % 4]
                ws = lin_fm(C, wg, c, KC, t * NT, NT, pb, wslot=ws)
                sg = sgs[(c * 2 + t) % 2]
                P.op("act", lambda e, pb=pb, sg=sg: e.activation(out=sg[:], in_=pb[:], func=AF.Sigmoid), reads=[pb], writes=[sg])
                P.dma("sp", sgo[:, c, g * G + t * NT:g * G + (t + 1) * NT], sg[:], reads=[sg], writes=[sgo])
        if stage < 3:
            continue
        def front(hd, t, i2):
            sig, fv, kk, bb, eb, enb, ek = scr[i2]
            tok0 = g * G + t * NT
            pq = C.ps[4 * i2]
            pf = C.ps[4 * i2 + 1]
            wsq = C.wsl[2 * i2]
            wsf = C.wsl[2 * i2 + 1]
            if t == 0 and hd < 2:
                P.dma("pool", wsq[:, 0:KC * 128], wq[hd], reads=[wq], writes=[wsq])
                P.dma("pool", wsf[:, 0:KC * 128], wf[hd], reads=[wf], writes=[wsf])
            lin_fm(C, wq, hd, KC, t * NT, NT, pq, wslot=wsq)
            lin_fm(C, wf, hd, KC, t * NT, NT, pf, wslot=wsf)
            if t == G // NT - 1 and hd + 2 < NH_A:
                P.dma("pool", wsq[:, 0:KC * 128], wq[hd + 2], reads=[wq], writes=[wsq])
                P.dma("pool", wsf[:, 0:KC * 128], wf[hd + 2], reads=[wf], writes=[wsf])
            c3 = lambda a: a[:].rearrange("p (c t) -> p c t", t=CH)
            P.op("act", lambda e, pf=pf: e.activation(out=sig[:], in_=pf[:], func=AF.Sigmoid), reads=[pf], writes=[sig])
            P.op("dve", lambda e, hd=hd: e.tensor_scalar(out=fv[:], in0=sig[:], scalar1=oml[:, hd:hd + 1], scalar2=lb[:, hd:hd + 1],
                                                       op0=ALU.mult, op1=ALU.add), reads=[sig, oml, lb], writes=[fv])
            P.op("act", lambda e: e.activation(out=fv[:], in_=fv[:], func=AF.Ln), reads=[fv], writes=[fv])
            P.op("dve", lambda e, hd=hd: e.tensor_scalar(out=kk[:], in0=sig[:], scalar1=noml[:, hd:hd + 1], scalar2=oml[:, hd:hd + 1],
                                                       op0=ALU.mult, op1=ALU.add), reads=[sig, noml, oml], writes=[kk])
            P.op("dve", lambda e: e.tensor_tensor_scan(out=bb[:], data0=rmask[:], data1=fv[:], initial=0.0, op0=ALU.mult, op1=ALU.add),
                 reads=[rmask, fv], writes=[bb])
            P.op("act", lambda e: e.activation(out=eb[:], in_=bb[:], func=AF.Exp), reads=[bb], writes=[eb])
            P.op("act", lambda e: e.activation(out=enb[:], in_=bb[:], func=AF.Exp, scale=-1.0), reads=[bb], writes=[enb])
            P.op("dve", lambda e, pq=pq, i2=i2: e.scalar_tensor_tensor(out=qt[i2][:], in0=pq[:], scalar=float(128 ** -0.5), in1=eb[:],
                                                                      op0=ALU.mult, op1=ALU.mult), reads=[pq, eb], writes=[qt[i2]])
            P.op("pool", lambda e, i2=i2: e.tensor_tensor(out=kt[i2][:], in0=kk[:], in1=enb[:], op=ALU.mult), reads=[kk, enb], writes=[kt[i2]])
            P.op("dve", lambda e: e.tensor_tensor(out=c3(ek), in0=c3(bb)[:, :, CH - 1:CH].to_broadcast([128, 8, CH]), in1=c3(bb),
                                                  op=ALU.subtract), reads=[bb], writes=[ek])
            P.op("act", lambda e: e.activation(out=ek[:], in_=ek[:], func=AF.Exp), reads=[ek], writes=[ek])
            P.op("pool", lambda e, i2=i2: e.tensor_tensor(out=kh[i2][:], in0=kk[:], in1=ek[:], op=ALU.mult), reads=[kk, ek], writes=[kh[i2]])
            P.op("act", lambda e, i2=i2: e.activation(out=dch[i2][:], in_=c3(bb)[:, :, CH - 1], func=AF.Exp), reads=[bb], writes=[dch[i2]])
            P.op("dve", lambda e, hd=hd, i2=i2: e.tensor_tensor_scan(out=Dexc[hd][:, 1:9], data0=dch[i2][:], data1=zer8[:],
                                                                     initial=Dexc[hd][:, 0:1], op0=ALU.mult, op1=ALU.add),
                 reads=[dch[i2], zer8, Dexc[hd]], writes=[Dexc[hd]])
            P.op("dve", lambda e, hd=hd, i2=i2: e.tensor_tensor(out=c3(qc[i2]), in0=c3(qt[i2]),
                                                                in1=Dexc[hd][:, 0:8].unsqueeze(2).to_broadcast([128, 8, CH]), op=ALU.mult),
                 reads=[qt[i2], Dexc[hd]], writes=[qc[i2]])
            P.dma("sp", qco[hd, :, tok0:tok0 + NT], qc[i2][:], reads=[qc[i2]], writes=[qco])
            P.op("dve", lambda e, hd=hd: e.tensor_copy(out=Dexc[hd][:, 0:1], in_=Dexc[hd][:, 8:9]), reads=[Dexc[hd]], writes=[Dexc[hd]])


        def back(hd, t, i2):
            tok0 = g * G + t * NT
            ptr = C.ps[4 * i2 + 2]
            ptrb = ptr[:].bitcast(BF16)
            for pr in range(4):
                P.op("pe", lambda e, pr=pr, i2=i2: e.transpose(ptrb[:, pr * 128:(pr + 1) * 128], kh[i2][:, pr * 128:(pr + 1) * 128], ident[:]),
                     reads=[kh[i2], ident], writes=[ptr])
            for par in range(2):
                P.op("act", lambda e, i2=i2, par=par: e.copy(out=khT[i2][par][par * 64:(par + 1) * 64, :], in_=ptrb[par * 64:(par + 1) * 64, 0:NT]),
                     reads=[ptr], writes=[khT[i2][par]])
            psc = C.ps[4 * i2 + 2]
            for pr in range(4):
                P.op("pe", lambda e, pr=pr, i2=i2: e.matmul(psc[:, pr * 128:(pr + 1) * 128], lhsT=kt[i2][:, pr * 128:(pr + 1) * 128],
                                                            rhs=qt[i2][:, pr * 128:(pr + 1) * 128], start=True, stop=True),
                     reads=[kt[i2], qt[i2]], writes=[psc])
            P.op("dve", lambda e, i2=i2: e.tensor_tensor(out=scm[i2][:], in0=psc[:], in1=hmask[:], op=ALU.mult), reads=[psc, hmask], writes=[scm[i2]])
            pU = C.ps[4 * i2 + 3]
            P.op("pool", lambda e, i2=i2, hd=hd: e.tensor_copy(out=Sb[i2][0][:], in_=Slast[hd][:]), reads=[Slast[hd]], writes=[Sb[i2][0]])
            for half in range(2):
                for c4 in range(4):
                    c = half * 4 + c4
                    pr, par = c // 2, c % 2
                    tb = t * 4 + pr
                    P.op("pe", lambda e, c4=c4, pr=pr, par=par, tb=tb, i2=i2, hd=hd: e.matmul(
                        pU[:, c4 * 128:(c4 + 1) * 128], lhsT=khT[i2][par][:, pr * 128:(pr + 1) * 128],
                        rhs=vtok[:, tb, hd * 128:(hd + 1) * 128], start=True, stop=True),
                        reads=[khT[i2][par], vtok], writes=[pU])
                for c4 in range(4):
                    c = half * 4 + c4
                    P.op("dve", lambda e, c=c, c4=c4, i2=i2, hd=hd: e.scalar_tensor_tensor(
                        out=S[hd][:], in0=S[hd][:], scalar=dch[i2][:, c:c + 1], in1=pU[:, c4 * 128:(c4 + 1) * 128],
                        op0=ALU.mult, op1=ALU.add), reads=[S[hd], dch[i2], pU], writes=[S[hd]])
                    dst = Sb[i2][c + 1] if c < 7 else Slast[hd]
                    P.op("pool", lambda e, dst=dst, hd=hd: e.tensor_copy(out=dst[:], in_=S[hd][:]), reads=[S[hd]], writes=[dst])
            po = C.ps[4 * i2 + 2]
            for pr in range(4):
                tb = t * 4 + pr
                P.op("pe", lambda e, pr=pr, tb=tb, i2=i2, hd=hd: e.matmul(po[:, pr * 128:(pr + 1) * 128], lhsT=vtok[:, tb, hd * 128:(hd + 1) * 128],
                                                                          rhs=scm[i2][:, pr * 128:(pr + 1) * 128], start=True, stop=False),
                     reads=[vtok, scm[i2]], writes=[po])
                for par in range(2):
                    c = pr * 2 + par
                    P.op("pe", lambda e, c=c, i2=i2: e.matmul(po[:, c * 64:(c + 1) * 64], lhsT=Sb[i2][c][:], rhs=qt[i2][:, c * 64:(c + 1) * 64],
                                                              start=False, stop=(c % 2 == 1)), reads=[Sb[i2][c], qt[i2]], writes=[po])
            P.op("act", lambda e, i2=i2: e.copy(out=oev[i2][:], in_=po[:]), reads=[po], writes=[oev[i2]])
            P.dma("sp", oTo[hd, :, tok0:tok0 + NT], oev[i2][:], reads=[oev[i2]], writes=[oTo])


        streams = []
        for s_ in range(2):
            P.defer = []
            for hd in range(s_, NH_A, 2):
                for t in range(G // NT):
                    front(hd, t, s_)
                    back(hd, t, s_)
            streams.append(P.defer)
            P.defer = None
        P.run_streams(streams)
        P.barrier()
    for hd in range(NH_A):
        P.dma("sp", STo[:, hd, :], S[hd][:], reads=[S[hd]], writes=[STo])
        P.op("dve", lambda e, hd=hd: e.tensor_copy(out=zer8[:, hd:hd + 1], in_=Dexc[hd][:, 0:1]), reads=[Dexc[hd]], writes=[zer8])
    P.dma("sp", DTo[:], zer8[:], reads=[zer8], writes=[DTo])


NC2 = 56


def ple_group(C, wgate, wple, pin, g, gcol0, pbf):
    P = C.P
    nt = G // NT
    for t in range(nt):
        rmsnorm_fm(C, lambda kc, t=t: C.x[:, kc, t * NT:(t + 1) * NT], C.x, KC, t * NT, NT,
                   lambda kc: C.gs[:, gcol0 + kc:gcol0 + kc + 1],
                   lambda kc, t=t: C.h[:, kc, t * NT:(t + 1) * NT], C.h, C.ps[7], D, src_all=C.x[:, :, t * NT:(t + 1) * NT])
    for kc in range(2):
        P.dma("pool", pbf[:, kc, :], pin[:, kc, g * G:(g + 1) * G], reads=[pin], writes=[pbf])
    i = 0
    for m in range(KC):
        wsg = None
        wsp = None
        for t in range(nt):
            pg = C.ps[(i % 2) * 2]
            pp = C.ps[(i % 2) * 2 + 1]
            sg = C.sg[i % 2]
            i += 1
            wsg = lin_fm(C, wgate, m, KC, t * NT, NT, pg, wslot=wsg)
            wsp = lin_fm(C, wple, m, 2, t * NT, NT, pp, wslot=wsp, src=pbf)
            P.op("act", lambda e, pg=pg, sg=sg: e.activation(out=sg[:], in_=pg[:], func=AF.Sigmoid), reads=[pg], writes=[sg])
            P.op("dve", lambda e, pp=pp, sg=sg: e.tensor_tensor(out=sg[:], in0=sg[:], in1=pp[:], op=ALU.mult), reads=[sg, pp], writes=[sg])
            P.op("dve", lambda e, sg=sg, m=m, t=t: e.tensor_tensor(out=C.x[:, m, t * NT:(t + 1) * NT], in0=C.x[:, m, t * NT:(t + 1) * NT],
                                                                 in1=sg[:], op=ALU.add), reads=[sg, C.x], writes=[C.x])


def rope_table(C, posd, tok0, n, tab, tmpf, tmpi, invcol, phcol):
    P = C.P
    P.dma("sp", tmpi[:, 0:n], posd[0:1, tok0:tok0 + n].to_broadcast([128, n]), reads=[posd], writes=[tmpi])
    P.op("dve", lambda e: e.tensor_copy(out=tab[:, 0:n], in_=tmpi[:, 0:n]), reads=[tmpi], writes=[tab])
    P.op("dve", lambda e: e.tensor_scalar(out=tab[:, 0:n], in0=tab[:, 0:n], scalar1=C.cons[:, invcol:invcol + 1],
                                          scalar2=C.cons[:, phcol:phcol + 1], op0=ALU.mult, op1=ALU.add), reads=[tab, C.cons], writes=[tab])
    P.op("dve", lambda e: e.tensor_scalar(out=tmpf[:, 0:n], in0=tab[:, 0:n], scalar1=float(1 / (2 * np.pi)), scalar2=None, op0=ALU.mult),
         reads=[tab], writes=[tmpf])
    P.op("dve", lambda e: e.tensor_copy(out=tmpi[:, 0:n], in_=tmpf[:, 0:n]), reads=[tmpf], writes=[tmpi])
    P.op("dve", lambda e: e.tensor_copy(out=tmpf[:, 0:n], in_=tmpi[:, 0:n]), reads=[tmpi], writes=[tmpf])
    P.op("dve", lambda e: e.scalar_tensor_tensor(out=tab[:, 0:n], in0=tmpf[:, 0:n], scalar=float(-2 * np.pi), in1=tab[:, 0:n],
                                                 op0=ALU.mult, op1=ALU.add), reads=[tmpf, tab], writes=[tab])
    P.op("dve", lambda e: e.tensor_scalar(out=tmpf[:, 0:n], in0=tab[:, 0:n], scalar1=float(np.pi), scalar2=float(-2 * np.pi),
                                          op0=ALU.is_gt, op1=ALU.mult), reads=[tab], writes=[tmpf])
    P.op("dve", lambda e: e.tensor_tensor(out=tab[:, 0:n], in0=tab[:, 0:n], in1=tmpf[:, 0:n], op=ALU.add), reads=[tab, tmpf], writes=[tab])
    P.op("dve", lambda e: e.tensor_scalar(out=tmpf[:, 0:n], in0=tab[:, 0:n], scalar1=float(-np.pi), scalar2=float(2 * np.pi),
                                          op0=ALU.is_lt, op1=ALU.mult), reads=[tab], writes=[tmpf])
    P.op("dve", lambda e: e.tensor_tensor(out=tab[:, 0:n], in0=tab[:, 0:n], in1=tmpf[:, 0:n], op=ALU.add), reads=[tab, tmpf], writes=[tab])
    P.op("act", lambda e: e.activation(out=tab[:, 0:n], in_=tab[:, 0:n], func=AF.Sin), reads=[tab], writes=[tab])


def phase2(P, C, t, dbg=False):
    x1i = t["x1i"]
    oTi = t["oTi"]
    qci = t["qci"]
    sgi = t["sgi"]
    STi = t["STi"]
    DTi = t["DTi"]
    pin = t["pin"]
    posd = t["pos"]
    cons = t["cons"]
    stk = t["stk"]
    wao = t["wao"]
    win0 = t["win0"]
    wout0 = t["wout0"]
    wpg = t["wpg"]
    wpi = t["wpi"]
    wkd = t["wkd"]
    win1 = t["win1"]
    wout1 = t["wout1"]
    wdq = t["wdq"]
    x5o = t["x5o"]
    lat_dst = t["lat_dst"]
    lat_done = t["lat_done"]
    Sst = P.sbuf("Sst", [128, NH_A, 128], F32)
    Sstb = P.sbuf("Sstb", [128, NH_A, 128], BF16)
    Ul = P.sbuf("Ul", [128, NH_A, 128], F32)
    Dl = P.sbuf("Dl", [128, NH_A], F32)
    ot = [P.sbuf("ot%d" % i, [128, NT], F32) for i in range(2)]
    qcl = [P.sbuf("qcl%d" % i, [128, NT], BF16) for i in range(2)]
    sgl = [P.sbuf("sgl%d" % i, [128, NT], F32) for i in range(2)]
    pbf = P.sbuf("pbf", [128, 2, G], BF16)
    latf = P.sbuf("latf", [128, 4, NT], F32)
    latb = P.sbuf("latb", [128, 4, NT], BF16)
    tab = P.sbuf("tab", [128, G], F32)
    tmpf = P.sbuf("tmpf", [128, G], F32)
    tmpi = P.sbuf("tmpi", [128, G], I32)
    stkb = P.sbuf("stkb", [128, 64], BF16)
    rT = P.sbuf("rT", [128, NT], BF16)
    krb = P.sbuf("krb", [64, NT], BF16)

    P.dma("sp", C.cons[:, 0:NC2], cons[:], reads=[cons], writes=[C.cons])
    P.dma("pool", stkb[:], stk[:], writes=[stkb])
    for (a, b, sc) in ((0, 40, np.sqrt(D)), (40, 42, np.sqrt(KVL)), (42, 46, np.sqrt(QL)), (46, 47, np.sqrt(128.0))):
        P.op("dve", lambda e, a=a, b=b, sc=sc: e.tensor_scalar(out=C.gs[:, a:b], in0=C.cons[:, a:b], scalar1=float(sc), scalar2=None, op0=ALU.mult),
             reads=[C.cons], writes=[C.gs])
    P.op("pool", lambda e: e.memset(Sst[:], 0.0), writes=[Sst])
    for r in range(4):
        P.dma("sp", Ul[:], STi[r], reads=[STi], writes=[Ul])
        P.dma("sp", Dl[:], DTi[r], reads=[DTi], writes=[Dl])
        w = C.cons[:, 49 + r:50 + r]
        P.op("dve", lambda e, w=w: e.tensor_scalar(out=Dl[:], in0=Dl[:], scalar1=-1.0, scalar2=w, op0=ALU.add, op1=ALU.mult), reads=[Dl, C.cons], writes=[Dl])
        P.op("dve", lambda e: e.tensor_scalar(out=Dl[:], in0=Dl[:], scalar1=1.0, scalar2=None, op0=ALU.add), reads=[Dl], writes=[Dl])
        P.op("dve", lambda e, w=w: e.tensor_scalar(out=Ul[:], in0=Ul[:], scalar1=w, scalar2=None, op0=ALU.mult), reads=[Ul, C.cons], writes=[Ul])
        for hd in range(NH_A):
            P.op("dve", lambda e, hd=hd: e.scalar_tensor_tensor(out=Sst[:, hd, :], in0=Sst[:, hd, :], scalar=Dl[:, hd:hd + 1], in1=Ul[:, hd, :],
                                                                op0=ALU.mult, op1=ALU.add), reads=[Sst, Dl, Ul], writes=[Sst])
    P.op("act", lambda e: e.copy(out=Sstb[:], in_=Sst[:]), reads=[Sst], writes=[Sstb])

    nt = G // NT
    it = 0
    for g in range(TOK // G):
        P.dma("sp", C.x[:], x1i[:, :, g * G:(g + 1) * G], reads=[x1i], writes=[C.x])
        for hd in range(NH_A):
            for t in range(nt):
                i2 = it % 2
                it += 1
                tok0 = g * G + t * NT
                P.dma("sp", ot[i2][:], oTi[hd, :, tok0:tok0 + NT], reads=[oTi], writes=[ot[i2]])
                P.dma("sp", qcl[i2][:], qci[hd, :, tok0:tok0 + NT], reads=[qci], writes=[qcl[i2]])
                P.dma("sp", sgl[i2][:], sgi[:, hd, tok0:tok0 + NT], reads=[sgi], writes=[sgl[i2]])
                po = C.ps[i2]
                P.op("pe", lambda e, hd=hd, i2=i2, po=po: e.matmul(po[:], lhsT=Sstb[:, hd, :], rhs=qcl[i2][:], start=True, stop=True),
                     reads=[Sstb, qcl[i2]], writes=[po])
                P.op("dve", lambda e, i2=i2, po=po: e.tensor_tensor(out=ot[i2][:], in0=ot[i2][:], in1=po[:], op=ALU.add), reads=[ot[i2], po], writes=[ot[i2]])
                rmsnorm_fm(C, lambda kc, i2=i2: ot[i2][:], ot[i2], 1, 0, NT, lambda kc: C.gs[:, 46:47],
                           lambda kc: latf[:, 0, :], latf, C.ps[7], 128)
                P.op("dve", lambda e, hd=hd, t=t, i2=i2: e.tensor_tensor(out=C.h[:, hd, t * NT:(t + 1) * NT], in0=latf[:, 0, :], in1=sgl[i2][:], op=ALU.mult),
                     reads=[latf, sgl[i2]], writes=[C.h])
        i = 0
        for m in range(KC):
            ws = None
            for t in range(nt):
                pb = C.ps[2 + i % 2]
                i += 1
                ws = lin_fm(C, wao, m, KC, t * NT, NT, pb, wslot=ws)
                P.op("dve", lambda e, pb=pb, m=m, t=t: e.tensor_tensor(out=C.x[:, m, t * NT:(t + 1) * NT], in0=C.x[:, m, t * NT:(t + 1) * NT], in1=pb[:], op=ALU.add),
                     reads=[pb, C.x], writes=[C.x])
        ffn_group(C, win0, wout0, 0)
        ple_group(C, wpg, wpi, pin, g, 8, pbf)
        rope_table(C, posd, g * G, G, tab, tmpf, tmpi, 47, 48)
        for t in range(nt):
            rmsnorm_fm(C, lambda kc, t=t: C.x[:, kc, t * NT:(t + 1) * NT], C.x, KC, t * NT, NT,
                       lambda kc: C.gs[:, 16 + kc:17 + kc],
                       lambda kc, t=t: C.h[:, kc, t * NT:(t + 1) * NT], C.h, C.ps[7], D, src_all=C.x[:, :, t * NT:(t + 1) * NT])
        wsl3 = [None, None, None]
        for t in range(nt):
            tok0 = g * G + t * NT
            for c in range(3):
                pb = C.ps[c]
                wsl3[c] = lin_fm(C, wkd, c, KC, t * NT, NT, pb, wslot=wsl3[c])
                if c < 2:
                    P.op("act", lambda e, c=c, pb=pb: e.copy(out=latf[:, c, :], in_=pb[:]), reads=[pb], writes=[latf])
                else:
                    P.op("dve", lambda e, pb=pb, t=t: e.tensor_tensor(out=rT[:], in0=pb[:], in1=tab[:, t * NT:(t + 1) * NT], op=ALU.mult),
                         reads=[pb, tab], writes=[rT])
            pk = C.ps[3]
            P.op("pe", lambda e, pk=pk: e.matmul(pk[0:64, :], lhsT=stkb[:], rhs=rT[:], start=True, stop=True), reads=[stkb, rT], writes=[pk])
            P.op("act", lambda e, pk=pk: e.copy(out=krb[0:64, :], in_=pk[0:64, :]), reads=[pk], writes=[krb])
            cq_ap, ckv_ap, kr_ap, latT = lat_dst(tok0 // NT)
            P.dma("sp", kr_ap, krb[0:64, :], reads=[krb], writes=[latT])
            rmsnorm_fm(C, lambda kc: latf[:, kc, :], latf, 2, 0, NT, lambda kc: C.gs[:, 40 + kc:41 + kc],
                       lambda kc: latb[:, kc, :], latb, C.ps[7], KVL)
            P.dma("sp", ckv_ap, latb[:, 0:2, :], reads=[latb], writes=[latT])
        ffn_group(C, win1, wout1, 24)
        P.dma("sp", x5o[:, :, g * G:(g + 1) * G], C.x[:], reads=[C.x], writes=[x5o])
        for t in range(nt):
            rmsnorm_fm(C, lambda kc, t=t: C.x[:, kc, t * NT:(t + 1) * NT], C.x, KC, t * NT, NT,
                       lambda kc: C.gs[:, 32 + kc:33 + kc],
                       lambda kc, t=t: C.h[:, kc, t * NT:(t + 1) * NT], C.h, C.ps[7], D, src_all=C.x[:, :, t * NT:(t + 1) * NT])
        wsl4 = [None] * 4
        for t in range(nt):
            tok0 = g * G + t * NT
            for c in range(4):
                pb = C.ps[c]
                wsl4[c] = lin_fm(C, wdq, c, KC, t * NT, NT, pb, wslot=wsl4[c])
                P.op("act", lambda e, c=c, pb=pb: e.copy(out=latf[:, c, :], in_=pb[:]), reads=[pb], writes=[latf])
            rmsnorm_fm(C, lambda kc: latf[:, kc, :], latf, 4, 0, NT, lambda kc: C.gs[:, 42 + kc:43 + kc],
                       lambda kc: latb[:, kc, :], latb, C.ps[7], QL)
            cq_ap, ckv_ap, kr_ap, latT = lat_dst(tok0 // NT)
            P.dma("sp", cq_ap, latb[:], reads=[latb], writes=[latT])
            lat_done(tok0 // NT)


HPC = 4
NQT = SEQ // NT
SCALE = float((128 + ROPE) ** -0.5)


def phase3(P, C0, t, nqt=NQT, nheads=HPC, stop=9):
    lat = t["lat"]
    cq_src = t["cq_src"]
    ckv_src = t["ckv_src"]
    kr_src = t["kr_src"]
    o_dst = t["o_dst"]
    head_done = t["head_done"]
    posd = t["pos"]
    cons = t["cons"]
    stk = t["stk"]
    trid = t["tri"]
    wuq = t["wuq"]
    wuk = t["wuk"]
    wuv = t["wuv"]

    class CC:
        pass
    C = CC()
    C.P = P
    C.ps = P.ps
    C.cons = P.sbuf("cons", [128, 8], F32)
    ones = P.sbuf("ones", [128, 128], BF16)
    onesf = P.sbuf("onesf", [128, 128], F32)
    ckv = P.sbuf("ckv", [128, 2, SEQ], BF16)
    Bk = P.sbuf("Bk", [128, SEQ], BF16)
    Ak = [P.sbuf("Ak%d" % i, [128, SEQ], BF16) for i in range(2)]
    Vh = [P.sbuf("Vh%d" % i, [128, SEQ // 128, 128], BF16) for i in range(2)]
    tab = P.sbuf("tab", [128, SEQ], F32)
    stkb = P.sbuf("stkb", [128, 64], BF16)
    tri = P.sbuf("trib", [128, 128], F32)
    wq = [P.sbuf("wq%d" % i, [128, 2, QL], BF16) for i in range(2)]
    wk = [P.sbuf("wk%d" % i, [128, KVL], BF16) for i in range(2)]
    wv = [P.sbuf("wv%d" % i, [128, KVL], BF16) for i in range(2)]
    mx = P.sbuf("mx", [128, 8], F32)
    mtmp = P.sbuf("setup_tmp", [128, 4096], F32)
    m_alias = P.mark()

    P.dma("sp", C.cons[:], cons[:], reads=[cons], writes=[C.cons])
    P.dma("pool", stkb[:], stk[:], writes=[stkb])
    P.dma("sp", tri[:], trid[:], writes=[tri])
    P.op("pool", lambda e: e.memset(ones[:], 1.0), writes=[ones])
    P.op("pool", lambda e: e.memset(onesf[:], 1.0), writes=[onesf])
    P.op("pool", lambda e: e.memset(Bk[:], 0.0), writes=[Bk])
    P.op("pool", lambda e: e.memset(Bk[64:65, :], 1.0), writes=[Bk])
    P.op("pool", lambda e: e.memset(mx[:], 0.0), writes=[mx])
    for q in range(nqt):
        for kc in range(2):
            P.dma("sp", ckv[:, kc, q * NT:(q + 1) * NT], ckv_src(kc, q), reads=[lat], writes=[ckv])
        P.dma("sp", Bk[0:64, q * NT:(q + 1) * NT], kr_src(q), reads=[lat], writes=[Bk])
    tmpf = T(mtmp[:, 0:2048], "tmpf")
    tmpi = T(mtmp[:, 2048:4096].bitcast(I32), "tmpi")
    for q4 in range(4):
        tv = T(tab[:, q4 * 2048:(q4 + 1) * 2048], "tabv%d" % q4)
        rope_table(C, posd, q4 * 2048, 2048, tv, tmpf, tmpi, 0, 1)
    P.barrier()
    P.atop = m_alias - 4096 * 4
    aq = [P.sbuf("aq%d" % i, [128, NT], BF16) for i in range(3)]
    bq = [P.sbuf("bq%d" % i, [128, NT], BF16) for i in range(3)]
    cql = [P.sbuf("cql%d" % i, [128, 4, NT], BF16) for i in range(3)]
    sqA = [P.sbuf("sqA%d" % i, [128, NT], BF16) for i in range(2)]
    sqB = [P.sbuf("sqB%d" % i, [128, NT], BF16) for i in range(2)]
    rT = [P.sbuf("rT%d" % i, [128, NT], BF16) for i in range(2)]
    tq = P.sbuf("tq", [128, NT], F32)
    NPT = 6
    pt = [P.sbuf("pt%d" % i, [128, NT], BF16) for i in range(NPT)]
    acc = [[P.sbuf("acc%d_%d" % (j, i), [128, NT], F32) for i in range(2)] for j in range(2)]
    rl = P.sbuf("rl", [128, NT], F32)
    on = [P.sbuf("on%d" % i, [128, NT], BF16) for i in range(2)]
    for i in range(3):
        P.op("pool", lambda e, i=i: e.memset(bq[i][:], 0.0), writes=[bq[i]])
    for i in range(2):
        P.op("pool", lambda e, i=i: e.memset(sqB[i][:], 0.0), writes=[sqB[i]])

    state = {"bank": 0, "wide": True, "sq": 0}

    def nxt_bank():
        state["bank"] += 1
        return C.ps[state["bank"] % 8] if state["wide"] else C.ps[6 + state["bank"] % 2]

    def colmax(pb, col):
        P.op("dve", lambda e: e.tensor_reduce(out=mx[:, 3:4], in_=pb[:], axis=mybir.AxisListType.X, op=ALU.max), reads=[pb], writes=[mx])
        P.op("dve", lambda e: e.tensor_tensor(out=mx[:, col:col + 1], in0=mx[:, col:col + 1], in1=mx[:, 3:4], op=ALU.max), reads=[mx], writes=[mx])

    for kt in range(nqt):
        pb = nxt_bank()
        sB = sqB[kt % 2]
        P.op("act", lambda e, kt=kt, sB=sB: e.activation(out=sB[0:64, :], in_=Bk[0:64, kt * NT:(kt + 1) * NT], func=AF.Square), reads=[Bk], writes=[sB])
        P.op("pe", lambda e, sB=sB, pb=pb: e.matmul(pb[:], lhsT=ones[:], rhs=sB[:], start=True, stop=True), reads=[ones, sB], writes=[pb])
        colmax(pb, 0)

    def load_w(hn):
        P.dma("pool", wq[hn % 2][:, 0, :], wuq[2 * hn], reads=[wuq], writes=[wq[hn % 2]])
        P.dma("pool", wq[hn % 2][:, 1, :], wuq[2 * hn + 1], reads=[wuq], writes=[wq[hn % 2]])
        P.dma("pool", wk[hn % 2][:], wuk[hn], reads=[wuk], writes=[wk[hn % 2]])
        P.dma("pool", wv[hn % 2][:], wuv[hn], reads=[wuv], writes=[wv[hn % 2]])
        P.op("pool", lambda e: e.memset(mx[:, 1 + hn % 2:2 + hn % 2], 0.0), writes=[mx])

    def kv_tile(hn, kt):
        A, V, wkk, wvv = Ak[hn % 2], Vh[hn % 2], wk[hn % 2], wv[hn % 2]
        pb = nxt_bank()
        for kc in range(2):
            P.op("pe", lambda e, kc=kc: e.matmul(pb[:], lhsT=wkk[:, kc * 128:(kc + 1) * 128], rhs=ckv[:, kc, kt * NT:(kt + 1) * NT],
                                                 start=(kc == 0), stop=(kc == 1)), reads=[wkk, ckv], writes=[pb])
        yield
        P.op("act", lambda e: e.copy(out=A[:, kt * NT:(kt + 1) * NT], in_=pb[:]), reads=[pb], writes=[A])
        state["sq"] += 1
        sA = sqA[state["sq"] % 2]
        P.op("act", lambda e: e.activation(out=sA[:], in_=pb[:], func=AF.Square), reads=[pb], writes=[sA])
        yield
        pn = nxt_bank()
        P.op("pe", lambda e: e.matmul(pn[:], lhsT=ones[:], rhs=sA[:], start=True, stop=True), reads=[ones, sA], writes=[pn])
        yield
        colmax(pn, 1 + hn % 2)
        pv = nxt_bank()
        for k4 in range(4):
            kb = kt * 4 + k4
            for kc in range(2):
                P.op("pe", lambda e, kc=kc, kb=kb, k4=k4: e.matmul(pv[:, k4 * 128:(k4 + 1) * 128], lhsT=ckv[:, kc, kb * 128:(kb + 1) * 128],
                                                                   rhs=wvv[:, kc * 128:(kc + 1) * 128], start=(kc == 0), stop=(kc == 1)),
                     reads=[ckv, wvv], writes=[pv])
        yield
        P.op("dve", lambda e: e.tensor_copy(out=V[:, kt * 4:(kt + 1) * 4, :].rearrange("p a b -> p (a b)"), in_=pv[:]), reads=[pv], writes=[V])

    def kv_finish(hn):
        c = 4 + hn % 2
        P.op("dve", lambda e: e.tensor_tensor(out=mx[:, c:c + 1], in0=mx[:, 1 + hn % 2:2 + hn % 2], in1=mx[:, 0:1], op=ALU.add), reads=[mx], writes=[mx])
        yield
        P.op("act", lambda e: e.activation(out=mx[:, c:c + 1], in_=mx[:, c:c + 1], func=AF.Ln), reads=[mx], writes=[mx])
        P.op("act", lambda e: e.activation(out=mx[:, c:c + 1], in_=mx[:, c:c + 1], func=AF.Exp, scale=0.5), reads=[mx], writes=[mx])
        yield
        P.op("dve", lambda e: e.tensor_scalar(out=mx[:, c:c + 1], in0=mx[:, c:c + 1], scalar1=-1.008 * SCALE, scalar2=None, op0=ALU.mult), reads=[mx], writes=[mx])
        yield

    cq_loaded = set()

    def load_cq(hn, qt):
        g = hn * nqt + qt
        if g in cq_loaded or hn >= nheads:
            return
        cq_loaded.add(g)
        P.dma("sp", cql[g % 3][:], cq_src(qt * NT, NT), reads=[lat], writes=[cql[g % 3]])

    def q_tile(hn, qt):
        g = hn * nqt + qt
        a, b, cq, wqq = aq[g % 3], bq[g % 3], cql[g % 3], wq[hn % 2]
        if g not in cq_loaded:
            load_cq(hn, qt)
        pa = nxt_bank()
        pr = nxt_bank()
        for half, pb in ((0, pa), (1, pr)):
            for kc in range(4):
                P.op("pe", lambda e, kc=kc, half=half, pb=pb: e.matmul(pb[:], lhsT=wqq[:, half, kc * 128:(kc + 1) * 128], rhs=cq[:, kc, :],
                                                                       start=(kc == 0), stop=(kc == 3)), reads=[wqq, cq], writes=[pb])
        yield
        P.op("act", lambda e: e.activation(out=a[:], in_=pa[:], func=AF.Identity, scale=SCALE), reads=[pa], writes=[a])
        state["sq"] += 1
        sA, sB = sqA[state["sq"] % 2], sqB[state["sq"] % 2]
        P.op("act", lambda e: e.activation(out=sA[:], in_=pa[:], func=AF.Square), reads=[pa], writes=[sA])
        r = rT[g % 2]
        P.op("dve", lambda e: e.tensor_tensor(out=r[:], in0=pr[:], in1=tab[:, qt * NT:(qt + 1) * NT], op=ALU.mult), reads=[pr, tab], writes=[r])
        yield
        pk = nxt_bank()
        P.op("pe", lambda e: e.matmul(pk[0:64, :], lhsT=stkb[:], rhs=r[:], start=True, stop=True), reads=[stkb, r], writes=[pk])
        yield
        P.op("act", lambda e: e.activation(out=b[0:64, :], in_=pk[0:64, :], func=AF.Identity, scale=SCALE), reads=[pk], writes=[b])
        P.op("act", lambda e: e.activation(out=sB[0:64, :], in_=pk[0:64, :], func=AF.Square), reads=[pk], writes=[sB])
        yield
        pn = nxt_bank()
        P.op("pe", lambda e: e.matmul(pn[:], lhsT=ones[:], rhs=sA[:], start=True, stop=False), reads=[ones, sA], writes=[pn])
        P.op("pe", lambda e: e.matmul(pn[:], lhsT=ones[:], rhs=sB[:], start=False, stop=True), reads=[ones, sB], writes=[pn])
        yield
        P.op("act", lambda e: e.activation(out=tq[64:65, :], in_=pn[64:65, :], func=AF.Ln), reads=[pn], writes=[tq])
        P.op("act", lambda e: e.activation(out=tq[64:65, :], in_=tq[64:65, :], func=AF.Exp, scale=0.5), reads=[tq], writes=[tq])
        yield
        P.op("dve", lambda e: e.tensor_scalar(out=b[64:65, :], in0=tq[64:65, :], scalar1=mx[64:65, 4 + hn % 2:5 + hn % 2], scalar2=None, op0=ALU.mult),
             reads=[tq, mx], writes=[b])

    if stop <= 1:
        return
    def run(gen):
        for _ in gen:
            pass

    def chain(*gens):
        for g_ in gens:
            yield from g_

    load_w(0)
    for kt in range(nqt):
        run(kv_tile(0, kt))
    run(kv_finish(0))
    if stop <= 2:
        return
    run(q_tile(0, 0))
    load_cq(0, 1)
    state["wide"] = False
    if stop <= 3 or stop >= 30:
        return

    side = []
    qgen = {}
    for h in range(nheads):
        if h + 1 < nheads:
            load_w(h + 1)
        blocks = [(qt, kb) for qt in range(nqt) for kb in range(4 * qt + 4)]

        def geom(qt, kb):
            c0 = 0 if kb < 4 * qt else (kb - 4 * qt) * 128
            return c0, NT - c0

        def emit_qk(i):
            qt, kb = blocks[i]
            c0, ncol = geom(qt, kb)
            g = h * nqt + qt
            a, b = aq[g % 3], bq[g % 3]
            A_ = Ak[h % 2]
            pss = C.ps[i % 3]
            P.op("pe", lambda e: e.matmul(pss[:, 0:ncol], lhsT=A_[:, kb * 128:(kb + 1) * 128], rhs=a[:, c0:NT], start=True, stop=False),
                 reads=[A_, a], writes=[pss])
            P.op("pe", lambda e: e.matmul(pss[:, 0:ncol], lhsT=Bk[:, kb * 128:(kb + 1) * 128], rhs=b[:, c0:NT], start=False, stop=True),
                 reads=[Bk, b], writes=[pss])

        def emit_rest(i):
            qt, kb = blocks[i]
            nkb = 4 * qt + 4
            c0, ncol = geom(qt, kb)
            pss = C.ps[i % 3]
            p = pt[i % NPT]
            po = C.ps[3 + qt % 2]
            ac = acc[qt % 2]
            V_ = Vh[h % 2]
            if kb == 0:
                P.op("pool", lambda e: e.memset(ac[0][:], 0.0), writes=[ac[0]])
                P.op("pool", lambda e: e.memset(ac[1][:], 0.0), writes=[ac[1]])
            P.op("act", lambda e: e.activation(out=p[:, 0:ncol], in_=pss[:, 0:ncol], func=AF.Exp), reads=[pss], writes=[p])
            if kb >= 4 * qt:
                P.op("pool", lambda e: e.tensor_tensor(out=p[:, 0:128], in0=p[:, 0:128], in1=tri[:], op=ALU.mult), reads=[p, tri], writes=[p])
            ai = i % 2
            P.op("dve" if ai == 0 else "pool", lambda e: e.tensor_tensor(out=ac[ai][:, c0:NT], in0=ac[ai][:, c0:NT], in1=p[:, 0:ncol], op=ALU.add),
                 reads=[p, ac[ai]], writes=[ac[ai]])
            P.op("pe", lambda e: e.matmul(po[:, c0:NT], lhsT=V_[:, kb, :], rhs=p[:, 0:ncol], start=(kb == 0), stop=(kb == nkb - 1)),
                 reads=[V_, p], writes=[po])
            if kb == nkb - 1:
                pending.append((i + 2, lambda qt=qt, po=po, ac=ac: finalize(qt, po, ac)))
            if kb == 0:
                if qt + 1 < nqt:
                    qgen[(h, qt + 1)] = q_tile(h, qt + 1)
                    side.append(qgen[(h, qt + 1)])
                    load_cq(h, qt + 2) if qt + 2 < nqt else load_cq(h + 1, 0)
                if h + 1 < nheads:
                    if qt == nqt - 1:
                        qgen[(h + 1, 0)] = chain(kv_tile(h + 1, qt), kv_finish(h + 1), q_tile(h + 1, 0))
                        side.append(qgen[(h + 1, 0)])
                        load_cq(h + 1, 1)
                    else:
                        side.append(kv_tile(h + 1, qt))

        def finalize(qt, po, ac):
            pl = C.ps[5]
            P.op("pe", lambda e: e.matmul(pl[:], lhsT=onesf[:], rhs=ac[0][:], start=True, stop=False), reads=[onesf, ac[0]], writes=[pl])
            P.op("pe", lambda e: e.matmul(pl[:], lhsT=onesf[:], rhs=ac[1][:], start=False, stop=True), reads=[onesf, ac[1]], writes=[pl])
            P.op("dve", lambda e: e.reciprocal(out=rl[:], in_=pl[:]), reads=[pl], writes=[rl])
            o = on[qt % 2]
            P.op("dve", lambda e: e.tensor_tensor(out=o[:], in0=po[:], in1=rl[:], op=ALU.mult), reads=[po, rl], writes=[o])
            oap, oT_ = o_dst(h, qt * NT, NT)
            P.dma("sp", oap, o[:], reads=[o], writes=[oT_])
            if qt % 8 == 7:
                head_done(h, qt // 8)

        LOOK = 2
        pending = []

        def qk(j):
            qt_, kb_ = blocks[j]
            if kb_ == 0 and (h, qt_) in qgen:
                gq = qgen.pop((h, qt_))
                while side:
                    g0 = side.pop(0)
                    run(g0)
                    if g0 is gq:
                        break
            emit_qk(j)

        for i in range(min(LOOK, len(blocks))):
            qk(i)
        for i in range(len(blocks)):
            if i + LOOK < len(blocks):
                qk(i + LOOK)
            emit_rest(i)
            while pending and (pending[0][0] <= i or i == len(blocks) - 1):
                pending.pop(0)[1]()
            if side:
                try:
                    next(side[0])
                except StopIteration:
                    side.pop(0)
        for g_ in side:
            run(g_)
        side.clear()


def phase4(P, C, t):
    x5i = t["x5i"]
    load_o = t["load_o"]
    pin = t["pin"]
    cons = t["cons"]
    wbo = t["wbo"]
    win = t["win"]
    wout = t["wout"]
    wpg = t["wpg"]
    wpi = t["wpi"]
    yo = t["yo"]
    ob = P.sbuf("ob", [128, NH_B, G], BF16)
    pbf = P.sbuf("pbf", [128, 2, G], BF16)
    yf = P.sbuf("yf", [128, KC, NT], F32)
    P.dma("sp", C.cons[:, 0:24], cons[:], reads=[cons], writes=[C.cons])
    P.op("dve", lambda e: e.tensor_scalar(out=C.gs[:, 0:24], in0=C.cons[:, 0:24], scalar1=float(np.sqrt(D)), scalar2=None, op0=ALU.mult),
         reads=[C.cons], writes=[C.gs])
    nt = G // NT
    for g in range(TOK // G):
        P.dma("sp", C.x[:], x5i[:, :, g * G:(g + 1) * G], reads=[x5i], writes=[C.x])
        load_o(g, ob)
        i = 0
        for m in range(KC):
            ws = None
            for t in range(nt):
                pb = C.ps[i % 2]
                i += 1
                ws = lin_fm(C, wbo, m, NH_B, t * NT, NT, pb, wslot=ws, src=ob)
                P.op("dve", lambda e, pb=pb, m=m, t=t: e.tensor_tensor(out=C.x[:, m, t * NT:(t + 1) * NT], in0=C.x[:, m, t * NT:(t + 1) * NT], in1=pb[:], op=ALU.add),
                     reads=[pb, C.x], writes=[C.x])
        ffn_group(C, win, wout, 0)
        ple_group(C, wpg, wpi, pin, g, 8, pbf)
        for t in range(nt):
            rmsnorm_fm(C, lambda kc, t=t: C.x[:, kc, t * NT:(t + 1) * NT], C.x, KC, t * NT, NT,
                       lambda kc: C.gs[:, 16 + kc:17 + kc], lambda kc: yf[:, kc, :], yf, C.ps[7], D, src_all=C.x[:, :, t * NT:(t + 1) * NT])
            P.dma("sp", yo[:, :, g * G + t * NT:g * G + (t + 1) * NT], yf[:], reads=[yf], writes=[yo])


def xfm(x):
    return np.ascontiguousarray(np.ascontiguousarray(x.T).reshape(-1, 128, x.shape[0]).transpose(1, 0, 2))


def unfm(a):
    a = np.asarray(a)
    return np.ascontiguousarray(a.transpose(1, 0, 2).reshape(-1, a.shape[2]).T)


def _rope_consts():
    invf = (1.0 / (np.float32(10000.0) ** (np.arange(0, ROPE, 2, dtype=np.float32) / np.float32(ROPE)))).astype(np.float32)
    invf4 = np.tile(invf, 4)
    ph = np.concatenate([np.full(64, np.pi / 2), np.full(32, np.pi), np.zeros(32)]).astype(np.float32)
    return invf4, ph


def _hmask():
    s = np.arange(128)[:, None]
    t = np.arange(128)[None, :]
    m = ((s // 64 == t // 64) & (s <= t)).astype(np.float32)
    return np.ascontiguousarray(np.tile(m, (1, 4)))


_NC_CACHE = {}
_PREP_ONLY = False

W_SHAPES = {
    "ffn00_in": [NJ, 128, KC * 256], "ffn00_out": [KC, 128, NJ * 128], "ffn01_in": [NJ, 128, KC * 256], "ffn01_out": [KC, 128, NJ * 128],
    "ffn10_in": [NJ, 128, KC * 256], "ffn10_out": [KC, 128, NJ * 128], "ffn11_in": [NJ, 128, KC * 256], "ffn11_out": [KC, 128, NJ * 128],
    "wq": [8, 128, D], "wf": [8, 128, D], "wg": [8, 128, D], "wv": [128, KC * D], "wao": [8, 128, D],
    "wpg0": [8, 128, D], "wpi0": [8, 128, 256], "wkd": [3, 128, D], "wdq": [4, 128, D],
    "wuq": [HPC * 2, 128, QL], "wuk": [HPC, 128, KVL], "wuv": [HPC, 128, KVL],
    "wbo": [8, 128, NH_B * 128], "wpg1": [8, 128, D], "wpi1": [8, 128, 256],
    "cons1": [128, 32], "cons2": [128, NC2], "cons3": [128, 8], "cons4": [128, 24],
    "ident": [128, 128], "hmask": [128, 512], "stk": [128, 64], "tri": [128, 128],
    "xin": [128, KC, TOK], "p0": [128, 2, TOK], "p1": [128, 2, TOK],
}


def build_fused(upto=4):
    P = Prog()
    nc = P.nc
    E = {k: P.dram(k, v, F32, "ExternalInput") for k, v in W_SHAPES.items()}
    E["posl"] = P.dram("posl", [1, TOK], I32, "ExternalInput")
    E["posb"] = P.dram("posb", [1, SEQ], I32, "ExternalInput")
    E["idxo"] = P.dram("idxo", [128, 8], I32, "ExternalInput")
    yo = P.dram("yo", [128, KC, TOK], F32, "ExternalOutput")

    def internal(name, shape, dtype):
        return T(nc.dram_tensor(name, list(shape), dtype), name)

    x1 = internal("x1", [128, KC, TOK], F32)
    oT = internal("oT", [NH_A, 128, TOK], F32)
    qc = internal("qc", [NH_A, 128, TOK], BF16)
    sg = internal("sg", [128, KC, TOK], F32)
    SDin = internal("SDin", [128, 1032], F32)
    SDall = internal("SDall", [512, 1032], F32)
    x5 = internal("x5", [128, KC, TOK], F32)
    NTL = TOK // NT
    latin = [internal("latin%d" % i, [128, 7 * NT], BF16) for i in range(NTL)]
    latall = [internal("latall%d" % i, [512, 7 * NT], BF16) for i in range(NTL)]
    latdep = Buf("latall")
    for q in latall:
        q.b = latdep
    ohin = [[internal("ohin%d_%d" % (h, hf), [128, SEQ // 2], BF16) for hf in range(2)] for h in range(HPC)]
    ohall = [internal("ohall%d" % h, [1024, SEQ // 2], BF16) for h in range(HPC)]

    m0 = P.mark()
    C = Ctx(P)
    mC = P.mark()
    STo = T(SDin.t[:, 0:1024].rearrange("p (h d) -> p h d", h=NH_A), "STo")
    DTo = T(SDin.t[:, 1024:1032], "DTo")
    phase1(P, C, {"xin": E["xin"], "win": E["ffn00_in"], "wout": E["ffn00_out"], "wq": E["wq"], "wf": E["wf"], "wg": E["wg"], "wv": E["wv"],
                  "cons": E["cons1"], "ident": E["ident"], "hmask": E["hmask"], "x1o": x1, "oTo": oT, "qco": qc, "sgo": sg, "STo": STo, "DTo": DTo})
    P.reset(mC)
    P.coll("AllGather", SDin.t.ap().opt(), SDall.t.ap().opt(), reads=[STo, DTo], writes=[SDall])
    if upto == 1:
        P.barrier()
        for g in range(TOK // G):
            tmpx = P.sbuf("tmpx1", [128, KC, G], F32)
            P.dma("sp", tmpx[:], x1[:, :, g * G:(g + 1) * G], reads=[x1], writes=[tmpx])
            P.dma("sp", yo[:, :, g * G:(g + 1) * G], tmpx[:], reads=[tmpx], writes=[yo])
        return P.finish()
    STi = TV(lambda r: SDall.t[r * 128:(r + 1) * 128, 0:1024].rearrange("p (h d) -> p h d", h=NH_A), "STi")
    STi.b = SDall.b
    DTi = TV(lambda r: SDall.t[r * 128:(r + 1) * 128, 1024:1032], "DTi")
    DTi.b = SDall.b
    def lat_dst(i):
        v = latin[i].t[:, :].rearrange("p (c t) -> p c t", c=7)
        return v[:, 0:4, :], v[:, 4:6, :], v[0:64, 6, :], latin[i]

    def lat_done(i):
        P.coll("AllGather", latin[i].t.ap().opt(), latall[i].t.ap().opt(), reads=[latin[i]], writes=[latall[i]])

    phase2(P, C, {"x1i": x1, "oTi": oT, "qci": qc, "sgi": sg, "STi": STi, "DTi": DTi, "pin": E["p0"], "pos": E["posl"], "cons": E["cons2"],
                  "stk": E["stk"], "wao": E["wao"], "win0": E["ffn01_in"], "wout0": E["ffn01_out"], "wpg": E["wpg0"], "wpi": E["wpi0"],
                  "wkd": E["wkd"], "win1": E["ffn10_in"], "wout1": E["ffn10_out"], "wdq": E["wdq"],
                  "x5o": x5, "lat_dst": lat_dst, "lat_done": lat_done})
    P.reset(m0)
    if upto == 2:
        P.barrier()
        for g in range(TOK // G):
            tmpx = P.sbuf("tmpx2", [128, KC, G], F32)
            P.dma("sp", tmpx[:], x5[:, :, g * G:(g + 1) * G], reads=[x5], writes=[tmpx])
            P.dma("sp", yo[:, :, g * G:(g + 1) * G], tmpx[:], reads=[tmpx], writes=[yo])
        return P.finish()
    def latv(q):
        rr, i = q // NTL, q % NTL
        return latall[i].t[rr * 128:(rr + 1) * 128, :].rearrange("p (c t) -> p c t", c=7)

    def head_done(h, hf):
        P.coll("AllGather", ohin[h][hf].t.ap().opt(), ohall[h].t[hf * 512:(hf + 1) * 512, :], reads=[ohin[h][hf]], writes=[ohall[h]])

    def o_dst(h, t0, n):
        hf, tl = t0 // (SEQ // 2), t0 % (SEQ // 2)
        return ohin[h][hf].t[:, tl:tl + n], ohin[h][hf]

    phase3(P, None, {"lat": latall[0], "cq_src": lambda t0, n: latv(t0 // NT)[:, 0:4, :],
                     "ckv_src": lambda kc, q: latv(q)[:, 4 + kc, :], "kr_src": lambda q: latv(q)[0:64, 6, :],
                     "o_dst": o_dst, "head_done": head_done,
                     "pos": E["posb"], "cons": E["cons3"], "stk": E["stk"], "tri": E["tri"], "wuq": E["wuq"], "wuk": E["wuk"], "wuv": E["wuv"]})
    P.reset(m0)
    if upto == 3:
        P.barrier()
        for g in range(TOK // G):
            tmpx = P.sbuf("tmpx3", [128, KC, G], F32)
            P.dma("sp", tmpx[:], x5[:, :, g * G:(g + 1) * G], reads=[x5], writes=[tmpx])
            P.dma("sp", yo[:, :, g * G:(g + 1) * G], tmpx[:], reads=[tmpx], writes=[yo])
        return P.finish()
    C = Ctx(P)
    idxs = P.sbuf("idxs", [128, 8], I32)
    P.dma("sp", idxs[:], E["idxo"][:], reads=[E["idxo"]], writes=[idxs])

    def load_o(g, ob):
        for h in range(HPC):
            rows = ohall[h].t[:, :].rearrange("r (b t) -> (r b) t", t=G)
            for rr in range(4):
                P.gather(ob[:, rr * HPC + h, :], rows, idxs[:, rr * 2 + g:rr * 2 + g + 1], reads=[ohall[h], idxs], writes=[ob])

    phase4(P, C, {"x5i": x5, "load_o": load_o, "pin": E["p1"], "cons": E["cons4"], "wbo": E["wbo"], "win": E["ffn11_in"], "wout": E["ffn11_out"],
                  "wpg": E["wpg1"], "wpi": E["wpi1"], "yo": yo})
    print("fused instructions", P.nins)
    return P.finish()


def _get(name, fn):
    if name not in _NC_CACHE:
        _NC_CACHE[name] = fn()
    return _NC_CACHE[name]


def kernel(x, p, positions, norm_gains, ffn_w_in, ffn_w_out, ple_w_gate, ple_w_in,
           a_w_in, a_lb_logits, a_out_gain, a_w_out,
           kv_norm_in, kv_w_down, kv_latent_norm, kv_w_up,
           b_w_dq, b_q_norm, b_w_uq, b_w_out, final_norm):
    f32 = lambda a: np.ascontiguousarray(np.asarray(a, dtype=np.float32))
    x, p, norm_gains, ffn_w_in, ffn_w_out = f32(x), f32(p), f32(norm_gains), f32(ffn_w_in), f32(ffn_w_out)
    ple_w_gate, ple_w_in, a_w_in, a_lb_logits, a_out_gain, a_w_out = map(f32, (ple_w_gate, ple_w_in, a_w_in, a_lb_logits, a_out_gain, a_w_out))
    kv_norm_in, kv_w_down, kv_latent_norm, kv_w_up = map(f32, (kv_norm_in, kv_w_down, kv_latent_norm, kv_w_up))
    b_w_dq, b_q_norm, b_w_uq, b_w_out, final_norm = map(f32, (b_w_dq, b_q_norm, b_w_uq, b_w_out, final_norm))
    positions = np.ascontiguousarray(np.asarray(positions, dtype=np.int32))
    cores = list(range(NCORES))
    invf4, ph = _rope_consts()
    g = norm_gains
    aw = a_w_in[0]
    wd = kv_w_down
    wkd = np.concatenate([wd[:, 0:256], wd[:, 256:320], wd[:, 288:320], wd[:, 256:288]], axis=1)
    cons3 = np.zeros((128, 8), np.float32)
    cons3[:, 0] = invf4
    cons3[:, 1] = ph
    kk = np.arange(128)[:, None]
    qq = np.arange(128)[None, :]
    shared = {
        "ffn00_in": tile_ffn_in(ffn_w_in[0, 0]), "ffn00_out": tile_ffn_out(ffn_w_out[0, 0]),
        "ffn01_in": tile_ffn_in(ffn_w_in[0, 1]), "ffn01_out": tile_ffn_out(ffn_w_out[0, 1]),
        "ffn10_in": tile_ffn_in(ffn_w_in[1, 0]), "ffn10_out": tile_ffn_out(ffn_w_out[1, 0]),
        "ffn11_in": tile_ffn_in(ffn_w_in[1, 1]), "ffn11_out": tile_ffn_out(ffn_w_out[1, 1]),
        "wq": tile_lin(aw[:, 0:1024]), "wf": tile_lin(aw[:, 1024:2048]), "wg": tile_lin(aw[:, 3072:4096]),
        "wv": tile_rows(aw[:, 2048:3072]).reshape(128, -1), "wao": tile_lin(a_w_out[0]),
        "wpg0": tile_lin(ple_w_gate[0]), "wpi0": tile_lin(ple_w_in[0]), "wkd": tile_lin(wkd), "wdq": tile_lin(b_w_dq[0]),
        "wbo": tile_lin(b_w_out[0]), "wpg1": tile_lin(ple_w_gate[1]), "wpi1": tile_lin(ple_w_in[1]),
        "cons1": np.ascontiguousarray(np.concatenate([col(g[0, 0]), col(g[0, 1]), col(a_lb_logits[0]), col(a_lb_logits[1])], axis=1)),
        "cons3": cons3,
        "cons4": np.ascontiguousarray(np.concatenate([col(g[1, 2]), col(g[1, 3]), col(final_norm)], axis=1)),
        "ident": np.eye(128, dtype=np.float32), "hmask": _hmask(),
        "stk": np.concatenate([np.eye(64), np.eye(64)], axis=0).astype(np.float32), "tri": (kk <= qq).astype(np.float32),
    }
    uq = b_w_uq[0]
    per_r = []
    for r in range(4):
        wuq, wuk, wuv = [], [], []
        for h in range(4 * r, 4 * r + 4):
            q = uq[:, h * 192:(h + 1) * 192]
            wuq.append(tile_lin(q[:, 0:128])[0])
            wuq.append(tile_lin(np.concatenate([q[:, 128:192], q[:, 160:192], q[:, 128:160]], axis=1))[0])
            kv = kv_w_up[:, h * 256:(h + 1) * 256]
            wuk.append(tile_lin(kv[:, 0:128])[0])
            wuv.append(tile_rows(kv[:, 128:256]).reshape(128, 256))
        wsel = np.zeros((128, 4), np.float32)
        wsel[:, :r] = 1.0
        cons2 = np.concatenate([col(g[0, 2]), col(g[0, 3]), col(kv_norm_in), col(g[1, 0]), col(g[1, 1]), col(kv_latent_norm),
                                col(b_q_norm[0]), col(a_out_gain[0]), invf4[:, None], ph[:, None], wsel,
                                np.zeros((128, NC2 - 53), np.float32)], axis=1).astype(np.float32)
        idxo = np.zeros((128, 8), np.int32)
        pp = np.arange(128)
        for rr in range(4):
            for gg in range(2):
                tb = r * 2 + gg
                idxo[:, rr * 2 + gg] = ((tb // 4) * 512 + rr * 128 + pp) * 4 + tb % 4
        per_r.append({"wuq": np.stack(wuq), "wuk": np.stack(wuk), "wuv": np.stack(wuv), "cons2": np.ascontiguousarray(cons2), "idxo": idxo})
    in_maps = []
    for c in cores:
        b, r = c // 4, c % 4
        sl = slice(r * TOK, (r + 1) * TOK)
        in_maps.append({"xin": xfm(x[b, sl]), "p0": xfm(p[0, b, sl]), "p1": xfm(p[1, b, sl]),
                        "posl": np.ascontiguousarray(positions[b:b + 1, sl]), "posb": np.ascontiguousarray(positions[b:b + 1]),
                        **shared, **per_r[r]})
    if _PREP_ONLY:
        return in_maps
    res = run_bass_kernel_spmd(_get("fused", build_fused), in_maps, core_ids=cores).results
    out = np.empty((2, SEQ, D), np.float32)
    for c in cores:
        b, r = c // 4, c % 4
        out[b, r * TOK:(r + 1) * TOK] = unfm(res[c]["yo"])
    return out
```

```python
import contextlib
import numpy as np
import ml_dtypes
import concourse.bass as bass
import concourse.mybir as mybir
from concourse.bass_utils import run_bass_kernel_spmd

F32 = mybir.dt.float32
BF16 = mybir.dt.bfloat16
I32 = mybir.dt.int32
AF = mybir.ActivationFunctionType
ALU = mybir.AluOpType

NDSEM = 8
NCORES = 8
D = 1024
KC = 8
DFF = 2816
NJ = 22
SEQ = 8192
TOK = 2048
G = 1024
NT = 512
EPS = 1e-6
NH_A = 8
CH = 64
PLE = 256
NH_B = 16
QL = 512
KVL = 256
ROPE = 64


class Buf:
    __slots__ = ("w", "r", "name", "relaxed", "wl", "pr")

    def __init__(self, name=""):
        self.w = None
        self.r = []
        self.name = name
        self.relaxed = False
        self.wl = {}
        self.pr = []


class T:
    def __init__(self, t, name):
        self.t = t
        self.b = Buf(name)

    def __getitem__(self, idx):
        return self.t[idx]


class TV(T):
    def __init__(self, fn, name):
        self.fn = fn
        self.b = Buf(name)

    def __getitem__(self, idx):
        return self.fn(idx)


def _b(x):
    return x.b if isinstance(x, T) else x


class Prog:
    def __init__(self):
        self.nc = bass.Bass("TRN2", target_bir_lowering=False)
        self.es = contextlib.ExitStack()
        nc = self.nc
        self.eng = {"pe": nc.tensor, "act": nc.scalar, "dve": nc.vector, "pool": nc.gpsimd, "sp": nc.sync}
        self.q = {k: [] for k in self.eng}
        self.cnt = {k: 0 for k in self.eng}
        self.seen = {k: {} for k in self.eng}
        self.esem = {}
        for k in ("pe", "act", "dve", "pool"):
            self.esem[k] = self.es.enter_context(nc.semaphore("es_" + k))
        self.dsem = {}
        self.dcnt = {}
        self.dtok = {}
        for k in ("sp", "act", "pool"):
            self.dsem[k] = [self.es.enter_context(nc.semaphore("ds_%s%d" % (k, i))) for i in range(NDSEM)]
            self.dcnt[k] = 0
            self.dtok[k] = [None] * NDSEM
        self.same_engine_sync = True
        self.nins = 0
        self.defer = None
        self.ccsem = self.es.enter_context(nc.semaphore("ccsem"))
        self.ccn = 0
        self.ARENA = 103936
        self.arena = self.es.enter_context(nc.sbuf_tensor("arena", [128, self.ARENA], BF16))
        self.atop = 0
        self.ps = [self.psum("ps%d" % i, [128, NT], F32) for i in range(8)]

    def dram(self, name, shape, dtype, kind):
        return T(self.nc.dram_tensor(name, list(shape), dtype, kind=kind), name)

    def sbuf(self, name, shape, dtype):
        shape = list(shape)
        n = int(np.prod(shape[1:]))
        nb = n * (4 if dtype in (F32, I32) else 2)
        nb = (nb + 63) // 64 * 64
        assert self.atop + nb <= self.ARENA * 2, "arena overflow: %s needs %d, top %d" % (name, nb, self.atop)
        ap = self.arena[:, self.atop // 2:(self.atop + nb) // 2]
        self.atop += nb
        self.hiwater = max(getattr(self, "hiwater", 0), self.atop)
        if dtype != BF16:
            ap = ap.bitcast(dtype)
        ap = ap[:, 0:n]
        if len(shape) == 3:
            ap = ap.rearrange("p (a b) -> p a b", b=shape[2])
        return T(ap, name)

    def mark(self):
        return self.atop

    def reset(self, m):
        self.barrier()
        print("arena high-water %d of %d bytes" % (getattr(self, "hiwater", 0), self.ARENA * 2))
        self.hiwater = 0
        self.atop = m

    def coll(self, kind, src, dst, reads, writes):
        waits = self._filter("pool", self._deps(reads, writes))
        self.ccn += 1
        tok = (self.ccsem, self.ccn)

        def run(engh, waits=waits, src=src, dst=dst, kind=kind):
            for (s_, v) in waits:
                engh.wait_ge(s_, v)
            engh.collective_compute(kind, ALU.bypass, replica_groups=[[0, 1, 2, 3], [4, 5, 6, 7]], ins=[src], outs=[dst]).then_inc(self.ccsem, 1)

        self.q["pool"].append(run)
        self._commit(tok, reads, writes)
        self.nins += 1
        return tok

    def gather(self, out, in_, idx, reads=(), writes=()):
        qn = "pool"
        n = self.dcnt[qn]
        slot = n % NDSEM
        sem = self.dsem[qn][slot]
        val = 16 * (n // NDSEM + 1)
        deps = self._deps(reads, writes)
        if self.dtok[qn][slot] is not None:
            deps.append(self.dtok[qn][slot])
        waits = self._filter(qn, deps)
        tok = (sem, val)
        self.dtok[qn][slot] = tok
        self.dcnt[qn] += 1

        def run(engh, waits=waits, out=out, in_=in_, idx=idx, sem=sem):
            for (s_, v) in waits:
                engh.wait_ge(s_, v)
            engh.indirect_dma_start(out=out, out_offset=None, in_=in_, in_offset=bass.IndirectOffsetOnAxis(ap=idx, axis=0)).then_inc(sem, 16)

        self.q[qn].append(run)
        self._commit(tok, reads, writes)
        self.nins += 1
        return tok

    def psum(self, name, shape, dtype):
        return T(self.es.enter_context(self.nc.psum_tensor(name, list(shape), dtype)), name)

    def _deps(self, reads, writes, e=None):
        deps = []
        own = self.esem.get(e)

        def add(b, tok):
            if b.relaxed and own is not None and tok[0] is own:
                return
            deps.append(tok)

        for b in reads:
            b = _b(b)
            if b.relaxed:
                for tk in b.wl.values():
                    add(b, tk)
            elif b.w is not None:
                add(b, b.w)
        for b in writes:
            b = _b(b)
            if b.relaxed:
                for tk in (b.r if b.r else b.pr):
                    add(b, tk)
                continue
            if b.w is not None:
                add(b, b.w)
            for tk in b.r:
                add(b, tk)
        return deps

    def _filter(self, e, deps):
        seen = self.seen[e]
        best = {}
        for (s, v) in deps:
            if e == "pe" and s is self.esem["pe"]:
                continue
            if (not self.same_engine_sync) and e in self.esem and s is self.esem[e]:
                continue
            k = id(s)
            if seen.get(k, 0) >= v:
                continue
            if k not in best or best[k][1] < v:
                best[k] = (s, v)
        for k, (s, v) in best.items():
            seen[k] = v
        return list(best.values())

    def _commit(self, tok, reads, writes):
        for b in reads:
            _b(b).r.append(tok)
        for b in writes:
            b = _b(b)
            b.w = tok
            if b.relaxed:
                if b.r:
                    b.wl = {}
                    b.pr = b.r
                k = id(tok[0])
                if k not in b.wl or b.wl[k][1] < tok[1]:
                    b.wl[k] = tok
            b.r = []

    def op(self, e, fn, reads=(), writes=()):
        if self.defer is not None:
            self.defer.append(lambda: self._op(e, fn, reads, writes))
            return None
        return self._op(e, fn, reads, writes)

    def _op(self, e, fn, reads=(), writes=()):
        waits = self._filter(e, self._deps(reads, writes, e))
        sem = self.esem[e]
        self.cnt[e] += 1
        tok = (sem, self.cnt[e])

        def run(engh, waits=waits, fn=fn, sem=sem):
            for (s, v) in waits:
                engh.wait_ge(s, v)
            fn(engh).then_inc(sem, 1)

        self.q[e].append(run)
        self._commit(tok, reads, writes)
        self.nins += 1
        return tok

    def dma(self, qn, out, in_, reads=(), writes=(), **kw):
        if self.defer is not None:
            self.defer.append(lambda: self._dma(qn, out, in_, reads, writes, **kw))
            return None
        return self._dma(qn, out, in_, reads, writes, **kw)

    def run_streams(self, streams):
        idx = [0] * len(streams)
        live = True
        while live:
            live = False
            for i, st in enumerate(streams):
                if idx[i] < len(st):
                    st[idx[i]]()
                    idx[i] += 1
                    live = True

    def _dma(self, qn, out, in_, reads=(), writes=(), **kw):
        n = self.dcnt[qn]
        slot = n % NDSEM
        sem = self.dsem[qn][slot]
        val = 16 * (n // NDSEM + 1)
        deps = self._deps(reads, writes)
        if self.dtok[qn][slot] is not None:
            deps.append(self.dtok[qn][slot])
        waits = self._filter(qn, deps)
        tok = (sem, val)
        self.dtok[qn][slot] = tok
        self.dcnt[qn] += 1

        def run(engh, waits=waits, out=out, in_=in_, sem=sem, kw=kw):
            for (s, v) in waits:
                engh.wait_ge(s, v)
            engh.dma_start(out=out, in_=in_, **kw).then_inc(sem, 16)

        self.q[qn].append(run)
        self._commit(tok, reads, writes)
        self.nins += 1
        return tok

    def _alltoks(self):
        deps = []
        for qn in self.dtok:
            deps.extend([t for t in self.dtok[qn] if t is not None])
        for e in self.esem:
            if self.cnt[e]:
                deps.append((self.esem[e], self.cnt[e]))
        return deps

    def barrier(self):
        deps = self._alltoks()
        for e in ("pe", "act", "dve", "pool", "sp"):
            waits = self._filter(e, deps)
            if waits:
                def run(engh, waits=waits):
                    for (s, v) in waits:
                        engh.wait_ge(s, v)
                self.q[e].append(run)

    def finish(self):
        waits = self._filter("sp", self._alltoks())

        def run(engh, waits=waits):
            for (s, v) in waits:
                engh.wait_ge(s, v)

        self.q["sp"].append(run)
        nc = self.nc
        with nc.Block() as block:
            for k, name in (("sp", "sync"), ("pe", "tensor"), ("act", "scalar"), ("dve", "vector"), ("pool", "gpsimd")):
                if not self.q[k]:
                    continue

                def body(engh, fns=self.q[k]):
                    for f in fns:
                        f(engh)

                getattr(block, name)(body)
        self.es.close()
        return nc


class Ctx:
    def __init__(self, P, ncons=56):
        self.P = P
        self.ps = P.ps
        self.cons = P.sbuf("cons", [128, ncons], F32)
        self.gs = P.sbuf("gs", [128, ncons], F32)
        self.ones = P.sbuf("ones", [128, 128], BF16)
        self.x = P.sbuf("x", [128, KC, G], F32)
        self.h = P.sbuf("h", [128, KC, G], BF16)
        self.sq = P.sbuf("sq", [128, KC, NT], BF16)
        self.rstd = P.sbuf("rstd", [128, NT], F32)
        self.rstd2 = P.sbuf("rstd2", [128, NT], F32)
        self.su = [P.sbuf("su%d" % i, [128, NT], F32) for i in range(2)]
        self.act = P.sbuf("act", [128, NJ, G], BF16)
        self.wsl = [P.sbuf("wsl%d" % i, [128, 3072], BF16) for i in range(4)]
        self.wi = 0
        self.sg = [P.sbuf("sg%d" % i, [128, NT], F32) for i in range(2)]
        P.op("pool", lambda e: e.memset(self.ones[:], 1.0), writes=[self.ones])
        for tt in (self.x, self.h, self.act, self.sq):
            tt.b.relaxed = True

    def wslot(self):
        s = self.wsl[self.wi % 4]
        self.wi += 1
        return s


def rmsnorm_fm(C, src, srcT, kcn, n0, n, gcol, dst, dstT, pb, dn, src_all=None):
    P = C.P
    sq = C.sq
    if src_all is not None:
        P.op("act", lambda e: e.activation(out=sq[:, 0:kcn, 0:n], in_=src_all, func=AF.Square), reads=[srcT], writes=[sq])
    else:
        for kc in range(kcn):
            P.op("act", lambda e, kc=kc: e.activation(out=sq[:, kc, 0:n], in_=src(kc), func=AF.Square),
                 reads=[srcT], writes=[sq])
    for kc in range(kcn):
        P.op("pe", lambda e, kc=kc: e.matmul(pb[:, 0:n], lhsT=C.ones[:], rhs=sq[:, kc, 0:n],
                                             start=(kc == 0), stop=(kc == kcn - 1)),
             reads=[C.ones, sq], writes=[pb])
    P.op("act", lambda e: e.activation(out=C.rstd[:, 0:n], in_=pb[:, 0:n], func=AF.Ln, bias=float(dn * EPS)),
         reads=[pb], writes=[C.rstd])
    P.op("act", lambda e: e.activation(out=C.rstd[:, 0:n], in_=C.rstd[:, 0:n], func=AF.Exp, scale=-0.5),
         reads=[C.rstd], writes=[C.rstd])
    for kc in range(kcn):
        if False and kcn == KC and kc % 2 == 1:
            tmp = C.sg[(kc // 2) % 2]
            P.op("act", lambda e, kc=kc, tmp=tmp: e.activation(out=tmp[:, 0:n], in_=src(kc), func=AF.Identity, scale=gcol(kc)),
                 reads=[srcT, C.gs], writes=[tmp])
            P.op("pool", lambda e, kc=kc, tmp=tmp: e.tensor_tensor(out=dst(kc), in0=tmp[:, 0:n], in1=C.rstd[:, 0:n], op=ALU.mult),
                 reads=[tmp, C.rstd], writes=[dstT])
        else:
            P.op("dve", lambda e, kc=kc: e.scalar_tensor_tensor(out=dst(kc), in0=src(kc), scalar=gcol(kc),
                                                                in1=C.rstd[:, 0:n], op0=ALU.mult, op1=ALU.mult),
                 reads=[srcT, C.gs, C.rstd], writes=[dstT])


def ffn_group(C, win, wout, gcol0, nt=G // NT):
    P = C.P
    rstds = [C.rstd, C.rstd2]
    for t in range(nt):
        sl = slice(t * NT, (t + 1) * NT)
        for kc in range(KC):
            if kc % 2 == 0:
                P.op("dve", lambda e, kc=kc, sl=sl: e.tensor_scalar(out=C.h[:, kc, sl], in0=C.x[:, kc, sl], scalar1=C.gs[:, gcol0 + kc:gcol0 + kc + 1],
                                                                   scalar2=None, op0=ALU.mult), reads=[C.x, C.gs], writes=[C.h])
            else:
                P.op("act", lambda e, kc=kc, sl=sl: e.activation(out=C.h[:, kc, sl], in_=C.x[:, kc, sl], func=AF.Identity,
                                                                 scale=C.gs[:, gcol0 + kc:gcol0 + kc + 1]), reads=[C.x, C.gs], writes=[C.h])
    for t in range(nt):
        sl = slice(t * NT, (t + 1) * NT)
        rs = rstds[t % 2]
        pb = C.ps[6 + t % 2]
        P.op("act", lambda e, sl=sl: e.activation(out=C.sq[:, :, 0:NT], in_=C.x[:, :, sl], func=AF.Square), reads=[C.x], writes=[C.sq])
        for kc in range(KC):
            P.op("pe", lambda e, kc=kc, pb=pb: e.matmul(pb[:], lhsT=C.ones[:], rhs=C.sq[:, kc, 0:NT], start=(kc == 0), stop=(kc == KC - 1)),
                 reads=[C.ones, C.sq], writes=[pb])
        P.op("act", lambda e, pb=pb, rs=rs: e.activation(out=rs[:], in_=pb[:], func=AF.Ln, bias=float(D * EPS)), reads=[pb], writes=[rs])
        P.op("act", lambda e, rs=rs: e.activation(out=rs[:], in_=rs[:], func=AF.Exp, scale=-0.5), reads=[rs], writes=[rs])
    pi = 0
    for j in range(NJ):
        ws = C.wslot()
        P.dma("pool", ws[:, 0:2048], win[j], reads=[win], writes=[ws])
        for t in range(nt):
            pg = C.ps[(pi % 2) * 2]
            pu = C.ps[(pi % 2) * 2 + 1]
            sg = C.sg[pi % 2]
            su = C.su[pi % 2]
            rs = rstds[t % 2]
            pi += 1
            for half, pb in ((0, pg), (1, pu)):
                for kc in range(KC):
                    P.op("pe", lambda e, kc=kc, half=half, pb=pb, ws=ws, t=t: e.matmul(
                        pb[:], lhsT=ws[:, kc * 256 + half * 128:kc * 256 + half * 128 + 128],
                        rhs=C.h[:, kc, t * NT:(t + 1) * NT], start=(kc == 0), stop=(kc == KC - 1)),
                        reads=[ws, C.h], writes=[pb])
            P.op("dve", lambda e, pg=pg, sg=sg, rs=rs: e.tensor_tensor(out=sg[:], in0=pg[:], in1=rs[:], op=ALU.mult), reads=[pg, rs], writes=[sg])
            P.op("act", lambda e, sg=sg: e.activation(out=sg[:], in_=sg[:], func=AF.Silu), reads=[sg], writes=[sg])
            P.op("dve", lambda e, pu=pu, su=su, rs=rs: e.tensor_tensor(out=su[:], in0=pu[:], in1=rs[:], op=ALU.mult), reads=[pu, rs], writes=[su])
            P.op("dve", lambda e, su=su, sg=sg, j=j, t=t: e.tensor_tensor(
                out=C.act[:, j, t * NT:(t + 1) * NT], in0=sg[:], in1=su[:], op=ALU.mult),
                reads=[sg, su], writes=[C.act])
    for m in range(KC):
        ws = C.wslot()
        P.dma("pool", ws[:, 0:NJ * 128], wout[m], reads=[wout], writes=[ws])
        for t in range(nt):
            pb = C.ps[4 + (pi % 2)]
            pi += 1
            for j in range(NJ):
                P.op("pe", lambda e, j=j, pb=pb, ws=ws, t=t: e.matmul(
                    pb[:], lhsT=ws[:, j * 128:(j + 1) * 128], rhs=C.act[:, j, t * NT:(t + 1) * NT],
                    start=(j == 0), stop=(j == NJ - 1)), reads=[ws, C.act], writes=[pb])
            P.op("dve", lambda e, pb=pb, m=m, t=t: e.scalar_tensor_tensor(
                out=C.x[:, m, t * NT:(t + 1) * NT], in0=pb[:], scalar=0.5, in1=C.x[:, m, t * NT:(t + 1) * NT],
                op0=ALU.mult, op1=ALU.add), reads=[pb, C.x], writes=[C.x])


def tile_ffn_in(w):
    wg = w[:, :DFF].reshape(KC, 128, NJ, 128)
    wu = w[:, DFF:].reshape(KC, 128, NJ, 128)
    s = np.stack([wg, wu], axis=3)
    return np.ascontiguousarray(s.transpose(2, 1, 0, 3, 4).reshape(NJ, 128, KC * 256))


def tile_ffn_out(w):
    return np.ascontiguousarray(w.reshape(NJ, 128, KC, 128).transpose(2, 1, 0, 3).reshape(KC, 128, NJ * 128))


def tile_lin(w):
    k, n = w.shape
    return np.ascontiguousarray(w.reshape(k // 128, 128, n // 128, 128).transpose(2, 1, 0, 3).reshape(n // 128, 128, k))


def tile_rows(w):
    k, n = w.shape
    return np.ascontiguousarray(w.reshape(k // 128, 128, n).transpose(1, 0, 2))


def col(v):
    return np.ascontiguousarray(v.reshape(-1, 128).T)


def fm(xtok):
    return np.ascontiguousarray(xtok.T)


def lin_fm(C, wdram, chunk, kcn, t0, n, pb, wslot=None, src=None):
    P = C.P
    src = C.h if src is None else src
    if wslot is None:
        wslot = C.wslot()
        P.dma("pool", wslot[:, 0:kcn * 128], wdram[chunk], reads=[wdram], writes=[wslot])
    for kc in range(kcn):
        P.op("pe", lambda e, kc=kc: e.matmul(pb[:, 0:n], lhsT=wslot[:, kc * 128:(kc + 1) * 128],
                                             rhs=src[:, kc, t0:t0 + n], start=(kc == 0), stop=(kc == kcn - 1)),
             reads=[wslot, src], writes=[pb])
    return wslot


def phase1(P, C, t, stage=99):
    NC1 = 32
    xin = t["xin"]
    win = t["win"]
    wout = t["wout"]
    wq = t["wq"]
    wf = t["wf"]
    wg = t["wg"]
    wv = t["wv"]
    cons = t["cons"]
    identd = t["ident"]
    hmaskd = t["hmask"]
    x1o = t["x1o"]
    oTo = t["oTo"]
    qco = t["qco"]
    sgo = t["sgo"]
    STo = t["STo"]
    DTo = t["DTo"]
    ident = P.sbuf("identb", [128, 128], BF16)
    hmask = P.sbuf("hmaskb", [128, 512], F32)
    wvs = P.sbuf("wvs", [128, KC, D], BF16)
    vtok = P.sbuf("vtok", [128, G // 128, D], BF16)
    lb = P.sbuf("lb", [128, 8], F32)
    oml = P.sbuf("oml", [128, 8], F32)
    noml = P.sbuf("noml", [128, 8], F32)
    rmask = P.sbuf("rmask", [128, NT], F32)
    zer8 = P.sbuf("zer8", [128, 8], F32)
    actf = C.act[:].rearrange("p j t -> p (j t)").bitcast(F32)
    scr = [[T(actf[:, (s_ * 7 + i) * NT:(s_ * 7 + i + 1) * NT], "hs%d_%d" % (s_, i)) for i in range(7)] for s_ in range(2)]
    qt = [P.sbuf("qt%d" % i, [128, NT], BF16) for i in range(2)]
    kt = [P.sbuf("kt%d" % i, [128, NT], BF16) for i in range(2)]
    kh = [P.sbuf("kh%d" % i, [128, NT], BF16) for i in range(2)]
    khT = [[P.sbuf("khT%d_%d" % (i, par), [128, NT], BF16) for par in range(2)] for i in range(2)]
    scm = [P.sbuf("scm%d" % i, [128, NT], BF16) for i in range(2)]
    qc = [P.sbuf("qc%d" % i, [128, NT], BF16) for i in range(2)]
    oev = [P.sbuf("oev%d" % i, [128, NT], F32) for i in range(2)]
    sgs = C.sg
    dch = [P.sbuf("dch%d" % i, [128, 8], F32) for i in range(2)]
    S = [P.sbuf("S%d" % i, [128, 128], F32) for i in range(NH_A)]
    Sb = [[P.sbuf("Sb%d_%d" % (i, c), [128, 128], BF16) for c in range(9)] for i in range(2)]
    Dexc = [P.sbuf("Dexc%d" % i, [128, 9], F32) for i in range(NH_A)]
    Slast = [P.sbuf("Slast%d" % i, [128, 128], BF16) for i in range(NH_A)]

    P.dma("sp", C.cons[:, 0:NC1], cons[:], reads=[cons], writes=[C.cons])
    P.dma("pool", ident[:], identd[:], writes=[ident])
    P.dma("sp", hmask[:], hmaskd[:], writes=[hmask])
    P.op("dve", lambda e: e.tensor_scalar(out=C.gs[:, 0:NC1], in0=C.cons[:, 0:NC1], scalar1=float(np.sqrt(D)), scalar2=None, op0=ALU.mult),
         reads=[C.cons], writes=[C.gs])
    P.op("dve", lambda e: e.tensor_tensor(out=lb[:], in0=C.cons[:, 16:24], in1=C.cons[:, 24:32], op=ALU.subtract), reads=[C.cons], writes=[lb])
    P.op("act", lambda e: e.activation(out=oml[:], in_=lb[:], func=AF.Sigmoid, scale=-1.0), reads=[lb], writes=[oml])
    P.op("act", lambda e: e.activation(out=lb[:], in_=lb[:], func=AF.Sigmoid), reads=[lb], writes=[lb])
    P.op("dve", lambda e: e.tensor_scalar(out=noml[:], in0=oml[:], scalar1=-1.0, scalar2=None, op0=ALU.mult), reads=[oml], writes=[noml])
    P.op("pool", lambda e: e.memset(rmask[:], 1.0), writes=[rmask])
    P.op("pool", lambda e: e.memset(rmask[:].rearrange("p (c t) -> p c t", t=CH)[:, :, 0:1], 0.0), writes=[rmask])
    P.op("pool", lambda e: e.memset(zer8[:], 0.0), writes=[zer8])
    for i in range(2):
        for par in range(2):
            P.op("pool", lambda e, i=i, par=par: e.memset(khT[i][par][:], 0.0), writes=[khT[i][par]])
    for hd in range(NH_A):
        P.op("pool", lambda e, hd=hd: e.memset(S[hd][:], 0.0), writes=[S[hd]])
        P.op("pool", lambda e, hd=hd: e.memset(Slast[hd][:], 0.0), writes=[Slast[hd]])
        P.op("pool", lambda e, hd=hd: e.memset(Dexc[hd][:], 1.0), writes=[Dexc[hd]])

    it = 0
    for g in range(TOK // G):
        P.dma("sp", C.x[:], xin[:, :, g * G:(g + 1) * G], reads=[xin], writes=[C.x])
        ffn_group(C, win, wout, 0)
        P.dma("sp", x1o[:, :, g * G:(g + 1) * G], C.x[:], reads=[C.x], writes=[x1o])
        P.barrier()
        for t in range(G // NT):
            rmsnorm_fm(C, lambda kc, t=t: C.x[:, kc, t * NT:(t + 1) * NT], C.x, KC, t * NT, NT,
                       lambda kc: C.gs[:, 8 + kc:9 + kc],
                       lambda kc, t=t: C.h[:, kc, t * NT:(t + 1) * NT], C.h, C.ps[7], D, src_all=C.x[:, :, t * NT:(t + 1) * NT])
        if g == 0:
            for kc in range(KC):
                P.dma("pool", wvs[:, kc, :], wv[:, kc * D:(kc + 1) * D], reads=[wv], writes=[wvs])
        for tb in range(G // 128):
            for half in range(2):
                pb = C.ps[(tb * 2 + half) % 4]
                for kc in range(KC):
                    P.op("pe", lambda e, kc=kc, tb=tb, half=half, pb=pb: e.matmul(
                        pb[:], lhsT=C.h[:, kc, tb * 128:(tb + 1) * 128], rhs=wvs[:, kc, half * 512:(half + 1) * 512],
                        start=(kc == 0), stop=(kc == KC - 1)), reads=[C.h, wvs], writes=[pb])
                P.op("act", lambda e, tb=tb, half=half, pb=pb: e.copy(out=vtok[:, tb, half * 512:(half + 1) * 512], in_=pb[:]),
                     reads=[pb], writes=[vtok])
        if stage < 2:
            continue
        for c in range(8):
            ws = None
            for t in range(G // NT):
                pb = C.ps[(c * 2 + t) % 4]
                ws = lin_fm(C, wg, c, KC, t * NT, NT, pb, wslot=ws)
                sg = sgs[(c * 2 + t) % 2]
                P.op("act", lambda e, pb=pb, sg=sg: e.activation(out=sg[:], in_=pb[:], func=AF.Sigmoid), reads=[pb], writes=[sg])
                P.dma("sp", sgo[:, c, g * G + t * NT:g * G + (t + 1) * NT], sg[:], reads=[sg], writes=[sgo])
        if stage < 3:
            continue
        def front(hd, t, i2):
            sig, fv, kk, bb, eb, enb, ek = scr[i2]
            tok0 = g * G + t * NT
            pq = C.ps[4 * i2]
            pf = C.ps[4 * i2 + 1]
            wsq = C.wsl[2 * i2]
            wsf = C.wsl[2 * i2 + 1]
            if t == 0 and hd < 2:
                P.dma("pool", wsq[:, 0:KC * 128], wq[hd], reads=[wq], writes=[wsq])
                P.dma("pool", wsf[:, 0:KC * 128], wf[hd], reads=[wf], writes=[wsf])
            lin_fm(C, wq, hd, KC, t * NT, NT, pq, wslot=wsq)
            lin_fm(C, wf, hd, KC, t * NT, NT, pf, wslot=wsf)
            if t == G // NT - 1 and hd + 2 < NH_A:
                P.dma("pool", wsq[:, 0:KC * 128], wq[hd + 2], reads=[wq], writes=[wsq])
                P.dma("pool", wsf[:, 0:KC * 128], wf[hd + 2], reads=[wf], writes=[wsf])
            c3 = lambda a: a[:].rearrange("p (c t) -> p c t", t=CH)
            P.op("act", lambda e, pf=pf: e.activation(out=sig[:], in_=pf[:], func=AF.Sigmoid), reads=[pf], writes=[sig])
            P.op("dve", lambda e, hd=hd: e.tensor_scalar(out=fv[:], in0=sig[:], scalar1=oml[:, hd:hd + 1], scalar2=lb[:, hd:hd + 1],
                                                       op0=ALU.mult, op1=ALU.add), reads=[sig, oml, lb], writes=[fv])
            P.op("act", lambda e: e.activation(out=fv[:], in_=fv[:], func=AF.Ln), reads=[fv], writes=[fv])
            P.op("dve", lambda e, hd=hd: e.tensor_scalar(out=kk[:], in0=sig[:], scalar1=noml[:, hd:hd + 1], scalar2=oml[:, hd:hd + 1],
                                                       op0=ALU.mult, op1=ALU.add), reads=[sig, noml, oml], writes=[kk])
            P.op("dve", lambda e: e.tensor_tensor_scan(out=bb[:], data0=rmask[:], data1=fv[:], initial=0.0, op0=ALU.mult, op1=ALU.add),
                 reads=[rmask, fv], writes=[bb])
            P.op("act", lambda e: e.activation(out=eb[:], in_=bb[:], func=AF.Exp), reads=[bb], writes=[eb])
            P.op("act", lambda e: e.activation(out=enb[:], in_=bb[:], func=AF.Exp, scale=-1.0), reads=[bb], writes=[enb])
            P.op("dve", lambda e, pq=pq, i2=i2: e.scalar_tensor_tensor(out=qt[i2][:], in0=pq[:], scalar=float(128 ** -0.5), in1=eb[:],
                                                                      op0=ALU.mult, op1=ALU.mult), reads=[pq, eb], writes=[qt[i2]])
            P.op("pool", lambda e, i2=i2: e.tensor_tensor(out=kt[i2][:], in0=kk[:], in1=enb[:], op=ALU.mult), reads=[kk, enb], writes=[kt[i2]])
            P.op("dve", lambda e: e.tensor_tensor(out=c3(ek), in0=c3(bb)[:, :, CH - 1:CH].to_broadcast([128, 8, CH]), in1=c3(bb),
                                                  op=ALU.subtract), reads=[bb], writes=[ek])
            P.op("act", lambda e: e.activation(out=ek[:], in_=ek[:], func=AF.Exp), reads=[ek], writes=[ek])
            P.op("pool", lambda e, i2=i2: e.tensor_tensor(out=kh[i2][:], in0=kk[:], in1=ek[:], op=ALU.mult), reads=[kk, ek], writes=[kh[i2]])
            P.op("act", lambda e, i2=i2: e.activation(out=dch[i2][:], in_=c3(bb)[:, :, CH - 1], func=AF.Exp), reads=[bb], writes=[dch[i2]])
            P.op("dve", lambda e, hd=hd, i2=i2: e.tensor_tensor_scan(out=Dexc[hd][:, 1:9], data0=dch[i2][:], data1=zer8[:],
                                                                     initial=Dexc[hd][:, 0:1], op0=ALU.mult, op1=ALU.add),
                 reads=[dch[i2], zer8, Dexc[hd]], writes=[Dexc[hd]])
            P.op("dve", lambda e, hd=hd, i2=i2: e.tensor_tensor(out=c3(qc[i2]), in0=c3(qt[i2]),
                                                                in1=Dexc[hd][:, 0:8].unsqueeze(2).to_broadcast([128, 8, CH]), op=ALU.mult),
                 reads=[qt[i2], Dexc[hd]], writes=[qc[i2]])
            P.dma("sp", qco[hd, :, tok0:tok0 + NT], qc[i2][:], reads=[qc[i2]], writes=[qco])
            P.op("dve", lambda e, hd=hd: e.tensor_copy(out=Dexc[hd][:, 0:1], in_=Dexc[hd][:, 8:9]), reads=[Dexc[hd]], writes=[Dexc[hd]])


        def back(hd, t, i2):
            tok0 = g * G + t * NT
            ptr = C.ps[4 * i2 + 2]
            ptrb = ptr[:].bitcast(BF16)
            for pr in range(4):
                P.op("pe", lambda e, pr=pr, i2=i2: e.transpose(ptrb[:, pr * 128:(pr + 1) * 128], kh[i2][:, pr * 128:(pr + 1) * 128], ident[:]),
                     reads=[kh[i2], ident], writes=[ptr])
            for par in range(2):
                P.op("act", lambda e, i2=i2, par=par: e.copy(out=khT[i2][par][par * 64:(par + 1) * 64, :], in_=ptrb[par * 64:(par + 1) * 64, 0:NT]),
                     reads=[ptr], writes=[khT[i2][par]])
            psc = C.ps[4 * i2 + 2]
            for pr in range(4):
                P.op("pe", lambda e, pr=pr, i2=i2: e.matmul(psc[:, pr * 128:(pr + 1) * 128], lhsT=kt[i2][:, pr * 128:(pr + 1) * 128],
                                                            rhs=qt[i2][:, pr * 128:(pr + 1) * 128], start=True, stop=True),
                     reads=[kt[i2], qt[i2]], writes=[psc])
            P.op("dve", lambda e, i2=i2: e.tensor_tensor(out=scm[i2][:], in0=psc[:], in1=hmask[:], op=ALU.mult), reads=[psc, hmask], writes=[scm[i2]])
            pU = C.ps[4 * i2 + 3]
            P.op("pool", lambda e, i2=i2, hd=hd: e.tensor_copy(out=Sb[i2][0][:], in_=Slast[hd][:]), reads=[Slast[hd]], writes=[Sb[i2][0]])
            for half in range(2):
                for c4 in range(4):
                    c = half * 4 + c4
                    pr, par = c // 2, c % 2
                    tb = t * 4 + pr
                    P.op("pe", lambda e, c4=c4, pr=pr, par=par, tb=tb, i2=i2, hd=hd: e.matmul(
                        pU[:, c4 * 128:(c4 + 1) * 128], lhsT=khT[i2][par][:, pr * 128:(pr + 1) * 128],
                        rhs=vtok[:, tb, hd * 128:(hd + 1) * 128], start=True, stop=True),
                        reads=[khT[i2][par], vtok], writes=[pU])
                for c4 in range(4):
                    c = half * 4 + c4
                    P.op("dve", lambda e, c=c, c4=c4, i2=i2, hd=hd: e.scalar_tensor_tensor(
                        out=S[hd][:], in0=S[hd][:], scalar=dch[i2][:, c:c + 1], in1=pU[:, c4 * 128:(c4 + 1) * 128],
                        op0=ALU.mult, op1=ALU.add), reads=[S[hd], dch[i2], pU], writes=[S[hd]])
                    dst = Sb[i2][c + 1] if c < 7 else Slast[hd]
                    P.op("pool", lambda e, dst=dst, hd=hd: e.tensor_copy(out=dst[:], in_=S[hd][:]), reads=[S[hd]], writes=[dst])
            po = C.ps[4 * i2 + 2]
            for pr in range(4):
                tb = t * 4 + pr
                P.op("pe", lambda e, pr=pr, tb=tb, i2=i2, hd=hd: e.matmul(po[:, pr * 128:(pr + 1) * 128], lhsT=vtok[:, tb, hd * 128:(hd + 1) * 128],
                                                                          rhs=scm[i2][:, pr * 128:(pr + 1) * 128], start=True, stop=False),
                     reads=[vtok, scm[i2]], writes=[po])
                for par in range(2):
                    c = pr * 2 + par
                    P.op("pe", lambda e, c=c, i2=i2: e.matmul(po[:, c * 64:(c + 1) * 64], lhsT=Sb[i2][c][:], rhs=qt[i2][:, c * 64:(c + 1) * 64],
                                                              start=False, stop=(c % 2 == 1)), reads=[Sb[i2][c], qt[i2]], writes=[po])
            P.op("act", lambda e, i2=i2: e.copy(out=oev[i2][:], in_=po[:]), reads=[po], writes=[oev[i2]])
            P.dma("sp", oTo[hd, :, tok0:tok0 + NT], oev[i2][:], reads=[oev[i2]], writes=[oTo])


        streams = []
        for s_ in range(2):
            P.defer = []
            for hd in range(s_, NH_A, 2):
                for t in range(G // NT):
                    front(hd, t, s_)
                    back(hd, t, s_)
            streams.append(P.defer)
            P.defer = None
        P.run_streams(streams)
        P.barrier()
    for hd in range(NH_A):
        P.dma("sp", STo[:, hd, :], S[hd][:], reads=[S[hd]], writes=[STo])
        P.op("dve", lambda e, hd=hd: e.tensor_copy(out=zer8[:, hd:hd + 1], in_=Dexc[hd][:, 0:1]), reads=[Dexc[hd]], writes=[zer8])
    P.dma("sp", DTo[:], zer8[:], reads=[zer8], writes=[DTo])


NC2 = 56


def ple_group(C, wgate, wple, pin, g, gcol0, pbf):
    P = C.P
    nt = G // NT
    for t in range(nt):
        rmsnorm_fm(C, lambda kc, t=t: C.x[:, kc, t * NT:(t + 1) * NT], C.x, KC, t * NT, NT,
                   lambda kc: C.gs[:, gcol0 + kc:gcol0 + kc + 1],
                   lambda kc, t=t: C.h[:, kc, t * NT:(t + 1) * NT], C.h, C.ps[7], D, src_all=C.x[:, :, t * NT:(t + 1) * NT])
    for kc in range(2):
        P.dma("pool", pbf[:, kc, :], pin[:, kc, g * G:(g + 1) * G], reads=[pin], writes=[pbf])
    i = 0
    for m in range(KC):
        wsg = None
        wsp = None
        for t in range(nt):
            pg = C.ps[(i % 2) * 2]
            pp = C.ps[(i % 2) * 2 + 1]
            sg = C.sg[i % 2]
            i += 1
            wsg = lin_fm(C, wgate, m, KC, t * NT, NT, pg, wslot=wsg)
            wsp = lin_fm(C, wple, m, 2, t * NT, NT, pp, wslot=wsp, src=pbf)
            P.op("act", lambda e, pg=pg, sg=sg: e.activation(out=sg[:], in_=pg[:], func=AF.Sigmoid), reads=[pg], writes=[sg])
            P.op("dve", lambda e, pp=pp, sg=sg: e.tensor_tensor(out=sg[:], in0=sg[:], in1=pp[:], op=ALU.mult), reads=[sg, pp], writes=[sg])
            P.op("dve", lambda e, sg=sg, m=m, t=t: e.tensor_tensor(out=C.x[:, m, t * NT:(t + 1) * NT], in0=C.x[:, m, t * NT:(t + 1) * NT],
                                                                 in1=sg[:], op=ALU.add), reads=[sg, C.x], writes=[C.x])


def rope_table(C, posd, tok0, n, tab, tmpf, tmpi, invcol, phcol):
    P = C.P
    P.dma("sp", tmpi[:, 0:n], posd[0:1, tok0:tok0 + n].to_broadcast([128, n]), reads=[posd], writes=[tmpi])
    P.op("dve", lambda e: e.tensor_copy(out=tab[:, 0:n], in_=tmpi[:, 0:n]), reads=[tmpi], writes=[tab])
    P.op("dve", lambda e: e.tensor_scalar(out=tab[:, 0:n], in0=tab[:, 0:n], scalar1=C.cons[:, invcol:invcol + 1],
                                          scalar2=C.cons[:, phcol:phcol + 1], op0=ALU.mult, op1=ALU.add), reads=[tab, C.cons], writes=[tab])
    P.op("dve", lambda e: e.tensor_scalar(out=tmpf[:, 0:n], in0=tab[:, 0:n], scalar1=float(1 / (2 * np.pi)), scalar2=None, op0=ALU.mult),
         reads=[tab], writes=[tmpf])
    P.op("dve", lambda e: e.tensor_copy(out=tmpi[:, 0:n], in_=tmpf[:, 0:n]), reads=[tmpf], writes=[tmpi])
    P.op("dve", lambda e: e.tensor_copy(out=tmpf[:, 0:n], in_=tmpi[:, 0:n]), reads=[tmpi], writes=[tmpf])
    P.op("dve", lambda e: e.scalar_tensor_tensor(out=tab[:, 0:n], in0=tmpf[:, 0:n], scalar=float(-2 * np.pi), in1=tab[:, 0:n],
                                                 op0=ALU.mult, op1=ALU.add), reads=[tmpf, tab], writes=[tab])
    P.op("dve", lambda e: e.tensor_scalar(out=tmpf[:, 0:n], in0=tab[:, 0:n], scalar1=float(np.pi), scalar2=float(-2 * np.pi),
                                          op0=ALU.is_gt, op1=ALU.mult), reads=[tab], writes=[tmpf])
    P.op("dve", lambda e: e.tensor_tensor(out=tab[:, 0:n], in0=tab[:, 0:n], in1=tmpf[:, 0:n], op=ALU.add), reads=[tab, tmpf], writes=[tab])
    P.op("dve", lambda e: e.tensor_scalar(out=tmpf[:, 0:n], in0=tab[:, 0:n], scalar1=float(-np.pi), scalar2=float(2 * np.pi),
                                          op0=ALU.is_lt, op1=ALU.mult), reads=[tab], writes=[tmpf])
    P.op("dve", lambda e: e.tensor_tensor(out=tab[:, 0:n], in0=tab[:, 0:n], in1=tmpf[:, 0:n], op=ALU.add), reads=[tab, tmpf], writes=[tab])
    P.op("act", lambda e: e.activation(out=tab[:, 0:n], in_=tab[:, 0:n], func=AF.Sin), reads=[tab], writes=[tab])


def phase2(P, C, t, dbg=False):
    x1i = t["x1i"]
    oTi = t["oTi"]
    qci = t["qci"]
    sgi = t["sgi"]
    STi = t["STi"]
    DTi = t["DTi"]
    pin = t["pin"]
    posd = t["pos"]
    cons = t["cons"]
    stk = t["stk"]
    wao = t["wao"]
    win0 = t["win0"]
    wout0 = t["wout0"]
    wpg = t["wpg"]
    wpi = t["wpi"]
    wkd = t["wkd"]
    win1 = t["win1"]
    wout1 = t["wout1"]
    wdq = t["wdq"]
    x5o = t["x5o"]
    lat_dst = t["lat_dst"]
    lat_done = t["lat_done"]
    Sst = P.sbuf("Sst", [128, NH_A, 128], F32)
    Sstb = P.sbuf("Sstb", [128, NH_A, 128], BF16)
    Ul = P.sbuf("Ul", [128, NH_A, 128], F32)
    Dl = P.sbuf("Dl", [128, NH_A], F32)
    ot = [P.sbuf("ot%d" % i, [128, NT], F32) for i in range(2)]
    qcl = [P.sbuf("qcl%d" % i, [128, NT], BF16) for i in range(2)]
    sgl = [P.sbuf("sgl%d" % i, [128, NT], F32) for i in range(2)]
    pbf = P.sbuf("pbf", [128, 2, G], BF16)
    latf = P.sbuf("latf", [128, 4, NT], F32)
    latb = P.sbuf("latb", [128, 4, NT], BF16)
    tab = P.sbuf("tab", [128, G], F32)
    tmpf = P.sbuf("tmpf", [128, G], F32)
    tmpi = P.sbuf("tmpi", [128, G], I32)
    stkb = P.sbuf("stkb", [128, 64], BF16)
    rT = P.sbuf("rT", [128, NT], BF16)
    krb = P.sbuf("krb", [64, NT], BF16)

    P.dma("sp", C.cons[:, 0:NC2], cons[:], reads=[cons], writes=[C.cons])
    P.dma("pool", stkb[:], stk[:], writes=[stkb])
    for (a, b, sc) in ((0, 40, np.sqrt(D)), (40, 42, np.sqrt(KVL)), (42, 46, np.sqrt(QL)), (46, 47, np.sqrt(128.0))):
        P.op("dve", lambda e, a=a, b=b, sc=sc: e.tensor_scalar(out=C.gs[:, a:b], in0=C.cons[:, a:b], scalar1=float(sc), scalar2=None, op0=ALU.mult),
             reads=[C.cons], writes=[C.gs])
    P.op("pool", lambda e: e.memset(Sst[:], 0.0), writes=[Sst])
    for r in range(4):
        P.dma("sp", Ul[:], STi[r], reads=[STi], writes=[Ul])
        P.dma("sp", Dl[:], DTi[r], reads=[DTi], writes=[Dl])
        w = C.cons[:, 49 + r:50 + r]
        P.op("dve", lambda e, w=w: e.tensor_scalar(out=Dl[:], in0=Dl[:], scalar1=-1.0, scalar2=w, op0=ALU.add, op1=ALU.mult), reads=[Dl, C.cons], writes=[Dl])
        P.op("dve", lambda e: e.tensor_scalar(out=Dl[:], in0=Dl[:], scalar1=1.0, scalar2=None, op0=ALU.add), reads=[Dl], writes=[Dl])
        P.op("dve", lambda e, w=w: e.tensor_scalar(out=Ul[:], in0=Ul[:], scalar1=w, scalar2=None, op0=ALU.mult), reads=[Ul, C.cons], writes=[Ul])
        for hd in range(NH_A):
            P.op("dve", lambda e, hd=hd: e.scalar_tensor_tensor(out=Sst[:, hd, :], in0=Sst[:, hd, :], scalar=Dl[:, hd:hd + 1], in1=Ul[:, hd, :],
                                                                op0=ALU.mult, op1=ALU.add), reads=[Sst, Dl, Ul], writes=[Sst])
    P.op("act", lambda e: e.copy(out=Sstb[:], in_=Sst[:]), reads=[Sst], writes=[Sstb])

    nt = G // NT
    it = 0
    for g in range(TOK // G):
        P.dma("sp", C.x[:], x1i[:, :, g * G:(g + 1) * G], reads=[x1i], writes=[C.x])
        for hd in range(NH_A):
            for t in range(nt):
                i2 = it % 2
                it += 1
                tok0 = g * G + t * NT
                P.dma("sp", ot[i2][:], oTi[hd, :, tok0:tok0 + NT], reads=[oTi], writes=[ot[i2]])
                P.dma("sp", qcl[i2][:], qci[hd, :, tok0:tok0 + NT], reads=[qci], writes=[qcl[i2]])
                P.dma("sp", sgl[i2][:], sgi[:, hd, tok0:tok0 + NT], reads=[sgi], writes=[sgl[i2]])
                po = C.ps[i2]
                P.op("pe", lambda e, hd=hd, i2=i2, po=po: e.matmul(po[:], lhsT=Sstb[:, hd, :], rhs=qcl[i2][:], start=True, stop=True),
                     reads=[Sstb, qcl[i2]], writes=[po])
                P.op("dve", lambda e, i2=i2, po=po: e.tensor_tensor(out=ot[i2][:], in0=ot[i2][:], in1=po[:], op=ALU.add), reads=[ot[i2], po], writes=[ot[i2]])
                rmsnorm_fm(C, lambda kc, i2=i2: ot[i2][:], ot[i2], 1, 0, NT, lambda kc: C.gs[:, 46:47],
                           lambda kc: latf[:, 0, :], latf, C.ps[7], 128)
                P.op("dve", lambda e, hd=hd, t=t, i2=i2: e.tensor_tensor(out=C.h[:, hd, t * NT:(t + 1) * NT], in0=latf[:, 0, :], in1=sgl[i2][:], op=ALU.mult),
                     reads=[latf, sgl[i2]], writes=[C.h])
        i = 0
        for m in range(KC):
            ws = None
            for t in range(nt):
                pb = C.ps[2 + i % 2]
                i += 1
                ws = lin_fm(C, wao, m, KC, t * NT, NT, pb, wslot=ws)
                P.op("dve", lambda e, pb=pb, m=m, t=t: e.tensor_tensor(out=C.x[:, m, t * NT:(t + 1) * NT], in0=C.x[:, m, t * NT:(t + 1) * NT], in1=pb[:], op=ALU.add),
                     reads=[pb, C.x], writes=[C.x])
        ffn_group(C, win0, wout0, 0)
        ple_group(C, wpg, wpi, pin, g, 8, pbf)
        rope_table(C, posd, g * G, G, tab, tmpf, tmpi, 47, 48)
        for t in range(nt):
            rmsnorm_fm(C, lambda kc, t=t: C.x[:, kc, t * NT:(t + 1) * NT], C.x, KC, t * NT, NT,
                       lambda kc: C.gs[:, 16 + kc:17 + kc],
                       lambda kc, t=t: C.h[:, kc, t * NT:(t + 1) * NT], C.h, C.ps[7], D, src_all=C.x[:, :, t * NT:(t + 1) * NT])
        wsl3 = [None, None, None]
        for t in range(nt):
            tok0 = g * G + t * NT
            for c in range(3):
                pb = C.ps[c]
                wsl3[c] = lin_fm(C, wkd, c, KC, t * NT, NT, pb, wslot=wsl3[c])
                if c < 2:
                    P.op("act", lambda e, c=c, pb=pb: e.copy(out=latf[:, c, :], in_=pb[:]), reads=[pb], writes=[latf])
                else:
                    P.op("dve", lambda e, pb=pb, t=t: e.tensor_tensor(out=rT[:], in0=pb[:], in1=tab[:, t * NT:(t + 1) * NT], op=ALU.mult),
                         reads=[pb, tab], writes=[rT])
            pk = C.ps[3]
            P.op("pe", lambda e, pk=pk: e.matmul(pk[0:64, :], lhsT=stkb[:], rhs=rT[:], start=True, stop=True), reads=[stkb, rT], writes=[pk])
            P.op("act", lambda e, pk=pk: e.copy(out=krb[0:64, :], in_=pk[0:64, :]), reads=[pk], writes=[krb])
            cq_ap, ckv_ap, kr_ap, latT = lat_dst(tok0 // NT)
            P.dma("sp", kr_ap, krb[0:64, :], reads=[krb], writes=[latT])
            rmsnorm_fm(C, lambda kc: latf[:, kc, :], latf, 2, 0, NT, lambda kc: C.gs[:, 40 + kc:41 + kc],
                       lambda kc: latb[:, kc, :], latb, C.ps[7], KVL)
            P.dma("sp", ckv_ap, latb[:, 0:2, :], reads=[latb], writes=[latT])
        ffn_group(C, win1, wout1, 24)
        P.dma("sp", x5o[:, :, g * G:(g + 1) * G], C.x[:], reads=[C.x], writes=[x5o])
        for t in range(nt):
            rmsnorm_fm(C, lambda kc, t=t: C.x[:, kc, t * NT:(t + 1) * NT], C.x, KC, t * NT, NT,
                       lambda kc: C.gs[:, 32 + kc:33 + kc],
                       lambda kc, t=t: C.h[:, kc, t * NT:(t + 1) * NT], C.h, C.ps[7], D, src_all=C.x[:, :, t * NT:(t + 1) * NT])
        wsl4 = [None] * 4
        for t in range(nt):
            tok0 = g * G + t * NT
            for c in range(4):
                pb = C.ps[c]
                wsl4[c] = lin_fm(C, wdq, c, KC, t * NT, NT, pb, wslot=wsl4[c])
                P.op("act", lambda e, c=c, pb=pb: e.copy(out=latf[:, c, :], in_=pb[:]), reads=[pb], writes=[latf])
            rmsnorm_fm(C, lambda kc: latf[:, kc, :], latf, 4, 0, NT, lambda kc: C.gs[:, 42 + kc:43 + kc],
                       lambda kc: latb[:, kc, :], latb, C.ps[7], QL)
            cq_ap, ckv_ap, kr_ap, latT = lat_dst(tok0 // NT)
            P.dma("sp", cq_ap, latb[:], reads=[latb], writes=[latT])
            lat_done(tok0 // NT)


HPC = 4
NQT = SEQ // NT
SCALE = float((128 + ROPE) ** -0.5)


def phase3(P, C0, t, nqt=NQT, nheads=HPC, stop=9):
    lat = t["lat"]
    cq_src = t["cq_src"]
    ckv_src = t["ckv_src"]
    kr_src = t["kr_src"]
    o_dst = t["o_dst"]
    head_done = t["head_done"]
    posd = t["pos"]
    cons = t["cons"]
    stk = t["stk"]
    trid = t["tri"]
    wuq = t["wuq"]
    wuk = t["wuk"]
    wuv = t["wuv"]

    class CC:
        pass
    C = CC()
    C.P = P
    C.ps = P.ps
    C.cons = P.sbuf("cons", [128, 8], F32)
    ones = P.sbuf("ones", [128, 128], BF16)
    onesf = P.sbuf("onesf", [128, 128], F32)
    ckv = P.sbuf("ckv", [128, 2, SEQ], BF16)
    Bk = P.sbuf("Bk", [128, SEQ], BF16)
    Ak = [P.sbuf("Ak%d" % i, [128, SEQ], BF16) for i in range(2)]
    Vh = [P.sbuf("Vh%d" % i, [128, SEQ // 128, 128], BF16) for i in range(2)]
    tab = P.sbuf("tab", [128, SEQ], F32)
    stkb = P.sbuf("stkb", [128, 64], BF16)
    tri = P.sbuf("trib", [128, 128], F32)
    wq = [P.sbuf("wq%d" % i, [128, 2, QL], BF16) for i in range(2)]
    wk = [P.sbuf("wk%d" % i, [128, KVL], BF16) for i in range(2)]
    wv = [P.sbuf("wv%d" % i, [128, KVL], BF16) for i in range(2)]
    mx = P.sbuf("mx", [128, 8], F32)
    mtmp = P.sbuf("setup_tmp", [128, 4096], F32)
    m_alias = P.mark()

    P.dma("sp", C.cons[:], cons[:], reads=[cons], writes=[C.cons])
    P.dma("pool", stkb[:], stk[:], writes=[stkb])
    P.dma("sp", tri[:], trid[:], writes=[tri])
    P.op("pool", lambda e: e.memset(ones[:], 1.0), writes=[ones])
    P.op("pool", lambda e: e.memset(onesf[:], 1.0), writes=[onesf])
    P.op("pool", lambda e: e.memset(Bk[:], 0.0), writes=[Bk])
    P.op("pool", lambda e: e.memset(Bk[64:65, :], 1.0), writes=[Bk])
    P.op("pool", lambda e: e.memset(mx[:], 0.0), writes=[mx])
    for q in range(nqt):
        for kc in range(2):
            P.dma("sp", ckv[:, kc, q * NT:(q + 1) * NT], ckv_src(kc, q), reads=[lat], writes=[ckv])
        P.dma("sp", Bk[0:64, q * NT:(q + 1) * NT], kr_src(q), reads=[lat], writes=[Bk])
    tmpf = T(mtmp[:, 0:2048], "tmpf")
    tmpi = T(mtmp[:, 2048:4096].bitcast(I32), "tmpi")
    for q4 in range(4):
        tv = T(tab[:, q4 * 2048:(q4 + 1) * 2048], "tabv%d" % q4)
        rope_table(C, posd, q4 * 2048, 2048, tv, tmpf, tmpi, 0, 1)
    P.barrier()
    P.atop = m_alias - 4096 * 4
    aq = [P.sbuf("aq%d" % i, [128, NT], BF16) for i in range(3)]
    bq = [P.sbuf("bq%d" % i, [128, NT], BF16) for i in range(3)]
    cql = [P.sbuf("cql%d" % i, [128, 4, NT], BF16) for i in range(3)]
    sqA = [P.sbuf("sqA%d" % i, [128, NT], BF16) for i in range(2)]
    sqB = [P.sbuf("sqB%d" % i, [128, NT], BF16) for i in range(2)]
    rT = [P.sbuf("rT%d" % i, [128, NT], BF16) for i in range(2)]
    tq = P.sbuf("tq", [128, NT], F32)
    NPT = 6
    pt = [P.sbuf("pt%d" % i, [128, NT], BF16) for i in range(NPT)]
    acc = [[P.sbuf("acc%d_%d" % (j, i), [128, NT], F32) for i in range(2)] for j in range(2)]
    rl = P.sbuf("rl", [128, NT], F32)
    on = [P.sbuf("on%d" % i, [128, NT], BF16) for i in range(2)]
    for i in range(3):
        P.op("pool", lambda e, i=i: e.memset(bq[i][:], 0.0), writes=[bq[i]])
    for i in range(2):
        P.op("pool", lambda e, i=i: e.memset(sqB[i][:], 0.0), writes=[sqB[i]])

    state = {"bank": 0, "wide": True, "sq": 0}

    def nxt_bank():
        state["bank"] += 1
        return C.ps[state["bank"] % 8] if state["wide"] else C.ps[6 + state["bank"] % 2]

    def colmax(pb, col):
        P.op("dve", lambda e: e.tensor_reduce(out=mx[:, 3:4], in_=pb[:], axis=mybir.AxisListType.X, op=ALU.max), reads=[pb], writes=[mx])
        P.op("dve", lambda e: e.tensor_tensor(out=mx[:, col:col + 1], in0=mx[:, col:col + 1], in1=mx[:, 3:4], op=ALU.max), reads=[mx], writes=[mx])

    for kt in range(nqt):
        pb = nxt_bank()
        sB = sqB[kt % 2]
        P.op("act", lambda e, kt=kt, sB=sB: e.activation(out=sB[0:64, :], in_=Bk[0:64, kt * NT:(kt + 1) * NT], func=AF.Square), reads=[Bk], writes=[sB])
        P.op("pe", lambda e, sB=sB, pb=pb: e.matmul(pb[:], lhsT=ones[:], rhs=sB[:], start=True, stop=True), reads=[ones, sB], writes=[pb])
        colmax(pb, 0)

    def load_w(hn):
        P.dma("pool", wq[hn % 2][:, 0, :], wuq[2 * hn], reads=[wuq], writes=[wq[hn % 2]])
        P.dma("pool", wq[hn % 2][:, 1, :], wuq[2 * hn + 1], reads=[wuq], writes=[wq[hn % 2]])
        P.dma("pool", wk[hn % 2][:], wuk[hn], reads=[wuk], writes=[wk[hn % 2]])
        P.dma("pool", wv[hn % 2][:], wuv[hn], reads=[wuv], writes=[wv[hn % 2]])
        P.op("pool", lambda e: e.memset(mx[:, 1 + hn % 2:2 + hn % 2], 0.0), writes=[mx])

    def kv_tile(hn, kt):
        A, V, wkk, wvv = Ak[hn % 2], Vh[hn % 2], wk[hn % 2], wv[hn % 2]
        pb = nxt_bank()
        for kc in range(2):
            P.op("pe", lambda e, kc=kc: e.matmul(pb[:], lhsT=wkk[:, kc * 128:(kc + 1) * 128], rhs=ckv[:, kc, kt * NT:(kt + 1) * NT],
                                                 start=(kc == 0), stop=(kc == 1)), reads=[wkk, ckv], writes=[pb])
        yield
        P.op("act", lambda e: e.copy(out=A[:, kt * NT:(kt + 1) * NT], in_=pb[:]), reads=[pb], writes=[A])
        state["sq"] += 1
        sA = sqA[state["sq"] % 2]
        P.op("act", lambda e: e.activation(out=sA[:], in_=pb[:], func=AF.Square), reads=[pb], writes=[sA])
        yield
        pn = nxt_bank()
        P.op("pe", lambda e: e.matmul(pn[:], lhsT=ones[:], rhs=sA[:], start=True, stop=True), reads=[ones, sA], writes=[pn])
        yield
        colmax(pn, 1 + hn % 2)
        pv = nxt_bank()
        for k4 in range(4):
            kb = kt * 4 + k4
            for kc in range(2):
                P.op("pe", lambda e, kc=kc, kb=kb, k4=k4: e.matmul(pv[:, k4 * 128:(k4 + 1) * 128], lhsT=ckv[:, kc, kb * 128:(kb + 1) * 128],
                                                                   rhs=wvv[:, kc * 128:(kc + 1) * 128], start=(kc == 0), stop=(kc == 1)),
                     reads=[ckv, wvv], writes=[pv])
        yield
        P.op("dve", lambda e: e.tensor_copy(out=V[:, kt * 4:(kt + 1) * 4, :].rearrange("p a b -> p (a b)"), in_=pv[:]), reads=[pv], writes=[V])

    def kv_finish(hn):
        c = 4 + hn % 2
        P.op("dve", lambda e: e.tensor_tensor(out=mx[:, c:c + 1], in0=mx[:, 1 + hn % 2:2 + hn % 2], in1=mx[:, 0:1], op=ALU.add), reads=[mx], writes=[mx])
        yield
        P.op("act", lambda e: e.activation(out=mx[:, c:c + 1], in_=mx[:, c:c + 1], func=AF.Ln), reads=[mx], writes=[mx])
        P.op("act", lambda e: e.activation(out=mx[:, c:c + 1], in_=mx[:, c:c + 1], func=AF.Exp, scale=0.5), reads=[mx], writes=[mx])
        yield
        P.op("dve", lambda e: e.tensor_scalar(out=mx[:, c:c + 1], in0=mx[:, c:c + 1], scalar1=-1.008 * SCALE, scalar2=None, op0=ALU.mult), reads=[mx], writes=[mx])
        yield

    cq_loaded = set()

    def load_cq(hn, qt):
        g = hn * nqt + qt
        if g in cq_loaded or hn >= nheads:
            return
        cq_loaded.add(g)
        P.dma("sp", cql[g % 3][:], cq_src(qt * NT, NT), reads=[lat], writes=[cql[g % 3]])

    def q_tile(hn, qt):
        g = hn * nqt + qt
        a, b, cq, wqq = aq[g % 3], bq[g % 3], cql[g % 3], wq[hn % 2]
        if g not in cq_loaded:
            load_cq(hn, qt)
        pa = nxt_bank()
        pr = nxt_bank()
        for half, pb in ((0, pa), (1, pr)):
            for kc in range(4):
                P.op("pe", lambda e, kc=kc, half=half, pb=pb: e.matmul(pb[:], lhsT=wqq[:, half, kc * 128:(kc + 1) * 128], rhs=cq[:, kc, :],
                                                                       start=(kc == 0), stop=(kc == 3)), reads=[wqq, cq], writes=[pb])
        yield
        P.op("act", lambda e: e.activation(out=a[:], in_=pa[:], func=AF.Identity, scale=SCALE), reads=[pa], writes=[a])
        state["sq"] += 1
        sA, sB = sqA[state["sq"] % 2], sqB[state["sq"] % 2]
        P.op("act", lambda e: e.activation(out=sA[:], in_=pa[:], func=AF.Square), reads=[pa], writes=[sA])
        r = rT[g % 2]
        P.op("dve", lambda e: e.tensor_tensor(out=r[:], in0=pr[:], in1=tab[:, qt * NT:(qt + 1) * NT], op=ALU.mult), reads=[pr, tab], writes=[r])
        yield
        pk = nxt_bank()
        P.op("pe", lambda e: e.matmul(pk[0:64, :], lhsT=stkb[:], rhs=r[:], start=True, stop=True), reads=[stkb, r], writes=[pk])
        yield
        P.op("act", lambda e: e.activation(out=b[0:64, :], in_=pk[0:64, :], func=AF.Identity, scale=SCALE), reads=[pk], writes=[b])
        P.op("act", lambda e: e.activation(out=sB[0:64, :], in_=pk[0:64, :], func=AF.Square), reads=[pk], writes=[sB])
        yield
        pn = nxt_bank()
        P.op("pe", lambda e: e.matmul(pn[:], lhsT=ones[:], rhs=sA[:], start=True, stop=False), reads=[ones, sA], writes=[pn])
        P.op("pe", lambda e: e.matmul(pn[:], lhsT=ones[:], rhs=sB[:], start=False, stop=True), reads=[ones, sB], writes=[pn])
        yield
        P.op("act", lambda e: e.activation(out=tq[64:65, :], in_=pn[64:65, :], func=AF.Ln), reads=[pn], writes=[tq])
        P.op("act", lambda e: e.activation(out=tq[64:65, :], in_=tq[64:65, :], func=AF.Exp, scale=0.5), reads=[tq], writes=[tq])
        yield
        P.op("dve", lambda e: e.tensor_scalar(out=b[64:65, :], in0=tq[64:65, :], scalar1=mx[64:65, 4 + hn % 2:5 + hn % 2], scalar2=None, op0=ALU.mult),
             reads=[tq, mx], writes=[b])

    if stop <= 1:
        return
    def run(gen):
        for _ in gen:
            pass

    def chain(*gens):
        for g_ in gens:
            yield from g_

    load_w(0)
    for kt in range(nqt):
        run(kv_tile(0, kt))
    run(kv_finish(0))
    if stop <= 2:
        return
    run(q_tile(0, 0))
    load_cq(0, 1)
    state["wide"] = False
    if stop <= 3 or stop >= 30:
        return

    side = []
    qgen = {}
    for h in range(nheads):
        if h + 1 < nheads:
            load_w(h + 1)
        blocks = [(qt, kb) for qt in range(nqt) for kb in range(4 * qt + 4)]

        def geom(qt, kb):
            c0 = 0 if kb < 4 * qt else (kb - 4 * qt) * 128
            return c0, NT - c0

        def emit_qk(i):
            qt, kb = blocks[i]
            c0, ncol = geom(qt, kb)
            g = h * nqt + qt
            a, b = aq[g % 3], bq[g % 3]
            A_ = Ak[h % 2]
            pss = C.ps[i % 3]
            P.op("pe", lambda e: e.matmul(pss[:, 0:ncol], lhsT=A_[:, kb * 128:(kb + 1) * 128], rhs=a[:, c0:NT], start=True, stop=False),
                 reads=[A_, a], writes=[pss])
            P.op("pe", lambda e: e.matmul(pss[:, 0:ncol], lhsT=Bk[:, kb * 128:(kb + 1) * 128], rhs=b[:, c0:NT], start=False, stop=True),
                 reads=[Bk, b], writes=[pss])

        def emit_rest(i):
            qt, kb = blocks[i]
            nkb = 4 * qt + 4
            c0, ncol = geom(qt, kb)
            pss = C.ps[i % 3]
            p = pt[i % NPT]
            po = C.ps[3 + qt % 2]
            ac = acc[qt % 2]
            V_ = Vh[h % 2]
            if kb == 0:
                P.op("pool", lambda e: e.memset(ac[0][:], 0.0), writes=[ac[0]])
                P.op("pool", lambda e: e.memset(ac[1][:], 0.0), writes=[ac[1]])
            P.op("act", lambda e: e.activation(out=p[:, 0:ncol], in_=pss[:, 0:ncol], func=AF.Exp), reads=[pss], writes=[p])
            if kb >= 4 * qt:
                P.op("pool", lambda e: e.tensor_tensor(out=p[:, 0:128], in0=p[:, 0:128], in1=tri[:], op=ALU.mult), reads=[p, tri], writes=[p])
            ai = i % 2
            P.op("dve" if ai == 0 else "pool", lambda e: e.tensor_tensor(out=ac[ai][:, c0:NT], in0=ac[ai][:, c0:NT], in1=p[:, 0:ncol], op=ALU.add),
                 reads=[p, ac[ai]], writes=[ac[ai]])
            P.op("pe", lambda e: e.matmul(po[:, c0:NT], lhsT=V_[:, kb, :], rhs=p[:, 0:ncol], start=(kb == 0), stop=(kb == nkb - 1)),
                 reads=[V_, p], writes=[po])
            if kb == nkb - 1:
                pending.append((i + 2, lambda qt=qt, po=po, ac=ac: finalize(qt, po, ac)))
            if kb == 0:
                if qt + 1 < nqt:
                    qgen[(h, qt + 1)] = q_tile(h, qt + 1)
                    side.append(qgen[(h, qt + 1)])
                    load_cq(h, qt + 2) if qt + 2 < nqt else load_cq(h + 1, 0)
                if h + 1 < nheads:
                    if qt == nqt - 1:
                        qgen[(h + 1, 0)] = chain(kv_tile(h + 1, qt), kv_finish(h + 1), q_tile(h + 1, 0))
                        side.append(qgen[(h + 1, 0)])
                        load_cq(h + 1, 1)
                    else:
                        side.append(kv_tile(h + 1, qt))

        def finalize(qt, po, ac):
            pl = C.ps[5]
            P.op("pool", lambda e: e.tensor_tensor(out=ac[1][:], in0=ac[1][:], in1=ac[0][:], op=ALU.add), reads=[ac[0], ac[1]], writes=[ac[1]])
            P.op("pe", lambda e: e.matmul(pl[:], lhsT=onesf[:], rhs=ac[1][:], start=True, stop=True), reads=[onesf, ac[1]], writes=[pl])
            P.op("act", lambda e: e.activation(out=rl[:], in_=pl[:], func=AF.Ln), reads=[pl], writes=[rl])
            P.op("act", lambda e: e.activation(out=rl[:], in_=rl[:], func=AF.Exp, scale=-1.0), reads=[rl], writes=[rl])
            o = on[qt % 2]
            P.op("dve", lambda e: e.tensor_tensor(out=o[:], in0=po[:], in1=rl[:], op=ALU.mult), reads=[po, rl], writes=[o])
            oap, oT_ = o_dst(h, qt * NT, NT)
            P.dma("sp", oap, o[:], reads=[o], writes=[oT_])
            if qt % 8 == 7:
                head_done(h, qt // 8)

        LOOK = 2
        pending = []

        def qk(j):
            qt_, kb_ = blocks[j]
            if kb_ == 0 and (h, qt_) in qgen:
                gq = qgen.pop((h, qt_))
                while side:
                    g0 = side.pop(0)
                    run(g0)
                    if g0 is gq:
                        break
            emit_qk(j)

        for i in range(min(LOOK, len(blocks))):
            qk(i)
        for i in range(len(blocks)):
            if i + LOOK < len(blocks):
                qk(i + LOOK)
            emit_rest(i)
            while pending and (pending[0][0] <= i or i == len(blocks) - 1):
                pending.pop(0)[1]()
            if side:
                try:
                    next(side[0])
                except StopIteration:
                    side.pop(0)
        for g_ in side:
            run(g_)
        side.clear()


def phase4(P, C, t):
    x5i = t["x5i"]
    load_o = t["load_o"]
    pin = t["pin"]
    cons = t["cons"]
    wbo = t["wbo"]
    win = t["win"]
    wout = t["wout"]
    wpg = t["wpg"]
    wpi = t["wpi"]
    yo = t["yo"]
    ob = P.sbuf("ob", [128, NH_B, G], BF16)
    pbf = P.sbuf("pbf", [128, 2, G], BF16)
    yf = P.sbuf("yf", [128, KC, NT], F32)
    P.dma("sp", C.cons[:, 0:24], cons[:], reads=[cons], writes=[C.cons])
    P.op("dve", lambda e: e.tensor_scalar(out=C.gs[:, 0:24], in0=C.cons[:, 0:24], scalar1=float(np.sqrt(D)), scalar2=None, op0=ALU.mult),
         reads=[C.cons], writes=[C.gs])
    nt = G // NT
    for g in range(TOK // G):
        P.dma("sp", C.x[:], x5i[:, :, g * G:(g + 1) * G], reads=[x5i], writes=[C.x])
        load_o(g, ob)
        i = 0
        for m in range(KC):
            ws = None
            for t in range(nt):
                pb = C.ps[i % 2]
                i += 1
                ws = lin_fm(C, wbo, m, NH_B, t * NT, NT, pb, wslot=ws, src=ob)
                P.op("dve", lambda e, pb=pb, m=m, t=t: e.tensor_tensor(out=C.x[:, m, t * NT:(t + 1) * NT], in0=C.x[:, m, t * NT:(t + 1) * NT], in1=pb[:], op=ALU.add),
                     reads=[pb, C.x], writes=[C.x])
        ffn_group(C, win, wout, 0)
        ple_group(C, wpg, wpi, pin, g, 8, pbf)
        for t in range(nt):
            rmsnorm_fm(C, lambda kc, t=t: C.x[:, kc, t * NT:(t + 1) * NT], C.x, KC, t * NT, NT,
                       lambda kc: C.gs[:, 16 + kc:17 + kc], lambda kc: yf[:, kc, :], yf, C.ps[7], D, src_all=C.x[:, :, t * NT:(t + 1) * NT])
            P.dma("sp", yo[:, :, g * G + t * NT:g * G + (t + 1) * NT], yf[:], reads=[yf], writes=[yo])


def xfm(x):
    return np.ascontiguousarray(np.ascontiguousarray(x.T).reshape(-1, 128, x.shape[0]).transpose(1, 0, 2))


def unfm(a):
    a = np.asarray(a)
    return np.ascontiguousarray(a.transpose(1, 0, 2).reshape(-1, a.shape[2]).T)


def _rope_consts():
    invf = (1.0 / (np.float32(10000.0) ** (np.arange(0, ROPE, 2, dtype=np.float32) / np.float32(ROPE)))).astype(np.float32)
    invf4 = np.tile(invf, 4)
    ph = np.concatenate([np.full(64, np.pi / 2), np.full(32, np.pi), np.zeros(32)]).astype(np.float32)
    return invf4, ph


def _hmask():
    s = np.arange(128)[:, None]
    t = np.arange(128)[None, :]
    m = ((s // 64 == t // 64) & (s <= t)).astype(np.float32)
    return np.ascontiguousarray(np.tile(m, (1, 4)))


_NC_CACHE = {}
_PREP_ONLY = False

W_SHAPES = {
    "ffn00_in": [NJ, 128, KC * 256], "ffn00_out": [KC, 128, NJ * 128], "ffn01_in": [NJ, 128, KC * 256], "ffn01_out": [KC, 128, NJ * 128],
    "ffn10_in": [NJ, 128, KC * 256], "ffn10_out": [KC, 128, NJ * 128], "ffn11_in": [NJ, 128, KC * 256], "ffn11_out": [KC, 128, NJ * 128],
    "wq": [8, 128, D], "wf": [8, 128, D], "wg": [8, 128, D], "wv": [128, KC * D], "wao": [8, 128, D],
    "wpg0": [8, 128, D], "wpi0": [8, 128, 256], "wkd": [3, 128, D], "wdq": [4, 128, D],
    "wuq": [HPC * 2, 128, QL], "wuk": [HPC, 128, KVL], "wuv": [HPC, 128, KVL],
    "wbo": [8, 128, NH_B * 128], "wpg1": [8, 128, D], "wpi1": [8, 128, 256],
    "cons1": [128, 32], "cons2": [128, NC2], "cons3": [128, 8], "cons4": [128, 24],
    "ident": [128, 128], "hmask": [128, 512], "stk": [128, 64], "tri": [128, 128],
    "xin": [128, KC, TOK], "p0": [128, 2, TOK], "p1": [128, 2, TOK],
}


def build_fused(upto=4):
    P = Prog()
    nc = P.nc
    E = {k: P.dram(k, v, F32, "ExternalInput") for k, v in W_SHAPES.items()}
    E["posl"] = P.dram("posl", [1, TOK], I32, "ExternalInput")
    E["posb"] = P.dram("posb", [1, SEQ], I32, "ExternalInput")
    E["idxo"] = P.dram("idxo", [128, 8], I32, "ExternalInput")
    yo = P.dram("yo", [128, KC, TOK], F32, "ExternalOutput")

    def internal(name, shape, dtype):
        return T(nc.dram_tensor(name, list(shape), dtype), name)

    x1 = internal("x1", [128, KC, TOK], F32)
    oT = internal("oT", [NH_A, 128, TOK], F32)
    qc = internal("qc", [NH_A, 128, TOK], BF16)
    sg = internal("sg", [128, KC, TOK], F32)
    SDin = internal("SDin", [128, 1032], F32)
    SDall = internal("SDall", [512, 1032], F32)
    x5 = internal("x5", [128, KC, TOK], F32)
    NTL = TOK // NT
    latin = [internal("latin%d" % i, [128, 7 * NT], BF16) for i in range(NTL)]
    latall = [internal("latall%d" % i, [512, 7 * NT], BF16) for i in range(NTL)]
    latdep = Buf("latall")
    for q in latall:
        q.b = latdep
    ohin = [[internal("ohin%d_%d" % (h, hf), [128, SEQ // 2], BF16) for hf in range(2)] for h in range(HPC)]
    ohall = [internal("ohall%d" % h, [1024, SEQ // 2], BF16) for h in range(HPC)]

    m0 = P.mark()
    C = Ctx(P)
    mC = P.mark()
    STo = T(SDin.t[:, 0:1024].rearrange("p (h d) -> p h d", h=NH_A), "STo")
    DTo = T(SDin.t[:, 1024:1032], "DTo")
    phase1(P, C, {"xin": E["xin"], "win": E["ffn00_in"], "wout": E["ffn00_out"], "wq": E["wq"], "wf": E["wf"], "wg": E["wg"], "wv": E["wv"],
                  "cons": E["cons1"], "ident": E["ident"], "hmask": E["hmask"], "x1o": x1, "oTo": oT, "qco": qc, "sgo": sg, "STo": STo, "DTo": DTo})
    P.reset(mC)
    P.coll("AllGather", SDin.t.ap().opt(), SDall.t.ap().opt(), reads=[STo, DTo], writes=[SDall])
    if upto == 1:
        P.barrier()
        for g in range(TOK // G):
            tmpx = P.sbuf("tmpx1", [128, KC, G], F32)
            P.dma("sp", tmpx[:], x1[:, :, g * G:(g + 1) * G], reads=[x1], writes=[tmpx])
            P.dma("sp", yo[:, :, g * G:(g + 1) * G], tmpx[:], reads=[tmpx], writes=[yo])
        return P.finish()
    STi = TV(lambda r: SDall.t[r * 128:(r + 1) * 128, 0:1024].rearrange("p (h d) -> p h d", h=NH_A), "STi")
    STi.b = SDall.b
    DTi = TV(lambda r: SDall.t[r * 128:(r + 1) * 128, 1024:1032], "DTi")
    DTi.b = SDall.b
    def lat_dst(i):
        v = latin[i].t[:, :].rearrange("p (c t) -> p c t", c=7)
        return v[:, 0:4, :], v[:, 4:6, :], v[0:64, 6, :], latin[i]

    def lat_done(i):
        P.coll("AllGather", latin[i].t.ap().opt(), latall[i].t.ap().opt(), reads=[latin[i]], writes=[latall[i]])

    phase2(P, C, {"x1i": x1, "oTi": oT, "qci": qc, "sgi": sg, "STi": STi, "DTi": DTi, "pin": E["p0"], "pos": E["posl"], "cons": E["cons2"],
                  "stk": E["stk"], "wao": E["wao"], "win0": E["ffn01_in"], "wout0": E["ffn01_out"], "wpg": E["wpg0"], "wpi": E["wpi0"],
                  "wkd": E["wkd"], "win1": E["ffn10_in"], "wout1": E["ffn10_out"], "wdq": E["wdq"],
                  "x5o": x5, "lat_dst": lat_dst, "lat_done": lat_done})
    P.reset(m0)
    if upto == 2:
        P.barrier()
        for g in range(TOK // G):
            tmpx = P.sbuf("tmpx2", [128, KC, G], F32)
            P.dma("sp", tmpx[:], x5[:, :, g * G:(g + 1) * G], reads=[x5], writes=[tmpx])
            P.dma("sp", yo[:, :, g * G:(g + 1) * G], tmpx[:], reads=[tmpx], writes=[yo])
        return P.finish()
    def latv(q):
        rr, i = q // NTL, q % NTL
        return latall[i].t[rr * 128:(rr + 1) * 128, :].rearrange("p (c t) -> p c t", c=7)

    def head_done(h, hf):
        P.coll("AllGather", ohin[h][hf].t.ap().opt(), ohall[h].t[hf * 512:(hf + 1) * 512, :], reads=[ohin[h][hf]], writes=[ohall[h]])

    def o_dst(h, t0, n):
        hf, tl = t0 // (SEQ // 2), t0 % (SEQ // 2)
        return ohin[h][hf].t[:, tl:tl + n], ohin[h][hf]

    phase3(P, None, {"lat": latall[0], "cq_src": lambda t0, n: latv(t0 // NT)[:, 0:4, :],
                     "ckv_src": lambda kc, q: latv(q)[:, 4 + kc, :], "kr_src": lambda q: latv(q)[0:64, 6, :],
                     "o_dst": o_dst, "head_done": head_done,
                     "pos": E["posb"], "cons": E["cons3"], "stk": E["stk"], "tri": E["tri"], "wuq": E["wuq"], "wuk": E["wuk"], "wuv": E["wuv"]})
    P.reset(m0)
    if upto == 3:
        P.barrier()
        for g in range(TOK // G):
            tmpx = P.sbuf("tmpx3", [128, KC, G], F32)
            P.dma("sp", tmpx[:], x5[:, :, g * G:(g + 1) * G], reads=[x5], writes=[tmpx])
            P.dma("sp", yo[:, :, g * G:(g + 1) * G], tmpx[:], reads=[tmpx], writes=[yo])
        return P.finish()
    C = Ctx(P)
    idxs = P.sbuf("idxs", [128, 8], I32)
    P.dma("sp", idxs[:], E["idxo"][:], reads=[E["idxo"]], writes=[idxs])

    def load_o(g, ob):
        for h in range(HPC):
            rows = ohall[h].t[:, :].rearrange("r (b t) -> (r b) t", t=G)
            for rr in range(4):
                P.gather(ob[:, rr * HPC + h, :], rows, idxs[:, rr * 2 + g:rr * 2 + g + 1], reads=[ohall[h], idxs], writes=[ob])

    phase4(P, C, {"x5i": x5, "load_o": load_o, "pin": E["p1"], "cons": E["cons4"], "wbo": E["wbo"], "win": E["ffn11_in"], "wout": E["ffn11_out"],
                  "wpg": E["wpg1"], "wpi": E["wpi1"], "yo": yo})
    print("fused instructions", P.nins)
    return P.finish()


def _get(name, fn):
    if name not in _NC_CACHE:
        _NC_CACHE[name] = fn()
    return _NC_CACHE[name]


def kernel(x, p, positions, norm_gains, ffn_w_in, ffn_w_out, ple_w_gate, ple_w_in,
           a_w_in, a_lb_logits, a_out_gain, a_w_out,
           kv_norm_in, kv_w_down, kv_latent_norm, kv_w_up,
           b_w_dq, b_q_norm, b_w_uq, b_w_out, final_norm):
    f32 = lambda a: np.ascontiguousarray(np.asarray(a, dtype=np.float32))
    x, p, norm_gains, ffn_w_in, ffn_w_out = f32(x), f32(p), f32(norm_gains), f32(ffn_w_in), f32(ffn_w_out)
    ple_w_gate, ple_w_in, a_w_in, a_lb_logits, a_out_gain, a_w_out = map(f32, (ple_w_gate, ple_w_in, a_w_in, a_lb_logits, a_out_gain, a_w_out))
    kv_norm_in, kv_w_down, kv_latent_norm, kv_w_up = map(f32, (kv_norm_in, kv_w_down, kv_latent_norm, kv_w_up))
    b_w_dq, b_q_norm, b_w_uq, b_w_out, final_norm = map(f32, (b_w_dq, b_q_norm, b_w_uq, b_w_out, final_norm))
    positions = np.ascontiguousarray(np.asarray(positions, dtype=np.int32))
    cores = list(range(NCORES))
    invf4, ph = _rope_consts()
    g = norm_gains
    aw = a_w_in[0]
    wd = kv_w_down
    wkd = np.concatenate([wd[:, 0:256], wd[:, 256:320], wd[:, 288:320], wd[:, 256:288]], axis=1)
    cons3 = np.zeros((128, 8), np.float32)
    cons3[:, 0] = invf4
    cons3[:, 1] = ph
    kk = np.arange(128)[:, None]
    qq = np.arange(128)[None, :]
    shared = {
        "ffn00_in": tile_ffn_in(ffn_w_in[0, 0]), "ffn00_out": tile_ffn_out(ffn_w_out[0, 0]),
        "ffn01_in": tile_ffn_in(ffn_w_in[0, 1]), "ffn01_out": tile_ffn_out(ffn_w_out[0, 1]),
        "ffn10_in": tile_ffn_in(ffn_w_in[1, 0]), "ffn10_out": tile_ffn_out(ffn_w_out[1, 0]),
        "ffn11_in": tile_ffn_in(ffn_w_in[1, 1]), "ffn11_out": tile_ffn_out(ffn_w_out[1, 1]),
        "wq": tile_lin(aw[:, 0:1024]), "wf": tile_lin(aw[:, 1024:2048]), "wg": tile_lin(aw[:, 3072:4096]),
        "wv": tile_rows(aw[:, 2048:3072]).reshape(128, -1), "wao": tile_lin(a_w_out[0]),
        "wpg0": tile_lin(ple_w_gate[0]), "wpi0": tile_lin(ple_w_in[0]), "wkd": tile_lin(wkd), "wdq": tile_lin(b_w_dq[0]),
        "wbo": tile_lin(b_w_out[0]), "wpg1": tile_lin(ple_w_gate[1]), "wpi1": tile_lin(ple_w_in[1]),
        "cons1": np.ascontiguousarray(np.concatenate([col(g[0, 0]), col(g[0, 1]), col(a_lb_logits[0]), col(a_lb_logits[1])], axis=1)),
        "cons3": cons3,
        "cons4": np.ascontiguousarray(np.concatenate([col(g[1, 2]), col(g[1, 3]), col(final_norm)], axis=1)),
        "ident": np.eye(128, dtype=np.float32), "hmask": _hmask(),
        "stk": np.concatenate([np.eye(64), np.eye(64)], axis=0).astype(np.float32), "tri": (kk <= qq).astype(np.float32),
    }
    uq = b_w_uq[0]
    per_r = []
    for r in range(4):
        wuq, wuk, wuv = [], [], []
        for h in range(4 * r, 4 * r + 4):
            q = uq[:, h * 192:(h + 1) * 192]
            wuq.append(tile_lin(q[:, 0:128])[0])
            wuq.append(tile_lin(np.concatenate([q[:, 128:192], q[:, 160:192], q[:, 128:160]], axis=1))[0])
            kv = kv_w_up[:, h * 256:(h + 1) * 256]
            wuk.append(tile_lin(kv[:, 0:128])[0])
            wuv.append(tile_rows(kv[:, 128:256]).reshape(128, 256))
        wsel = np.zeros((128, 4), np.float32)
        wsel[:, :r] = 1.0
        cons2 = np.concatenate([col(g[0, 2]), col(g[0, 3]), col(kv_norm_in), col(g[1, 0]), col(g[1, 1]), col(kv_latent_norm),
                                col(b_q_norm[0]), col(a_out_gain[0]), invf4[:, None], ph[:, None], wsel,
                                np.zeros((128, NC2 - 53), np.float32)], axis=1).astype(np.float32)
        idxo = np.zeros((128, 8), np.int32)
        pp = np.arange(128)
        for rr in range(4):
            for gg in range(2):
                tb = r * 2 + gg
                idxo[:, rr * 2 + gg] = ((tb // 4) * 512 + rr * 128 + pp) * 4 + tb % 4
        per_r.append({"wuq": np.stack(wuq), "wuk": np.stack(wuk), "wuv": np.stack(wuv), "cons2": np.ascontiguousarray(cons2), "idxo": idxo})
    in_maps = []
    for c in cores:
        b, r = c // 4, c % 4
        sl = slice(r * TOK, (r + 1) * TOK)
        in_maps.append({"xin": xfm(x[b, sl]), "p0": xfm(p[0, b, sl]), "p1": xfm(p[1, b, sl]),
                        "posl": np.ascontiguousarray(positions[b:b + 1, sl]), "posb": np.ascontiguousarray(positions[b:b + 1]),
                        **shared, **per_r[r]})
    if _PREP_ONLY:
        return in_maps
    res = run_bass_kernel_spmd(_get("fused", build_fused), in_maps, core_ids=cores).results
    out = np.empty((2, SEQ, D), np.float32)
    for c in cores:
        b, r = c // 4, c % 4
        out[b, r * TOK:(r + 1) * TOK] = unfm(res[c]["yo"])
    return out
```

```python
import contextlib
import numpy as np
import ml_dtypes
import concourse.bass as bass
import concourse.mybir as mybir
from concourse.bass_utils import run_bass_kernel_spmd

F32 = mybir.dt.float32
BF16 = mybir.dt.bfloat16
I32 = mybir.dt.int32
AF = mybir.ActivationFunctionType
ALU = mybir.AluOpType

NDSEM = 8
NCORES = 8
D = 1024
KC = 8
DFF = 2816
NJ = 22
SEQ = 8192
TOK = 2048
G = 1024
NT = 512
EPS = 1e-6
NH_A = 8
CH = 64
PLE = 256
NH_B = 16
QL = 512
KVL = 256
ROPE = 64


class Buf:
    __slots__ = ("w", "r", "name", "relaxed", "wl", "pr")

    def __init__(self, name=""):
        self.w = None
        self.r = []
        self.name = name
        self.relaxed = False
        self.wl = {}
        self.pr = []


class T:
    def __init__(self, t, name):
        self.t = t
        self.b = Buf(name)

    def __getitem__(self, idx):
        return self.t[idx]


class TV(T):
    def __init__(self, fn, name):
        self.fn = fn
        self.b = Buf(name)

    def __getitem__(self, idx):
        return self.fn(idx)


def _b(x):
    return x.b if isinstance(x, T) else x


class Prog:
    def __init__(self):
        self.nc = bass.Bass("TRN2", target_bir_lowering=False)
        self.es = contextlib.ExitStack()
        nc = self.nc
        self.eng = {"pe": nc.tensor, "act": nc.scalar, "dve": nc.vector, "pool": nc.gpsimd, "sp": nc.sync}
        self.q = {k: [] for k in self.eng}
        self.cnt = {k: 0 for k in self.eng}
        self.seen = {k: {} for k in self.eng}
        self.esem = {}
        for k in ("pe", "act", "dve", "pool"):
            self.esem[k] = self.es.enter_context(nc.semaphore("es_" + k))
        self.dsem = {}
        self.dcnt = {}
        self.dtok = {}
        for k in ("sp", "act", "pool"):
            self.dsem[k] = [self.es.enter_context(nc.semaphore("ds_%s%d" % (k, i))) for i in range(NDSEM)]
            self.dcnt[k] = 0
            self.dtok[k] = [None] * NDSEM
        self.same_engine_sync = True
        self.nins = 0
        self.defer = None
        self.ccsem = self.es.enter_context(nc.semaphore("ccsem"))
        self.ccn = 0
        self.ARENA = 103936
        self.arena = self.es.enter_context(nc.sbuf_tensor("arena", [128, self.ARENA], BF16))
        self.atop = 0
        self.ps = [self.psum("ps%d" % i, [128, NT], F32) for i in range(8)]

    def dram(self, name, shape, dtype, kind):
        return T(self.nc.dram_tensor(name, list(shape), dtype, kind=kind), name)

    def sbuf(self, name, shape, dtype):
        shape = list(shape)
        n = int(np.prod(shape[1:]))
        nb = n * (4 if dtype in (F32, I32) else 2)
        nb = (nb + 63) // 64 * 64
        assert self.atop + nb <= self.ARENA * 2, "arena overflow: %s needs %d, top %d" % (name, nb, self.atop)
        ap = self.arena[:, self.atop // 2:(self.atop + nb) // 2]
        self.atop += nb
        self.hiwater = max(getattr(self, "hiwater", 0), self.atop)
        if dtype != BF16:
            ap = ap.bitcast(dtype)
        ap = ap[:, 0:n]
        if len(shape) == 3:
            ap = ap.rearrange("p (a b) -> p a b", b=shape[2])
        return T(ap, name)

    def mark(self):
        return self.atop

    def reset(self, m):
        self.barrier()
        print("arena high-water %d of %d bytes" % (getattr(self, "hiwater", 0), self.ARENA * 2))
        self.hiwater = 0
        self.atop = m

    def coll(self, kind, src, dst, reads, writes):
        waits = self._filter("pool", self._deps(reads, writes))
        self.ccn += 1
        tok = (self.ccsem, self.ccn)

        def run(engh, waits=waits, src=src, dst=dst, kind=kind):
            for (s_, v) in waits:
                engh.wait_ge(s_, v)
            engh.collective_compute(kind, ALU.bypass, replica_groups=[[0, 1, 2, 3], [4, 5, 6, 7]], ins=[src], outs=[dst]).then_inc(self.ccsem, 1)

        self.q["pool"].append(run)
        self._commit(tok, reads, writes)
        self.nins += 1
        return tok

    def gather(self, out, in_, idx, reads=(), writes=()):
        qn = "pool"
        n = self.dcnt[qn]
        slot = n % NDSEM
        sem = self.dsem[qn][slot]
        val = 16 * (n // NDSEM + 1)
        deps = self._deps(reads, writes)
        if self.dtok[qn][slot] is not None:
            deps.append(self.dtok[qn][slot])
        waits = self._filter(qn, deps)
        tok = (sem, val)
        self.dtok[qn][slot] = tok
        self.dcnt[qn] += 1

        def run(engh, waits=waits, out=out, in_=in_, idx=idx, sem=sem):
            for (s_, v) in waits:
                engh.wait_ge(s_, v)
            engh.indirect_dma_start(out=out, out_offset=None, in_=in_, in_offset=bass.IndirectOffsetOnAxis(ap=idx, axis=0)).then_inc(sem, 16)

        self.q[qn].append(run)
        self._commit(tok, reads, writes)
        self.nins += 1
        return tok

    def psum(self, name, shape, dtype):
        return T(self.es.enter_context(self.nc.psum_tensor(name, list(shape), dtype)), name)

    def _deps(self, reads, writes, e=None):
        deps = []
        own = self.esem.get(e)

        def add(b, tok):
            if b.relaxed and own is not None and tok[0] is own:
                return
            deps.append(tok)

        for b in reads:
            b = _b(b)
            if b.relaxed:
                for tk in b.wl.values():
                    add(b, tk)
            elif b.w is not None:
                add(b, b.w)
        for b in writes:
            b = _b(b)
            if b.relaxed:
                for tk in (b.r if b.r else b.pr):
                    add(b, tk)
                continue
            if b.w is not None:
                add(b, b.w)
            for tk in b.r:
                add(b, tk)
        return deps

    def _filter(self, e, deps):
        seen = self.seen[e]
        best = {}
        for (s, v) in deps:
            if e == "pe" and s is self.esem["pe"]:
                continue
            if (not self.same_engine_sync) and e in self.esem and s is self.esem[e]:
                continue
            k = id(s)
            if seen.get(k, 0) >= v:
                continue
            if k not in best or best[k][1] < v:
                best[k] = (s, v)
        for k, (s, v) in best.items():
            seen[k] = v
        return list(best.values())

    def _commit(self, tok, reads, writes):
        for b in reads:
            _b(b).r.append(tok)
        for b in writes:
            b = _b(b)
            b.w = tok
            if b.relaxed:
                if b.r:
                    b.wl = {}
                    b.pr = b.r
                k = id(tok[0])
                if k not in b.wl or b.wl[k][1] < tok[1]:
                    b.wl[k] = tok
            b.r = []

    def op(self, e, fn, reads=(), writes=()):
        if self.defer is not None:
            self.defer.append(lambda: self._op(e, fn, reads, writes))
            return None
        return self._op(e, fn, reads, writes)

    def _op(self, e, fn, reads=(), writes=()):
        waits = self._filter(e, self._deps(reads, writes, e))
        sem = self.esem[e]
        self.cnt[e] += 1
        tok = (sem, self.cnt[e])

        def run(engh, waits=waits, fn=fn, sem=sem):
            for (s, v) in waits:
                engh.wait_ge(s, v)
            fn(engh).then_inc(sem, 1)

        self.q[e].append(run)
        self._commit(tok, reads, writes)
        self.nins += 1
        return tok

    def dma(self, qn, out, in_, reads=(), writes=(), **kw):
        if self.defer is not None:
            self.defer.append(lambda: self._dma(qn, out, in_, reads, writes, **kw))
            return None
        return self._dma(qn, out, in_, reads, writes, **kw)

    def run_streams(self, streams):
        idx = [0] * len(streams)
        live = True
        while live:
            live = False
            for i, st in enumerate(streams):
                if idx[i] < len(st):
                    st[idx[i]]()
                    idx[i] += 1
                    live = True

    def _dma(self, qn, out, in_, reads=(), writes=(), **kw):
        n = self.dcnt[qn]
        slot = n % NDSEM
        sem = self.dsem[qn][slot]
        val = 16 * (n // NDSEM + 1)
        deps = self._deps(reads, writes)
        if self.dtok[qn][slot] is not None:
            deps.append(self.dtok[qn][slot])
        waits = self._filter(qn, deps)
        tok = (sem, val)
        self.dtok[qn][slot] = tok
        self.dcnt[qn] += 1

        def run(engh, waits=waits, out=out, in_=in_, sem=sem, kw=kw):
            for (s, v) in waits:
                engh.wait_ge(s, v)
            engh.dma_start(out=out, in_=in_, **kw).then_inc(sem, 16)

        self.q[qn].append(run)
        self._commit(tok, reads, writes)
        self.nins += 1
        return tok

    def _alltoks(self):
        deps = []
        for qn in self.dtok:
            deps.extend([t for t in self.dtok[qn] if t is not None])
        for e in self.esem:
            if self.cnt[e]:
                deps.append((self.esem[e], self.cnt[e]))
        return deps

    def barrier(self):
        deps = self._alltoks()
        for e in ("pe", "act", "dve", "pool", "sp"):
            waits = self._filter(e, deps)
            if waits:
                def run(engh, waits=waits):
                    for (s, v) in waits:
                        engh.wait_ge(s, v)
                self.q[e].append(run)

    def finish(self):
        waits = self._filter("sp", self._alltoks())

        def run(engh, waits=waits):
            for (s, v) in waits:
                engh.wait_ge(s, v)

        self.q["sp"].append(run)
        nc = self.nc
        with nc.Block() as block:
            for k, name in (("sp", "sync"), ("pe", "tensor"), ("act", "scalar"), ("dve", "vector"), ("pool", "gpsimd")):
                if not self.q[k]:
                    continue

                def body(engh, fns=self.q[k]):
                    for f in fns:
                        f(engh)

                getattr(block, name)(body)
        self.es.close()
        return nc


class Ctx:
    def __init__(self, P, ncons=56):
        self.P = P
        self.ps = P.ps
        self.cons = P.sbuf("cons", [128, ncons], F32)
        self.gs = P.sbuf("gs", [128, ncons], F32)
        self.ones = P.sbuf("ones", [128, 128], BF16)
        self.x = P.sbuf("x", [128, KC, G], F32)
        self.h = P.sbuf("h", [128, KC, G], BF16)
        self.sq = P.sbuf("sq", [128, KC, NT], BF16)
        self.rstd = P.sbuf("rstd", [128, NT], F32)
        self.rstd2 = P.sbuf("rstd2", [128, NT], F32)
        self.su = [P.sbuf("su%d" % i, [128, NT], F32) for i in range(2)]
        self.act = P.sbuf("act", [128, NJ, G], BF16)
        self.wsl = [P.sbuf("wsl%d" % i, [128, 3072], BF16) for i in range(4)]
        self.wi = 0
        self.sg = [P.sbuf("sg%d" % i, [128, NT], F32) for i in range(2)]
        P.op("pool", lambda e: e.memset(self.ones[:], 1.0), writes=[self.ones])
        for tt in (self.x, self.h, self.act, self.sq):
            tt.b.relaxed = True

    def wslot(self):
        s = self.wsl[self.wi % 4]
        self.wi += 1
        return s


def rmsnorm_fm(C, src, srcT, kcn, n0, n, gcol, dst, dstT, pb, dn, src_all=None):
    P = C.P
    sq = C.sq
    if src_all is not None:
        P.op("act", lambda e: e.activation(out=sq[:, 0:kcn, 0:n], in_=src_all, func=AF.Square), reads=[srcT], writes=[sq])
    else:
        for kc in range(kcn):
            P.op("act", lambda e, kc=kc: e.activation(out=sq[:, kc, 0:n], in_=src(kc), func=AF.Square),
                 reads=[srcT], writes=[sq])
    for kc in range(kcn):
        P.op("pe", lambda e, kc=kc: e.matmul(pb[:, 0:n], lhsT=C.ones[:], rhs=sq[:, kc, 0:n],
                                             start=(kc == 0), stop=(kc == kcn - 1)),
             reads=[C.ones, sq], writes=[pb])
    P.op("act", lambda e: e.activation(out=C.rstd[:, 0:n], in_=pb[:, 0:n], func=AF.Ln, bias=float(dn * EPS)),
         reads=[pb], writes=[C.rstd])
    P.op("act", lambda e: e.activation(out=C.rstd[:, 0:n], in_=C.rstd[:, 0:n], func=AF.Exp, scale=-0.5),
         reads=[C.rstd], writes=[C.rstd])
    for kc in range(kcn):
        if False and kcn == KC and kc % 2 == 1:
            tmp = C.sg[(kc // 2) % 2]
            P.op("act", lambda e, kc=kc, tmp=tmp: e.activation(out=tmp[:, 0:n], in_=src(kc), func=AF.Identity, scale=gcol(kc)),
                 reads=[srcT, C.gs], writes=[tmp])
            P.op("pool", lambda e, kc=kc, tmp=tmp: e.tensor_tensor(out=dst(kc), in0=tmp[:, 0:n], in1=C.rstd[:, 0:n], op=ALU.mult),
                 reads=[tmp, C.rstd], writes=[dstT])
        else:
            P.op("dve", lambda e, kc=kc: e.scalar_tensor_tensor(out=dst(kc), in0=src(kc), scalar=gcol(kc),
                                                                in1=C.rstd[:, 0:n], op0=ALU.mult, op1=ALU.mult),
                 reads=[srcT, C.gs, C.rstd], writes=[dstT])


def ffn_group(C, win, wout, gcol0, nt=G // NT):
    P = C.P
    rstds = [C.rstd, C.rstd2]
    for t in range(nt):
        sl = slice(t * NT, (t + 1) * NT)
        for kc in range(KC):
            if kc % 2 == 0:
                P.op("dve", lambda e, kc=kc, sl=sl: e.tensor_scalar(out=C.h[:, kc, sl], in0=C.x[:, kc, sl], scalar1=C.gs[:, gcol0 + kc:gcol0 + kc + 1],
                                                                   scalar2=None, op0=ALU.mult), reads=[C.x, C.gs], writes=[C.h])
            else:
                P.op("act", lambda e, kc=kc, sl=sl: e.activation(out=C.h[:, kc, sl], in_=C.x[:, kc, sl], func=AF.Identity,
                                                                 scale=C.gs[:, gcol0 + kc:gcol0 + kc + 1]), reads=[C.x, C.gs], writes=[C.h])
    for t in range(nt):
        sl = slice(t * NT, (t + 1) * NT)
        rs = rstds[t % 2]
        pb = C.ps[6 + t % 2]
        P.op("act", lambda e, sl=sl: e.activation(out=C.sq[:, :, 0:NT], in_=C.x[:, :, sl], func=AF.Square), reads=[C.x], writes=[C.sq])
        for kc in range(KC):
            P.op("pe", lambda e, kc=kc, pb=pb: e.matmul(pb[:], lhsT=C.ones[:], rhs=C.sq[:, kc, 0:NT], start=(kc == 0), stop=(kc == KC - 1)),
                 reads=[C.ones, C.sq], writes=[pb])
        P.op("act", lambda e, pb=pb, rs=rs: e.activation(out=rs[:], in_=pb[:], func=AF.Ln, bias=float(D * EPS)), reads=[pb], writes=[rs])
        P.op("act", lambda e, rs=rs: e.activation(out=rs[:], in_=rs[:], func=AF.Exp, scale=-0.5), reads=[rs], writes=[rs])
    pi = 0
    for j in range(NJ):
        ws = C.wslot()
        P.dma("pool", ws[:, 0:2048], win[j], reads=[win], writes=[ws])
        for t in range(nt):
            pg = C.ps[(pi % 2) * 2]
            pu = C.ps[(pi % 2) * 2 + 1]
            sg = C.sg[pi % 2]
            su = C.su[pi % 2]
            rs = rstds[t % 2]
            pi += 1
            for half, pb in ((0, pg), (1, pu)):
                for kc in range(KC):
                    P.op("pe", lambda e, kc=kc, half=half, pb=pb, ws=ws, t=t: e.matmul(
                        pb[:], lhsT=ws[:, kc * 256 + half * 128:kc * 256 + half * 128 + 128],
                        rhs=C.h[:, kc, t * NT:(t + 1) * NT], start=(kc == 0), stop=(kc == KC - 1)),
                        reads=[ws, C.h], writes=[pb])
            P.op("dve", lambda e, pg=pg, sg=sg, rs=rs: e.tensor_tensor(out=sg[:], in0=pg[:], in1=rs[:], op=ALU.mult), reads=[pg, rs], writes=[sg])
            P.op("act", lambda e, sg=sg: e.activation(out=sg[:], in_=sg[:], func=AF.Silu), reads=[sg], writes=[sg])
            P.op("dve", lambda e, pu=pu, su=su, rs=rs: e.tensor_tensor(out=su[:], in0=pu[:], in1=rs[:], op=ALU.mult), reads=[pu, rs], writes=[su])
            P.op("dve", lambda e, su=su, sg=sg, j=j, t=t: e.tensor_tensor(
                out=C.act[:, j, t * NT:(t + 1) * NT], in0=sg[:], in1=su[:], op=ALU.mult),
                reads=[sg, su], writes=[C.act])
    for m in range(KC):
        ws = C.wslot()
        P.dma("pool", ws[:, 0:NJ * 128], wout[m], reads=[wout], writes=[ws])
        for t in range(nt):
            pb = C.ps[4 + (pi % 2)]
            pi += 1
            for j in range(NJ):
                P.op("pe", lambda e, j=j, pb=pb, ws=ws, t=t: e.matmul(
                    pb[:], lhsT=ws[:, j * 128:(j + 1) * 128], rhs=C.act[:, j, t * NT:(t + 1) * NT],
                    start=(j == 0), stop=(j == NJ - 1)), reads=[ws, C.act], writes=[pb])
            P.op("dve", lambda e, pb=pb, m=m, t=t: e.scalar_tensor_tensor(
                out=C.x[:, m, t * NT:(t + 1) * NT], in0=pb[:], scalar=0.5, in1=C.x[:, m, t * NT:(t + 1) * NT],
                op0=ALU.mult, op1=ALU.add), reads=[pb, C.x], writes=[C.x])


def tile_ffn_in(w):
    wg = w[:, :DFF].reshape(KC, 128, NJ, 128)
    wu = w[:, DFF:].reshape(KC, 128, NJ, 128)
    s = np.stack([wg, wu], axis=3)
    return np.ascontiguousarray(s.transpose(2, 1, 0, 3, 4).reshape(NJ, 128, KC * 256))


def tile_ffn_out(w):
    return np.ascontiguousarray(w.reshape(NJ, 128, KC, 128).transpose(2, 1, 0, 3).reshape(KC, 128, NJ * 128))


def tile_lin(w):
    k, n = w.shape
    return np.ascontiguousarray(w.reshape(k // 128, 128, n // 128, 128).transpose(2, 1, 0, 3).reshape(n // 128, 128, k))


def tile_rows(w):
    k, n = w.shape
    return np.ascontiguousarray(w.reshape(k // 128, 128, n).transpose(1, 0, 2))


def col(v):
    return np.ascontiguousarray(v.reshape(-1, 128).T)


def fm(xtok):
    return np.ascontiguousarray(xtok.T)


def lin_fm(C, wdram, chunk, kcn, t0, n, pb, wslot=None, src=None):
    P = C.P
    src = C.h if src is None else src
    if wslot is None:
        wslot = C.wslot()
        P.dma("pool", wslot[:, 0:kcn * 128], wdram[chunk], reads=[wdram], writes=[wslot])
    for kc in range(kcn):
        P.op("pe", lambda e, kc=kc: e.matmul(pb[:, 0:n], lhsT=wslot[:, kc * 128:(kc + 1) * 128],
                                             rhs=src[:, kc, t0:t0 + n], start=(kc == 0), stop=(kc == kcn - 1)),
             reads=[wslot, src], writes=[pb])
    return wslot


def phase1(P, C, t, stage=99):
    NC1 = 32
    xin = t["xin"]
    win = t["win"]
    wout = t["wout"]
    wq = t["wq"]
    wf = t["wf"]
    wg = t["wg"]
    wv = t["wv"]
    cons = t["cons"]
    identd = t["ident"]
    hmaskd = t["hmask"]
    x1o = t["x1o"]
    oTo = t["oTo"]
    qco = t["qco"]
    sgo = t["sgo"]
    STo = t["STo"]
    DTo = t["DTo"]
    ident = P.sbuf("identb", [128, 128], BF16)
    hmask = P.sbuf("hmaskb", [128, 512], F32)
    wvs = P.sbuf("wvs", [128, KC, D], BF16)
    vtok = P.sbuf("vtok", [128, G // 128, D], BF16)
    lb = P.sbuf("lb", [128, 8], F32)
    oml = P.sbuf("oml", [128, 8], F32)
    noml = P.sbuf("noml", [128, 8], F32)
    rmask = P.sbuf("rmask", [128, NT], F32)
    zer8 = P.sbuf("zer8", [128, 8], F32)
    actf = C.act[:].rearrange("p j t -> p (j t)").bitcast(F32)
    scr = [[T(actf[:, (s_ * 7 + i) * NT:(s_ * 7 + i + 1) * NT], "hs%d_%d" % (s_, i)) for i in range(7)] for s_ in range(2)]
    qt = [P.sbuf("qt%d" % i, [128, NT], BF16) for i in range(2)]
    kt = [P.sbuf("kt%d" % i, [128, NT], BF16) for i in range(2)]
    kh = [P.sbuf("kh%d" % i, [128, NT], BF16) for i in range(2)]
    khT = [[P.sbuf("khT%d_%d" % (i, par), [128, NT], BF16) for par in range(2)] for i in range(2)]
    scm = [P.sbuf("scm%d" % i, [128, NT], BF16) for i in range(2)]
    qc = [P.sbuf("qc%d" % i, [128, NT], BF16) for i in range(2)]
    oev = [P.sbuf("oev%d" % i, [128, NT], F32) for i in range(2)]
    sgs = C.sg
    dch = [P.sbuf("dch%d" % i, [128, 8], F32) for i in range(2)]
    S = [P.sbuf("S%d" % i, [128, 128], F32) for i in range(NH_A)]
    Sb = [[P.sbuf("Sb%d_%d" % (i, c), [128, 128], BF16) for c in range(9)] for i in range(2)]
    Dexc = [P.sbuf("Dexc%d" % i, [128, 9], F32) for i in range(NH_A)]
    Slast = [P.sbuf("Slast%d" % i, [128, 128], BF16) for i in range(NH_A)]

    P.dma("sp", C.cons[:, 0:NC1], cons[:], reads=[cons], writes=[C.cons])
    P.dma("pool", ident[:], identd[:], writes=[ident])
    P.dma("sp", hmask[:], hmaskd[:], writes=[hmask])
    P.op("dve", lambda e: e.tensor_scalar(out=C.gs[:, 0:NC1], in0=C.cons[:, 0:NC1], scalar1=float(np.sqrt(D)), scalar2=None, op0=ALU.mult),
         reads=[C.cons], writes=[C.gs])
    P.op("dve", lambda e: e.tensor_tensor(out=lb[:], in0=C.cons[:, 16:24], in1=C.cons[:, 24:32], op=ALU.subtract), reads=[C.cons], writes=[lb])
    P.op("act", lambda e: e.activation(out=oml[:], in_=lb[:], func=AF.Sigmoid, scale=-1.0), reads=[lb], writes=[oml])
    P.op("act", lambda e: e.activation(out=lb[:], in_=lb[:], func=AF.Sigmoid), reads=[lb], writes=[lb])
    P.op("dve", lambda e: e.tensor_scalar(out=noml[:], in0=oml[:], scalar1=-1.0, scalar2=None, op0=ALU.mult), reads=[oml], writes=[noml])
    P.op("pool", lambda e: e.memset(rmask[:], 1.0), writes=[rmask])
    P.op("pool", lambda e: e.memset(rmask[:].rearrange("p (c t) -> p c t", t=CH)[:, :, 0:1], 0.0), writes=[rmask])
    P.op("pool", lambda e: e.memset(zer8[:], 0.0), writes=[zer8])
    for i in range(2):
        for par in range(2):
            P.op("pool", lambda e, i=i, par=par: e.memset(khT[i][par][:], 0.0), writes=[khT[i][par]])
    for hd in range(NH_A):
        P.op("pool", lambda e, hd=hd: e.memset(S[hd][:], 0.0), writes=[S[hd]])
        P.op("pool", lambda e, hd=hd: e.memset(Slast[hd][:], 0.0), writes=[Slast[hd]])
        P.op("pool", lambda e, hd=hd: e.memset(Dexc[hd][:], 1.0), writes=[Dexc[hd]])

    it = 0
    for g in range(TOK // G):
        P.dma("sp", C.x[:], xin[:, :, g * G:(g + 1) * G], reads=[xin], writes=[C.x])
        ffn_group(C, win, wout, 0)
        if g < TOK // G - 1:
            P.dma("sp", x1o[:, :, g * G:(g + 1) * G], C.x[:], reads=[C.x], writes=[x1o])
        P.barrier()
        for t in range(G // NT):
            rmsnorm_fm(C, lambda kc, t=t: C.x[:, kc, t * NT:(t + 1) * NT], C.x, KC, t * NT, NT,
                       lambda kc: C.gs[:, 8 + kc:9 + kc],
                       lambda kc, t=t: C.h[:, kc, t * NT:(t + 1) * NT], C.h, C.ps[7], D, src_all=C.x[:, :, t * NT:(t + 1) * NT])
        if g == 0:
            for kc in range(KC):
                P.dma("pool", wvs[:, kc, :], wv[:, kc * D:(kc + 1) * D], reads=[wv], writes=[wvs])
        for tb in range(G // 128):
            for half in range(2):
                pb = C.ps[(tb * 2 + half) % 4]
                for kc in range(KC):
                    P.op("pe", lambda e, kc=kc, tb=tb, half=half, pb=pb: e.matmul(
                        pb[:], lhsT=C.h[:, kc, tb * 128:(tb + 1) * 128], rhs=wvs[:, kc, half * 512:(half + 1) * 512],
                        start=(kc == 0), stop=(kc == KC - 1)), reads=[C.h, wvs], writes=[pb])
                P.op("act", lambda e, tb=tb, half=half, pb=pb: e.copy(out=vtok[:, tb, half * 512:(half + 1) * 512], in_=pb[:]),
                     reads=[pb], writes=[vtok])
        if stage < 2:
            continue
        for c in range(8):
            ws = None
            for t in range(G // NT):
                pb = C.ps[(c * 2 + t) % 4]
                ws = lin_fm(C, wg, c, KC, t * NT, NT, pb, wslot=ws)
                sg = sgs[(c * 2 + t) % 2]
                P.op("act", lambda e, pb=pb, sg=sg: e.activation(out=sg[:], in_=pb[:], func=AF.Sigmoid), reads=[pb], writes=[sg])
                P.dma("sp", sgo[:, c, g * G + t * NT:g * G + (t + 1) * NT], sg[:], reads=[sg], writes=[sgo])
        if stage < 3:
            continue
        def front(hd, t, i2):
            sig, fv, kk, bb, eb, enb, ek = scr[i2]
            tok0 = g * G + t * NT
            pq = C.ps[4 * i2]
            pf = C.ps[4 * i2 + 1]
            wsq = C.wsl[2 * i2]
            wsf = C.wsl[2 * i2 + 1]
            if t == 0 and hd < 2:
                P.dma("pool", wsq[:, 0:KC * 128], wq[hd], reads=[wq], writes=[wsq])
                P.dma("pool", wsf[:, 0:KC * 128], wf[hd], reads=[wf], writes=[wsf])
            lin_fm(C, wq, hd, KC, t * NT, NT, pq, wslot=wsq)
            lin_fm(C, wf, hd, KC, t * NT, NT, pf, wslot=wsf)
            if t == G // NT - 1 and hd + 2 < NH_A:
                P.dma("pool", wsq[:, 0:KC * 128], wq[hd + 2], reads=[wq], writes=[wsq])
                P.dma("pool", wsf[:, 0:KC * 128], wf[hd + 2], reads=[wf], writes=[wsf])
            c3 = lambda a: a[:].rearrange("p (c t) -> p c t", t=CH)
            P.op("act", lambda e, pf=pf: e.activation(out=sig[:], in_=pf[:], func=AF.Sigmoid), reads=[pf], writes=[sig])
            P.op("dve", lambda e, hd=hd: e.tensor_scalar(out=fv[:], in0=sig[:], scalar1=oml[:, hd:hd + 1], scalar2=lb[:, hd:hd + 1],
                                                       op0=ALU.mult, op1=ALU.add), reads=[sig, oml, lb], writes=[fv])
            P.op("act", lambda e: e.activation(out=fv[:], in_=fv[:], func=AF.Ln), reads=[fv], writes=[fv])
            P.op("dve", lambda e, hd=hd: e.tensor_scalar(out=kk[:], in0=sig[:], scalar1=noml[:, hd:hd + 1], scalar2=oml[:, hd:hd + 1],
                                                       op0=ALU.mult, op1=ALU.add), reads=[sig, noml, oml], writes=[kk])
            P.op("dve", lambda e: e.tensor_tensor_scan(out=bb[:], data0=rmask[:], data1=fv[:], initial=0.0, op0=ALU.mult, op1=ALU.add),
                 reads=[rmask, fv], writes=[bb])
            P.op("act", lambda e: e.activation(out=eb[:], in_=bb[:], func=AF.Exp), reads=[bb], writes=[eb])
            P.op("act", lambda e: e.activation(out=enb[:], in_=bb[:], func=AF.Exp, scale=-1.0), reads=[bb], writes=[enb])
            P.op("dve", lambda e, pq=pq, i2=i2: e.scalar_tensor_tensor(out=qt[i2][:], in0=pq[:], scalar=float(128 ** -0.5), in1=eb[:],
                                                                      op0=ALU.mult, op1=ALU.mult), reads=[pq, eb], writes=[qt[i2]])
            P.op("pool", lambda e, i2=i2: e.tensor_tensor(out=kt[i2][:], in0=kk[:], in1=enb[:], op=ALU.mult), reads=[kk, enb], writes=[kt[i2]])
            P.op("dve", lambda e: e.tensor_tensor(out=c3(ek), in0=c3(bb)[:, :, CH - 1:CH].to_broadcast([128, 8, CH]), in1=c3(bb),
                                                  op=ALU.subtract), reads=[bb], writes=[ek])
            P.op("act", lambda e: e.activation(out=ek[:], in_=ek[:], func=AF.Exp), reads=[ek], writes=[ek])
            P.op("pool", lambda e, i2=i2: e.tensor_tensor(out=kh[i2][:], in0=kk[:], in1=ek[:], op=ALU.mult), reads=[kk, ek], writes=[kh[i2]])
            P.op("act", lambda e, i2=i2: e.activation(out=dch[i2][:], in_=c3(bb)[:, :, CH - 1], func=AF.Exp), reads=[bb], writes=[dch[i2]])
            P.op("dve", lambda e, hd=hd, i2=i2: e.tensor_tensor_scan(out=Dexc[hd][:, 1:9], data0=dch[i2][:], data1=zer8[:],
                                                                     initial=Dexc[hd][:, 0:1], op0=ALU.mult, op1=ALU.add),
                 reads=[dch[i2], zer8, Dexc[hd]], writes=[Dexc[hd]])
            P.op("dve", lambda e, hd=hd, i2=i2: e.tensor_tensor(out=c3(qc[i2]), in0=c3(qt[i2]),
                                                                in1=Dexc[hd][:, 0:8].unsqueeze(2).to_broadcast([128, 8, CH]), op=ALU.mult),
                 reads=[qt[i2], Dexc[hd]], writes=[qc[i2]])
            P.dma("sp", qco[hd, :, tok0:tok0 + NT], qc[i2][:], reads=[qc[i2]], writes=[qco])
            P.op("dve", lambda e, hd=hd: e.tensor_copy(out=Dexc[hd][:, 0:1], in_=Dexc[hd][:, 8:9]), reads=[Dexc[hd]], writes=[Dexc[hd]])


        def back(hd, t, i2):
            tok0 = g * G + t * NT
            ptr = C.ps[4 * i2 + 2]
            ptrb = ptr[:].bitcast(BF16)
            for pr in range(4):
                P.op("pe", lambda e, pr=pr, i2=i2: e.transpose(ptrb[:, pr * 128:(pr + 1) * 128], kh[i2][:, pr * 128:(pr + 1) * 128], ident[:]),
                     reads=[kh[i2], ident], writes=[ptr])
            for par in range(2):
                P.op("act", lambda e, i2=i2, par=par: e.copy(out=khT[i2][par][par * 64:(par + 1) * 64, :], in_=ptrb[par * 64:(par + 1) * 64, 0:NT]),
                     reads=[ptr], writes=[khT[i2][par]])
            psc = C.ps[4 * i2 + 2]
            for pr in range(4):
                P.op("pe", lambda e, pr=pr, i2=i2: e.matmul(psc[:, pr * 128:(pr + 1) * 128], lhsT=kt[i2][:, pr * 128:(pr + 1) * 128],
                                                            rhs=qt[i2][:, pr * 128:(pr + 1) * 128], start=True, stop=True),
                     reads=[kt[i2], qt[i2]], writes=[psc])
            P.op("dve", lambda e, i2=i2: e.tensor_tensor(out=scm[i2][:], in0=psc[:], in1=hmask[:], op=ALU.mult), reads=[psc, hmask], writes=[scm[i2]])
            pU = C.ps[4 * i2 + 3]
            P.op("pool", lambda e, i2=i2, hd=hd: e.tensor_copy(out=Sb[i2][0][:], in_=Slast[hd][:]), reads=[Slast[hd]], writes=[Sb[i2][0]])
            for half in range(2):
                for c4 in range(4):
                    c = half * 4 + c4
                    pr, par = c // 2, c % 2
                    tb = t * 4 + pr
                    P.op("pe", lambda e, c4=c4, pr=pr, par=par, tb=tb, i2=i2, hd=hd: e.matmul(
                        pU[:, c4 * 128:(c4 + 1) * 128], lhsT=khT[i2][par][:, pr * 128:(pr + 1) * 128],
                        rhs=vtok[:, tb, hd * 128:(hd + 1) * 128], start=True, stop=True),
                        reads=[khT[i2][par], vtok], writes=[pU])
                for c4 in range(4):
                    c = half * 4 + c4
                    P.op("dve", lambda e, c=c, c4=c4, i2=i2, hd=hd: e.scalar_tensor_tensor(
                        out=S[hd][:], in0=S[hd][:], scalar=dch[i2][:, c:c + 1], in1=pU[:, c4 * 128:(c4 + 1) * 128],
                        op0=ALU.mult, op1=ALU.add), reads=[S[hd], dch[i2], pU], writes=[S[hd]])
                    dst = Sb[i2][c + 1] if c < 7 else Slast[hd]
                    P.op("pool", lambda e, dst=dst, hd=hd: e.tensor_copy(out=dst[:], in_=S[hd][:]), reads=[S[hd]], writes=[dst])
            po = C.ps[4 * i2 + 2]
            for pr in range(4):
                tb = t * 4 + pr
                P.op("pe", lambda e, pr=pr, tb=tb, i2=i2, hd=hd: e.matmul(po[:, pr * 128:(pr + 1) * 128], lhsT=vtok[:, tb, hd * 128:(hd + 1) * 128],
                                                                          rhs=scm[i2][:, pr * 128:(pr + 1) * 128], start=True, stop=False),
                     reads=[vtok, scm[i2]], writes=[po])
                for par in range(2):
                    c = pr * 2 + par
                    P.op("pe", lambda e, c=c, i2=i2: e.matmul(po[:, c * 64:(c + 1) * 64], lhsT=Sb[i2][c][:], rhs=qt[i2][:, c * 64:(c + 1) * 64],
                                                              start=False, stop=(c % 2 == 1)), reads=[Sb[i2][c], qt[i2]], writes=[po])
            P.op("act", lambda e, i2=i2: e.copy(out=oev[i2][:], in_=po[:]), reads=[po], writes=[oev[i2]])
            P.dma("sp", oTo[hd, :, tok0:tok0 + NT], oev[i2][:], reads=[oev[i2]], writes=[oTo])


        streams = []
        for s_ in range(2):
            P.defer = []
            for hd in range(s_, NH_A, 2):
                for t in range(G // NT):
                    front(hd, t, s_)
                    back(hd, t, s_)
            streams.append(P.defer)
            P.defer = None
        P.run_streams(streams)
        P.barrier()
    for hd in range(NH_A):
        P.dma("sp", STo[:, hd, :], S[hd][:], reads=[S[hd]], writes=[STo])
        P.op("dve", lambda e, hd=hd: e.tensor_copy(out=zer8[:, hd:hd + 1], in_=Dexc[hd][:, 0:1]), reads=[Dexc[hd]], writes=[zer8])
    P.dma("sp", DTo[:], zer8[:], reads=[zer8], writes=[DTo])


NC2 = 56


def ple_group(C, wgate, wple, pin, g, gcol0, pbf):
    P = C.P
    nt = G // NT
    for t in range(nt):
        rmsnorm_fm(C, lambda kc, t=t: C.x[:, kc, t * NT:(t + 1) * NT], C.x, KC, t * NT, NT,
                   lambda kc: C.gs[:, gcol0 + kc:gcol0 + kc + 1],
                   lambda kc, t=t: C.h[:, kc, t * NT:(t + 1) * NT], C.h, C.ps[7], D, src_all=C.x[:, :, t * NT:(t + 1) * NT])
    for kc in range(2):
        P.dma("pool", pbf[:, kc, :], pin[:, kc, g * G:(g + 1) * G], reads=[pin], writes=[pbf])
    i = 0
    for m in range(KC):
        wsg = None
        wsp = None
        for t in range(nt):
            pg = C.ps[(i % 2) * 2]
            pp = C.ps[(i % 2) * 2 + 1]
            sg = C.sg[i % 2]
            i += 1
            wsg = lin_fm(C, wgate, m, KC, t * NT, NT, pg, wslot=wsg)
            wsp = lin_fm(C, wple, m, 2, t * NT, NT, pp, wslot=wsp, src=pbf)
            P.op("act", lambda e, pg=pg, sg=sg: e.activation(out=sg[:], in_=pg[:], func=AF.Sigmoid), reads=[pg], writes=[sg])
            P.op("dve", lambda e, pp=pp, sg=sg: e.tensor_tensor(out=sg[:], in0=sg[:], in1=pp[:], op=ALU.mult), reads=[sg, pp], writes=[sg])
            P.op("dve", lambda e, sg=sg, m=m, t=t: e.tensor_tensor(out=C.x[:, m, t * NT:(t + 1) * NT], in0=C.x[:, m, t * NT:(t + 1) * NT],
                                                                 in1=sg[:], op=ALU.add), reads=[sg, C.x], writes=[C.x])


def rope_table(C, posd, tok0, n, tab, tmpf, tmpi, invcol, phcol):
    P = C.P
    P.dma("sp", tmpi[:, 0:n], posd[0:1, tok0:tok0 + n].to_broadcast([128, n]), reads=[posd], writes=[tmpi])
    P.op("dve", lambda e: e.tensor_copy(out=tab[:, 0:n], in_=tmpi[:, 0:n]), reads=[tmpi], writes=[tab])
    P.op("dve", lambda e: e.tensor_scalar(out=tab[:, 0:n], in0=tab[:, 0:n], scalar1=C.cons[:, invcol:invcol + 1],
                                          scalar2=C.cons[:, phcol:phcol + 1], op0=ALU.mult, op1=ALU.add), reads=[tab, C.cons], writes=[tab])
    P.op("dve", lambda e: e.tensor_scalar(out=tmpf[:, 0:n], in0=tab[:, 0:n], scalar1=float(1 / (2 * np.pi)), scalar2=None, op0=ALU.mult),
         reads=[tab], writes=[tmpf])
    P.op("dve", lambda e: e.tensor_copy(out=tmpi[:, 0:n], in_=tmpf[:, 0:n]), reads=[tmpf], writes=[tmpi])
    P.op("dve", lambda e: e.tensor_copy(out=tmpf[:, 0:n], in_=tmpi[:, 0:n]), reads=[tmpi], writes=[tmpf])
    P.op("dve", lambda e: e.scalar_tensor_tensor(out=tab[:, 0:n], in0=tmpf[:, 0:n], scalar=float(-2 * np.pi), in1=tab[:, 0:n],
                                                 op0=ALU.mult, op1=ALU.add), reads=[tmpf, tab], writes=[tab])
    P.op("dve", lambda e: e.tensor_scalar(out=tmpf[:, 0:n], in0=tab[:, 0:n], scalar1=float(np.pi), scalar2=float(-2 * np.pi),
                                          op0=ALU.is_gt, op1=ALU.mult), reads=[tab], writes=[tmpf])
    P.op("dve", lambda e: e.tensor_tensor(out=tab[:, 0:n], in0=tab[:, 0:n], in1=tmpf[:, 0:n], op=ALU.add), reads=[tab, tmpf], writes=[tab])
    P.op("dve", lambda e: e.tensor_scalar(out=tmpf[:, 0:n], in0=tab[:, 0:n], scalar1=float(-np.pi), scalar2=float(2 * np.pi),
                                          op0=ALU.is_lt, op1=ALU.mult), reads=[tab], writes=[tmpf])
    P.op("dve", lambda e: e.tensor_tensor(out=tab[:, 0:n], in0=tab[:, 0:n], in1=tmpf[:, 0:n], op=ALU.add), reads=[tab, tmpf], writes=[tab])
    P.op("act", lambda e: e.activation(out=tab[:, 0:n], in_=tab[:, 0:n], func=AF.Sin), reads=[tab], writes=[tab])


def phase2(P, C, t, dbg=False):
    x1i = t["x1i"]
    oTi = t["oTi"]
    qci = t["qci"]
    sgi = t["sgi"]
    STi = t["STi"]
    DTi = t["DTi"]
    pin = t["pin"]
    posd = t["pos"]
    cons = t["cons"]
    stk = t["stk"]
    wao = t["wao"]
    win0 = t["win0"]
    wout0 = t["wout0"]
    wpg = t["wpg"]
    wpi = t["wpi"]
    wkd = t["wkd"]
    win1 = t["win1"]
    wout1 = t["wout1"]
    wdq = t["wdq"]
    x5o = t["x5o"]
    lat_dst = t["lat_dst"]
    lat_done = t["lat_done"]
    tab_dst = t["tab_dst"]
    tab_done = t["tab_done"]
    Sst = P.sbuf("Sst", [128, NH_A, 128], F32)
    Sstb = P.sbuf("Sstb", [128, NH_A, 128], BF16)
    Ul = P.sbuf("Ul", [128, NH_A, 128], F32)
    Dl = P.sbuf("Dl", [128, NH_A], F32)
    ot = [P.sbuf("ot%d" % i, [128, NT], F32) for i in range(2)]
    qcl = [P.sbuf("qcl%d" % i, [128, NT], BF16) for i in range(2)]
    sgl = [P.sbuf("sgl%d" % i, [128, NT], F32) for i in range(2)]
    pbf = P.sbuf("pbf", [128, 2, G], BF16)
    latf = P.sbuf("latf", [128, 4, NT], F32)
    latb = P.sbuf("latb", [128, 4, NT], BF16)
    tab = P.sbuf("tab", [128, G], F32)
    tmpf = P.sbuf("tmpf", [128, G], F32)
    tmpi = P.sbuf("tmpi", [128, G], I32)
    stkb = P.sbuf("stkb", [128, 64], BF16)
    rT = P.sbuf("rT", [128, NT], BF16)
    krb = P.sbuf("krb", [64, NT], BF16)

    P.dma("sp", C.cons[:, 0:NC2], cons[:], reads=[cons], writes=[C.cons])
    P.dma("pool", stkb[:], stk[:], writes=[stkb])
    for (a, b, sc) in ((0, 40, np.sqrt(D)), (40, 42, np.sqrt(KVL)), (42, 46, np.sqrt(QL)), (46, 47, np.sqrt(128.0))):
        P.op("dve", lambda e, a=a, b=b, sc=sc: e.tensor_scalar(out=C.gs[:, a:b], in0=C.cons[:, a:b], scalar1=float(sc), scalar2=None, op0=ALU.mult),
             reads=[C.cons], writes=[C.gs])
    P.op("pool", lambda e: e.memset(Sst[:], 0.0), writes=[Sst])
    for r in range(4):
        P.dma("sp", Ul[:], STi[r], reads=[STi], writes=[Ul])
        P.dma("sp", Dl[:], DTi[r], reads=[DTi], writes=[Dl])
        w = C.cons[:, 49 + r:50 + r]
        P.op("dve", lambda e, w=w: e.tensor_scalar(out=Dl[:], in0=Dl[:], scalar1=-1.0, scalar2=w, op0=ALU.add, op1=ALU.mult), reads=[Dl, C.cons], writes=[Dl])
        P.op("dve", lambda e: e.tensor_scalar(out=Dl[:], in0=Dl[:], scalar1=1.0, scalar2=None, op0=ALU.add), reads=[Dl], writes=[Dl])
        P.op("dve", lambda e, w=w: e.tensor_scalar(out=Ul[:], in0=Ul[:], scalar1=w, scalar2=None, op0=ALU.mult), reads=[Ul, C.cons], writes=[Ul])
        for hd in range(NH_A):
            P.op("dve", lambda e, hd=hd: e.scalar_tensor_tensor(out=Sst[:, hd, :], in0=Sst[:, hd, :], scalar=Dl[:, hd:hd + 1], in1=Ul[:, hd, :],
                                                                op0=ALU.mult, op1=ALU.add), reads=[Sst, Dl, Ul], writes=[Sst])
    P.op("act", lambda e: e.copy(out=Sstb[:], in_=Sst[:]), reads=[Sst], writes=[Sstb])

    nt = G // NT
    it = 0
    for g in reversed(range(TOK // G)):
        if g < TOK // G - 1:
            P.dma("sp", C.x[:], x1i[:, :, g * G:(g + 1) * G], reads=[x1i], writes=[C.x])
        rstds2 = [C.rstd, C.rstd2]
        lat1 = [T(latf[:, s_, :], "latf_s%d" % s_) for s_ in range(2)]

        def fix_iter(hd, t, s_):
            tok0 = g * G + t * NT
            o_, q_, g_ = ot[s_], qcl[s_], sgl[s_]
            po, pn, rs, lf = C.ps[s_], C.ps[6 + s_], rstds2[s_], lat1[s_]
            P.dma("sp", o_[:], oTi[hd, :, tok0:tok0 + NT], reads=[oTi], writes=[o_])
            P.dma("sp", q_[:], qci[hd, :, tok0:tok0 + NT], reads=[qci], writes=[q_])
            P.dma("sp", g_[:], sgi[:, hd, tok0:tok0 + NT], reads=[sgi], writes=[g_])
            P.op("pe", lambda e: e.matmul(po[:], lhsT=Sstb[:, hd, :], rhs=q_[:], start=True, stop=True), reads=[Sstb, q_], writes=[po])
            P.op("dve", lambda e: e.tensor_tensor(out=o_[:], in0=o_[:], in1=po[:], op=ALU.add), reads=[o_, po], writes=[o_])
            P.op("act", lambda e: e.activation(out=C.sq[:, s_, :], in_=o_[:], func=AF.Square), reads=[o_], writes=[C.sq])
            P.op("pe", lambda e: e.matmul(pn[:], lhsT=C.ones[:], rhs=C.sq[:, s_, :], start=True, stop=True), reads=[C.ones, C.sq], writes=[pn])
            P.op("act", lambda e: e.activation(out=rs[:], in_=pn[:], func=AF.Ln, bias=float(128 * EPS)), reads=[pn], writes=[rs])
            P.op("act", lambda e: e.activation(out=rs[:], in_=rs[:], func=AF.Exp, scale=-0.5), reads=[rs], writes=[rs])
            P.op("dve", lambda e: e.scalar_tensor_tensor(out=lf[:], in0=o_[:], scalar=C.gs[:, 46:47], in1=rs[:], op0=ALU.mult, op1=ALU.mult),
                 reads=[o_, C.gs, rs], writes=[lf])
            P.op("dve", lambda e: e.tensor_tensor(out=C.h[:, hd, t * NT:(t + 1) * NT], in0=lf[:], in1=g_[:], op=ALU.mult),
                 reads=[lf, g_], writes=[C.h])

        streams = []
        for s_ in range(2):
            P.defer = []
            for hd in range(s_, NH_A, 2):
                for t in range(nt):
                    fix_iter(hd, t, s_)
            streams.append(P.defer)
            P.defer = None
        P.run_streams(streams)
        i = 0
        for m in range(KC):
            ws = None
            for t in range(nt):
                pb = C.ps[2 + i % 2]
                i += 1
                ws = lin_fm(C, wao, m, KC, t * NT, NT, pb, wslot=ws)
                P.op("dve", lambda e, pb=pb, m=m, t=t: e.tensor_tensor(out=C.x[:, m, t * NT:(t + 1) * NT], in0=C.x[:, m, t * NT:(t + 1) * NT], in1=pb[:], op=ALU.add),
                     reads=[pb, C.x], writes=[C.x])
        ffn_group(C, win0, wout0, 0)
        ple_group(C, wpg, wpi, pin, g, 8, pbf)
        rope_table(C, posd, g * G, G, tab, tmpf, tmpi, 47, 48)
        tab_ap, tabT = tab_dst(g)
        P.dma("sp", tab_ap, tab[:], reads=[tab], writes=[tabT])
        tab_done(g)
        for t in range(nt):
            rmsnorm_fm(C, lambda kc, t=t: C.x[:, kc, t * NT:(t + 1) * NT], C.x, KC, t * NT, NT,
                       lambda kc: C.gs[:, 16 + kc:17 + kc],
                       lambda kc, t=t: C.h[:, kc, t * NT:(t + 1) * NT], C.h, C.ps[7], D, src_all=C.x[:, :, t * NT:(t + 1) * NT])
        wsl3 = [None, None, None]
        for t in range(nt):
            tok0 = g * G + t * NT
            for c in range(3):
                pb = C.ps[c]
                wsl3[c] = lin_fm(C, wkd, c, KC, t * NT, NT, pb, wslot=wsl3[c])
                if c < 2:
                    P.op("act", lambda e, c=c, pb=pb: e.copy(out=latf[:, c, :], in_=pb[:]), reads=[pb], writes=[latf])
                else:
                    P.op("dve", lambda e, pb=pb, t=t: e.tensor_tensor(out=rT[:], in0=pb[:], in1=tab[:, t * NT:(t + 1) * NT], op=ALU.mult),
                         reads=[pb, tab], writes=[rT])
            pk = C.ps[3]
            P.op("pe", lambda e, pk=pk: e.matmul(pk[0:64, :], lhsT=stkb[:], rhs=rT[:], start=True, stop=True), reads=[stkb, rT], writes=[pk])
            P.op("act", lambda e, pk=pk: e.copy(out=krb[0:64, :], in_=pk[0:64, :]), reads=[pk], writes=[krb])
            cq_ap, ckv_ap, kr_ap, latT = lat_dst(tok0 // NT)
            P.dma("sp", kr_ap, krb[0:64, :], reads=[krb], writes=[latT])
            rmsnorm_fm(C, lambda kc: latf[:, kc, :], latf, 2, 0, NT, lambda kc: C.gs[:, 40 + kc:41 + kc],
                       lambda kc: latb[:, kc, :], latb, C.ps[7], KVL)
            P.dma("sp", ckv_ap, latb[:, 0:2, :], reads=[latb], writes=[latT])
        ffn_group(C, win1, wout1, 24)
        P.dma("sp", x5o[:, :, g * G:(g + 1) * G], C.x[:], reads=[C.x], writes=[x5o])
        for t in range(nt):
            rmsnorm_fm(C, lambda kc, t=t: C.x[:, kc, t * NT:(t + 1) * NT], C.x, KC, t * NT, NT,
                       lambda kc: C.gs[:, 32 + kc:33 + kc],
                       lambda kc, t=t: C.h[:, kc, t * NT:(t + 1) * NT], C.h, C.ps[7], D, src_all=C.x[:, :, t * NT:(t + 1) * NT])
        wsl4 = [None] * 4
        for t in range(nt):
            tok0 = g * G + t * NT
            for c in range(4):
                pb = C.ps[c]
                wsl4[c] = lin_fm(C, wdq, c, KC, t * NT, NT, pb, wslot=wsl4[c])
                P.op("act", lambda e, c=c, pb=pb: e.copy(out=latf[:, c, :], in_=pb[:]), reads=[pb], writes=[latf])
            rmsnorm_fm(C, lambda kc: latf[:, kc, :], latf, 4, 0, NT, lambda kc: C.gs[:, 42 + kc:43 + kc],
                       lambda kc: latb[:, kc, :], latb, C.ps[7], QL)
            cq_ap, ckv_ap, kr_ap, latT = lat_dst(tok0 // NT)
            P.dma("sp", cq_ap, latb[:], reads=[latb], writes=[latT])
            lat_done(tok0 // NT)


HPC = 4
NQT = SEQ // NT
SCALE = float((128 + ROPE) ** -0.5)


def phase3(P, C0, t, nqt=NQT, nheads=HPC, stop=9):
    lat = t["lat"]
    cq_src = t["cq_src"]
    ckv_src = t["ckv_src"]
    kr_src = t["kr_src"]
    o_dst = t["o_dst"]
    head_done = t["head_done"]
    posd = t["pos"]
    cons = t["cons"]
    stk = t["stk"]
    trid = t["tri"]
    wuq = t["wuq"]
    wuk = t["wuk"]
    wuv = t["wuv"]

    class CC:
        pass
    C = CC()
    C.P = P
    C.ps = P.ps
    C.cons = P.sbuf("cons", [128, 8], F32)
    ones = P.sbuf("ones", [128, 128], BF16)
    onesf = P.sbuf("onesf", [128, 128], F32)
    ckv = P.sbuf("ckv", [128, 2, SEQ], BF16)
    Bk = P.sbuf("Bk", [128, SEQ], BF16)
    Ak = [P.sbuf("Ak%d" % i, [128, SEQ], BF16) for i in range(2)]
    Vh = [P.sbuf("Vh%d" % i, [128, SEQ // 128, 128], BF16) for i in range(2)]
    tab = P.sbuf("tab", [128, SEQ], F32)
    stkb = P.sbuf("stkb", [128, 64], BF16)
    tri = P.sbuf("trib", [128, 128], F32)
    wq = [P.sbuf("wq%d" % i, [128, 2, QL], BF16) for i in range(2)]
    wk = [P.sbuf("wk%d" % i, [128, KVL], BF16) for i in range(2)]
    wv = [P.sbuf("wv%d" % i, [128, KVL], BF16) for i in range(2)]
    mx = P.sbuf("mx", [128, 8], F32)
    mtmp = P.sbuf("setup_tmp", [128, 4096], F32)
    m_alias = P.mark()

    P.dma("sp", C.cons[:], cons[:], reads=[cons], writes=[C.cons])
    P.dma("pool", stkb[:], stk[:], writes=[stkb])
    P.dma("sp", tri[:], trid[:], writes=[tri])
    P.op("pool", lambda e: e.memset(ones[:], 1.0), writes=[ones])
    P.op("pool", lambda e: e.memset(onesf[:], 1.0), writes=[onesf])
    P.op("pool", lambda e: e.memset(Bk[:], 0.0), writes=[Bk])
    P.op("pool", lambda e: e.memset(Bk[64:65, :], 1.0), writes=[Bk])
    P.op("pool", lambda e: e.memset(mx[:], 0.0), writes=[mx])
    for q in range(nqt):
        for kc in range(2):
            P.dma("sp", ckv[:, kc, q * NT:(q + 1) * NT], ckv_src(kc, q), reads=[lat], writes=[ckv])
        P.dma("sp", Bk[0:64, q * NT:(q + 1) * NT], kr_src(q), reads=[lat], writes=[Bk])
    if "tab_src" in t:
        for q8 in range(SEQ // G):
            ap_, T_ = t["tab_src"](q8)
            P.dma("sp", tab[:, q8 * G:(q8 + 1) * G], ap_, reads=[T_], writes=[tab])
    else:
        tmpf = T(mtmp[:, 0:2048], "tmpf")
        tmpi = T(mtmp[:, 2048:4096].bitcast(I32), "tmpi")
        for q4 in range(4):
            tv = T(tab[:, q4 * 2048:(q4 + 1) * 2048], "tabv%d" % q4)
            rope_table(C, posd, q4 * 2048, 2048, tv, tmpf, tmpi, 0, 1)
    P.barrier()
    P.atop = m_alias - 4096 * 4
    aq = [P.sbuf("aq%d" % i, [128, NT], BF16) for i in range(3)]
    bq = [P.sbuf("bq%d" % i, [128, NT], BF16) for i in range(3)]
    cql = [P.sbuf("cql%d" % i, [128, 4, NT], BF16) for i in range(3)]
    sqA = [P.sbuf("sqA%d" % i, [128, NT], BF16) for i in range(2)]
    sqB = [P.sbuf("sqB%d" % i, [128, NT], BF16) for i in range(2)]
    rT = [P.sbuf("rT%d" % i, [128, NT], BF16) for i in range(2)]
    tq = P.sbuf("tq", [128, NT], F32)
    NPT = 6
    pt = [P.sbuf("pt%d" % i, [128, NT], BF16) for i in range(NPT)]
    acc = [[P.sbuf("acc%d_%d" % (j, i), [128, NT], F32) for i in range(2)] for j in range(2)]
    rl = P.sbuf("rl", [128, NT], F32)
    on = [P.sbuf("on%d" % i, [128, NT], BF16) for i in range(2)]
    for i in range(3):
        P.op("pool", lambda e, i=i: e.memset(bq[i][:], 0.0), writes=[bq[i]])
    for i in range(2):
        P.op("pool", lambda e, i=i: e.memset(sqB[i][:], 0.0), writes=[sqB[i]])

    state = {"bank": 0, "wide": True, "sq": 0}

    def nxt_bank():
        state["bank"] += 1
        return C.ps[state["bank"] % 8] if state["wide"] else C.ps[6 + state["bank"] % 2]

    def colmax(pb, col):
        P.op("dve", lambda e: e.tensor_reduce(out=mx[:, 3:4], in_=pb[:], axis=mybir.AxisListType.X, op=ALU.max), reads=[pb], writes=[mx])
        P.op("dve", lambda e: e.tensor_tensor(out=mx[:, col:col + 1], in0=mx[:, col:col + 1], in1=mx[:, 3:4], op=ALU.max), reads=[mx], writes=[mx])

    for kt in range(nqt):
        pb = nxt_bank()
        sB = sqB[kt % 2]
        P.op("act", lambda e, kt=kt, sB=sB: e.activation(out=sB[0:64, :], in_=Bk[0:64, kt * NT:(kt + 1) * NT], func=AF.Square), reads=[Bk], writes=[sB])
        P.op("pe", lambda e, sB=sB, pb=pb: e.matmul(pb[:], lhsT=ones[:], rhs=sB[:], start=True, stop=True), reads=[ones, sB], writes=[pb])
        colmax(pb, 0)

    def load_w(hn):
        P.dma("pool", wq[hn % 2][:, 0, :], wuq[2 * hn], reads=[wuq], writes=[wq[hn % 2]])
        P.dma("pool", wq[hn % 2][:, 1, :], wuq[2 * hn + 1], reads=[wuq], writes=[wq[hn % 2]])
        P.dma("pool", wk[hn % 2][:], wuk[hn], reads=[wuk], writes=[wk[hn % 2]])
        P.dma("pool", wv[hn % 2][:], wuv[hn], reads=[wuv], writes=[wv[hn % 2]])
        P.op("pool", lambda e: e.memset(mx[:, 1 + hn % 2:2 + hn % 2], 0.0), writes=[mx])

    def kv_tile(hn, kt):
        A, V, wkk, wvv = Ak[hn % 2], Vh[hn % 2], wk[hn % 2], wv[hn % 2]
        pb = nxt_bank()
        for kc in range(2):
            P.op("pe", lambda e, kc=kc: e.matmul(pb[:], lhsT=wkk[:, kc * 128:(kc + 1) * 128], rhs=ckv[:, kc, kt * NT:(kt + 1) * NT],
                                                 start=(kc == 0), stop=(kc == 1)), reads=[wkk, ckv], writes=[pb])
        yield
        P.op("act", lambda e: e.copy(out=A[:, kt * NT:(kt + 1) * NT], in_=pb[:]), reads=[pb], writes=[A])
        state["sq"] += 1
        sA = sqA[state["sq"] % 2]
        P.op("act", lambda e: e.activation(out=sA[:], in_=pb[:], func=AF.Square), reads=[pb], writes=[sA])
        yield
        pn = nxt_bank()
        P.op("pe", lambda e: e.matmul(pn[:], lhsT=ones[:], rhs=sA[:], start=True, stop=True), reads=[ones, sA], writes=[pn])
        yield
        colmax(pn, 1 + hn % 2)
        pv = nxt_bank()
        for k4 in range(4):
            kb = kt * 4 + k4
            for kc in range(2):
                P.op("pe", lambda e, kc=kc, kb=kb, k4=k4: e.matmul(pv[:, k4 * 128:(k4 + 1) * 128], lhsT=ckv[:, kc, kb * 128:(kb + 1) * 128],
                                                                   rhs=wvv[:, kc * 128:(kc + 1) * 128], start=(kc == 0), stop=(kc == 1)),
                     reads=[ckv, wvv], writes=[pv])
        yield
        P.op("dve", lambda e: e.tensor_copy(out=V[:, kt * 4:(kt + 1) * 4, :].rearrange("p a b -> p (a b)"), in_=pv[:]), reads=[pv], writes=[V])

    def kv_finish(hn):
        c = 4 + hn % 2
        P.op("dve", lambda e: e.tensor_tensor(out=mx[:, c:c + 1], in0=mx[:, 1 + hn % 2:2 + hn % 2], in1=mx[:, 0:1], op=ALU.add), reads=[mx], writes=[mx])
        yield
        P.op("act", lambda e: e.activation(out=mx[:, c:c + 1], in_=mx[:, c:c + 1], func=AF.Ln), reads=[mx], writes=[mx])
        P.op("act", lambda e: e.activation(out=mx[:, c:c + 1], in_=mx[:, c:c + 1], func=AF.Exp, scale=0.5), reads=[mx], writes=[mx])
        yield
        P.op("dve", lambda e: e.tensor_scalar(out=mx[:, c:c + 1], in0=mx[:, c:c + 1], scalar1=-1.008 * SCALE, scalar2=None, op0=ALU.mult), reads=[mx], writes=[mx])
        yield

    cq_loaded = set()

    def load_cq(hn, qt):
        g = hn * nqt + qt
        if g in cq_loaded or hn >= nheads:
            return
        cq_loaded.add(g)
        P.dma("sp", cql[g % 3][:], cq_src(qt * NT, NT), reads=[lat], writes=[cql[g % 3]])

    def q_tile(hn, qt):
        g = hn * nqt + qt
        a, b, cq, wqq = aq[g % 3], bq[g % 3], cql[g % 3], wq[hn % 2]
        if g not in cq_loaded:
            load_cq(hn, qt)
        pa = nxt_bank()
        pr = nxt_bank()
        for half, pb in ((0, pa), (1, pr)):
            for kc in range(4):
                P.op("pe", lambda e, kc=kc, half=half, pb=pb: e.matmul(pb[:], lhsT=wqq[:, half, kc * 128:(kc + 1) * 128], rhs=cq[:, kc, :],
                                                                       start=(kc == 0), stop=(kc == 3)), reads=[wqq, cq], writes=[pb])
        yield
        P.op("act", lambda e: e.activation(out=a[:], in_=pa[:], func=AF.Identity, scale=SCALE), reads=[pa], writes=[a])
        state["sq"] += 1
        sA, sB = sqA[state["sq"] % 2], sqB[state["sq"] % 2]
        P.op("act", lambda e: e.activation(out=sA[:], in_=pa[:], func=AF.Square), reads=[pa], writes=[sA])
        r = rT[g % 2]
        P.op("dve", lambda e: e.tensor_tensor(out=r[:], in0=pr[:], in1=tab[:, qt * NT:(qt + 1) * NT], op=ALU.mult), reads=[pr, tab], writes=[r])
        yield
        pk = nxt_bank()
        P.op("pe", lambda e: e.matmul(pk[0:64, :], lhsT=stkb[:], rhs=r[:], start=True, stop=True), reads=[stkb, r], writes=[pk])
        yield
        P.op("act", lambda e: e.activation(out=b[0:64, :], in_=pk[0:64, :], func=AF.Identity, scale=SCALE), reads=[pk], writes=[b])
        P.op("act", lambda e: e.activation(out=sB[0:64, :], in_=pk[0:64, :], func=AF.Square), reads=[pk], writes=[sB])
        yield
        pn = nxt_bank()
        P.op("pe", lambda e: e.matmul(pn[:], lhsT=ones[:], rhs=sA[:], start=True, stop=False), reads=[ones, sA], writes=[pn])
        P.op("pe", lambda e: e.matmul(pn[:], lhsT=ones[:], rhs=sB[:], start=False, stop=True), reads=[ones, sB], writes=[pn])
        yield
        P.op("act", lambda e: e.activation(out=tq[64:65, :], in_=pn[64:65, :], func=AF.Ln), reads=[pn], writes=[tq])
        P.op("act", lambda e: e.activation(out=tq[64:65, :], in_=tq[64:65, :], func=AF.Exp, scale=0.5), reads=[tq], writes=[tq])
        yield
        P.op("dve", lambda e: e.tensor_scalar(out=b[64:65, :], in0=tq[64:65, :], scalar1=mx[64:65, 4 + hn % 2:5 + hn % 2], scalar2=None, op0=ALU.mult),
             reads=[tq, mx], writes=[b])

    if stop <= 1:
        return
    def run(gen):
        for _ in gen:
            pass

    def chain(*gens):
        for g_ in gens:
            yield from g_

    load_w(0)
    for kt in range(nqt):
        run(kv_tile(0, kt))
    run(kv_finish(0))
    if stop <= 2:
        return
    run(q_tile(0, 0))
    load_cq(0, 1)
    state["wide"] = False
    if stop <= 3 or stop >= 30:
        return

    side = []
    qgen = {}
    for h in range(nheads):
        if h + 1 < nheads:
            load_w(h + 1)
        blocks = [(qt, kb) for qt in range(nqt) for kb in range(4 * qt + 4)]

        def geom(qt, kb):
            c0 = 0 if kb < 4 * qt else (kb - 4 * qt) * 128
            return c0, NT - c0

        def emit_qk(i):
            qt, kb = blocks[i]
            c0, ncol = geom(qt, kb)
            g = h * nqt + qt
            a, b = aq[g % 3], bq[g % 3]
            A_ = Ak[h % 2]
            pss = C.ps[i % 3]
            P.op("pe", lambda e: e.matmul(pss[:, 0:ncol], lhsT=A_[:, kb * 128:(kb + 1) * 128], rhs=a[:, c0:NT], start=True, stop=False),
                 reads=[A_, a], writes=[pss])
            P.op("pe", lambda e: e.matmul(pss[:, 0:ncol], lhsT=Bk[:, kb * 128:(kb + 1) * 128], rhs=b[:, c0:NT], start=False, stop=True),
                 reads=[Bk, b], writes=[pss])

        def emit_rest(i):
            qt, kb = blocks[i]
            nkb = 4 * qt + 4
            c0, ncol = geom(qt, kb)
            pss = C.ps[i % 3]
            p = pt[i % NPT]
            po = C.ps[3 + qt % 2]
            ac = acc[qt % 2]
            V_ = Vh[h % 2]
            if kb == 0:
                P.op("pool", lambda e: e.memset(ac[0][:], 0.0), writes=[ac[0]])
                P.op("pool", lambda e: e.memset(ac[1][:], 0.0), writes=[ac[1]])
            P.op("act", lambda e: e.activation(out=p[:, 0:ncol], in_=pss[:, 0:ncol], func=AF.Exp), reads=[pss], writes=[p])
            if kb >= 4 * qt:
                P.op("pool", lambda e: e.tensor_tensor(out=p[:, 0:128], in0=p[:, 0:128], in1=tri[:], op=ALU.mult), reads=[p, tri], writes=[p])
            ai = i % 2
            P.op("dve" if ai == 0 else "pool", lambda e: e.tensor_tensor(out=ac[ai][:, c0:NT], in0=ac[ai][:, c0:NT], in1=p[:, 0:ncol], op=ALU.add),
                 reads=[p, ac[ai]], writes=[ac[ai]])
            P.op("pe", lambda e: e.matmul(po[:, c0:NT], lhsT=V_[:, kb, :], rhs=p[:, 0:ncol], start=(kb == 0), stop=(kb == nkb - 1)),
                 reads=[V_, p], writes=[po])
            if kb == nkb - 1:
                pending.append((i + 2, lambda qt=qt, po=po, ac=ac: finalize(qt, po, ac)))
            if kb == 0:
                if qt + 1 < nqt:
                    qgen[(h, qt + 1)] = q_tile(h, qt + 1)
                    side.append(qgen[(h, qt + 1)])
                    load_cq(h, qt + 2) if qt + 2 < nqt else load_cq(h + 1, 0)
                if h + 1 < nheads:
                    if qt == nqt - 1:
                        qgen[(h + 1, 0)] = chain(kv_tile(h + 1, qt), kv_finish(h + 1), q_tile(h + 1, 0))
                        side.append(qgen[(h + 1, 0)])
                        load_cq(h + 1, 1)
                    else:
                        side.append(kv_tile(h + 1, qt))

        def finalize(qt, po, ac):
            pl = C.ps[5]
            P.op("pool", lambda e: e.tensor_tensor(out=ac[1][:], in0=ac[1][:], in1=ac[0][:], op=ALU.add), reads=[ac[0], ac[1]], writes=[ac[1]])
            P.op("pe", lambda e: e.matmul(pl[:], lhsT=onesf[:], rhs=ac[1][:], start=True, stop=True), reads=[onesf, ac[1]], writes=[pl])
            P.op("act", lambda e: e.activation(out=rl[:], in_=pl[:], func=AF.Ln), reads=[pl], writes=[rl])
            P.op("act", lambda e: e.activation(out=rl[:], in_=rl[:], func=AF.Exp, scale=-1.0), reads=[rl], writes=[rl])
            o = on[qt % 2]
            P.op("dve", lambda e: e.tensor_tensor(out=o[:], in0=po[:], in1=rl[:], op=ALU.mult), reads=[po, rl], writes=[o])
            oap, oT_ = o_dst(h, qt * NT, NT)
            P.dma("sp", oap, o[:], reads=[o], writes=[oT_])
            if qt % 8 == 7:
                head_done(h, qt // 8)

        LOOK = 2
        pending = []

        def qk(j):
            qt_, kb_ = blocks[j]
            if kb_ == 0 and (h, qt_) in qgen:
                gq = qgen.pop((h, qt_))
                while side:
                    g0 = side.pop(0)
                    run(g0)
                    if g0 is gq:
                        break
            emit_qk(j)

        for i in range(min(LOOK, len(blocks))):
            qk(i)
        for i in range(len(blocks)):
            if i + LOOK < len(blocks):
                qk(i + LOOK)
            emit_rest(i)
            while pending and (pending[0][0] <= i or i == len(blocks) - 1):
                pending.pop(0)[1]()
            if side:
                try:
                    next(side[0])
                except StopIteration:
                    side.pop(0)
        for g_ in side:
            run(g_)
        side.clear()


def phase4(P, C, t):
    x5i = t["x5i"]
    load_o = t["load_o"]
    pin = t["pin"]
    cons = t["cons"]
    wbo = t["wbo"]
    win = t["win"]
    wout = t["wout"]
    wpg = t["wpg"]
    wpi = t["wpi"]
    yo = t["yo"]
    ob = P.sbuf("ob", [128, NH_B, G], BF16)
    pbf = P.sbuf("pbf", [128, 2, G], BF16)
    yf = P.sbuf("yf", [128, KC, NT], F32)
    P.dma("sp", C.cons[:, 0:24], cons[:], reads=[cons], writes=[C.cons])
    P.op("dve", lambda e: e.tensor_scalar(out=C.gs[:, 0:24], in0=C.cons[:, 0:24], scalar1=float(np.sqrt(D)), scalar2=None, op0=ALU.mult),
         reads=[C.cons], writes=[C.gs])
    nt = G // NT
    for g in range(TOK // G):
        P.dma("sp", C.x[:], x5i[:, :, g * G:(g + 1) * G], reads=[x5i], writes=[C.x])
        load_o(g, ob)
        i = 0
        for m in range(KC):
            ws = None
            for t in range(nt):
                pb = C.ps[i % 2]
                i += 1
                ws = lin_fm(C, wbo, m, NH_B, t * NT, NT, pb, wslot=ws, src=ob)
                P.op("dve", lambda e, pb=pb, m=m, t=t: e.tensor_tensor(out=C.x[:, m, t * NT:(t + 1) * NT], in0=C.x[:, m, t * NT:(t + 1) * NT], in1=pb[:], op=ALU.add),
                     reads=[pb, C.x], writes=[C.x])
        ffn_group(C, win, wout, 0)
        ple_group(C, wpg, wpi, pin, g, 8, pbf)
        for t in range(nt):
            rmsnorm_fm(C, lambda kc, t=t: C.x[:, kc, t * NT:(t + 1) * NT], C.x, KC, t * NT, NT,
                       lambda kc: C.gs[:, 16 + kc:17 + kc], lambda kc: yf[:, kc, :], yf, C.ps[7], D, src_all=C.x[:, :, t * NT:(t + 1) * NT])
            P.dma("sp", yo[:, :, g * G + t * NT:g * G + (t + 1) * NT], yf[:], reads=[yf], writes=[yo])


def xfm(x):
    return np.ascontiguousarray(np.ascontiguousarray(x.T).reshape(-1, 128, x.shape[0]).transpose(1, 0, 2))


def unfm(a):
    a = np.asarray(a)
    return np.ascontiguousarray(a.transpose(1, 0, 2).reshape(-1, a.shape[2]).T)


def _rope_consts():
    invf = (1.0 / (np.float32(10000.0) ** (np.arange(0, ROPE, 2, dtype=np.float32) / np.float32(ROPE)))).astype(np.float32)
    invf4 = np.tile(invf, 4)
    ph = np.concatenate([np.full(64, np.pi / 2), np.full(32, np.pi), np.zeros(32)]).astype(np.float32)
    return invf4, ph


def _hmask():
    s = np.arange(128)[:, None]
    t = np.arange(128)[None, :]
    m = ((s // 64 == t // 64) & (s <= t)).astype(np.float32)
    return np.ascontiguousarray(np.tile(m, (1, 4)))


_NC_CACHE = {}
_PREP_ONLY = False

W_SHAPES = {
    "ffn00_in": [NJ, 128, KC * 256], "ffn00_out": [KC, 128, NJ * 128], "ffn01_in": [NJ, 128, KC * 256], "ffn01_out": [KC, 128, NJ * 128],
    "ffn10_in": [NJ, 128, KC * 256], "ffn10_out": [KC, 128, NJ * 128], "ffn11_in": [NJ, 128, KC * 256], "ffn11_out": [KC, 128, NJ * 128],
    "wq": [8, 128, D], "wf": [8, 128, D], "wg": [8, 128, D], "wv": [128, KC * D], "wao": [8, 128, D],
    "wpg0": [8, 128, D], "wpi0": [8, 128, 256], "wkd": [3, 128, D], "wdq": [4, 128, D],
    "wuq": [HPC * 2, 128, QL], "wuk": [HPC, 128, KVL], "wuv": [HPC, 128, KVL],
    "wbo": [8, 128, NH_B * 128], "wpg1": [8, 128, D], "wpi1": [8, 128, 256],
    "cons1": [128, 32], "cons2": [128, NC2], "cons3": [128, 8], "cons4": [128, 24],
    "ident": [128, 128], "hmask": [128, 512], "stk": [128, 64], "tri": [128, 128],
    "xin": [128, KC, TOK], "p0": [128, 2, TOK], "p1": [128, 2, TOK],
}


def build_fused(upto=4):
    P = Prog()
    nc = P.nc
    E = {k: P.dram(k, v, F32, "ExternalInput") for k, v in W_SHAPES.items()}
    E["posl"] = P.dram("posl", [1, TOK], I32, "ExternalInput")
    E["posb"] = P.dram("posb", [1, SEQ], I32, "ExternalInput")
    E["idxo"] = P.dram("idxo", [128, 8], I32, "ExternalInput")
    yo = P.dram("yo", [128, KC, TOK], F32, "ExternalOutput")

    def internal(name, shape, dtype):
        return T(nc.dram_tensor(name, list(shape), dtype), name)

    x1 = internal("x1", [128, KC, TOK], F32)
    oT = internal("oT", [NH_A, 128, TOK], F32)
    qc = internal("qc", [NH_A, 128, TOK], BF16)
    sg = internal("sg", [128, KC, TOK], F32)
    SDin = internal("SDin", [128, 1032], F32)
    SDall = internal("SDall", [512, 1032], F32)
    x5 = internal("x5", [128, KC, TOK], F32)
    NTL = TOK // NT
    latin = [internal("latin%d" % i, [128, 7 * NT], BF16) for i in range(NTL)]
    latall = [internal("latall%d" % i, [512, 7 * NT], BF16) for i in range(NTL)]
    latdep = Buf("latall")
    for q in latall:
        q.b = latdep
    ohin = [[internal("ohin%d_%d" % (h, hf), [128, SEQ // 2], BF16) for hf in range(2)] for h in range(HPC)]
    tabin = [internal("tabin%d" % i, [128, G], F32) for i in range(TOK // G)]
    taball = [internal("taball%d" % i, [512, G], F32) for i in range(TOK // G)]

    def tab_done(g):
        P.coll("AllGather", tabin[g].t.ap().opt(), taball[g].t.ap().opt(), reads=[tabin[g]], writes=[taball[g]])

    def tab_src(q8):
        rr, g = q8 // (TOK // G), q8 % (TOK // G)
        return taball[g].t[rr * 128:(rr + 1) * 128, :], taball[g]
    ohall = [internal("ohall%d" % h, [1024, SEQ // 2], BF16) for h in range(HPC)]

    m0 = P.mark()
    C = Ctx(P)
    mC = P.mark()
    STo = T(SDin.t[:, 0:1024].rearrange("p (h d) -> p h d", h=NH_A), "STo")
    DTo = T(SDin.t[:, 1024:1032], "DTo")
    phase1(P, C, {"xin": E["xin"], "win": E["ffn00_in"], "wout": E["ffn00_out"], "wq": E["wq"], "wf": E["wf"], "wg": E["wg"], "wv": E["wv"],
                  "cons": E["cons1"], "ident": E["ident"], "hmask": E["hmask"], "x1o": x1, "oTo": oT, "qco": qc, "sgo": sg, "STo": STo, "DTo": DTo})
    P.reset(mC)
    P.coll("AllGather", SDin.t.ap().opt(), SDall.t.ap().opt(), reads=[STo, DTo], writes=[SDall])
    if upto == 1:
        P.barrier()
        for g in range(TOK // G):
            tmpx = P.sbuf("tmpx1", [128, KC, G], F32)
            P.dma("sp", tmpx[:], x1[:, :, g * G:(g + 1) * G], reads=[x1], writes=[tmpx])
            P.dma("sp", yo[:, :, g * G:(g + 1) * G], tmpx[:], reads=[tmpx], writes=[yo])
        return P.finish()
    STi = TV(lambda r: SDall.t[r * 128:(r + 1) * 128, 0:1024].rearrange("p (h d) -> p h d", h=NH_A), "STi")
    STi.b = SDall.b
    DTi = TV(lambda r: SDall.t[r * 128:(r + 1) * 128, 1024:1032], "DTi")
    DTi.b = SDall.b
    def lat_dst(i):
        v = latin[i].t[:, :].rearrange("p (c t) -> p c t", c=7)
        return v[:, 0:4, :], v[:, 4:6, :], v[0:64, 6, :], latin[i]

    def lat_done(i):
        P.coll("AllGather", latin[i].t.ap().opt(), latall[i].t.ap().opt(), reads=[latin[i]], writes=[latall[i]])

    phase2(P, C, {"x1i": x1, "oTi": oT, "qci": qc, "sgi": sg, "STi": STi, "DTi": DTi, "pin": E["p0"], "pos": E["posl"], "cons": E["cons2"],
                  "stk": E["stk"], "wao": E["wao"], "win0": E["ffn01_in"], "wout0": E["ffn01_out"], "wpg": E["wpg0"], "wpi": E["wpi0"],
                  "wkd": E["wkd"], "win1": E["ffn10_in"], "wout1": E["ffn10_out"], "wdq": E["wdq"],
                  "x5o": x5, "lat_dst": lat_dst, "lat_done": lat_done,
                  "tab_dst": lambda g: (tabin[g].t[:, :], tabin[g]), "tab_done": tab_done})
    P.reset(m0)
    if upto == 2:
        P.barrier()
        for g in range(TOK // G):
            tmpx = P.sbuf("tmpx2", [128, KC, G], F32)
            P.dma("sp", tmpx[:], x5[:, :, g * G:(g + 1) * G], reads=[x5], writes=[tmpx])
            P.dma("sp", yo[:, :, g * G:(g + 1) * G], tmpx[:], reads=[tmpx], writes=[yo])
        return P.finish()
    def latv(q):
        rr, i = q // NTL, q % NTL
        return latall[i].t[rr * 128:(rr + 1) * 128, :].rearrange("p (c t) -> p c t", c=7)

    def head_done(h, hf):
        P.coll("AllGather", ohin[h][hf].t.ap().opt(), ohall[h].t[hf * 512:(hf + 1) * 512, :], reads=[ohin[h][hf]], writes=[ohall[h]])

    def o_dst(h, t0, n):
        hf, tl = t0 // (SEQ // 2), t0 % (SEQ // 2)
        return ohin[h][hf].t[:, tl:tl + n], ohin[h][hf]

    phase3(P, None, {"lat": latall[0], "cq_src": lambda t0, n: latv(t0 // NT)[:, 0:4, :],
                     "ckv_src": lambda kc, q: latv(q)[:, 4 + kc, :], "kr_src": lambda q: latv(q)[0:64, 6, :],
                     "o_dst": o_dst, "head_done": head_done, "tab_src": tab_src,
                     "pos": E["posb"], "cons": E["cons3"], "stk": E["stk"], "tri": E["tri"], "wuq": E["wuq"], "wuk": E["wuk"], "wuv": E["wuv"]})
    P.reset(m0)
    if upto == 3:
        P.barrier()
        for g in range(TOK // G):
            tmpx = P.sbuf("tmpx3", [128, KC, G], F32)
            P.dma("sp", tmpx[:], x5[:, :, g * G:(g + 1) * G], reads=[x5], writes=[tmpx])
            P.dma("sp", yo[:, :, g * G:(g + 1) * G], tmpx[:], reads=[tmpx], writes=[yo])
        return P.finish()
    C = Ctx(P)
    idxs = P.sbuf("idxs", [128, 8], I32)
    P.dma("sp", idxs[:], E["idxo"][:], reads=[E["idxo"]], writes=[idxs])

    def load_o(g, ob):
        for h in range(HPC):
            rows = ohall[h].t[:, :].rearrange("r (b t) -> (r b) t", t=G)
            for rr in range(4):
                P.gather(ob[:, rr * HPC + h, :], rows, idxs[:, rr * 2 + g:rr * 2 + g + 1], reads=[ohall[h], idxs], writes=[ob])

    phase4(P, C, {"x5i": x5, "load_o": load_o, "pin": E["p1"], "cons": E["cons4"], "wbo": E["wbo"], "win": E["ffn11_in"], "wout": E["ffn11_out"],
                  "wpg": E["wpg1"], "wpi": E["wpi1"], "yo": yo})
    print("fused instructions", P.nins)
    return P.finish()


def _get(name, fn):
    if name not in _NC_CACHE:
        _NC_CACHE[name] = fn()
    return _NC_CACHE[name]


def kernel(x, p, positions, norm_gains, ffn_w_in, ffn_w_out, ple_w_gate, ple_w_in,
           a_w_in, a_lb_logits, a_out_gain, a_w_out,
           kv_norm_in, kv_w_down, kv_latent_norm, kv_w_up,
           b_w_dq, b_q_norm, b_w_uq, b_w_out, final_norm):
    f32 = lambda a: np.ascontiguousarray(np.asarray(a, dtype=np.float32))
    x, p, norm_gains, ffn_w_in, ffn_w_out = f32(x), f32(p), f32(norm_gains), f32(ffn_w_in), f32(ffn_w_out)
    ple_w_gate, ple_w_in, a_w_in, a_lb_logits, a_out_gain, a_w_out = map(f32, (ple_w_gate, ple_w_in, a_w_in, a_lb_logits, a_out_gain, a_w_out))
    kv_norm_in, kv_w_down, kv_latent_norm, kv_w_up = map(f32, (kv_norm_in, kv_w_down, kv_latent_norm, kv_w_up))
    b_w_dq, b_q_norm, b_w_uq, b_w_out, final_norm = map(f32, (b_w_dq, b_q_norm, b_w_uq, b_w_out, final_norm))
    positions = np.ascontiguousarray(np.asarray(positions, dtype=np.int32))
    cores = list(range(NCORES))
    invf4, ph = _rope_consts()
    g = norm_gains
    aw = a_w_in[0]
    wd = kv_w_down
    wkd = np.concatenate([wd[:, 0:256], wd[:, 256:320], wd[:, 288:320], wd[:, 256:288]], axis=1)
    cons3 = np.zeros((128, 8), np.float32)
    cons3[:, 0] = invf4
    cons3[:, 1] = ph
    kk = np.arange(128)[:, None]
    qq = np.arange(128)[None, :]
    shared = {
        "ffn00_in": tile_ffn_in(ffn_w_in[0, 0]), "ffn00_out": tile_ffn_out(ffn_w_out[0, 0]),
        "ffn01_in": tile_ffn_in(ffn_w_in[0, 1]), "ffn01_out": tile_ffn_out(ffn_w_out[0, 1]),
        "ffn10_in": tile_ffn_in(ffn_w_in[1, 0]), "ffn10_out": tile_ffn_out(ffn_w_out[1, 0]),
        "ffn11_in": tile_ffn_in(ffn_w_in[1, 1]), "ffn11_out": tile_ffn_out(ffn_w_out[1, 1]),
        "wq": tile_lin(aw[:, 0:1024]), "wf": tile_lin(aw[:, 1024:2048]), "wg": tile_lin(aw[:, 3072:4096]),
        "wv": tile_rows(aw[:, 2048:3072]).reshape(128, -1), "wao": tile_lin(a_w_out[0]),
        "wpg0": tile_lin(ple_w_gate[0]), "wpi0": tile_lin(ple_w_in[0]), "wkd": tile_lin(wkd), "wdq": tile_lin(b_w_dq[0]),
        "wbo": tile_lin(b_w_out[0]), "wpg1": tile_lin(ple_w_gate[1]), "wpi1": tile_lin(ple_w_in[1]),
        "cons1": np.ascontiguousarray(np.concatenate([col(g[0, 0]), col(g[0, 1]), col(a_lb_logits[0]), col(a_lb_logits[1])], axis=1)),
        "cons3": cons3,
        "cons4": np.ascontiguousarray(np.concatenate([col(g[1, 2]), col(g[1, 3]), col(final_norm)], axis=1)),
        "ident": np.eye(128, dtype=np.float32), "hmask": _hmask(),
        "stk": np.concatenate([np.eye(64), np.eye(64)], axis=0).astype(np.float32), "tri": (kk <= qq).astype(np.float32),
    }
    uq = b_w_uq[0]
    per_r = []
    for r in range(4):
        wuq, wuk, wuv = [], [], []
        for h in range(4 * r, 4 * r + 4):
            q = uq[:, h * 192:(h + 1) * 192]
            wuq.append(tile_lin(q[:, 0:128])[0])
            wuq.append(tile_lin(np.concatenate([q[:, 128:192], q[:, 160:192], q[:, 128:160]], axis=1))[0])
            kv = kv_w_up[:, h * 256:(h + 1) * 256]
            wuk.append(tile_lin(kv[:, 0:128])[0])
            wuv.append(tile_rows(kv[:, 128:256]).reshape(128, 256))
        wsel = np.zeros((128, 4), np.float32)
        wsel[:, :r] = 1.0
        cons2 = np.concatenate([col(g[0, 2]), col(g[0, 3]), col(kv_norm_in), col(g[1, 0]), col(g[1, 1]), col(kv_latent_norm),
                                col(b_q_norm[0]), col(a_out_gain[0]), invf4[:, None], ph[:, None], wsel,
                                np.zeros((128, NC2 - 53), np.float32)], axis=1).astype(np.float32)
        idxo = np.zeros((128, 8), np.int32)
        pp = np.arange(128)
        for rr in range(4):
            for gg in range(2):
                tb = r * 2 + gg
                idxo[:, rr * 2 + gg] = ((tb // 4) * 512 + rr * 128 + pp) * 4 + tb % 4
        per_r.append({"wuq": np.stack(wuq), "wuk": np.stack(wuk), "wuv": np.stack(wuv), "cons2": np.ascontiguousarray(cons2), "idxo": idxo})
    in_maps = []
    for c in cores:
        b, r = c // 4, c % 4
        sl = slice(r * TOK, (r + 1) * TOK)
        in_maps.append({"xin": xfm(x[b, sl]), "p0": xfm(p[0, b, sl]), "p1": xfm(p[1, b, sl]),
                        "posl": np.ascontiguousarray(positions[b:b + 1, sl]), "posb": np.ascontiguousarray(positions[b:b + 1]),
                        **shared, **per_r[r]})
    if _PREP_ONLY:
        return in_maps
    res = run_bass_kernel_spmd(_get("fused", build_fused), in_maps, core_ids=cores).results
    out = np.empty((2, SEQ, D), np.float32)
    for c in cores:
        b, r = c // 4, c % 4
        out[b, r * TOK:(r + 1) * TOK] = unfm(res[c]["yo"])
    return out
```

```python
import contextlib
import numpy as np
import ml_dtypes
import concourse.bass as bass
import concourse.mybir as mybir
from concourse.bass_utils import run_bass_kernel_spmd

F32 = mybir.dt.float32
BF16 = mybir.dt.bfloat16
I32 = mybir.dt.int32
AF = mybir.ActivationFunctionType
ALU = mybir.AluOpType

NDSEM = 8
NCORES = 8
D = 1024
KC = 8
DFF = 2816
NJ = 22
SEQ = 8192
TOK = 2048
G = 1024
NT = 512
EPS = 1e-6
NH_A = 8
CH = 64
PLE = 256
NH_B = 16
QL = 512
KVL = 256
ROPE = 64


class Buf:
    __slots__ = ("w", "r", "name", "relaxed", "wl", "pr")

    def __init__(self, name=""):
        self.w = None
        self.r = []
        self.name = name
        self.relaxed = False
        self.wl = {}
        self.pr = []


class T:
    def __init__(self, t, name):
        self.t = t
        self.b = Buf(name)

    def __getitem__(self, idx):
        return self.t[idx]


class TV(T):
    def __init__(self, fn, name):
        self.fn = fn
        self.b = Buf(name)

    def __getitem__(self, idx):
        return self.fn(idx)


def _b(x):
    return x.b if isinstance(x, T) else x


class Prog:
    def __init__(self):
        self.nc = bass.Bass("TRN2", target_bir_lowering=False)
        self.es = contextlib.ExitStack()
        nc = self.nc
        self.eng = {"pe": nc.tensor, "act": nc.scalar, "dve": nc.vector, "pool": nc.gpsimd, "sp": nc.sync}
        self.q = {k: [] for k in self.eng}
        self.cnt = {k: 0 for k in self.eng}
        self.seen = {k: {} for k in self.eng}
        self.esem = {}
        for k in ("pe", "act", "dve", "pool"):
            self.esem[k] = self.es.enter_context(nc.semaphore("es_" + k))
        self.dsem = {}
        self.dcnt = {}
        self.dtok = {}
        for k in ("sp", "act", "pool"):
            self.dsem[k] = [self.es.enter_context(nc.semaphore("ds_%s%d" % (k, i))) for i in range(NDSEM)]
            self.dcnt[k] = 0
            self.dtok[k] = [None] * NDSEM
        self.same_engine_sync = True
        self.nins = 0
        self.defer = None
        self.ccsem = self.es.enter_context(nc.semaphore("ccsem"))
        self.ccn = 0
        self.ARENA = 103936
        self.arena = self.es.enter_context(nc.sbuf_tensor("arena", [128, self.ARENA], BF16))
        self.atop = 0
        self.ps = [self.psum("ps%d" % i, [128, NT], F32) for i in range(8)]

    def dram(self, name, shape, dtype, kind):
        return T(self.nc.dram_tensor(name, list(shape), dtype, kind=kind), name)

    def sbuf(self, name, shape, dtype):
        shape = list(shape)
        n = int(np.prod(shape[1:]))
        nb = n * (4 if dtype in (F32, I32) else 2)
        nb = (nb + 63) // 64 * 64
        assert self.atop + nb <= self.ARENA * 2, "arena overflow: %s needs %d, top %d" % (name, nb, self.atop)
        ap = self.arena[:, self.atop // 2:(self.atop + nb) // 2]
        self.atop += nb
        self.hiwater = max(getattr(self, "hiwater", 0), self.atop)
        if dtype != BF16:
            ap = ap.bitcast(dtype)
        ap = ap[:, 0:n]
        if len(shape) == 3:
            ap = ap.rearrange("p (a b) -> p a b", b=shape[2])
        return T(ap, name)

    def mark(self):
        return self.atop

    def reset(self, m):
        self.barrier()
        print("arena high-water %d of %d bytes" % (getattr(self, "hiwater", 0), self.ARENA * 2))
        self.hiwater = 0
        self.atop = m

    def coll(self, kind, src, dst, reads, writes):
        waits = self._filter("pool", self._deps(reads, writes))
        self.ccn += 1
        tok = (self.ccsem, self.ccn)

        def run(engh, waits=waits, src=src, dst=dst, kind=kind):
            for (s_, v) in waits:
                engh.wait_ge(s_, v)
            engh.collective_compute(kind, ALU.bypass, replica_groups=[[0, 1, 2, 3], [4, 5, 6, 7]], ins=[src], outs=[dst]).then_inc(self.ccsem, 1)

        self.q["pool"].append(run)
        self._commit(tok, reads, writes)
        self.nins += 1
        return tok

    def gather(self, out, in_, idx, reads=(), writes=()):
        qn = "pool"
        n = self.dcnt[qn]
        slot = n % NDSEM
        sem = self.dsem[qn][slot]
        val = 16 * (n // NDSEM + 1)
        deps = self._deps(reads, writes)
        if self.dtok[qn][slot] is not None:
            deps.append(self.dtok[qn][slot])
        waits = self._filter(qn, deps)
        tok = (sem, val)
        self.dtok[qn][slot] = tok
        self.dcnt[qn] += 1

        def run(engh, waits=waits, out=out, in_=in_, idx=idx, sem=sem):
            for (s_, v) in waits:
                engh.wait_ge(s_, v)
            engh.indirect_dma_start(out=out, out_offset=None, in_=in_, in_offset=bass.IndirectOffsetOnAxis(ap=idx, axis=0)).then_inc(sem, 16)

        self.q[qn].append(run)
        self._commit(tok, reads, writes)
        self.nins += 1
        return tok

    def psum(self, name, shape, dtype):
        return T(self.es.enter_context(self.nc.psum_tensor(name, list(shape), dtype)), name)

    def _deps(self, reads, writes, e=None):
        deps = []
        own = self.esem.get(e)

        def add(b, tok):
            if b.relaxed and own is not None and tok[0] is own:
                return
            deps.append(tok)

        for b in reads:
            b = _b(b)
            if b.relaxed:
                for tk in b.wl.values():
                    add(b, tk)
            elif b.w is not None:
                add(b, b.w)
        for b in writes:
            b = _b(b)
            if b.relaxed:
                for tk in (b.r if b.r else b.pr):
                    add(b, tk)
                continue
            if b.w is not None:
                add(b, b.w)
            for tk in b.r:
                add(b, tk)
        return deps

    def _filter(self, e, deps):
        seen = self.seen[e]
        best = {}
        for (s, v) in deps:
            if e == "pe" and s is self.esem["pe"]:
                continue
            if (not self.same_engine_sync) and e in self.esem and s is self.esem[e]:
                continue
            k = id(s)
            if seen.get(k, 0) >= v:
                continue
            if k not in best or best[k][1] < v:
                best[k] = (s, v)
        for k, (s, v) in best.items():
            seen[k] = v
        return list(best.values())

    def _commit(self, tok, reads, writes):
        for b in reads:
            _b(b).r.append(tok)
        for b in writes:
            b = _b(b)
            b.w = tok
            if b.relaxed:
                if b.r:
                    b.wl = {}
                    b.pr = b.r
                k = id(tok[0])
                if k not in b.wl or b.wl[k][1] < tok[1]:
                    b.wl[k] = tok
            b.r = []

    def op(self, e, fn, reads=(), writes=()):
        if self.defer is not None:
            self.defer.append(lambda: self._op(e, fn, reads, writes))
            return None
        return self._op(e, fn, reads, writes)

    def _op(self, e, fn, reads=(), writes=()):
        waits = self._filter(e, self._deps(reads, writes, e))
        sem = self.esem[e]
        self.cnt[e] += 1
        tok = (sem, self.cnt[e])

        def run(engh, waits=waits, fn=fn, sem=sem):
            for (s, v) in waits:
                engh.wait_ge(s, v)
            fn(engh).then_inc(sem, 1)

        self.q[e].append(run)
        self._commit(tok, reads, writes)
        self.nins += 1
        return tok

    def dma(self, qn, out, in_, reads=(), writes=(), **kw):
        if self.defer is not None:
            self.defer.append(lambda: self._dma(qn, out, in_, reads, writes, **kw))
            return None
        return self._dma(qn, out, in_, reads, writes, **kw)

    def run_streams(self, streams):
        idx = [0] * len(streams)
        live = True
        while live:
            live = False
            for i, st in enumerate(streams):
                if idx[i] < len(st):
                    st[idx[i]]()
                    idx[i] += 1
                    live = True

    def _dma(self, qn, out, in_, reads=(), writes=(), **kw):
        n = self.dcnt[qn]
        slot = n % NDSEM
        sem = self.dsem[qn][slot]
        val = 16 * (n // NDSEM + 1)
        deps = self._deps(reads, writes)
        if self.dtok[qn][slot] is not None:
            deps.append(self.dtok[qn][slot])
        waits = self._filter(qn, deps)
        tok = (sem, val)
        self.dtok[qn][slot] = tok
        self.dcnt[qn] += 1

        def run(engh, waits=waits, out=out, in_=in_, sem=sem, kw=kw):
            for (s, v) in waits:
                engh.wait_ge(s, v)
            engh.dma_start(out=out, in_=in_, **kw).then_inc(sem, 16)

        self.q[qn].append(run)
        self._commit(tok, reads, writes)
        self.nins += 1
        return tok

    def _alltoks(self):
        deps = []
        for qn in self.dtok:
            deps.extend([t for t in self.dtok[qn] if t is not None])
        for e in self.esem:
            if self.cnt[e]:
                deps.append((self.esem[e], self.cnt[e]))
        return deps

    def barrier(self):
        deps = self._alltoks()
        for e in ("pe", "act", "dve", "pool", "sp"):
            waits = self._filter(e, deps)
            if waits:
                def run(engh, waits=waits):
                    for (s, v) in waits:
                        engh.wait_ge(s, v)
                self.q[e].append(run)

    def finish(self):
        waits = self._filter("sp", self._alltoks())

        def run(engh, waits=waits):
            for (s, v) in waits:
                engh.wait_ge(s, v)

        self.q["sp"].append(run)
        nc = self.nc
        with nc.Block() as block:
            for k, name in (("sp", "sync"), ("pe", "tensor"), ("act", "scalar"), ("dve", "vector"), ("pool", "gpsimd")):
                if not self.q[k]:
                    continue

                def body(engh, fns=self.q[k]):
                    for f in fns:
                        f(engh)

                getattr(block, name)(body)
        self.es.close()
        return nc


class Ctx:
    def __init__(self, P, ncons=56):
        self.P = P
        self.ps = P.ps
        self.cons = P.sbuf("cons", [128, ncons], F32)
        self.gs = P.sbuf("gs", [128, ncons], F32)
        self.ones = P.sbuf("ones", [128, 128], BF16)
        self.x = P.sbuf("x", [128, KC, G], F32)
        self.h = P.sbuf("h", [128, KC, G], BF16)
        self.sq = P.sbuf("sq", [128, KC, NT], BF16)
        self.rstd = P.sbuf("rstd", [128, NT], F32)
        self.rstd2 = P.sbuf("rstd2", [128, NT], F32)
        self.su = [P.sbuf("su%d" % i, [128, NT], F32) for i in range(2)]
        self.act = P.sbuf("act", [128, NJ, G], BF16)
        self.wsl = [P.sbuf("wsl%d" % i, [128, 3072], BF16) for i in range(4)]
        self.wi = 0
        self.sg = [P.sbuf("sg%d" % i, [128, NT], F32) for i in range(2)]
        P.op("pool", lambda e: e.memset(self.ones[:], 1.0), writes=[self.ones])
        for tt in (self.x, self.h, self.act, self.sq):
            tt.b.relaxed = True

    def wslot(self):
        s = self.wsl[self.wi % 4]
        self.wi += 1
        return s


def rmsnorm_fm(C, src, srcT, kcn, n0, n, gcol, dst, dstT, pb, dn, src_all=None):
    P = C.P
    sq = C.sq
    if src_all is not None:
        P.op("act", lambda e: e.activation(out=sq[:, 0:kcn, 0:n], in_=src_all, func=AF.Square), reads=[srcT], writes=[sq])
    else:
        for kc in range(kcn):
            P.op("act", lambda e, kc=kc: e.activation(out=sq[:, kc, 0:n], in_=src(kc), func=AF.Square),
                 reads=[srcT], writes=[sq])
    for kc in range(kcn):
        P.op("pe", lambda e, kc=kc: e.matmul(pb[:, 0:n], lhsT=C.ones[:], rhs=sq[:, kc, 0:n],
                                             start=(kc == 0), stop=(kc == kcn - 1)),
             reads=[C.ones, sq], writes=[pb])
    P.op("act", lambda e: e.activation(out=C.rstd[:, 0:n], in_=pb[:, 0:n], func=AF.Ln, bias=float(dn * EPS)),
         reads=[pb], writes=[C.rstd])
    P.op("act", lambda e: e.activation(out=C.rstd[:, 0:n], in_=C.rstd[:, 0:n], func=AF.Exp, scale=-0.5),
         reads=[C.rstd], writes=[C.rstd])
    for kc in range(kcn):
        if False and kcn == KC and kc % 2 == 1:
            tmp = C.sg[(kc // 2) % 2]
            P.op("act", lambda e, kc=kc, tmp=tmp: e.activation(out=tmp[:, 0:n], in_=src(kc), func=AF.Identity, scale=gcol(kc)),
                 reads=[srcT, C.gs], writes=[tmp])
            P.op("pool", lambda e, kc=kc, tmp=tmp: e.tensor_tensor(out=dst(kc), in0=tmp[:, 0:n], in1=C.rstd[:, 0:n], op=ALU.mult),
                 reads=[tmp, C.rstd], writes=[dstT])
        else:
            P.op("dve", lambda e, kc=kc: e.scalar_tensor_tensor(out=dst(kc), in0=src(kc), scalar=gcol(kc),
                                                                in1=C.rstd[:, 0:n], op0=ALU.mult, op1=ALU.mult),
                 reads=[srcT, C.gs, C.rstd], writes=[dstT])


def ffn_group(C, win, wout, gcol0, nt=G // NT):
    P = C.P
    rstds = [C.rstd, C.rstd2]
    for t in range(nt):
        sl = slice(t * NT, (t + 1) * NT)
        for kc in range(KC):
            if kc % 2 == 0:
                P.op("dve", lambda e, kc=kc, sl=sl: e.tensor_scalar(out=C.h[:, kc, sl], in0=C.x[:, kc, sl], scalar1=C.gs[:, gcol0 + kc:gcol0 + kc + 1],
                                                                   scalar2=None, op0=ALU.mult), reads=[C.x, C.gs], writes=[C.h])
            else:
                P.op("act", lambda e, kc=kc, sl=sl: e.activation(out=C.h[:, kc, sl], in_=C.x[:, kc, sl], func=AF.Identity,
                                                                 scale=C.gs[:, gcol0 + kc:gcol0 + kc + 1]), reads=[C.x, C.gs], writes=[C.h])
    for t in range(nt):
        sl = slice(t * NT, (t + 1) * NT)
        rs = rstds[t % 2]
        pb = C.ps[6 + t % 2]
        P.op("act", lambda e, sl=sl: e.activation(out=C.sq[:, :, 0:NT], in_=C.x[:, :, sl], func=AF.Square), reads=[C.x], writes=[C.sq])
        for kc in range(KC):
            P.op("pe", lambda e, kc=kc, pb=pb: e.matmul(pb[:], lhsT=C.ones[:], rhs=C.sq[:, kc, 0:NT], start=(kc == 0), stop=(kc == KC - 1)),
                 reads=[C.ones, C.sq], writes=[pb])
        P.op("act", lambda e, pb=pb, rs=rs: e.activation(out=rs[:], in_=pb[:], func=AF.Ln, bias=float(D * EPS)), reads=[pb], writes=[rs])
        P.op("act", lambda e, rs=rs: e.activation(out=rs[:], in_=rs[:], func=AF.Exp, scale=-0.5), reads=[rs], writes=[rs])
    pi = 0
    for j in range(NJ):
        ws = C.wslot()
        P.dma("pool", ws[:, 0:2048], win[j], reads=[win], writes=[ws])
        for t in range(nt):
            pg = C.ps[(pi % 2) * 2]
            pu = C.ps[(pi % 2) * 2 + 1]
            sg = C.sg[pi % 2]
            su = C.su[pi % 2]
            rs = rstds[t % 2]
            pi += 1
            for half, pb in ((0, pg), (1, pu)):
                for kc in range(KC):
                    P.op("pe", lambda e, kc=kc, half=half, pb=pb, ws=ws, t=t: e.matmul(
                        pb[:], lhsT=ws[:, kc * 256 + half * 128:kc * 256 + half * 128 + 128],
                        rhs=C.h[:, kc, t * NT:(t + 1) * NT], start=(kc == 0), stop=(kc == KC - 1)),
                        reads=[ws, C.h], writes=[pb])
            P.op("dve", lambda e, pg=pg, sg=sg, rs=rs: e.tensor_tensor(out=sg[:], in0=pg[:], in1=rs[:], op=ALU.mult), reads=[pg, rs], writes=[sg])
            P.op("act", lambda e, sg=sg: e.activation(out=sg[:], in_=sg[:], func=AF.Silu), reads=[sg], writes=[sg])
            P.op("dve", lambda e, pu=pu, su=su, rs=rs: e.tensor_tensor(out=su[:], in0=pu[:], in1=rs[:], op=ALU.mult), reads=[pu, rs], writes=[su])
            P.op("dve", lambda e, su=su, sg=sg, j=j, t=t: e.tensor_tensor(
                out=C.act[:, j, t * NT:(t + 1) * NT], in0=sg[:], in1=su[:], op=ALU.mult),
                reads=[sg, su], writes=[C.act])
    for m in range(KC):
        ws = C.wslot()
        P.dma("pool", ws[:, 0:NJ * 128], wout[m], reads=[wout], writes=[ws])
        for t in range(nt):
            pb = C.ps[4 + (pi % 2)]
            pi += 1
            for j in range(NJ):
                P.op("pe", lambda e, j=j, pb=pb, ws=ws, t=t: e.matmul(
                    pb[:], lhsT=ws[:, j * 128:(j + 1) * 128], rhs=C.act[:, j, t * NT:(t + 1) * NT],
                    start=(j == 0), stop=(j == NJ - 1)), reads=[ws, C.act], writes=[pb])
            P.op("dve", lambda e, pb=pb, m=m, t=t: e.scalar_tensor_tensor(
                out=C.x[:, m, t * NT:(t + 1) * NT], in0=pb[:], scalar=0.5, in1=C.x[:, m, t * NT:(t + 1) * NT],
                op0=ALU.mult, op1=ALU.add), reads=[pb, C.x], writes=[C.x])


def tile_ffn_in(w):
    wg = w[:, :DFF].reshape(KC, 128, NJ, 128)
    wu = w[:, DFF:].reshape(KC, 128, NJ, 128)
    s = np.stack([wg, wu], axis=3)
    return np.ascontiguousarray(s.transpose(2, 1, 0, 3, 4).reshape(NJ, 128, KC * 256))


def tile_ffn_out(w):
    return np.ascontiguousarray(w.reshape(NJ, 128, KC, 128).transpose(2, 1, 0, 3).reshape(KC, 128, NJ * 128))


def tile_lin(w):
    k, n = w.shape
    return np.ascontiguousarray(w.reshape(k // 128, 128, n // 128, 128).transpose(2, 1, 0, 3).reshape(n // 128, 128, k))


def tile_rows(w):
    k, n = w.shape
    return np.ascontiguousarray(w.reshape(k // 128, 128, n).transpose(1, 0, 2))


def col(v):
    return np.ascontiguousarray(v.reshape(-1, 128).T)


def fm(xtok):
    return np.ascontiguousarray(xtok.T)


def lin_fm(C, wdram, chunk, kcn, t0, n, pb, wslot=None, src=None):
    P = C.P
    src = C.h if src is None else src
    if wslot is None:
        wslot = C.wslot()
        P.dma("pool", wslot[:, 0:kcn * 128], wdram[chunk], reads=[wdram], writes=[wslot])
    for kc in range(kcn):
        P.op("pe", lambda e, kc=kc: e.matmul(pb[:, 0:n], lhsT=wslot[:, kc * 128:(kc + 1) * 128],
                                             rhs=src[:, kc, t0:t0 + n], start=(kc == 0), stop=(kc == kcn - 1)),
             reads=[wslot, src], writes=[pb])
    return wslot


def phase1(P, C, t, stage=99):
    NC1 = 32
    xin = t["xin"]
    win = t["win"]
    wout = t["wout"]
    wq = t["wq"]
    wf = t["wf"]
    wg = t["wg"]
    wv = t["wv"]
    cons = t["cons"]
    identd = t["ident"]
    hmaskd = t["hmask"]
    x1o = t["x1o"]
    oTo = t["oTo"]
    qco = t["qco"]
    sgo = t["sgo"]
    STo = t["STo"]
    DTo = t["DTo"]
    ident = P.sbuf("identb", [128, 128], BF16)
    hmask = P.sbuf("hmaskb", [128, 512], F32)
    wvs = P.sbuf("wvs", [128, KC, D], BF16)
    vtok = P.sbuf("vtok", [128, G // 128, D], BF16)
    lb = P.sbuf("lb", [128, 8], F32)
    oml = P.sbuf("oml", [128, 8], F32)
    noml = P.sbuf("noml", [128, 8], F32)
    rmask = P.sbuf("rmask", [128, NT], F32)
    zer8 = P.sbuf("zer8", [128, 8], F32)
    actf = C.act[:].rearrange("p j t -> p (j t)").bitcast(F32)
    scr = [[T(actf[:, (s_ * 7 + i) * NT:(s_ * 7 + i + 1) * NT], "hs%d_%d" % (s_, i)) for i in range(7)] for s_ in range(2)]
    qt = [P.sbuf("qt%d" % i, [128, NT], BF16) for i in range(2)]
    kt = [P.sbuf("kt%d" % i, [128, NT], BF16) for i in range(2)]
    kh = [P.sbuf("kh%d" % i, [128, NT], BF16) for i in range(2)]
    khT = [[P.sbuf("khT%d_%d" % (i, par), [128, NT], BF16) for par in range(2)] for i in range(2)]
    scm = [P.sbuf("scm%d" % i, [128, NT], BF16) for i in range(2)]
    qc = [P.sbuf("qc%d" % i, [128, NT], BF16) for i in range(2)]
    oev = [P.sbuf("oev%d" % i, [128, NT], F32) for i in range(2)]
    sgs = C.sg
    dch = [P.sbuf("dch%d" % i, [128, 8], F32) for i in range(2)]
    S = [P.sbuf("S%d" % i, [128, 128], F32) for i in range(NH_A)]
    Sb = [[P.sbuf("Sb%d_%d" % (i, c), [128, 128], BF16) for c in range(9)] for i in range(2)]
    Dexc = [P.sbuf("Dexc%d" % i, [128, 9], F32) for i in range(NH_A)]
    Slast = [P.sbuf("Slast%d" % i, [128, 128], BF16) for i in range(NH_A)]

    P.dma("sp", C.cons[:, 0:NC1], cons[:], reads=[cons], writes=[C.cons])
    P.dma("pool", ident[:], identd[:], writes=[ident])
    P.dma("sp", hmask[:], hmaskd[:], writes=[hmask])
    P.op("dve", lambda e: e.tensor_scalar(out=C.gs[:, 0:NC1], in0=C.cons[:, 0:NC1], scalar1=float(np.sqrt(D)), scalar2=None, op0=ALU.mult),
         reads=[C.cons], writes=[C.gs])
    P.op("dve", lambda e: e.tensor_tensor(out=lb[:], in0=C.cons[:, 16:24], in1=C.cons[:, 24:32], op=ALU.subtract), reads=[C.cons], writes=[lb])
    P.op("act", lambda e: e.activation(out=oml[:], in_=lb[:], func=AF.Sigmoid, scale=-1.0), reads=[lb], writes=[oml])
    P.op("act", lambda e: e.activation(out=lb[:], in_=lb[:], func=AF.Sigmoid), reads=[lb], writes=[lb])
    P.op("dve", lambda e: e.tensor_scalar(out=noml[:], in0=oml[:], scalar1=-1.0, scalar2=None, op0=ALU.mult), reads=[oml], writes=[noml])
    P.op("pool", lambda e: e.memset(rmask[:], 1.0), writes=[rmask])
    P.op("pool", lambda e: e.memset(rmask[:].rearrange("p (c t) -> p c t", t=CH)[:, :, 0:1], 0.0), writes=[rmask])
    P.op("pool", lambda e: e.memset(zer8[:], 0.0), writes=[zer8])
    for i in range(2):
        for par in range(2):
            P.op("pool", lambda e, i=i, par=par: e.memset(khT[i][par][:], 0.0), writes=[khT[i][par]])
    for hd in range(NH_A):
        P.op("pool", lambda e, hd=hd: e.memset(S[hd][:], 0.0), writes=[S[hd]])
        P.op("pool", lambda e, hd=hd: e.memset(Slast[hd][:], 0.0), writes=[Slast[hd]])
        P.op("pool", lambda e, hd=hd: e.memset(Dexc[hd][:], 1.0), writes=[Dexc[hd]])

    it = 0
    for g in range(TOK // G):
        P.dma("sp", C.x[:], xin[:, :, g * G:(g + 1) * G], reads=[xin], writes=[C.x])
        ffn_group(C, win, wout, 0)
        if g < TOK // G - 1:
            P.dma("sp", x1o[:, :, g * G:(g + 1) * G], C.x[:], reads=[C.x], writes=[x1o])
        P.barrier()
        for t in range(G // NT):
            rmsnorm_fm(C, lambda kc, t=t: C.x[:, kc, t * NT:(t + 1) * NT], C.x, KC, t * NT, NT,
                       lambda kc: C.gs[:, 8 + kc:9 + kc],
                       lambda kc, t=t: C.h[:, kc, t * NT:(t + 1) * NT], C.h, C.ps[7], D, src_all=C.x[:, :, t * NT:(t + 1) * NT])
        if g == 0:
            for kc in range(KC):
                P.dma("pool", wvs[:, kc, :], wv[:, kc * D:(kc + 1) * D], reads=[wv], writes=[wvs])
        for tb in range(G // 128):
            for half in range(2):
                pb = C.ps[(tb * 2 + half) % 4]
                for kc in range(KC):
                    P.op("pe", lambda e, kc=kc, tb=tb, half=half, pb=pb: e.matmul(
                        pb[:], lhsT=C.h[:, kc, tb * 128:(tb + 1) * 128], rhs=wvs[:, kc, half * 512:(half + 1) * 512],
                        start=(kc == 0), stop=(kc == KC - 1)), reads=[C.h, wvs], writes=[pb])
                P.op("act", lambda e, tb=tb, half=half, pb=pb: e.copy(out=vtok[:, tb, half * 512:(half + 1) * 512], in_=pb[:]),
                     reads=[pb], writes=[vtok])
        if stage < 2:
            continue
        for c in range(8):
            ws = None
            for t in range(G // NT):
                pb = C.ps[(c * 2 + t) % 4]
                ws = lin_fm(C, wg, c, KC, t * NT, NT, pb, wslot=ws)
                sg = sgs[(c * 2 + t) % 2]
                P.op("act", lambda e, pb=pb, sg=sg: e.activation(out=sg[:], in_=pb[:], func=AF.Sigmoid), reads=[pb], writes=[sg])
                P.dma("sp", sgo[:, c, g * G + t * NT:g * G + (t + 1) * NT], sg[:], reads=[sg], writes=[sgo])
        if stage < 3:
            continue
        def front(hd, t, i2):
            sig, fv, kk, bb, eb, enb, ek = scr[i2]
            tok0 = g * G + t * NT
            pq = C.ps[4 * i2]
            pf = C.ps[4 * i2 + 1]
            wsq = C.wsl[2 * i2]
            wsf = C.wsl[2 * i2 + 1]
            if t == 0 and hd < 2:
                P.dma("pool", wsq[:, 0:KC * 128], wq[hd], reads=[wq], writes=[wsq])
                P.dma("pool", wsf[:, 0:KC * 128], wf[hd], reads=[wf], writes=[wsf])
            lin_fm(C, wq, hd, KC, t * NT, NT, pq, wslot=wsq)
            lin_fm(C, wf, hd, KC, t * NT, NT, pf, wslot=wsf)
            if t == G // NT - 1 and hd + 2 < NH_A:
                P.dma("pool", wsq[:, 0:KC * 128], wq[hd + 2], reads=[wq], writes=[wsq])
                P.dma("pool", wsf[:, 0:KC * 128], wf[hd + 2], reads=[wf], writes=[wsf])
            c3 = lambda a: a[:].rearrange("p (c t) -> p c t", t=CH)
            P.op("act", lambda e, pf=pf: e.activation(out=sig[:], in_=pf[:], func=AF.Sigmoid), reads=[pf], writes=[sig])
            P.op("dve", lambda e, hd=hd: e.tensor_scalar(out=fv[:], in0=sig[:], scalar1=oml[:, hd:hd + 1], scalar2=lb[:, hd:hd + 1],
                                                       op0=ALU.mult, op1=ALU.add), reads=[sig, oml, lb], writes=[fv])
            P.op("act", lambda e: e.activation(out=fv[:], in_=fv[:], func=AF.Ln), reads=[fv], writes=[fv])
            P.op("dve", lambda e, hd=hd: e.tensor_scalar(out=kk[:], in0=sig[:], scalar1=noml[:, hd:hd + 1], scalar2=oml[:, hd:hd + 1],
                                                       op0=ALU.mult, op1=ALU.add), reads=[sig, noml, oml], writes=[kk])
            P.op("dve", lambda e: e.tensor_tensor_scan(out=bb[:], data0=rmask[:], data1=fv[:], initial=0.0, op0=ALU.mult, op1=ALU.add),
                 reads=[rmask, fv], writes=[bb])
            P.op("act", lambda e: e.activation(out=eb[:], in_=bb[:], func=AF.Exp), reads=[bb], writes=[eb])
            P.op("act", lambda e: e.activation(out=enb[:], in_=bb[:], func=AF.Exp, scale=-1.0), reads=[bb], writes=[enb])
            P.op("dve", lambda e, pq=pq, i2=i2: e.scalar_tensor_tensor(out=qt[i2][:], in0=pq[:], scalar=float(128 ** -0.5), in1=eb[:],
                                                                      op0=ALU.mult, op1=ALU.mult), reads=[pq, eb], writes=[qt[i2]])
            P.op("pool", lambda e, i2=i2: e.tensor_tensor(out=kt[i2][:], in0=kk[:], in1=enb[:], op=ALU.mult), reads=[kk, enb], writes=[kt[i2]])
            P.op("dve", lambda e: e.tensor_tensor(out=c3(ek), in0=c3(bb)[:, :, CH - 1:CH].to_broadcast([128, 8, CH]), in1=c3(bb),
                                                  op=ALU.subtract), reads=[bb], writes=[ek])
            P.op("act", lambda e: e.activation(out=ek[:], in_=ek[:], func=AF.Exp), reads=[ek], writes=[ek])
            P.op("pool", lambda e, i2=i2: e.tensor_tensor(out=kh[i2][:], in0=kk[:], in1=ek[:], op=ALU.mult), reads=[kk, ek], writes=[kh[i2]])
            P.op("act", lambda e, i2=i2: e.activation(out=dch[i2][:], in_=c3(bb)[:, :, CH - 1], func=AF.Exp), reads=[bb], writes=[dch[i2]])
            P.op("dve", lambda e, hd=hd, i2=i2: e.tensor_tensor_scan(out=Dexc[hd][:, 1:9], data0=dch[i2][:], data1=zer8[:],
                                                                     initial=Dexc[hd][:, 0:1], op0=ALU.mult, op1=ALU.add),
                 reads=[dch[i2], zer8, Dexc[hd]], writes=[Dexc[hd]])
            P.op("dve", lambda e, hd=hd, i2=i2: e.tensor_tensor(out=c3(qc[i2]), in0=c3(qt[i2]),
                                                                in1=Dexc[hd][:, 0:8].unsqueeze(2).to_broadcast([128, 8, CH]), op=ALU.mult),
                 reads=[qt[i2], Dexc[hd]], writes=[qc[i2]])
            P.dma("sp", qco[hd, :, tok0:tok0 + NT], qc[i2][:], reads=[qc[i2]], writes=[qco])
            P.op("dve", lambda e, hd=hd: e.tensor_copy(out=Dexc[hd][:, 0:1], in_=Dexc[hd][:, 8:9]), reads=[Dexc[hd]], writes=[Dexc[hd]])


        def back(hd, t, i2):
            tok0 = g * G + t * NT
            ptr = C.ps[4 * i2 + 2]
            ptrb = ptr[:].bitcast(BF16)
            for pr in range(4):
                P.op("pe", lambda e, pr=pr, i2=i2: e.transpose(ptrb[:, pr * 128:(pr + 1) * 128], kh[i2][:, pr * 128:(pr + 1) * 128], ident[:]),
                     reads=[kh[i2], ident], writes=[ptr])
            for par in range(2):
                P.op("act", lambda e, i2=i2, par=par: e.copy(out=khT[i2][par][par * 64:(par + 1) * 64, :], in_=ptrb[par * 64:(par + 1) * 64, 0:NT]),
                     reads=[ptr], writes=[khT[i2][par]])
            psc = C.ps[4 * i2 + 2]
            for pr in range(4):
                P.op("pe", lambda e, pr=pr, i2=i2: e.matmul(psc[:, pr * 128:(pr + 1) * 128], lhsT=kt[i2][:, pr * 128:(pr + 1) * 128],
                                                            rhs=qt[i2][:, pr * 128:(pr + 1) * 128], start=True, stop=True),
                     reads=[kt[i2], qt[i2]], writes=[psc])
            P.op("dve", lambda e, i2=i2: e.tensor_tensor(out=scm[i2][:], in0=psc[:], in1=hmask[:], op=ALU.mult), reads=[psc, hmask], writes=[scm[i2]])
            pU = C.ps[4 * i2 + 3]
            P.op("pool", lambda e, i2=i2, hd=hd: e.tensor_copy(out=Sb[i2][0][:], in_=Slast[hd][:]), reads=[Slast[hd]], writes=[Sb[i2][0]])
            for half in range(2):
                for c4 in range(4):
                    c = half * 4 + c4
                    pr, par = c // 2, c % 2
                    tb = t * 4 + pr
                    P.op("pe", lambda e, c4=c4, pr=pr, par=par, tb=tb, i2=i2, hd=hd: e.matmul(
                        pU[:, c4 * 128:(c4 + 1) * 128], lhsT=khT[i2][par][:, pr * 128:(pr + 1) * 128],
                        rhs=vtok[:, tb, hd * 128:(hd + 1) * 128], start=True, stop=True),
                        reads=[khT[i2][par], vtok], writes=[pU])
                for c4 in range(4):
                    c = half * 4 + c4
                    P.op("dve", lambda e, c=c, c4=c4, i2=i2, hd=hd: e.scalar_tensor_tensor(
                        out=S[hd][:], in0=S[hd][:], scalar=dch[i2][:, c:c + 1], in1=pU[:, c4 * 128:(c4 + 1) * 128],
                        op0=ALU.mult, op1=ALU.add), reads=[S[hd], dch[i2], pU], writes=[S[hd]])
                    dst = Sb[i2][c + 1] if c < 7 else Slast[hd]
                    P.op("pool", lambda e, dst=dst, hd=hd: e.tensor_copy(out=dst[:], in_=S[hd][:]), reads=[S[hd]], writes=[dst])
            po = C.ps[4 * i2 + 2]
            for pr in range(4):
                tb = t * 4 + pr
                P.op("pe", lambda e, pr=pr, tb=tb, i2=i2, hd=hd: e.matmul(po[:, pr * 128:(pr + 1) * 128], lhsT=vtok[:, tb, hd * 128:(hd + 1) * 128],
                                                                          rhs=scm[i2][:, pr * 128:(pr + 1) * 128], start=True, stop=False),
                     reads=[vtok, scm[i2]], writes=[po])
                for par in range(2):
                    c = pr * 2 + par
                    P.op("pe", lambda e, c=c, i2=i2: e.matmul(po[:, c * 64:(c + 1) * 64], lhsT=Sb[i2][c][:], rhs=qt[i2][:, c * 64:(c + 1) * 64],
                                                              start=False, stop=(c % 2 == 1)), reads=[Sb[i2][c], qt[i2]], writes=[po])
            P.op("act", lambda e, i2=i2: e.copy(out=oev[i2][:], in_=po[:]), reads=[po], writes=[oev[i2]])
            P.dma("sp", oTo[hd, :, tok0:tok0 + NT], oev[i2][:], reads=[oev[i2]], writes=[oTo])


        streams = []
        for s_ in range(2):
            P.defer = []
            for hd in range(s_, NH_A, 2):
                for t in range(G // NT):
                    front(hd, t, s_)
                    back(hd, t, s_)
            streams.append(P.defer)
            P.defer = None
        P.run_streams(streams)
        P.barrier()
    for hd in range(NH_A):
        P.dma("sp", STo[:, hd, :], S[hd][:], reads=[S[hd]], writes=[STo])
        P.op("dve", lambda e, hd=hd: e.tensor_copy(out=zer8[:, hd:hd + 1], in_=Dexc[hd][:, 0:1]), reads=[Dexc[hd]], writes=[zer8])
    P.dma("sp", DTo[:], zer8[:], reads=[zer8], writes=[DTo])


NC2 = 56


def ple_group(C, wgate, wple, pin, g, gcol0, pbf):
    P = C.P
    nt = G // NT
    for t in range(nt):
        rmsnorm_fm(C, lambda kc, t=t: C.x[:, kc, t * NT:(t + 1) * NT], C.x, KC, t * NT, NT,
                   lambda kc: C.gs[:, gcol0 + kc:gcol0 + kc + 1],
                   lambda kc, t=t: C.h[:, kc, t * NT:(t + 1) * NT], C.h, C.ps[7], D, src_all=C.x[:, :, t * NT:(t + 1) * NT])
    for kc in range(2):
        P.dma("pool", pbf[:, kc, :], pin[:, kc, g * G:(g + 1) * G], reads=[pin], writes=[pbf])
    i = 0
    for m in range(KC):
        wsg = None
        wsp = None
        for t in range(nt):
            pg = C.ps[(i % 2) * 2]
            pp = C.ps[(i % 2) * 2 + 1]
            sg = C.sg[i % 2]
            i += 1
            wsg = lin_fm(C, wgate, m, KC, t * NT, NT, pg, wslot=wsg)
            wsp = lin_fm(C, wple, m, 2, t * NT, NT, pp, wslot=wsp, src=pbf)
            P.op("act", lambda e, pg=pg, sg=sg: e.activation(out=sg[:], in_=pg[:], func=AF.Sigmoid), reads=[pg], writes=[sg])
            P.op("dve", lambda e, pp=pp, sg=sg: e.tensor_tensor(out=sg[:], in0=sg[:], in1=pp[:], op=ALU.mult), reads=[sg, pp], writes=[sg])
            P.op("dve", lambda e, sg=sg, m=m, t=t: e.tensor_tensor(out=C.x[:, m, t * NT:(t + 1) * NT], in0=C.x[:, m, t * NT:(t + 1) * NT],
                                                                 in1=sg[:], op=ALU.add), reads=[sg, C.x], writes=[C.x])


def rope_table(C, posd, tok0, n, tab, tmpf, tmpi, invcol, phcol):
    P = C.P
    P.dma("sp", tmpi[:, 0:n], posd[0:1, tok0:tok0 + n].to_broadcast([128, n]), reads=[posd], writes=[tmpi])
    P.op("dve", lambda e: e.tensor_copy(out=tab[:, 0:n], in_=tmpi[:, 0:n]), reads=[tmpi], writes=[tab])
    P.op("dve", lambda e: e.tensor_scalar(out=tab[:, 0:n], in0=tab[:, 0:n], scalar1=C.cons[:, invcol:invcol + 1],
                                          scalar2=C.cons[:, phcol:phcol + 1], op0=ALU.mult, op1=ALU.add), reads=[tab, C.cons], writes=[tab])
    P.op("dve", lambda e: e.tensor_scalar(out=tmpf[:, 0:n], in0=tab[:, 0:n], scalar1=float(1 / (2 * np.pi)), scalar2=None, op0=ALU.mult),
         reads=[tab], writes=[tmpf])
    P.op("dve", lambda e: e.tensor_copy(out=tmpi[:, 0:n], in_=tmpf[:, 0:n]), reads=[tmpf], writes=[tmpi])
    P.op("dve", lambda e: e.tensor_copy(out=tmpf[:, 0:n], in_=tmpi[:, 0:n]), reads=[tmpi], writes=[tmpf])
    P.op("dve", lambda e: e.scalar_tensor_tensor(out=tab[:, 0:n], in0=tmpf[:, 0:n], scalar=float(-2 * np.pi), in1=tab[:, 0:n],
                                                 op0=ALU.mult, op1=ALU.add), reads=[tmpf, tab], writes=[tab])
    P.op("dve", lambda e: e.tensor_scalar(out=tmpf[:, 0:n], in0=tab[:, 0:n], scalar1=float(np.pi), scalar2=float(-2 * np.pi),
                                          op0=ALU.is_gt, op1=ALU.mult), reads=[tab], writes=[tmpf])
    P.op("dve", lambda e: e.tensor_tensor(out=tab[:, 0:n], in0=tab[:, 0:n], in1=tmpf[:, 0:n], op=ALU.add), reads=[tab, tmpf], writes=[tab])
    P.op("dve", lambda e: e.tensor_scalar(out=tmpf[:, 0:n], in0=tab[:, 0:n], scalar1=float(-np.pi), scalar2=float(2 * np.pi),
                                          op0=ALU.is_lt, op1=ALU.mult), reads=[tab], writes=[tmpf])
    P.op("dve", lambda e: e.tensor_tensor(out=tab[:, 0:n], in0=tab[:, 0:n], in1=tmpf[:, 0:n], op=ALU.add), reads=[tab, tmpf], writes=[tab])
    P.op("act", lambda e: e.activation(out=tab[:, 0:n], in_=tab[:, 0:n], func=AF.Sin), reads=[tab], writes=[tab])


def phase2(P, C, t, dbg=False):
    x1i = t["x1i"]
    oTi = t["oTi"]
    qci = t["qci"]
    sgi = t["sgi"]
    STi = t["STi"]
    DTi = t["DTi"]
    pin = t["pin"]
    posd = t["pos"]
    cons = t["cons"]
    stk = t["stk"]
    wao = t["wao"]
    win0 = t["win0"]
    wout0 = t["wout0"]
    wpg = t["wpg"]
    wpi = t["wpi"]
    wkd = t["wkd"]
    win1 = t["win1"]
    wout1 = t["wout1"]
    wdq = t["wdq"]
    x5o = t["x5o"]
    lat_dst = t["lat_dst"]
    lat_done = t["lat_done"]
    tab_dst = t["tab_dst"]
    tab_done = t["tab_done"]
    Sst = P.sbuf("Sst", [128, NH_A, 128], F32)
    Sstb = P.sbuf("Sstb", [128, NH_A, 128], BF16)
    Ul = P.sbuf("Ul", [128, NH_A, 128], F32)
    Dl = P.sbuf("Dl", [128, NH_A], F32)
    ot = [P.sbuf("ot%d" % i, [128, NT], F32) for i in range(2)]
    qcl = [P.sbuf("qcl%d" % i, [128, NT], BF16) for i in range(2)]
    sgl = [P.sbuf("sgl%d" % i, [128, NT], F32) for i in range(2)]
    pbf = P.sbuf("pbf", [128, 2, G], BF16)
    latf = P.sbuf("latf", [128, 4, NT], F32)
    latb = P.sbuf("latb", [128, 4, NT], BF16)
    tab = P.sbuf("tab", [128, G], F32)
    tmpf = P.sbuf("tmpf", [128, G], F32)
    tmpi = P.sbuf("tmpi", [128, G], I32)
    stkb = P.sbuf("stkb", [128, 64], BF16)
    rT = P.sbuf("rT", [128, NT], BF16)
    krb = P.sbuf("krb", [64, NT], BF16)

    P.dma("sp", C.cons[:, 0:NC2], cons[:], reads=[cons], writes=[C.cons])
    P.dma("pool", stkb[:], stk[:], writes=[stkb])
    for (a, b, sc) in ((0, 40, np.sqrt(D)), (40, 42, np.sqrt(KVL)), (42, 46, np.sqrt(QL)), (46, 47, np.sqrt(128.0))):
        P.op("dve", lambda e, a=a, b=b, sc=sc: e.tensor_scalar(out=C.gs[:, a:b], in0=C.cons[:, a:b], scalar1=float(sc), scalar2=None, op0=ALU.mult),
             reads=[C.cons], writes=[C.gs])
    P.op("pool", lambda e: e.memset(Sst[:], 0.0), writes=[Sst])
    for r in range(4):
        P.dma("sp", Ul[:], STi[r], reads=[STi], writes=[Ul])
        P.dma("sp", Dl[:], DTi[r], reads=[DTi], writes=[Dl])
        w = C.cons[:, 49 + r:50 + r]
        P.op("dve", lambda e, w=w: e.tensor_scalar(out=Dl[:], in0=Dl[:], scalar1=-1.0, scalar2=w, op0=ALU.add, op1=ALU.mult), reads=[Dl, C.cons], writes=[Dl])
        P.op("dve", lambda e: e.tensor_scalar(out=Dl[:], in0=Dl[:], scalar1=1.0, scalar2=None, op0=ALU.add), reads=[Dl], writes=[Dl])
        P.op("dve", lambda e, w=w: e.tensor_scalar(out=Ul[:], in0=Ul[:], scalar1=w, scalar2=None, op0=ALU.mult), reads=[Ul, C.cons], writes=[Ul])
        for hd in range(NH_A):
            P.op("dve", lambda e, hd=hd: e.scalar_tensor_tensor(out=Sst[:, hd, :], in0=Sst[:, hd, :], scalar=Dl[:, hd:hd + 1], in1=Ul[:, hd, :],
                                                                op0=ALU.mult, op1=ALU.add), reads=[Sst, Dl, Ul], writes=[Sst])
    P.op("act", lambda e: e.copy(out=Sstb[:], in_=Sst[:]), reads=[Sst], writes=[Sstb])

    nt = G // NT
    it = 0
    for g in reversed(range(TOK // G)):
        if g < TOK // G - 1:
            P.dma("sp", C.x[:], x1i[:, :, g * G:(g + 1) * G], reads=[x1i], writes=[C.x])
        rstds2 = [C.rstd, C.rstd2]
        lat1 = [T(latf[:, s_, :], "latf_s%d" % s_) for s_ in range(2)]

        def fix_iter(hd, t, s_):
            tok0 = g * G + t * NT
            o_, q_, g_ = ot[s_], qcl[s_], sgl[s_]
            po, pn, rs, lf = C.ps[s_], C.ps[6 + s_], rstds2[s_], lat1[s_]
            P.dma("sp", o_[:], oTi[hd, :, tok0:tok0 + NT], reads=[oTi], writes=[o_])
            P.dma("sp", q_[:], qci[hd, :, tok0:tok0 + NT], reads=[qci], writes=[q_])
            P.dma("sp", g_[:], sgi[:, hd, tok0:tok0 + NT], reads=[sgi], writes=[g_])
            P.op("pe", lambda e: e.matmul(po[:], lhsT=Sstb[:, hd, :], rhs=q_[:], start=True, stop=True), reads=[Sstb, q_], writes=[po])
            P.op("dve", lambda e: e.tensor_tensor(out=o_[:], in0=o_[:], in1=po[:], op=ALU.add), reads=[o_, po], writes=[o_])
            P.op("act", lambda e: e.activation(out=C.sq[:, s_, :], in_=o_[:], func=AF.Square), reads=[o_], writes=[C.sq])
            P.op("pe", lambda e: e.matmul(pn[:], lhsT=C.ones[:], rhs=C.sq[:, s_, :], start=True, stop=True), reads=[C.ones, C.sq], writes=[pn])
            P.op("act", lambda e: e.activation(out=rs[:], in_=pn[:], func=AF.Ln, bias=float(128 * EPS)), reads=[pn], writes=[rs])
            P.op("act", lambda e: e.activation(out=rs[:], in_=rs[:], func=AF.Exp, scale=-0.5), reads=[rs], writes=[rs])
            P.op("dve", lambda e: e.scalar_tensor_tensor(out=lf[:], in0=o_[:], scalar=C.gs[:, 46:47], in1=rs[:], op0=ALU.mult, op1=ALU.mult),
                 reads=[o_, C.gs, rs], writes=[lf])
            P.op("dve", lambda e: e.tensor_tensor(out=C.h[:, hd, t * NT:(t + 1) * NT], in0=lf[:], in1=g_[:], op=ALU.mult),
                 reads=[lf, g_], writes=[C.h])

        streams = []
        for s_ in range(2):
            P.defer = []
            for hd in range(s_, NH_A, 2):
                for t in range(nt):
                    fix_iter(hd, t, s_)
            streams.append(P.defer)
            P.defer = None
        P.run_streams(streams)
        i = 0
        for m in range(KC):
            ws = None
            for t in range(nt):
                pb = C.ps[2 + i % 2]
                i += 1
                ws = lin_fm(C, wao, m, KC, t * NT, NT, pb, wslot=ws)
                P.op("dve", lambda e, pb=pb, m=m, t=t: e.tensor_tensor(out=C.x[:, m, t * NT:(t + 1) * NT], in0=C.x[:, m, t * NT:(t + 1) * NT], in1=pb[:], op=ALU.add),
                     reads=[pb, C.x], writes=[C.x])
        ffn_group(C, win0, wout0, 0)
        ple_group(C, wpg, wpi, pin, g, 8, pbf)
        rope_table(C, posd, g * G, G, tab, tmpf, tmpi, 47, 48)
        tab_ap, tabT = tab_dst(g)
        P.dma("sp", tab_ap, tab[:], reads=[tab], writes=[tabT])
        tab_done(g)
        for t in range(nt):
            rmsnorm_fm(C, lambda kc, t=t: C.x[:, kc, t * NT:(t + 1) * NT], C.x, KC, t * NT, NT,
                       lambda kc: C.gs[:, 16 + kc:17 + kc],
                       lambda kc, t=t: C.h[:, kc, t * NT:(t + 1) * NT], C.h, C.ps[7], D, src_all=C.x[:, :, t * NT:(t + 1) * NT])
        wsl3 = [None, None, None]
        for t in range(nt):
            tok0 = g * G + t * NT
            for c in range(3):
                pb = C.ps[c]
                wsl3[c] = lin_fm(C, wkd, c, KC, t * NT, NT, pb, wslot=wsl3[c])
                if c < 2:
                    P.op("act", lambda e, c=c, pb=pb: e.copy(out=latf[:, c, :], in_=pb[:]), reads=[pb], writes=[latf])
                else:
                    P.op("dve", lambda e, pb=pb, t=t: e.tensor_tensor(out=rT[:], in0=pb[:], in1=tab[:, t * NT:(t + 1) * NT], op=ALU.mult),
                         reads=[pb, tab], writes=[rT])
            pk = C.ps[3]
            P.op("pe", lambda e, pk=pk: e.matmul(pk[0:64, :], lhsT=stkb[:], rhs=rT[:], start=True, stop=True), reads=[stkb, rT], writes=[pk])
            P.op("act", lambda e, pk=pk: e.copy(out=krb[0:64, :], in_=pk[0:64, :]), reads=[pk], writes=[krb])
            cq_ap, ckv_ap, kr_ap, latT = lat_dst(tok0 // NT)
            P.dma("sp", kr_ap, krb[0:64, :], reads=[krb], writes=[latT])
            rmsnorm_fm(C, lambda kc: latf[:, kc, :], latf, 2, 0, NT, lambda kc: C.gs[:, 40 + kc:41 + kc],
                       lambda kc: latb[:, kc, :], latb, C.ps[7], KVL)
            P.dma("sp", ckv_ap, latb[:, 0:2, :], reads=[latb], writes=[latT])
        ffn_group(C, win1, wout1, 24)
        P.dma("sp", x5o[:, :, g * G:(g + 1) * G], C.x[:], reads=[C.x], writes=[x5o])
        for t in range(nt):
            rmsnorm_fm(C, lambda kc, t=t: C.x[:, kc, t * NT:(t + 1) * NT], C.x, KC, t * NT, NT,
                       lambda kc: C.gs[:, 32 + kc:33 + kc],
                       lambda kc, t=t: C.h[:, kc, t * NT:(t + 1) * NT], C.h, C.ps[7], D, src_all=C.x[:, :, t * NT:(t + 1) * NT])
        wsl4 = [None] * 4
        for t in range(nt):
            tok0 = g * G + t * NT
            for c in range(4):
                pb = C.ps[c]
                wsl4[c] = lin_fm(C, wdq, c, KC, t * NT, NT, pb, wslot=wsl4[c])
                P.op("act", lambda e, c=c, pb=pb: e.copy(out=latf[:, c, :], in_=pb[:]), reads=[pb], writes=[latf])
            rmsnorm_fm(C, lambda kc: latf[:, kc, :], latf, 4, 0, NT, lambda kc: C.gs[:, 42 + kc:43 + kc],
                       lambda kc: latb[:, kc, :], latb, C.ps[7], QL)
            cq_ap, ckv_ap, kr_ap, latT = lat_dst(tok0 // NT)
            P.dma("sp", cq_ap, latb[:], reads=[latb], writes=[latT])
            lat_done(tok0 // NT)


HPC = 4
NQT = SEQ // NT
SCALE = float((128 + ROPE) ** -0.5)


def phase3(P, C0, t, nqt=NQT, nheads=HPC, stop=9):
    lat = t["lat"]
    cq_src = t["cq_src"]
    ckv_src = t["ckv_src"]
    kr_src = t["kr_src"]
    o_dst = t["o_dst"]
    head_done = t["head_done"]
    posd = t["pos"]
    cons = t["cons"]
    stk = t["stk"]
    trid = t["tri"]
    wuq = t["wuq"]
    wuk = t["wuk"]
    wuv = t["wuv"]

    class CC:
        pass
    C = CC()
    C.P = P
    C.ps = P.ps
    C.cons = P.sbuf("cons", [128, 8], F32)
    ones = P.sbuf("ones", [128, 128], BF16)
    onesf = P.sbuf("onesf", [128, 128], F32)
    ckv = P.sbuf("ckv", [128, 2, SEQ], BF16)
    Bk = P.sbuf("Bk", [128, SEQ], BF16)
    Ak = [P.sbuf("Ak%d" % i, [128, SEQ], BF16) for i in range(2)]
    Vh = [P.sbuf("Vh%d" % i, [128, SEQ // 128, 128], BF16) for i in range(2)]
    tab = P.sbuf("tab", [128, SEQ], F32)
    stkb = P.sbuf("stkb", [128, 64], BF16)
    tri = P.sbuf("trib", [128, 128], F32)
    wq = [P.sbuf("wq%d" % i, [128, 2, QL], BF16) for i in range(2)]
    wk = [P.sbuf("wk%d" % i, [128, KVL], BF16) for i in range(2)]
    wv = [P.sbuf("wv%d" % i, [128, KVL], BF16) for i in range(2)]
    mx = P.sbuf("mx", [128, 8], F32)
    mtmp = P.sbuf("setup_tmp", [128, 4096], F32)
    m_alias = P.mark()

    P.dma("sp", C.cons[:], cons[:], reads=[cons], writes=[C.cons])
    P.dma("pool", stkb[:], stk[:], writes=[stkb])
    P.dma("sp", tri[:], trid[:], writes=[tri])
    P.op("pool", lambda e: e.memset(ones[:], 1.0), writes=[ones])
    P.op("pool", lambda e: e.memset(onesf[:], 1.0), writes=[onesf])
    P.op("pool", lambda e: e.memset(Bk[:], 0.0), writes=[Bk])
    P.op("pool", lambda e: e.memset(Bk[64:65, :], 1.0), writes=[Bk])
    P.op("pool", lambda e: e.memset(mx[:], 0.0), writes=[mx])
    for q in range(nqt):
        for kc in range(2):
            P.dma("sp", ckv[:, kc, q * NT:(q + 1) * NT], ckv_src(kc, q), reads=[lat], writes=[ckv])
        P.dma("sp", Bk[0:64, q * NT:(q + 1) * NT], kr_src(q), reads=[lat], writes=[Bk])
    if "tab_src" in t:
        for q8 in range(SEQ // G):
            ap_, T_ = t["tab_src"](q8)
            P.dma("sp", tab[:, q8 * G:(q8 + 1) * G], ap_, reads=[T_], writes=[tab])
    else:
        tmpf = T(mtmp[:, 0:2048], "tmpf")
        tmpi = T(mtmp[:, 2048:4096].bitcast(I32), "tmpi")
        for q4 in range(4):
            tv = T(tab[:, q4 * 2048:(q4 + 1) * 2048], "tabv%d" % q4)
            rope_table(C, posd, q4 * 2048, 2048, tv, tmpf, tmpi, 0, 1)
    P.barrier()
    P.atop = m_alias - 4096 * 4
    aq = [P.sbuf("aq%d" % i, [128, NT], BF16) for i in range(3)]
    bq = [P.sbuf("bq%d" % i, [128, NT], BF16) for i in range(3)]
    cql = [P.sbuf("cql%d" % i, [128, 4, NT], BF16) for i in range(3)]
    sqA = [P.sbuf("sqA%d" % i, [128, NT], BF16) for i in range(2)]
    sqB = [P.sbuf("sqB%d" % i, [128, NT], BF16) for i in range(2)]
    rT = [P.sbuf("rT%d" % i, [128, NT], BF16) for i in range(2)]
    tq = P.sbuf("tq", [128, NT], F32)
    NPT = 6
    pt = [P.sbuf("pt%d" % i, [128, NT], BF16) for i in range(NPT)]
    acc = [[P.sbuf("acc%d_%d" % (j, i), [128, NT], F32) for i in range(2)] for j in range(2)]
    rl = P.sbuf("rl", [128, NT], F32)
    on = [P.sbuf("on%d" % i, [128, NT], BF16) for i in range(2)]
    for i in range(3):
        P.op("pool", lambda e, i=i: e.memset(bq[i][:], 0.0), writes=[bq[i]])
    for i in range(2):
        P.op("pool", lambda e, i=i: e.memset(sqB[i][:], 0.0), writes=[sqB[i]])

    state = {"bank": 0, "wide": True, "sq": 0}

    def nxt_bank():
        state["bank"] += 1
        return C.ps[state["bank"] % 8] if state["wide"] else C.ps[6 + state["bank"] % 2]

    def colmax(pb, col):
        P.op("dve", lambda e: e.tensor_reduce(out=mx[:, 3:4], in_=pb[:], axis=mybir.AxisListType.X, op=ALU.max), reads=[pb], writes=[mx])
        P.op("dve", lambda e: e.tensor_tensor(out=mx[:, col:col + 1], in0=mx[:, col:col + 1], in1=mx[:, 3:4], op=ALU.max), reads=[mx], writes=[mx])

    for kt in range(nqt):
        pb = nxt_bank()
        sB = sqB[kt % 2]
        P.op("act", lambda e, kt=kt, sB=sB: e.activation(out=sB[0:64, :], in_=Bk[0:64, kt * NT:(kt + 1) * NT], func=AF.Square), reads=[Bk], writes=[sB])
        P.op("pe", lambda e, sB=sB, pb=pb: e.matmul(pb[:], lhsT=ones[:], rhs=sB[:], start=True, stop=True), reads=[ones, sB], writes=[pb])
        colmax(pb, 0)

    def load_w(hn):
        P.dma("pool", wq[hn % 2][:, 0, :], wuq[2 * hn], reads=[wuq], writes=[wq[hn % 2]])
        P.dma("pool", wq[hn % 2][:, 1, :], wuq[2 * hn + 1], reads=[wuq], writes=[wq[hn % 2]])
        P.dma("pool", wk[hn % 2][:], wuk[hn], reads=[wuk], writes=[wk[hn % 2]])
        P.dma("pool", wv[hn % 2][:], wuv[hn], reads=[wuv], writes=[wv[hn % 2]])
        P.op("pool", lambda e: e.memset(mx[:, 1 + hn % 2:2 + hn % 2], 0.0), writes=[mx])

    def kv_tile(hn, kt):
        A, V, wkk, wvv = Ak[hn % 2], Vh[hn % 2], wk[hn % 2], wv[hn % 2]
        pb = nxt_bank()
        for kc in range(2):
            P.op("pe", lambda e, kc=kc: e.matmul(pb[:], lhsT=wkk[:, kc * 128:(kc + 1) * 128], rhs=ckv[:, kc, kt * NT:(kt + 1) * NT],
                                                 start=(kc == 0), stop=(kc == 1)), reads=[wkk, ckv], writes=[pb])
        yield
        P.op("act", lambda e: e.copy(out=A[:, kt * NT:(kt + 1) * NT], in_=pb[:]), reads=[pb], writes=[A])
        state["sq"] += 1
        sA = sqA[state["sq"] % 2]
        P.op("act", lambda e: e.activation(out=sA[:], in_=pb[:], func=AF.Square), reads=[pb], writes=[sA])
        yield
        pn = nxt_bank()
        P.op("pe", lambda e: e.matmul(pn[:], lhsT=ones[:], rhs=sA[:], start=True, stop=True), reads=[ones, sA], writes=[pn])
        yield
        colmax(pn, 1 + hn % 2)
        pv = nxt_bank()
        for k4 in range(4):
            kb = kt * 4 + k4
            for kc in range(2):
                P.op("pe", lambda e, kc=kc, kb=kb, k4=k4: e.matmul(pv[:, k4 * 128:(k4 + 1) * 128], lhsT=ckv[:, kc, kb * 128:(kb + 1) * 128],
                                                                   rhs=wvv[:, kc * 128:(kc + 1) * 128], start=(kc == 0), stop=(kc == 1)),
                     reads=[ckv, wvv], writes=[pv])
        yield
        P.op("dve", lambda e: e.tensor_copy(out=V[:, kt * 4:(kt + 1) * 4, :].rearrange("p a b -> p (a b)"), in_=pv[:]), reads=[pv], writes=[V])

    def kv_finish(hn):
        c = 4 + hn % 2
        P.op("dve", lambda e: e.tensor_tensor(out=mx[:, c:c + 1], in0=mx[:, 1 + hn % 2:2 + hn % 2], in1=mx[:, 0:1], op=ALU.add), reads=[mx], writes=[mx])
        yield
        P.op("act", lambda e: e.activation(out=mx[:, c:c + 1], in_=mx[:, c:c + 1], func=AF.Ln), reads=[mx], writes=[mx])
        P.op("act", lambda e: e.activation(out=mx[:, c:c + 1], in_=mx[:, c:c + 1], func=AF.Exp, scale=0.5), reads=[mx], writes=[mx])
        yield
        P.op("dve", lambda e: e.tensor_scalar(out=mx[:, c:c + 1], in0=mx[:, c:c + 1], scalar1=-1.008 * SCALE, scalar2=None, op0=ALU.mult), reads=[mx], writes=[mx])
        yield

    cq_loaded = set()
    order = list(reversed(range(nqt)))

    def slot(hn, qt):
        return (hn * nqt + (nqt - 1 - qt)) % 3

    def at_pos(hn, p):
        if p >= nqt:
            hn, p = hn + 1, p - nqt
        return (hn, order[p]) if hn < nheads else None

    def load_cq(hn, qt):
        g = hn * nqt + qt
        if g in cq_loaded or hn >= nheads:
            return
        cq_loaded.add(g)
        P.dma("sp", cql[slot(hn, qt)][:], cq_src(qt * NT, NT), reads=[lat], writes=[cql[slot(hn, qt)]])

    def q_tile(hn, qt):
        g = hn * nqt + qt
        a, b, cq, wqq = aq[slot(hn, qt)], bq[slot(hn, qt)], cql[slot(hn, qt)], wq[hn % 2]
        if g not in cq_loaded:
            load_cq(hn, qt)
        pa = nxt_bank()
        pr = nxt_bank()
        for half, pb in ((0, pa), (1, pr)):
            for kc in range(4):
                P.op("pe", lambda e, kc=kc, half=half, pb=pb: e.matmul(pb[:], lhsT=wqq[:, half, kc * 128:(kc + 1) * 128], rhs=cq[:, kc, :],
                                                                       start=(kc == 0), stop=(kc == 3)), reads=[wqq, cq], writes=[pb])
        yield
        P.op("act", lambda e: e.activation(out=a[:], in_=pa[:], func=AF.Identity, scale=SCALE), reads=[pa], writes=[a])
        state["sq"] += 1
        sA, sB = sqA[state["sq"] % 2], sqB[state["sq"] % 2]
        P.op("act", lambda e: e.activation(out=sA[:], in_=pa[:], func=AF.Square), reads=[pa], writes=[sA])
        r = rT[g % 2]
        P.op("dve", lambda e: e.tensor_tensor(out=r[:], in0=pr[:], in1=tab[:, qt * NT:(qt + 1) * NT], op=ALU.mult), reads=[pr, tab], writes=[r])
        yield
        pk = nxt_bank()
        P.op("pe", lambda e: e.matmul(pk[0:64, :], lhsT=stkb[:], rhs=r[:], start=True, stop=True), reads=[stkb, r], writes=[pk])
        yield
        P.op("act", lambda e: e.activation(out=b[0:64, :], in_=pk[0:64, :], func=AF.Identity, scale=SCALE), reads=[pk], writes=[b])
        P.op("act", lambda e: e.activation(out=sB[0:64, :], in_=pk[0:64, :], func=AF.Square), reads=[pk], writes=[sB])
        yield
        pn = nxt_bank()
        P.op("pe", lambda e: e.matmul(pn[:], lhsT=ones[:], rhs=sA[:], start=True, stop=False), reads=[ones, sA], writes=[pn])
        P.op("pe", lambda e: e.matmul(pn[:], lhsT=ones[:], rhs=sB[:], start=False, stop=True), reads=[ones, sB], writes=[pn])
        yield
        P.op("act", lambda e: e.activation(out=tq[64:65, :], in_=pn[64:65, :], func=AF.Ln), reads=[pn], writes=[tq])
        P.op("act", lambda e: e.activation(out=tq[64:65, :], in_=tq[64:65, :], func=AF.Exp, scale=0.5), reads=[tq], writes=[tq])
        yield
        P.op("dve", lambda e: e.tensor_scalar(out=b[64:65, :], in0=tq[64:65, :], scalar1=mx[64:65, 4 + hn % 2:5 + hn % 2], scalar2=None, op0=ALU.mult),
             reads=[tq, mx], writes=[b])

    if stop <= 1:
        return
    def run(gen):
        for _ in gen:
            pass

    def chain(*gens):
        for g_ in gens:
            yield from g_

    load_w(0)
    for kt in range(nqt):
        run(kv_tile(0, kt))
    run(kv_finish(0))
    if stop <= 2:
        return
    run(q_tile(0, order[0]))
    run(q_tile(0, order[1]))
    load_cq(*at_pos(0, 2))
    state["wide"] = False
    if stop <= 3 or stop >= 30:
        return

    side = []
    qgen = {}
    for h in range(nheads):
        if h + 1 < nheads:
            load_w(h + 1)
        blocks = [(qt, kb) for qt in order for kb in range(4 * qt + 4)]

        def geom(qt, kb):
            c0 = 0 if kb < 4 * qt else (kb - 4 * qt) * 128
            return c0, NT - c0

        def emit_qk(i):
            qt, kb = blocks[i]
            c0, ncol = geom(qt, kb)
            a, b = aq[slot(h, qt)], bq[slot(h, qt)]
            A_ = Ak[h % 2]
            pss = C.ps[i % 3]
            P.op("pe", lambda e: e.matmul(pss[:, 0:ncol], lhsT=A_[:, kb * 128:(kb + 1) * 128], rhs=a[:, c0:NT], start=True, stop=False),
                 reads=[A_, a], writes=[pss])
            P.op("pe", lambda e: e.matmul(pss[:, 0:ncol], lhsT=Bk[:, kb * 128:(kb + 1) * 128], rhs=b[:, c0:NT], start=False, stop=True),
                 reads=[Bk, b], writes=[pss])

        def emit_rest(i):
            qt, kb = blocks[i]
            nkb = 4 * qt + 4
            c0, ncol = geom(qt, kb)
            pss = C.ps[i % 3]
            p = pt[i % NPT]
            po = C.ps[3 + qt % 2]
            ac = acc[qt % 2]
            V_ = Vh[h % 2]
            if kb == 0:
                P.op("pool", lambda e: e.memset(ac[0][:], 0.0), writes=[ac[0]])
                P.op("pool", lambda e: e.memset(ac[1][:], 0.0), writes=[ac[1]])
            P.op("act", lambda e: e.activation(out=p[:, 0:ncol], in_=pss[:, 0:ncol], func=AF.Exp), reads=[pss], writes=[p])
            if kb >= 4 * qt:
                P.op("pool", lambda e: e.tensor_tensor(out=p[:, 0:128], in0=p[:, 0:128], in1=tri[:], op=ALU.mult), reads=[p, tri], writes=[p])
            ai = i % 2
            P.op("dve" if ai == 0 else "pool", lambda e: e.tensor_tensor(out=ac[ai][:, c0:NT], in0=ac[ai][:, c0:NT], in1=p[:, 0:ncol], op=ALU.add),
                 reads=[p, ac[ai]], writes=[ac[ai]])
            P.op("pe", lambda e: e.matmul(po[:, c0:NT], lhsT=V_[:, kb, :], rhs=p[:, 0:ncol], start=(kb == 0), stop=(kb == nkb - 1)),
                 reads=[V_, p], writes=[po])
            if kb == nkb - 1:
                pending.append((i + 2, lambda qt=qt, po=po, ac=ac: finalize(qt, po, ac)))
            if kb == 0:
                ppos = nqt - 1 - qt
                nx = at_pos(h, ppos + 2)
                if nx is not None:
                    qgen[nx] = q_tile(*nx)
                    side.append(qgen[nx])
                nx3 = at_pos(h, ppos + 3)
                if nx3 is not None:
                    load_cq(*nx3)
                if h + 1 < nheads:
                    for kt_ in (2 * ppos, 2 * ppos + 1):
                        if kt_ < nqt:
                            side.append(kv_tile(h + 1, kt_))
                    if ppos == nqt // 2:
                        side.append(kv_finish(h + 1))

        def finalize(qt, po, ac):
            pl = C.ps[5]
            P.op("pool", lambda e: e.tensor_tensor(out=ac[1][:], in0=ac[1][:], in1=ac[0][:], op=ALU.add), reads=[ac[0], ac[1]], writes=[ac[1]])
            P.op("pe", lambda e: e.matmul(pl[:], lhsT=onesf[:], rhs=ac[1][:], start=True, stop=True), reads=[onesf, ac[1]], writes=[pl])
            P.op("act", lambda e: e.activation(out=rl[:], in_=pl[:], func=AF.Ln), reads=[pl], writes=[rl])
            P.op("act", lambda e: e.activation(out=rl[:], in_=rl[:], func=AF.Exp, scale=-1.0), reads=[rl], writes=[rl])
            o = on[qt % 2]
            P.op("dve", lambda e: e.tensor_tensor(out=o[:], in0=po[:], in1=rl[:], op=ALU.mult), reads=[po, rl], writes=[o])
            oap, oT_ = o_dst(h, qt * NT, NT)
            P.dma("sp", oap, o[:], reads=[o], writes=[oT_])
            if qt % 8 == 0:
                head_done(h, qt // 8)

        LOOK = 2
        pending = []

        def qk(j):
            qt_, kb_ = blocks[j]
            if kb_ == 0 and (h, qt_) in qgen:
                gq = qgen.pop((h, qt_))
                while side:
                    g0 = side.pop(0)
                    run(g0)
                    if g0 is gq:
                        break
            emit_qk(j)

        for i in range(min(LOOK, len(blocks))):
            qk(i)
        for i in range(len(blocks)):
            if i + LOOK < len(blocks):
                qk(i + LOOK)
            emit_rest(i)
            while pending and (pending[0][0] <= i or i == len(blocks) - 1):
                pending.pop(0)[1]()
            if side:
                try:
                    next(side[0])
                except StopIteration:
                    side.pop(0)
        for g_ in side:
            run(g_)
        side.clear()


def phase4(P, C, t):
    x5i = t["x5i"]
    load_o = t["load_o"]
    pin = t["pin"]
    cons = t["cons"]
    wbo = t["wbo"]
    win = t["win"]
    wout = t["wout"]
    wpg = t["wpg"]
    wpi = t["wpi"]
    yo = t["yo"]
    ob = P.sbuf("ob", [128, NH_B, G], BF16)
    pbf = P.sbuf("pbf", [128, 2, G], BF16)
    yf = P.sbuf("yf", [128, KC, NT], F32)
    P.dma("sp", C.cons[:, 0:24], cons[:], reads=[cons], writes=[C.cons])
    P.op("dve", lambda e: e.tensor_scalar(out=C.gs[:, 0:24], in0=C.cons[:, 0:24], scalar1=float(np.sqrt(D)), scalar2=None, op0=ALU.mult),
         reads=[C.cons], writes=[C.gs])
    nt = G // NT
    for g in range(TOK // G):
        P.dma("sp", C.x[:], x5i[:, :, g * G:(g + 1) * G], reads=[x5i], writes=[C.x])
        load_o(g, ob)
        i = 0
        for m in range(KC):
            ws = None
            for t in range(nt):
                pb = C.ps[i % 2]
                i += 1
                ws = lin_fm(C, wbo, m, NH_B, t * NT, NT, pb, wslot=ws, src=ob)
                P.op("dve", lambda e, pb=pb, m=m, t=t: e.tensor_tensor(out=C.x[:, m, t * NT:(t + 1) * NT], in0=C.x[:, m, t * NT:(t + 1) * NT], in1=pb[:], op=ALU.add),
                     reads=[pb, C.x], writes=[C.x])
        ffn_group(C, win, wout, 0)
        ple_group(C, wpg, wpi, pin, g, 8, pbf)
        for t in range(nt):
            rmsnorm_fm(C, lambda kc, t=t: C.x[:, kc, t * NT:(t + 1) * NT], C.x, KC, t * NT, NT,
                       lambda kc: C.gs[:, 16 + kc:17 + kc], lambda kc: yf[:, kc, :], yf, C.ps[7], D, src_all=C.x[:, :, t * NT:(t + 1) * NT])
            P.dma("sp", yo[:, :, g * G + t * NT:g * G + (t + 1) * NT], yf[:], reads=[yf], writes=[yo])


def xfm(x):
    return np.ascontiguousarray(np.ascontiguousarray(x.T).reshape(-1, 128, x.shape[0]).transpose(1, 0, 2))


def unfm(a):
    a = np.asarray(a)
    return np.ascontiguousarray(a.transpose(1, 0, 2).reshape(-1, a.shape[2]).T)


def _rope_consts():
    invf = (1.0 / (np.float32(10000.0) ** (np.arange(0, ROPE, 2, dtype=np.float32) / np.float32(ROPE)))).astype(np.float32)
    invf4 = np.tile(invf, 4)
    ph = np.concatenate([np.full(64, np.pi / 2), np.full(32, np.pi), np.zeros(32)]).astype(np.float32)
    return invf4, ph


def _hmask():
    s = np.arange(128)[:, None]
    t = np.arange(128)[None, :]
    m = ((s // 64 == t // 64) & (s <= t)).astype(np.float32)
    return np.ascontiguousarray(np.tile(m, (1, 4)))


_NC_CACHE = {}
_PREP_ONLY = False

W_SHAPES = {
    "ffn00_in": [NJ, 128, KC * 256], "ffn00_out": [KC, 128, NJ * 128], "ffn01_in": [NJ, 128, KC * 256], "ffn01_out": [KC, 128, NJ * 128],
    "ffn10_in": [NJ, 128, KC * 256], "ffn10_out": [KC, 128, NJ * 128], "ffn11_in": [NJ, 128, KC * 256], "ffn11_out": [KC, 128, NJ * 128],
    "wq": [8, 128, D], "wf": [8, 128, D], "wg": [8, 128, D], "wv": [128, KC * D], "wao": [8, 128, D],
    "wpg0": [8, 128, D], "wpi0": [8, 128, 256], "wkd": [3, 128, D], "wdq": [4, 128, D],
    "wuq": [HPC * 2, 128, QL], "wuk": [HPC, 128, KVL], "wuv": [HPC, 128, KVL],
    "wbo": [8, 128, NH_B * 128], "wpg1": [8, 128, D], "wpi1": [8, 128, 256],
    "cons1": [128, 32], "cons2": [128, NC2], "cons3": [128, 8], "cons4": [128, 24],
    "ident": [128, 128], "hmask": [128, 512], "stk": [128, 64], "tri": [128, 128],
    "xin": [128, KC, TOK], "p0": [128, 2, TOK], "p1": [128, 2, TOK],
}


def build_fused(upto=4):
    P = Prog()
    nc = P.nc
    E = {k: P.dram(k, v, F32, "ExternalInput") for k, v in W_SHAPES.items()}
    E["posl"] = P.dram("posl", [1, TOK], I32, "ExternalInput")
    E["posb"] = P.dram("posb", [1, SEQ], I32, "ExternalInput")
    E["idxo"] = P.dram("idxo", [128, 8], I32, "ExternalInput")
    yo = P.dram("yo", [128, KC, TOK], F32, "ExternalOutput")

    def internal(name, shape, dtype):
        return T(nc.dram_tensor(name, list(shape), dtype), name)

    x1 = internal("x1", [128, KC, TOK], F32)
    oT = internal("oT", [NH_A, 128, TOK], F32)
    qc = internal("qc", [NH_A, 128, TOK], BF16)
    sg = internal("sg", [128, KC, TOK], F32)
    SDin = internal("SDin", [128, 1032], F32)
    SDall = internal("SDall", [512, 1032], F32)
    x5 = internal("x5", [128, KC, TOK], F32)
    NTL = TOK // NT
    latin = [internal("latin%d" % i, [128, 7 * NT], BF16) for i in range(NTL)]
    latall = [internal("latall%d" % i, [512, 7 * NT], BF16) for i in range(NTL)]
    latdep = Buf("latall")
    for q in latall:
        q.b = latdep
    ohin = [[internal("ohin%d_%d" % (h, hf), [128, SEQ // 2], BF16) for hf in range(2)] for h in range(HPC)]
    tabin = [internal("tabin%d" % i, [128, G], F32) for i in range(TOK // G)]
    taball = [internal("taball%d" % i, [512, G], F32) for i in range(TOK // G)]

    def tab_done(g):
        P.coll("AllGather", tabin[g].t.ap().opt(), taball[g].t.ap().opt(), reads=[tabin[g]], writes=[taball[g]])

    def tab_src(q8):
        rr, g = q8 // (TOK // G), q8 % (TOK // G)
        return taball[g].t[rr * 128:(rr + 1) * 128, :], taball[g]
    ohall = [internal("ohall%d" % h, [1024, SEQ // 2], BF16) for h in range(HPC)]

    m0 = P.mark()
    C = Ctx(P)
    mC = P.mark()
    STo = T(SDin.t[:, 0:1024].rearrange("p (h d) -> p h d", h=NH_A), "STo")
    DTo = T(SDin.t[:, 1024:1032], "DTo")
    phase1(P, C, {"xin": E["xin"], "win": E["ffn00_in"], "wout": E["ffn00_out"], "wq": E["wq"], "wf": E["wf"], "wg": E["wg"], "wv": E["wv"],
                  "cons": E["cons1"], "ident": E["ident"], "hmask": E["hmask"], "x1o": x1, "oTo": oT, "qco": qc, "sgo": sg, "STo": STo, "DTo": DTo})
    P.reset(mC)
    P.coll("AllGather", SDin.t.ap().opt(), SDall.t.ap().opt(), reads=[STo, DTo], writes=[SDall])
    if upto == 1:
        P.barrier()
        for g in range(TOK // G):
            tmpx = P.sbuf("tmpx1", [128, KC, G], F32)
            P.dma("sp", tmpx[:], x1[:, :, g * G:(g + 1) * G], reads=[x1], writes=[tmpx])
            P.dma("sp", yo[:, :, g * G:(g + 1) * G], tmpx[:], reads=[tmpx], writes=[yo])
        return P.finish()
    STi = TV(lambda r: SDall.t[r * 128:(r + 1) * 128, 0:1024].rearrange("p (h d) -> p h d", h=NH_A), "STi")
    STi.b = SDall.b
    DTi = TV(lambda r: SDall.t[r * 128:(r + 1) * 128, 1024:1032], "DTi")
    DTi.b = SDall.b
    def lat_dst(i):
        v = latin[i].t[:, :].rearrange("p (c t) -> p c t", c=7)
        return v[:, 0:4, :], v[:, 4:6, :], v[0:64, 6, :], latin[i]

    def lat_done(i):
        P.coll("AllGather", latin[i].t.ap().opt(), latall[i].t.ap().opt(), reads=[latin[i]], writes=[latall[i]])

    phase2(P, C, {"x1i": x1, "oTi": oT, "qci": qc, "sgi": sg, "STi": STi, "DTi": DTi, "pin": E["p0"], "pos": E["posl"], "cons": E["cons2"],
                  "stk": E["stk"], "wao": E["wao"], "win0": E["ffn01_in"], "wout0": E["ffn01_out"], "wpg": E["wpg0"], "wpi": E["wpi0"],
                  "wkd": E["wkd"], "win1": E["ffn10_in"], "wout1": E["ffn10_out"], "wdq": E["wdq"],
                  "x5o": x5, "lat_dst": lat_dst, "lat_done": lat_done,
                  "tab_dst": lambda g: (tabin[g].t[:, :], tabin[g]), "tab_done": tab_done})
    P.reset(m0)
    if upto == 2:
        P.barrier()
        for g in range(TOK // G):
            tmpx = P.sbuf("tmpx2", [128, KC, G], F32)
            P.dma("sp", tmpx[:], x5[:, :, g * G:(g + 1) * G], reads=[x5], writes=[tmpx])
            P.dma("sp", yo[:, :, g * G:(g + 1) * G], tmpx[:], reads=[tmpx], writes=[yo])
        return P.finish()
    def latv(q):
        rr, i = q // NTL, q % NTL
        return latall[i].t[rr * 128:(rr + 1) * 128, :].rearrange("p (c t) -> p c t", c=7)

    def head_done(h, hf):
        P.coll("AllGather", ohin[h][hf].t.ap().opt(), ohall[h].t[hf * 512:(hf + 1) * 512, :], reads=[ohin[h][hf]], writes=[ohall[h]])

    def o_dst(h, t0, n):
        hf, tl = t0 // (SEQ // 2), t0 % (SEQ // 2)
        return ohin[h][hf].t[:, tl:tl + n], ohin[h][hf]

    phase3(P, None, {"lat": latall[0], "cq_src": lambda t0, n: latv(t0 // NT)[:, 0:4, :],
                     "ckv_src": lambda kc, q: latv(q)[:, 4 + kc, :], "kr_src": lambda q: latv(q)[0:64, 6, :],
                     "o_dst": o_dst, "head_done": head_done, "tab_src": tab_src,
                     "pos": E["posb"], "cons": E["cons3"], "stk": E["stk"], "tri": E["tri"], "wuq": E["wuq"], "wuk": E["wuk"], "wuv": E["wuv"]})
    P.reset(m0)
    if upto == 3:
        P.barrier()
        for g in range(TOK // G):
            tmpx = P.sbuf("tmpx3", [128, KC, G], F32)
            P.dma("sp", tmpx[:], x5[:, :, g * G:(g + 1) * G], reads=[x5], writes=[tmpx])
            P.dma("sp", yo[:, :, g * G:(g + 1) * G], tmpx[:], reads=[tmpx], writes=[yo])
        return P.finish()
    C = Ctx(P)
    idxs = P.sbuf("idxs", [128, 8], I32)
    P.dma("sp", idxs[:], E["idxo"][:], reads=[E["idxo"]], writes=[idxs])

    def load_o(g, ob):
        for h in range(HPC):
            rows = ohall[h].t[:, :].rearrange("r (b t) -> (r b) t", t=G)
            for rr in range(4):
                P.gather(ob[:, rr * HPC + h, :], rows, idxs[:, rr * 2 + g:rr * 2 + g + 1], reads=[ohall[h], idxs], writes=[ob])

    phase4(P, C, {"x5i": x5, "load_o": load_o, "pin": E["p1"], "cons": E["cons4"], "wbo": E["wbo"], "win": E["ffn11_in"], "wout": E["ffn11_out"],
                  "wpg": E["wpg1"], "wpi": E["wpi1"], "yo": yo})
    print("fused instructions", P.nins)
    return P.finish()


def _get(name, fn):
    if name not in _NC_CACHE:
        _NC_CACHE[name] = fn()
    return _NC_CACHE[name]


def kernel(x, p, positions, norm_gains, ffn_w_in, ffn_w_out, ple_w_gate, ple_w_in,
           a_w_in, a_lb_logits, a_out_gain, a_w_out,
           kv_norm_in, kv_w_down, kv_latent_norm, kv_w_up,
           b_w_dq, b_q_norm, b_w_uq, b_w_out, final_norm):
    f32 = lambda a: np.ascontiguousarray(np.asarray(a, dtype=np.float32))
    x, p, norm_gains, ffn_w_in, ffn_w_out = f32(x), f32(p), f32(norm_gains), f32(ffn_w_in), f32(ffn_w_out)
    ple_w_gate, ple_w_in, a_w_in, a_lb_logits, a_out_gain, a_w_out = map(f32, (ple_w_gate, ple_w_in, a_w_in, a_lb_logits, a_out_gain, a_w_out))
    kv_norm_in, kv_w_down, kv_latent_norm, kv_w_up = map(f32, (kv_norm_in, kv_w_down, kv_latent_norm, kv_w_up))
    b_w_dq, b_q_norm, b_w_uq, b_w_out, final_norm = map(f32, (b_w_dq, b_q_norm, b_w_uq, b_w_out, final_norm))
    positions = np.ascontiguousarray(np.asarray(positions, dtype=np.int32))
    cores = list(range(NCORES))
    invf4, ph = _rope_consts()
    g = norm_gains
    aw = a_w_in[0]
    wd = kv_w_down
    wkd = np.concatenate([wd[:, 0:256], wd[:, 256:320], wd[:, 288:320], wd[:, 256:288]], axis=1)
    cons3 = np.zeros((128, 8), np.float32)
    cons3[:, 0] = invf4
    cons3[:, 1] = ph
    kk = np.arange(128)[:, None]
    qq = np.arange(128)[None, :]
    shared = {
        "ffn00_in": tile_ffn_in(ffn_w_in[0, 0]), "ffn00_out": tile_ffn_out(ffn_w_out[0, 0]),
        "ffn01_in": tile_ffn_in(ffn_w_in[0, 1]), "ffn01_out": tile_ffn_out(ffn_w_out[0, 1]),
        "ffn10_in": tile_ffn_in(ffn_w_in[1, 0]), "ffn10_out": tile_ffn_out(ffn_w_out[1, 0]),
        "ffn11_in": tile_ffn_in(ffn_w_in[1, 1]), "ffn11_out": tile_ffn_out(ffn_w_out[1, 1]),
        "wq": tile_lin(aw[:, 0:1024]), "wf": tile_lin(aw[:, 1024:2048]), "wg": tile_lin(aw[:, 3072:4096]),
        "wv": tile_rows(aw[:, 2048:3072]).reshape(128, -1), "wao": tile_lin(a_w_out[0]),
        "wpg0": tile_lin(ple_w_gate[0]), "wpi0": tile_lin(ple_w_in[0]), "wkd": tile_lin(wkd), "wdq": tile_lin(b_w_dq[0]),
        "wbo": tile_lin(b_w_out[0]), "wpg1": tile_lin(ple_w_gate[1]), "wpi1": tile_lin(ple_w_in[1]),
        "cons1": np.ascontiguousarray(np.concatenate([col(g[0, 0]), col(g[0, 1]), col(a_lb_logits[0]), col(a_lb_logits[1])], axis=1)),
        "cons3": cons3,
        "cons4": np.ascontiguousarray(np.concatenate([col(g[1, 2]), col(g[1, 3]), col(final_norm)], axis=1)),
        "ident": np.eye(128, dtype=np.float32), "hmask": _hmask(),
        "stk": np.concatenate([np.eye(64), np.eye(64)], axis=0).astype(np.float32), "tri": (kk <= qq).astype(np.float32),
    }
    uq = b_w_uq[0]
    per_r = []
    for r in range(4):
        wuq, wuk, wuv = [], [], []
        for h in range(4 * r, 4 * r + 4):
            q = uq[:, h * 192:(h + 1) * 192]
            wuq.append(tile_lin(q[:, 0:128])[0])
            wuq.append(tile_lin(np.concatenate([q[:, 128:192], q[:, 160:192], q[:, 128:160]], axis=1))[0])
            kv = kv_w_up[:, h * 256:(h + 1) * 256]
            wuk.append(tile_lin(kv[:, 0:128])[0])
            wuv.append(tile_rows(kv[:, 128:256]).reshape(128, 256))
        wsel = np.zeros((128, 4), np.float32)
        wsel[:, :r] = 1.0
        cons2 = np.concatenate([col(g[0, 2]), col(g[0, 3]), col(kv_norm_in), col(g[1, 0]), col(g[1, 1]), col(kv_latent_norm),
                                col(b_q_norm[0]), col(a_out_gain[0]), invf4[:, None], ph[:, None], wsel,
                                np.zeros((128, NC2 - 53), np.float32)], axis=1).astype(np.float32)
        idxo = np.zeros((128, 8), np.int32)
        pp = np.arange(128)
        for rr in range(4):
            for gg in range(2):
                tb = r * 2 + gg
                idxo[:, rr * 2 + gg] = ((tb // 4) * 512 + rr * 128 + pp) * 4 + tb % 4
        per_r.append({"wuq": np.stack(wuq), "wuk": np.stack(wuk), "wuv": np.stack(wuv), "cons2": np.ascontiguousarray(cons2), "idxo": idxo})
    in_maps = []
    for c in cores:
        b, r = c // 4, c % 4
        sl = slice(r * TOK, (r + 1) * TOK)
        in_maps.append({"xin": xfm(x[b, sl]), "p0": xfm(p[0, b, sl]), "p1": xfm(p[1, b, sl]),
                        "posl": np.ascontiguousarray(positions[b:b + 1, sl]), "posb": np.ascontiguousarray(positions[b:b + 1]),
                        **shared, **per_r[r]})
    if _PREP_ONLY:
        return in_maps
    res = run_bass_kernel_spmd(_get("fused", build_fused), in_maps, core_ids=cores).results
    out = np.empty((2, SEQ, D), np.float32)
    for c in cores:
        b, r = c // 4, c % 4
        out[b, r * TOK:(r + 1) * TOK] = unfm(res[c]["yo"])
    return out
```
